# Optimizing a Trainium2 kernel written in Bass

```python
import math
import jax, jax.numpy as jnp
from jax import lax
import numpy as np

D_MODEL = 1024
BATCH = 2
SEQ = 16384
DEPTH = 1
DEC_BATCH = 8
DEC_SEQ = 8192
PAST_LEN = 128

N_HEADS = 12
HEAD_DIM = 64
D_ATTN = N_HEADS * HEAD_DIM
WINDOWS = (128, 512, 2048)
DILATIONS = (1, 4, 16)
N_BUCKETS = 32
MAX_DISTANCE = 1024
D_HYENA = 768
HYENA_ORDER = 2
SHORT_WIDTH = 3
FILTER_EMB = 33
FILTER_HIDDEN = 64
FAST_DECAY_PCT = 0.3
SLOW_DECAY_PCT = 1.5
DECAY_TARGET = 0.01
EPS = 1e-6
NEG_INF = -1e30
D_PROJ = 4 * D_ATTN + (HYENA_ORDER + 2) * D_HYENA + 2 * D_MODEL
SPLIT_POINTS = (D_ATTN, 2 * D_ATTN, 3 * D_ATTN, 4 * D_ATTN,
                4 * D_ATTN + (HYENA_ORDER + 1) * D_HYENA,
                4 * D_ATTN + (HYENA_ORDER + 2) * D_HYENA,
                4 * D_ATTN + (HYENA_ORDER + 2) * D_HYENA + D_MODEL)

kernel_name = 'hybrid_dilated_attn_hyena_encoder'


def _rms_norm(x, g):
    xf = x.astype(jnp.float32)
    y = xf * lax.rsqrt(jnp.mean(xf * xf, axis=-1, keepdims=True) + EPS)
    return (y * g.astype(jnp.float32)).astype(x.dtype)


def _t5_bucket(rel):
    nb = N_BUCKETS // 2
    max_exact = nb // 2
    n = np.abs(rel)
    large = max_exact + (np.log(np.maximum(n, 1) / max_exact) / math.log(MAX_DISTANCE / max_exact)
                         * (nb - max_exact)).astype(np.int32)
    large = np.minimum(large, nb - 1)
    return ((rel > 0).astype(np.int32) * nb + np.where(n < max_exact, n, large)).astype(np.int32)


def _dilated_band_attention(q, k, v, r, half, rel_bias):
    B, L, H, Dh = q.shape
    blk = half
    Lr = L // r
    nb = -(-Lr // blk)
    Lp = nb * blk

    def to_strided(t):
        return t.reshape(B, Lr, r, H, Dh).transpose(0, 2, 1, 3, 4).reshape(B * r, Lr, H, Dh)

    def key_bands(t):
        tp = jnp.pad(to_strided(t), ((0, 0), (blk, Lp - Lr + blk), (0, 0), (0, 0)))
        tp = tp.reshape(B * r, nb + 2, blk, H, Dh)
        return jnp.concatenate([tp[:, :-2], tp[:, 1:-1], tp[:, 2:]], axis=2)

    qs = jnp.pad(to_strided(q), ((0, 0), (0, Lp - Lr), (0, 0), (0, 0))).reshape(B * r, nb, blk, H, Dh)
    kb = key_bands(k)
    vb = key_bands(v)

    a = np.arange(blk)[:, None]
    b = np.arange(3 * blk)[None, :]
    rel = b - blk - a
    kpos = np.arange(nb)[:, None] * blk - blk + np.arange(3 * blk)[None, :]
    valid = (np.abs(rel) <= half)[None] & ((kpos >= 0) & (kpos < Lr))[:, None, :]
    bias = rel_bias[_t5_bucket(rel * r)].astype(jnp.float32).transpose(2, 0, 1)

    logits = jnp.einsum('znqhd,znkhd->znhqk', qs.astype(jnp.float32), kb.astype(jnp.float32)) \
        * (1.0 / math.sqrt(Dh)) + bias[None, None]
    logits = jnp.where(valid[None, :, None], logits, NEG_INF)
    m = jnp.max(logits, axis=-1, keepdims=True)
    p = jnp.exp(logits - m)
    s = jnp.sum(p, axis=-1, keepdims=True)
    o = jnp.einsum('znhqk,znkhd->znqhd', p, vb.astype(jnp.float32)) / jnp.swapaxes(s, 2, 3)
    lse = jnp.swapaxes((m + jnp.log(s))[..., 0], 2, 3)

    o = o.reshape(B * r, Lp, H, Dh)[:, :Lr]
    lse = lse.reshape(B * r, Lp, H)[:, :Lr]
    o = o.reshape(B, r, Lr, H, Dh).swapaxes(1, 2).reshape(B, L, H, Dh)
    lse = lse.reshape(B, r, Lr, H).swapaxes(1, 2).reshape(B, L, H)
    return o, lse


def _dilated_mixture_attention(q, k, v, rel_bias):
    outs, lses = [], []
    for w, r in zip(WINDOWS, DILATIONS):
        o, lse = _dilated_band_attention(q, k, v, r, w // (2 * r), rel_bias)
        outs.append(o)
        lses.append(lse)
    alpha = jax.nn.softmax(jnp.stack(lses, axis=0), axis=0)
    return jnp.sum(alpha[..., None] * jnp.stack(outs, axis=0), axis=0)


def _short_conv(x, w, b):
    xp = jnp.pad(x, ((0, 0), (1, 1), (0, 0)))
    return xp[:, :-2] * w[0] + xp[:, 1:-1] * w[1] + xp[:, 2:] * w[2] + b


def _hyena_filters(L, w1, b1, fr1, w2, b2, fr2, w3):
    t = jnp.linspace(0.0, 1.0, L, dtype=jnp.float32)[:, None]
    bands = (FILTER_EMB - 1) // 2
    w = (2.0 * math.pi / L) * jnp.arange(L, dtype=jnp.float32)[:, None]
    f = jnp.linspace(1e-4, bands - 1, bands, dtype=jnp.float32)[None, :]
    z = jnp.concatenate([t, jnp.cos(f * w), -jnp.sin(f * w)], axis=-1)
    hdn = jnp.sin(fr1 * (z @ w1 + b1))
    hdn = jnp.sin(fr2 * (hdn @ w2 + b2))
    filt = (hdn @ w3).astype(jnp.float32).reshape(L, HYENA_ORDER, 2, D_HYENA)
    deltas = jnp.abs(jnp.linspace(math.log(DECAY_TARGET) / SLOW_DECAY_PCT,
                                  math.log(DECAY_TARGET) / FAST_DECAY_PCT, D_HYENA, dtype=jnp.float32))
    decay = jnp.exp(-t * deltas[None, :])
    return filt * decay[:, None, None, :]


def _two_sided_kernel(hf, hb):
    return jnp.concatenate([hf.at[0].add(hb[0]), jnp.zeros_like(hf[:1]), hb[:0:-1]], axis=0)


def _long_conv(u, kern, d_skip):
    L = u.shape[1]
    uf32 = u.astype(jnp.float32)
    uf = jnp.fft.rfft(uf32, n=2 * L, axis=1)
    kf = jnp.fft.rfft(kern, n=2 * L, axis=0)
    y = jnp.fft.irfft(uf * kf[None], n=2 * L, axis=1)[:, :L]
    return (y + uf32 * d_skip.astype(jnp.float32)).astype(u.dtype)


def _hyena(u, short_w, short_b, filters, hyena_d):
    u = _short_conv(u, short_w, short_b)
    parts = jnp.split(u, HYENA_ORDER + 1, axis=-1)
    z = parts[0]
    for n in range(HYENA_ORDER):
        kern = _two_sided_kernel(filters[:, n, 0], filters[:, n, 1])
        z = parts[n + 1] * _long_conv(z, kern, hyena_d[n])
    return z


def _encoder_layer(x, c, w_ada, b_ada, norm_g, w_in, short_w, short_b, filt_w1, filt_b1, filt_freq1,
                   filt_w2, filt_b2, filt_freq2, filt_w3, hyena_d, w_proj_attn, w_proj_hyena, w_out, rel_bias):
    B, L, _ = x.shape
    shift, scale, gate = jnp.split(jax.nn.silu(c) @ w_ada + b_ada, 3, axis=-1)
    h = _rms_norm(x, norm_g) * (1.0 + scale[:, None, :]) + shift[:, None, :]
    proj = h @ w_in
    q, k, v, g_attn, u_hy, g_hy, m_attn, m_hy = jnp.split(proj, SPLIT_POINTS, axis=-1)

    heads = lambda t: t.reshape(B, L, N_HEADS, HEAD_DIM)
    o_attn = _dilated_mixture_attention(heads(q), heads(k), heads(v), rel_bias)
    o_attn = o_attn.reshape(B, L, D_ATTN).astype(x.dtype)
    y_attn = (o_attn * jax.nn.silu(g_attn)) @ w_proj_attn

    filters = _hyena_filters(L, filt_w1, filt_b1, filt_freq1, filt_w2, filt_b2, filt_freq2, filt_w3)
    o_hy = _hyena(u_hy, short_w, short_b, filters, hyena_d)
    y_hy = (o_hy * jax.nn.silu(g_hy)) @ w_proj_hyena

    mixed = jax.nn.sigmoid(m_attn) * y_attn + jax.nn.sigmoid(m_hy) * y_hy
    return x + gate[:, None, :] * (mixed @ w_out)


def setup_inputs(seed: int = 0) -> dict:
    key = jax.random.key(seed)
    ks = jax.random.split(key, 24)
    nrm = lambda k, shape, s: s * jax.random.normal(k, shape, jnp.float32)
    Ld = DEPTH
    return {
        'x_prompt': nrm(ks[0], (BATCH, SEQ, D_MODEL), 1.0),
        'x_sample': nrm(ks[1], (DEC_BATCH, DEC_SEQ, D_MODEL), 1.0),
        'c_prompt': nrm(ks[2], (BATCH, D_MODEL), 1.0),
        'c_sample': nrm(ks[3], (DEC_BATCH, D_MODEL), 1.0),
        'w_ada': nrm(ks[4], (Ld, D_MODEL, 3 * D_MODEL), 0.5 * D_MODEL ** -0.5),
        'b_ada': nrm(ks[5], (Ld, 3 * D_MODEL), 0.01),
        'norm_g': 1.0 + nrm(ks[6], (Ld, D_MODEL), 0.02),
        'w_in': nrm(ks[7], (Ld, D_MODEL, D_PROJ), D_MODEL ** -0.5),
        'short_w': nrm(ks[8], (Ld, SHORT_WIDTH, (HYENA_ORDER + 1) * D_HYENA), SHORT_WIDTH ** -0.5),
        'short_b': nrm(ks[9], (Ld, (HYENA_ORDER + 1) * D_HYENA), 0.01),
        'filt_w1': nrm(ks[10], (Ld, FILTER_EMB, FILTER_HIDDEN), FILTER_EMB ** -0.5),
        'filt_b1': nrm(ks[11], (Ld, FILTER_HIDDEN), 0.1),
        'filt_freq1': 1.0 + nrm(ks[12], (Ld, FILTER_HIDDEN), 0.02),
        'filt_w2': nrm(ks[13], (Ld, FILTER_HIDDEN, FILTER_HIDDEN), FILTER_HIDDEN ** -0.5),
        'filt_b2': nrm(ks[14], (Ld, FILTER_HIDDEN), 0.1),
        'filt_freq2': 1.0 + nrm(ks[15], (Ld, FILTER_HIDDEN), 0.02),
        'filt_w3': nrm(ks[16], (Ld, FILTER_HIDDEN, HYENA_ORDER * 2 * D_HYENA), 0.02 * FILTER_HIDDEN ** -0.5),
        'hyena_d': nrm(ks[17], (Ld, HYENA_ORDER, D_HYENA), 0.1),
        'w_proj_attn': nrm(ks[18], (Ld, D_ATTN, D_MODEL), D_ATTN ** -0.5),
        'w_proj_hyena': nrm(ks[19], (Ld, D_HYENA, D_MODEL), D_HYENA ** -0.5),
        'w_out': nrm(ks[20], (Ld, D_MODEL, D_MODEL), D_MODEL ** -0.5),
        'rel_bias': nrm(ks[21], (N_BUCKETS, N_HEADS), 0.2),
        'final_g': 1.0 + nrm(ks[22], (D_MODEL,), 0.02),
    }


def reference(x_prompt, x_sample, c_prompt, c_sample, w_ada, b_ada, norm_g, w_in, short_w, short_b,
              filt_w1, filt_b1, filt_freq1, filt_w2, filt_b2, filt_freq2, filt_w3, hyena_d,
              w_proj_attn, w_proj_hyena, w_out, rel_bias, final_g):
    def trunk(x, c):
        for l in range(DEPTH):
            x = _encoder_layer(x, c, w_ada[l], b_ada[l], norm_g[l], w_in[l], short_w[l], short_b[l],
                               filt_w1[l], filt_b1[l], filt_freq1[l], filt_w2[l], filt_b2[l], filt_freq2[l],
                               filt_w3[l], hyena_d[l], w_proj_attn[l], w_proj_hyena[l], w_out[l], rel_bias)
        return _rms_norm(x, final_g)

    y_prompt = trunk(x_prompt, c_prompt)
    y_sample = trunk(x_sample, c_sample)
    return (y_prompt, y_sample)
```

```python
import math
import numpy as np
import ml_dtypes
import concourse.bass as bass
import concourse.mybir as mybir
from concourse.ap import AP
from concourse.bass_utils import run_bass_kernel_spmd

F32 = mybir.dt.float32
BF16 = mybir.dt.bfloat16
AF = mybir.ActivationFunctionType
ALU = mybir.AluOpType

T = 16384
DM = 1024
NCHUNK = T // 512
NFFT = 32768
EPS = 1e-6
CB = 64
NCB = 768 // CB
PAD = 1024

OQ, OK_, OV, OGA, OU, OGH, OMA, OMH = 0, 768, 1536, 2304, 3072, 5376, 6144, 7168

DEBUG = {}


class Sched:
    ENG = ("pe", "act", "dve", "pool", "sp")

    def __init__(self, nc, sems, dma_ring):
        self.nc = nc
        self.ops = []
        self.eng = {"pe": nc.tensor, "act": nc.scalar, "dve": nc.vector, "pool": nc.gpsimd, "sp": nc.sync}
        self.sems = sems
        self.ring = dma_ring
        self.cnt = {e: 0 for e in self.ENG}
        self.ndma = 0
        self.last_w = {}
        self.readers = {}
        self.waited = {e: {d: 0 for d in self.ENG} for e in self.ENG}
        self.waited_dma = {e: {} for e in self.ENG}
        self.last_op = {e: None for e in self.ENG}
        self.pending = []
        self.dma_eng = "sp"

    def op(self, eng, fn, reads=(), writes=()):
        self.ops.append(("op", eng, fn, tuple(reads), tuple(writes)))

    def dma(self, out, in_, reads=(), writes=(), eng="sp"):
        self.ops.append(("dma", eng, (out, in_), tuple(reads), tuple(writes)))

    def barrier(self):
        self.ops.append(("bar",))

    def flush(self):
        ops = self.ops
        n = len(ops)
        last_w, readers = {}, {}
        last_on = {e: -1 for e in self.ENG}
        deps = [None] * n
        marked = [False] * n
        for i, o in enumerate(ops):
            if o[0] == "bar":
                deps[i] = dict(last_on)
                for e, j in last_on.items():
                    if j >= 0:
                        marked[j] = True
                last_w, readers = {}, {}
                continue
            _, e, _, rd, wr = o
            d = set()
            for r in rd:
                if r in last_w:
                    d.add(last_w[r])
            for w in wr:
                if w in last_w:
                    d.add(last_w[w])
                for j in readers.get(w, ()):
                    d.add(j)
            d.discard(i)
            deps[i] = d
            for j in d:
                marked[j] = True
            for w in wr:
                last_w[w] = i
                readers[w] = []
            for r in rd:
                if r not in wr:
                    readers.setdefault(r, []).append(i)
            last_on[e] = i
        ordinal = [0] * n
        cnt = {e: 0 for e in self.ENG}
        dslot = [None] * n
        nd = 0
        P = len(self.ring)
        for i, o in enumerate(ops):
            if o[0] == "dma":
                dslot[i] = (nd % P, 16 * (nd // P + 1))
                nd += 1
            elif o[0] == "op" and marked[i]:
                cnt[o[1]] += 1
                ordinal[i] = cnt[o[1]]
        waited = {e: {d: 0 for d in self.ENG} for e in self.ENG}
        wdma = {e: {} for e in self.ENG}

        def wait_for(e, j):
            oj = ops[j]
            if oj[0] == "dma":
                slot, val = dslot[j]
                if wdma[e].get(slot, 0) >= val:
                    return
                wdma[e][slot] = val
                self.eng[e].wait_ge(self.ring[slot], val)
            else:
                dsrc = oj[1]
                if dsrc == e and e == "pe":
                    return
                if waited[e][dsrc] >= ordinal[j]:
                    return
                waited[e][dsrc] = ordinal[j]
                self.eng[e].wait_ge(self.sems[dsrc], ordinal[j])

        nd = 0
        dma_hist = []
        for i, o in enumerate(ops):
            if o[0] == "bar":
                for e in self.ENG:
                    for dsrc, j in deps[i].items():
                        if j >= 0 and not (dsrc == e and ops[j][0] == "op"):
                            wait_for(e, j)
                    for j in dma_hist[-P:]:
                        wait_for(e, j)
                continue
            kind, e, payload, rd, wr = o
            for j in sorted(deps[i]):
                wait_for(e, j)
            if kind == "dma":
                slot, val = dslot[i]
                if val > 16:
                    if wdma[e].get(slot, 0) < val - 16:
                        wdma[e][slot] = val - 16
                        self.eng[e].wait_ge(self.ring[slot], val - 16)
                out, in_ = payload
                self.eng[e].dma_start(out=out, in_=in_).then_inc(self.ring[slot], 16)
                dma_hist.append(i)
                nd += 1
            else:
                ins = payload(self.eng[e])
                if marked[i]:
                    ins.then_inc(self.sems[e], 1)
        for j in dma_hist[-P:]:
            wait_for("sp", j)
        self.ops = []
        return n


def bc(ap, shape):
    return ap.to_broadcast(list(shape))


def _t5_bucket(rel):
    nb = 16
    max_exact = 8
    n = np.abs(rel)
    large = max_exact + (np.log(np.maximum(n, 1) / max_exact) / math.log(1024 / max_exact) * (nb - max_exact)).astype(np.int32)
    large = np.minimum(large, nb - 1)
    return ((rel > 0).astype(np.int32) * nb + np.where(n < max_exact, n, large)).astype(np.int32)


def bf(a):
    return np.ascontiguousarray(a.astype(np.float32)).astype(ml_dtypes.bfloat16)


_CONST_CACHE = {}


def build_consts(is_prompt):
    key = bool(is_prompt)
    if key in _CONST_CACHE:
        return _CONST_CACHE[key]
    c = {}
    L = 16384 if is_prompt else 8192
    tt = np.linspace(0.0, 1.0, L, dtype=np.float32)[:, None]
    w = (np.float32(2.0 * math.pi / L) * np.arange(L, dtype=np.float32))[:, None]
    f = np.linspace(1e-4, 15, 16, dtype=np.float32)[None, :]
    z = np.concatenate([tt, np.cos(f * w), -np.sin(f * w)], axis=-1).astype(np.float32)
    zf = np.zeros((T, 33), np.float32)
    zf[:L] = z
    c["zfT"] = np.ascontiguousarray(zf.T)
    tn = np.zeros(T, np.float32)
    tn[:L] = tt[:, 0]
    c["negtn"] = np.ascontiguousarray(-tn.reshape(128, 128))
    deltas = np.abs(np.linspace(math.log(0.01) / 1.5, math.log(0.01) / 0.3, 768, dtype=np.float32))
    c["deltas_rep"] = np.ascontiguousarray(np.broadcast_to(deltas[None, :], (128, 768))).astype(np.float32)
    slot = np.arange(128) if is_prompt else np.concatenate([np.arange(64), np.arange(64) + 128])
    k2 = np.arange(128)
    n1 = np.arange(128)
    k1 = np.arange(128)
    ang = -2 * np.pi * np.outer(slot, k2 + 0.5) / 256.0
    Er, Ei = np.cos(ang), np.sin(ang)
    c["E_d"] = bf(np.concatenate([-Ei, Er, Ei], axis=1))
    angk = -2 * np.pi * np.outer(np.arange(128), k2 + 0.5) / 256.0
    Ekr, Eki = np.cos(angk), np.sin(angk)
    if not is_prompt:
        Ekr[64:] = 0
        Eki[64:] = 0
    c["E_kS"] = bf(np.concatenate([Ekr, -Eki], axis=1))
    c["E_kD"] = bf(np.concatenate([Eki, Ekr], axis=1))
    angM = -2 * np.pi * (n1[None, :, None] * (k2[:, None, None] + 0.5) / NFFT + n1[None, :, None] * k1[None, None, :] / 128.0)
    c["Mf"] = bf(np.concatenate([np.cos(angM), np.sin(angM)], axis=2).transpose(1, 0, 2))
    angG = 2 * np.pi * np.outer(k1, n1) / 128.0
    c["G1"] = bf(np.concatenate([np.cos(angG), np.sin(angG)], axis=1))
    c["G2"] = bf(np.concatenate([-np.sin(angG), np.cos(angG)], axis=1))
    angI = 2 * np.pi * (n1[:, None, None] + 128 * slot[None, None, :]) * (k2[None, :, None] + 0.5) / NFFT
    sc = 2.0 / NFFT
    c["Minv"] = bf(np.concatenate([sc * np.cos(angI), -sc * np.sin(angI)], axis=2).transpose(1, 0, 2))
    OH = np.zeros((32, 3 * 384), np.float32)
    mrow = np.zeros((12, 3 * 384), np.float32)
    for ri, r in enumerate((1, 4, 16)):
        for j in range(384):
            d = j - 127
            if 0 <= d <= 128:
                rel = (64 - d) * r
                OH[_t5_bucket(np.array(rel)), ri * 384 + j] = 1.0
            else:
                mrow[:, ri * 384 + j] = -30000.0
    c["OH"] = OH
    c["mrow"] = mrow
    bm = np.zeros((128, 256), np.float32)
    if not is_prompt:
        bm[:64, 128:] = -30000.0
        bm[64:, :128] = -30000.0
    c["bmask"] = bf(bm)
    c["bflag"] = np.full((128, 1), 1.0 if is_prompt else 0.0, np.float32)
    ident = np.eye(128, dtype=np.float32)
    c["ident"] = bf(ident)
    c["antiid"] = bf(ident[::-1])
    c["swap"] = bf(np.roll(ident, 64, axis=1))
    _CONST_CACHE[key] = c
    return c


CONST_SHAPES = {
    "zfT": ([33, T], F32), "negtn": ([128, 128], F32), "deltas_rep": ([128, 768], F32),
    "E_d": ([128, 384], BF16), "E_kS": ([128, 256], BF16), "E_kD": ([128, 256], BF16),
    "Mf": ([128, 128, 256], BF16), "G1": ([128, 256], BF16), "G2": ([128, 256], BF16),
    "Minv": ([128, 128, 256], BF16), "OH": ([32, 1152], F32), "mrow": ([12, 1152], F32),
    "bmask": ([128, 256], BF16), "bflag": ([128, 1], F32), "ident": ([128, 128], BF16),
    "antiid": ([128, 128], BF16), "swap": ([128, 128], BF16),
}

INPUT_SHAPES = {
    "x": [T, DM], "cT": [128, 8, 2], "w_ada": [DM, 3 * DM], "b_ada": [1, 3 * DM], "norm_g": [1, DM],
    "w_in": [DM, 8192], "short_wT": [128, 18, 3], "short_bT": [128, 18],
    "filt_w1": [33, 64], "filt_b1": [64, 1], "filt_fr1": [64, 1], "filt_w2": [64, 64], "filt_b2": [64, 1],
    "filt_fr2": [64, 1], "filt_w3": [64, 3072], "hyena_d": [2, 768],
    "w_proj_attn": [768, DM], "w_proj_hyena": [768, DM], "w_out": [DM, DM], "rel_bias": [32, 12],
    "final_g": [1, DM],
}


from contextlib import ExitStack


class K:
    pass


def build_program(phases=("p0", "pk", "p1", "p1b", "pa", "ph", "pf"), debug_outs=()):
    nc = bass.Bass("TRN2", target_bir_lowering=False)
    k = K()
    k.nc = nc
    din = {}
    for name, shp in INPUT_SHAPES.items():
        din[name] = nc.dram_tensor(name, shp, F32, kind="ExternalInput").ap()
    for name, (shp, dt_) in CONST_SHAPES.items():
        din[name] = nc.dram_tensor(name, shp, dt_, kind="ExternalInput").ap()
    k.din = din
    y = nc.dram_tensor("y", [T, DM], F32, kind="ExternalOutput").ap()
    k.y = y

    def scratch(name, shp, dt_):
        kind = "ExternalOutput" if name in debug_outs else "Internal"
        return nc.dram_tensor(name, shp, dt_, kind=kind).ap()

    k.qkT = scratch("qkT", [1536, T], BF16)
    k.gaT = scratch("gaT", [768, T], BF16)
    k.ghT = scratch("ghT", [768, T], BF16)
    k.mT = scratch("mT", [2048, T], BF16)
    k.uraw = scratch("uraw", [2304, T], BF16)
    k.uT = scratch("uT", [2304, T], BF16)
    k.vaug = scratch("vaug", [T, 1536], BF16)
    k.oaT = scratch("oaT", [768, T], BF16)
    k.ohT = scratch("ohT", [768, T], BF16)
    k.KFd = scratch("KFd", [2, NCB, 128, 128 * 3 * CB], BF16)
    k.Avec = scratch("Avec", [12, 1152], BF16)
    k.modrep = scratch("modrep", [2, 3 * DM], F32)

    with ExitStack() as top:
        sems = {e: top.enter_context(nc.semaphore("sem_" + e)) for e in ("pe", "act", "dve", "pool")}
        sems["sp"] = None
        ring = [top.enter_context(nc.semaphore("dr%d" % i)) for i in range(24)]
        S = Sched(nc, sems, ring)
        k.S = S
        ps = [top.enter_context(nc.psum_tensor("psb%d" % i, [128, 512], F32)) for i in range(8)]
        k.ps = ps
        k.ident = top.enter_context(nc.sbuf_tensor("s_ident", [128, 128], BF16))
        k.fg_rep = top.enter_context(nc.sbuf_tensor("s_fg_rep", [128, DM], F32))
        S.dma(k.ident[:], din["ident"][:, :], writes=["ident"])
        k.epsc = top.enter_context(nc.sbuf_tensor("s_epsc", [128, 2], F32))
        S.op("pool", lambda e: e.memset(k.epsc[:], EPS), writes=["epsc"])
        S.dma(k.fg_rep[:], din["final_g"][0:1, :].partition_broadcast(128), writes=["fg_rep"])

        if "p0" in phases:
            phase0(k)
            S.barrier()
        if "pk" in phases:
            phaseK(k)
            S.barrier()
        if "p1" in phases:
            phase1(k)
            S.barrier()
        if "p1b" in phases:
            phase1b(k)
            S.barrier()
        if "ph" in phases:
            phaseH(k)
            S.barrier()
        if "pa" in phases:
            phaseA(k)
            S.barrier()
        if "pf" in phases:
            phaseF(k)
            S.barrier()
        S.flush()
    return nc


def phase0(k):
    nc, S, din, ps = k.nc, k.S, k.din, k.ps
    with ExitStack() as st:
        wada = st.enter_context(nc.sbuf_tensor("s_wada", [128, 8, 3 * DM], F32))
        cT = st.enter_context(nc.sbuf_tensor("s_cT", [128, 8, 2], F32))
        scT = st.enter_context(nc.sbuf_tensor("s_scT", [128, 8, 2], F32))
        screp = st.enter_context(nc.sbuf_tensor("s_screp", [128, 2, 8, 128], F32))
        brep = st.enter_context(nc.sbuf_tensor("s_brep", [128, 3 * DM], F32))
        ngrep = st.enter_context(nc.sbuf_tensor("s_ngrep", [128, DM], F32))
        k.modr = st.enter_context(nc.sbuf_tensor("s_modr", [128, 2, 3 * DM], F32))
        S.dma(cT[:], din["cT"][:, :, :], writes=["cT"])
        for kk in range(8):
            S.dma(wada[:, kk, :], din["w_ada"][kk * 128:(kk + 1) * 128, :], writes=[("wada", kk)])
        S.dma(brep[:], din["b_ada"][0:1, :].partition_broadcast(128), writes=["brep"])
        S.dma(ngrep[:], din["norm_g"][0:1, :].partition_broadcast(128), writes=["ngrep"])
        S.op("act", lambda e: e.activation(out=scT[:], in_=cT[:], func=AF.Silu), reads=["cT"], writes=["scT"])
        for s in range(2):
            S.op("dve", lambda e, s=s: e.tensor_copy(out=screp[:, s, :, :], in_=bc(scT[:, :, s:s + 1], [128, 8, 128])),
                 reads=["scT"], writes=[("screp", s)])
        for s in range(2):
            for cc in range(6):
                pb = ps[(s * 6 + cc) % 4]
                pr = ("ps", (s * 6 + cc) % 4)
                for kk in range(8):
                    S.op("pe", lambda e, s=s, cc=cc, kk=kk, pb=pb: e.matmul(
                        pb[:, :], lhsT=screp[:, s, kk, :], rhs=wada[:, kk, cc * 512:(cc + 1) * 512],
                        start=(kk == 0), stop=(kk == 7)),
                        reads=[("screp", s), ("wada", kk)], writes=[pr])
                S.op("dve", lambda e, s=s, cc=cc, pb=pb: e.tensor_tensor(
                    out=k.modr[:, s, cc * 512:(cc + 1) * 512], in0=pb[:, :], in1=brep[:, cc * 512:(cc + 1) * 512], op=ALU.add),
                    reads=[pr, "brep"], writes=[("modr", s)])
            S.op("dve", lambda e, s=s: e.scalar_tensor_tensor(
                out=k.modr[:, s, DM:2 * DM], in0=k.modr[:, s, DM:2 * DM], scalar=1.0, in1=ngrep[:], op0=ALU.add, op1=ALU.mult),
                reads=[("modr", s), "ngrep"], writes=[("modr", s)])
            S.dma(k.modrep[s:s + 1, :], k.modr[0:1, s, :], reads=[("modr", s)])
        S.barrier()


def phase1(k):
    nc, S, din, ps = k.nc, k.S, k.din, k.ps
    with ExitStack() as st:
        winb = st.enter_context(nc.sbuf_tensor("s_winb", [128, 8, 8192], BF16))
        stg_ctx = ExitStack()
        stg = [stg_ctx.enter_context(nc.sbuf_tensor("s_wstg%d" % i, [128, 2048], F32)) for i in range(2)]
        n = 0
        for kk in range(8):
            for cc in range(4):
                b = n % 2
                S.dma(stg[b][:], din["w_in"][kk * 128:(kk + 1) * 128, cc * 2048:(cc + 1) * 2048], writes=[("wstg", b)])
                eng = ("act", "dve", "pool")[n % 3]
                if eng == "act":
                    S.op("act", lambda e, b=b, kk=kk, cc=cc: e.copy(out=winb[:, kk, cc * 2048:(cc + 1) * 2048], in_=stg[b][:]),
                         reads=[("wstg", b)], writes=[("winb", kk)])
                else:
                    S.op(eng, lambda e, b=b, kk=kk, cc=cc: e.tensor_copy(out=winb[:, kk, cc * 2048:(cc + 1) * 2048], in_=stg[b][:]),
                         reads=[("wstg", b)], writes=[("winb", kk)])
                n += 1
        S.barrier()
        stg_ctx.close()
        modr1 = st.enter_context(nc.sbuf_tensor("s_modr1", [128, 2, 2 * DM], F32))
        for s_ in range(2):
            S.dma(modr1[:, s_, :], k.modrep[s_:s_ + 1, 0:2 * DM].partition_broadcast(128), writes=[("modr", s_)])
        xt = [st.enter_context(nc.sbuf_tensor("s_xt%d" % i, [128, DM], F32)) for i in range(2)]
        xm = [st.enter_context(nc.sbuf_tensor("s_xm%d" % i, [128, DM], F32)) for i in range(1)]
        hb = [st.enter_context(nc.sbuf_tensor("s_hb%d" % i, [128, DM], BF16)) for i in range(2)]
        sq = st.enter_context(nc.sbuf_tensor("s_sqj", [128, DM], BF16))
        ss = [st.enter_context(nc.sbuf_tensor("s_ss%d" % i, [128, 2], F32)) for i in range(3)]
        hT = [st.enter_context(nc.sbuf_tensor("s_hT%d" % i, [128, 8, 512], BF16)) for i in range(2)]
        ev = [st.enter_context(nc.sbuf_tensor("s_ev%d" % i, [128, 512], BF16)) for i in range(6)]
        vst = [st.enter_context(nc.sbuf_tensor("s_vst%d" % i, [128, 12, 128], BF16)) for i in range(2)]
        for i in range(2):
            S.op("pool", lambda e, i=i: e.memset(vst[i][:], 1.0), writes=[("vst", i)])

        blocks = []
        for j in range(6):
            blocks.append((OQ + j * 128, k.qkT, j * 128, "q"))
        for j in range(6):
            blocks.append((OK_ + j * 128, k.qkT, 768 + j * 128, "copy"))
        for j in range(18):
            blocks.append((OU + j * 128, k.uraw, j * 128, "copy"))
        for j in range(6):
            blocks.append((OGA + j * 128, k.gaT, j * 128, "silu"))
        for j in range(6):
            blocks.append((OGH + j * 128, k.ghT, j * 128, "silu"))
        for j in range(8):
            blocks.append((OMA + j * 128, k.mT, j * 128, "sig"))
        for j in range(8):
            blocks.append((OMH + j * 128, k.mT, 1024 + j * 128, "sig"))

        tcount = 0
        evn = 0
        for ci in range(NCHUNK):
            seg = 0 if ci < NCHUNK // 2 else 1
            hTc = hT[ci % 2]
            hres = ("hT", ci % 2)
            for tt in range(4):
                t = ci * 4 + tt
                xb, xr = xt[t % 2], ("xt", t % 2)
                sb, sr = ss[t % 3], ("ss", t % 3)
                mb, mr = xm[0], ("xm", 0)
                hbb, hbr = hb[t % 2], ("hb", t % 2)
                S.dma(xb[:], din["x"][t * 128:(t + 1) * 128, :], writes=[xr])
                S.op("act", lambda e, xb=xb, sb=sb: e.activation(out=sq[:], in_=xb[:], func=AF.Square, scale=1.0 / 32.0,
                                                                 accum_out=sb[:, 0:1]),
                     reads=[xr], writes=["sqj", sr])
                S.op("act", lambda e, sb=sb: e.activation(out=sb[:, 1:2], in_=sb[:, 0:1], func=AF.Sqrt, bias=k.epsc[:, 0:1]),
                     reads=[sr], writes=[sr])
                S.op("dve", lambda e, sb=sb: e.reciprocal(out=sb[:, 1:2], in_=sb[:, 1:2]), reads=[sr], writes=[sr])
                S.op("dve", lambda e, xb=xb, sb=sb, mb=mb, seg=seg: e.scalar_tensor_tensor(
                    out=mb[:], in0=xb[:], scalar=sb[:, 1:2], in1=modr1[:, seg, DM:2 * DM], op0=ALU.mult, op1=ALU.mult),
                    reads=[xr, sr, ("modr", seg)], writes=[mr])
                S.op("pool", lambda e, mb=mb, hbb=hbb, seg=seg: e.tensor_tensor(
                    out=hbb[:], in0=mb[:], in1=modr1[:, seg, 0:DM], op=ALU.add),
                    reads=[mr, ("modr", seg)], writes=[hbr])
                pbank = ps[6 + (t % 2)]
                pres = ("ps", 6 + (t % 2))
                pT = pbank[:].bitcast(BF16)
                for kk in range(8):
                    S.op("pe", lambda e, kk=kk, pT=pT, hbb=hbb: e.transpose(
                        out=pT[:, kk * 128:(kk + 1) * 128], in_=hbb[:, kk * 128:(kk + 1) * 128], identity=k.ident[:]),
                        reads=[hbr, "ident"], writes=[pres])
                S.op("act", lambda e, pT=pT, hTc=hTc, tt=tt: e.copy(
                    out=hTc[:, :, tt * 128:(tt + 1) * 128], in_=pT.rearrange("p (k t) -> p k t", k=8)),
                    reads=[pres], writes=[hres])
            for bi, (wc, dst, drow, kind) in enumerate(blocks):
                pb = ps[bi % 4]
                pr = ("ps", bi % 4)
                for kk in range(8):
                    S.op("pe", lambda e, kk=kk, pb=pb, wc=wc, hTc=hTc: e.matmul(
                        pb[:, :], lhsT=winb[:, kk, wc:wc + 128], rhs=hTc[:, kk, :], start=(kk == 0), stop=(kk == 7)),
                        reads=[hres, ("winb", kk)], writes=[pr])
                eb, er = ev[evn % 6], ("ev", evn % 6)
                evn += 1
                if kind == "q":
                    S.op("act", lambda e, pb=pb, eb=eb: e.activation(out=eb[:], in_=pb[:, :], func=AF.Copy, scale=0.125),
                         reads=[pr], writes=[er])
                elif kind == "copy":
                    S.op("dve", lambda e, pb=pb, eb=eb: e.tensor_copy(out=eb[:], in_=pb[:, :]), reads=[pr], writes=[er])
                elif kind == "silu":
                    S.op("act", lambda e, pb=pb, eb=eb: e.activation(out=eb[:], in_=pb[:, :], func=AF.Silu), reads=[pr], writes=[er])
                else:
                    S.op("act", lambda e, pb=pb, eb=eb: e.activation(out=eb[:], in_=pb[:, :], func=AF.Sigmoid), reads=[pr], writes=[er])
                S.dma(dst[drow:drow + 128, ci * 512:(ci + 1) * 512], eb[:], reads=[er])
            for tt in range(4):
                t = ci * 4 + tt
                pa, pb2 = ps[4], ps[5]
                for kk in range(8):
                    S.op("pe", lambda e, kk=kk, tt=tt, hTc=hTc: e.matmul(
                        ps[4][:, :], lhsT=hTc[:, kk, tt * 128:(tt + 1) * 128], rhs=winb[:, kk, OV:OV + 512],
                        start=(kk == 0), stop=(kk == 7)), reads=[hres, ("winb", kk)], writes=[("ps", 4)])
                for kk in range(8):
                    S.op("pe", lambda e, kk=kk, tt=tt, hTc=hTc: e.matmul(
                        ps[5][:, 0:256], lhsT=hTc[:, kk, tt * 128:(tt + 1) * 128], rhs=winb[:, kk, OV + 512:OV + 768],
                        start=(kk == 0), stop=(kk == 7)), reads=[hres, ("winb", kk)], writes=[("ps", 5)])
                vb, vr = vst[t % 2], ("vst", t % 2)
                def vdst(vb, p0, npair):
                    base = vb[:, 2 * p0:2 * p0 + 1, 0:1]
                    return AP(base.tensor, base.offset, [list(base.ap[0]), [256, npair], [192, 2], [1, 64]])
                S.op("dve", lambda e, vb=vb, vdst=vdst: e.tensor_copy(
                    out=vdst(vb, 0, 4), in_=ps[4][:, :].rearrange("p (a b c) -> p a b c", a=4, b=2)),
                    reads=[("ps", 4)], writes=[vr])
                S.op("dve", lambda e, vb=vb, vdst=vdst: e.tensor_copy(
                    out=vdst(vb, 4, 2), in_=ps[5][:, 0:256].rearrange("p (a b c) -> p a b c", a=2, b=2)),
                    reads=[("ps", 5)], writes=[vr])
                S.dma(k.vaug[t * 128:(t + 1) * 128, :], vb[:].rearrange("p a b -> p (a b)"), reads=[vr])


def core_assignment():
    return [("p", 0), ("p", 1), ("s", 0, 1), ("s", 2, 3), ("s", 4, 5), ("s", 6, 7), ("s", 6, 7), ("s", 6, 7)]


def prep_core_inputs(inp, role):
    f32 = lambda a: np.ascontiguousarray(np.asarray(a, dtype=np.float32))
    m = {}
    if role[0] == "p":
        b = role[1]
        m["x"] = f32(inp["x_prompt"][b])
        c2 = np.stack([inp["c_prompt"][b], inp["c_prompt"][b]], 0)
    else:
        m["x"] = f32(np.concatenate([inp["x_sample"][role[1]], inp["x_sample"][role[2]]], 0))
        c2 = np.stack([inp["c_sample"][role[1]], inp["c_sample"][role[2]]], 0)
    c2 = np.asarray(c2, np.float32)
    m["cT"] = f32(c2.reshape(2, 8, 128).transpose(2, 1, 0))
    m["w_ada"] = f32(inp["w_ada"][0])
    m["b_ada"] = f32(inp["b_ada"][0][None])
    m["norm_g"] = f32(inp["norm_g"][0][None])
    m["w_in"] = f32(inp["w_in"][0])
    m["short_wT"] = f32(np.asarray(inp["short_w"][0]).reshape(3, 18, 128).transpose(2, 1, 0))
    m["short_bT"] = f32(np.asarray(inp["short_b"][0]).reshape(18, 128).T)
    m["filt_w1"] = f32(inp["filt_w1"][0])
    m["filt_b1"] = f32(np.asarray(inp["filt_b1"][0])[:, None])
    m["filt_fr1"] = f32(np.asarray(inp["filt_freq1"][0])[:, None])
    m["filt_w2"] = f32(inp["filt_w2"][0])
    m["filt_b2"] = f32(np.asarray(inp["filt_b2"][0])[:, None])
    m["filt_fr2"] = f32(np.asarray(inp["filt_freq2"][0])[:, None])
    m["filt_w3"] = f32(inp["filt_w3"][0])
    m["hyena_d"] = f32(inp["hyena_d"][0])
    m["w_proj_attn"] = f32(inp["w_proj_attn"][0])
    m["w_proj_hyena"] = f32(inp["w_proj_hyena"][0])
    m["w_out"] = f32(inp["w_out"][0])
    m["rel_bias"] = f32(inp["rel_bias"])
    m["final_g"] = f32(np.asarray(inp["final_g"])[None])
    m.update(build_consts(role[0] == "p"))
    return m


_NC_CACHE = {}


def kernel(**inputs):
    inp = {k_: np.asarray(v) for k_, v in inputs.items()}
    if "full" not in _NC_CACHE:
        _NC_CACHE["full"] = build_program()
    nc = _NC_CACHE["full"]
    roles = core_assignment()
    in_maps = [prep_core_inputs(inp, r) for r in roles]
    res = run_bass_kernel_spmd(nc, in_maps, core_ids=list(range(8)))
    outs = [np.asarray(r["y"], dtype=np.float32) for r in res.results]
    y_prompt = np.stack([outs[0], outs[1]], 0)
    ys = []
    for c in range(2, 6):
        ys.append(outs[c][:8192])
        ys.append(outs[c][8192:])
    y_sample = np.stack(ys, 0)
    return (y_prompt, y_sample)


def phase1b(k):
    nc, S, din = k.nc, k.S, k.din
    W = 2048
    with ExitStack() as st:
        swT = st.enter_context(nc.sbuf_tensor("s_swT", [128, 18, 3], F32))
        sbT = st.enter_context(nc.sbuf_tensor("s_sbT", [128, 18], F32))
        bfl = st.enter_context(nc.sbuf_tensor("s_bfl", [128, 1], F32))
        S.dma(swT[:], din["short_wT"][:, :, :], writes=["swT"])
        S.dma(sbT[:], din["short_bT"][:, :], writes=["sbT"])
        S.dma(bfl[:], din["bflag"][:, :], writes=["bfl"])
        ib = [st.enter_context(nc.sbuf_tensor("s_cin%d" % i, [128, W + 2], BF16)) for i in range(3)]
        t1 = [st.enter_context(nc.sbuf_tensor("s_ct%d" % i, [128, W], F32)) for i in range(2)]
        ob = [st.enter_context(nc.sbuf_tensor("s_cout%d" % i, [128, W], BF16)) for i in range(3)]
        n = 0
        for ub in range(18):
            for tc in range(T // W):
                a, ar = ib[n % 3], ("cin", n % 3)
                tb, tr = t1[n % 2], ("ct", n % 2)
                o, orr = ob[n % 3], ("cout", n % 3)
                eng = "pool"
                lo = tc * W - 1
                hi = tc * W + W + 1
                c0 = 0
                if lo < 0:
                    S.op("pool", lambda e, a=a: e.memset(a[:, 0:1], 0.0), writes=[ar])
                    lo, c0 = 0, 1
                c1 = W + 2
                if hi > T:
                    S.op("pool", lambda e, a=a: e.memset(a[:, W + 1:W + 2], 0.0), writes=[ar])
                    hi, c1 = T, W + 1
                S.dma(a[:, c0:c1], k.uraw[ub * 128:(ub + 1) * 128, lo:hi], writes=[ar])
                if tc * W == T // 2:
                    S.op(eng, lambda e, a=a: e.tensor_scalar(out=a[:, 0:1], in0=a[:, 0:1], scalar1=bfl[:, 0:1], scalar2=None,
                                                             op0=ALU.mult), reads=[ar, "bfl"], writes=[ar])
                if tc * W + W == T // 2:
                    S.op(eng, lambda e, a=a: e.tensor_scalar(out=a[:, W + 1:W + 2], in0=a[:, W + 1:W + 2], scalar1=bfl[:, 0:1],
                                                             scalar2=None, op0=ALU.mult), reads=[ar, "bfl"], writes=[ar])
                S.op("pool", lambda e, a=a, tb=tb, ub=ub: e.tensor_scalar(
                    out=tb[:], in0=a[:, 1:W + 1], scalar1=swT[:, ub, 1:2], scalar2=sbT[:, ub:ub + 1], op0=ALU.mult, op1=ALU.add),
                    reads=[ar, "swT", "sbT"], writes=[tr])
                S.op("dve", lambda e, a=a, tb=tb, ub=ub: e.scalar_tensor_tensor(
                    out=tb[:], in0=a[:, 0:W], scalar=swT[:, ub, 0:1], in1=tb[:], op0=ALU.mult, op1=ALU.add),
                    reads=[ar, tr, "swT"], writes=[tr])
                S.op("dve", lambda e, a=a, tb=tb, o=o, ub=ub: e.scalar_tensor_tensor(
                    out=o[:], in0=a[:, 2:W + 2], scalar=swT[:, ub, 2:3], in1=tb[:], op0=ALU.mult, op1=ALU.add),
                    reads=[ar, tr, "swT"], writes=[orr])
                S.dma(k.uT[ub * 128:(ub + 1) * 128, tc * W:(tc + 1) * W], o[:], reads=[orr])
                n += 1


def sin_wrapped(S, src_ps, pres, dst, dres, scale_ap, bias_ap, tmp, tres, tmp2, t2res, nparts, ncols):
    PI = math.pi
    S.op("dve", lambda e: e.tensor_scalar(out=tmp[0:nparts, 0:ncols], in0=src_ps, scalar1=scale_ap, scalar2=bias_ap,
                                          op0=ALU.mult, op1=ALU.add), reads=[pres], writes=[tres])
    S.op("dve", lambda e: e.tensor_scalar(out=tmp2[0:nparts, 0:ncols], in0=tmp[0:nparts, 0:ncols], scalar1=PI, scalar2=-2 * PI,
                                          op0=ALU.is_gt, op1=ALU.mult), reads=[tres], writes=[t2res])
    S.op("dve", lambda e: e.tensor_tensor(out=tmp2[0:nparts, 0:ncols], in0=tmp2[0:nparts, 0:ncols], in1=tmp[0:nparts, 0:ncols],
                                          op=ALU.add), reads=[tres, t2res], writes=[t2res])
    S.op("dve", lambda e: e.tensor_scalar(out=tmp[0:nparts, 0:ncols], in0=tmp[0:nparts, 0:ncols], scalar1=-PI, scalar2=2 * PI,
                                          op0=ALU.is_lt, op1=ALU.mult), reads=[tres], writes=[tres])
    S.op("dve", lambda e: e.tensor_tensor(out=tmp[0:nparts, 0:ncols], in0=tmp2[0:nparts, 0:ncols], in1=tmp[0:nparts, 0:ncols],
                                          op=ALU.add), reads=[tres, t2res], writes=[tres])
    S.op("act", lambda e: e.activation(out=dst, in_=tmp[0:nparts, 0:ncols], func=AF.Sin), reads=[tres], writes=[dres])


def phaseK(k):
    nc, S, din, ps = k.nc, k.S, k.din, k.ps
    with ExitStack() as st:
        hdn = st.enter_context(nc.sbuf_tensor("s_hdn2T", [64, T], BF16))
        w3sd = st.enter_context(nc.sbuf_tensor("s_w3sd", [64, 2, NCB, 2, CB], BF16))
        with ExitStack() as s2:
            w1 = s2.enter_context(nc.sbuf_tensor("s_fw1", [33, 64], F32))
            w2 = s2.enter_context(nc.sbuf_tensor("s_fw2", [64, 64], F32))
            w3 = s2.enter_context(nc.sbuf_tensor("s_fw3", [64, 3072], F32))
            fv = s2.enter_context(nc.sbuf_tensor("s_fv", [64, 6], F32))
            zc = [s2.enter_context(nc.sbuf_tensor("s_zc%d" % i, [33, 512], F32)) for i in range(2)]
            ta = s2.enter_context(nc.sbuf_tensor("s_fta", [64, 512], F32))
            tb = s2.enter_context(nc.sbuf_tensor("s_ftb", [64, 512], F32))
            h1 = s2.enter_context(nc.sbuf_tensor("s_fh1", [64, 512], F32))
            S.dma(w1[:], din["filt_w1"][:, :], writes=["fw1"])
            S.dma(w2[:], din["filt_w2"][:, :], writes=["fw2"])
            S.dma(w3[:], din["filt_w3"][:, :], writes=["fw3"])
            for i, nm in enumerate(("filt_b1", "filt_fr1", "filt_b2", "filt_fr2")):
                S.dma(fv[:, i:i + 1], din[nm][:, :], writes=["fv"])
            S.op("dve", lambda e: e.tensor_tensor(out=fv[:, 4:5], in0=fv[:, 0:1], in1=fv[:, 1:2], op=ALU.mult), reads=["fv"], writes=["fv"])
            S.op("dve", lambda e: e.tensor_tensor(out=fv[:, 5:6], in0=fv[:, 2:3], in1=fv[:, 3:4], op=ALU.mult), reads=["fv"], writes=["fv"])
            w3v = w3[:].rearrange("p (o d b c) -> p o d b c", o=2, d=2, b=NCB)
            for o in range(2):
                S.op("dve", lambda e, o=o: e.tensor_tensor(out=w3sd[:, o, :, 0, :], in0=w3v[:, o, 0], in1=w3v[:, o, 1], op=ALU.add),
                     reads=["fw3"], writes=["w3sd"])
                S.op("dve", lambda e, o=o: e.tensor_tensor(out=w3sd[:, o, :, 1, :], in0=w3v[:, o, 0], in1=w3v[:, o, 1], op=ALU.subtract),
                     reads=["fw3"], writes=["w3sd"])
            for ci in range(T // 512):
                z, zr = zc[ci % 2], ("zc", ci % 2)
                S.dma(z[:], din["zfT"][:, ci * 512:(ci + 1) * 512], writes=[zr])
                S.op("pe", lambda e, z=z: e.matmul(ps[0][0:64, :], lhsT=w1[:], rhs=z[:], start=True, stop=True),
                     reads=[zr, "fw1"], writes=[("ps", 0)])
                sin_wrapped(S, ps[0][0:64, :], ("ps", 0), h1[:], "fh1", fv[:, 1:2], fv[:, 4:5], ta, "fta", tb, "ftb", 64, 512)
                S.op("pe", lambda e: e.matmul(ps[1][0:64, :], lhsT=w2[:], rhs=h1[:], start=True, stop=True),
                     reads=["fh1", "fw2"], writes=[("ps", 1)])
                sin_wrapped(S, ps[1][0:64, :], ("ps", 1), hdn[:, ci * 512:(ci + 1) * 512], "hdn", fv[:, 3:4], fv[:, 5:6],
                            ta, "fta", tb, "ftb", 64, 512)
            S.barrier()
        H = st.enter_context(nc.sbuf_tensor("s_H", [128, 2, CB, 128], BF16))
        Yk = st.enter_context(nc.sbuf_tensor("s_Yk", [128, 128, 4, CB], BF16))
        dec = st.enter_context(nc.sbuf_tensor("s_dec", [128, 128, CB], BF16))
        negtn = st.enter_context(nc.sbuf_tensor("s_negtn", [128, 128], F32))
        drep = st.enter_context(nc.sbuf_tensor("s_drep", [128, 768], F32))
        EkS = st.enter_context(nc.sbuf_tensor("s_EkS", [128, 256], BF16))
        EkD = st.enter_context(nc.sbuf_tensor("s_EkD", [128, 256], BF16))
        Mfb = [st.enter_context(nc.sbuf_tensor("s_Mfb%d" % i, [128, 16, 256], BF16)) for i in range(2)]
        KFs = [st.enter_context(nc.sbuf_tensor("s_KFs%d" % i, [128, 4, 3, CB], BF16)) for i in range(3)]
        S.dma(negtn[:], din["negtn"][:, :], writes=["negtn"])
        S.dma(drep[:], din["deltas_rep"][:, :], writes=["drep"])
        S.dma(EkS[:], din["E_kS"][:, :], writes=["EkS"])
        S.dma(EkD[:], din["E_kD"][:, :], writes=["EkD"])
        mfn = 0
        kfn = 0
        pn = 0
        for cb in range(NCB):
            c0 = cb * CB
            for n1 in range(128):
                S.op("act", lambda e, n1=n1, c0=c0: e.activation(out=dec[:, n1, :], in_=drep[:, c0:c0 + CB], func=AF.Exp,
                                                                scale=negtn[:, n1:n1 + 1]),
                     reads=["drep", "negtn"], writes=["dec"])
            for o in range(2):
                for g in range(32):
                    pb, pr = ps[pn % 4], ("ps", pn % 4)
                    pn += 1
                    for j in range(4):
                        n1 = 4 * g + j
                        S.op("pe", lambda e, pb=pb, j=j, n1=n1, o=o, cb=cb: e.matmul(
                            pb[:, j * 128:(j + 1) * 128], lhsT=hdn[:, n1:T:128],
                            rhs=w3sd[:, o, cb].rearrange("p a b -> p (a b)"), start=True, stop=True),
                            reads=["hdn", "w3sd"], writes=[pr])
                    hv = H[:, 0:1, 0:1, 4 * g:4 * g + 1]
                    hout = AP(hv.tensor, hv.offset, [list(hv.ap[0]), [1, 4], [CB * 128, 2], [128, CB]])
                    dv = dec[:, 4 * g:4 * g + 1, 0:1]
                    din1 = AP(dv.tensor, dv.offset, [list(dv.ap[0]), [CB, 4], [0, 2], [1, CB]])
                    S.op("dve", lambda e, pb=pb, hout=hout, din1=din1: e.tensor_tensor(
                        out=hout, in0=pb[:, :].rearrange("p (j s c) -> p j s c", j=4, s=2), in1=din1, op=ALU.mult),
                        reads=[pr, "dec"], writes=["H"])
                for c in range(CB):
                    pb, pr = ps[pn % 4], ("ps", pn % 4)
                    pn += 1
                    S.op("pe", lambda e, pb=pb, c=c: e.matmul(pb[:, 0:256], lhsT=H[:, 0, c, :], rhs=EkS[:], start=True, stop=True),
                         reads=["H", "EkS"], writes=[pr])
                    S.op("pe", lambda e, pb=pb, c=c: e.matmul(pb[:, 256:512], lhsT=H[:, 1, c, :], rhs=EkD[:], start=True, stop=True),
                         reads=["H", "EkD"], writes=[pr])
                    yv = Yk[:, 0:1, 0:1, c:c + 1]
                    yout = AP(yv.tensor, yv.offset, [list(yv.ap[0]), [CB, 2], [2 * CB, 2], [4 * CB, 128]])
                    pin = pb[:, :].rearrange("p (a b k) -> p a b k", a=2, b=2)
                    if c % 2 == 0:
                        S.op("act", lambda e, yout=yout, pin=pin: e.copy(out=yout, in_=pin), reads=[pr], writes=["Yk"])
                    else:
                        S.op("dve", lambda e, yout=yout, pin=pin: e.tensor_copy(out=yout, in_=pin), reads=[pr], writes=["Yk"])
                for g in range(32):
                    pb, pr = ps[4 + pn % 4], ("ps", 4 + pn % 4)
                    pn += 1
                    for j in range(4):
                        k2 = 4 * g + j
                        if k2 % 16 == 0:
                            mb_, mr_ = Mfb[mfn % 2], ("Mfb", mfn % 2)
                            mfn += 1
                            S.dma(mb_[:], din["Mf"][:, k2:k2 + 16, :], writes=[mr_])
                        S.op("pe", lambda e, pb=pb, j=j, k2=k2, mb_=mb_: e.matmul(
                            pb[:, j * 128:(j + 1) * 128], lhsT=mb_[:, k2 % 16, 0:128],
                            rhs=Yk[:, k2, 0:2, :].rearrange("p a b -> p (a b)"), start=True, stop=False),
                            reads=["Yk", mr_], writes=[pr])
                        S.op("pe", lambda e, pb=pb, j=j, k2=k2, mb_=mb_: e.matmul(
                            pb[:, j * 128:(j + 1) * 128], lhsT=mb_[:, k2 % 16, 128:256],
                            rhs=Yk[:, k2, 2:4, :].rearrange("p a b -> p (a b)"), start=False, stop=True),
                            reads=["Yk", mr_], writes=[pr])
                    kb, kr = KFs[kfn % 3], ("KFs", kfn % 3)
                    kfn += 1
                    pv = pb[:, :].rearrange("p (j s c) -> p j s c", j=4, s=2)
                    S.op("act", lambda e, kb=kb, pv=pv: e.copy(out=kb[:, :, 0:2, :], in_=pv), reads=[pr], writes=[kr])
                    S.op("dve", lambda e, kb=kb, pv=pv: e.tensor_scalar(out=kb[:, :, 2, :], in0=pv[:, :, 1, :], scalar1=-1.0, scalar2=None,
                                                                        op0=ALU.mult), reads=[pr], writes=[kr])
                    S.dma(k.KFd[o, cb, :, g * 4 * 3 * CB:(g + 1) * 4 * 3 * CB], kb[:].rearrange("p a b c -> p (a b c)"), reads=[kr])


def phaseH(k):
    nc, S, din, ps = k.nc, k.S, k.din, k.ps
    with ExitStack() as st:
        bufA = st.enter_context(nc.sbuf_tensor("s_hA", [128, CB, 128], BF16))
        bufB = st.enter_context(nc.sbuf_tensor("s_hB", [128, CB, 128], BF16))
        bufC = st.enter_context(nc.sbuf_tensor("s_hC", [128, CB, 128], BF16))
        Yd = st.enter_context(nc.sbuf_tensor("s_Yd", [128, 128, 3, CB], BF16))
        Pb = st.enter_context(nc.sbuf_tensor("s_P", [128, 128, 2, CB], BF16))
        Zs = st.enter_context(nc.sbuf_tensor("s_Zs", [128, 128, 2, CB], BF16))
        Ed = st.enter_context(nc.sbuf_tensor("s_Ed", [128, 384], BF16))
        G1 = st.enter_context(nc.sbuf_tensor("s_G1", [128, 256], BF16))
        G2 = st.enter_context(nc.sbuf_tensor("s_G2", [128, 256], BF16))
        drep = st.enter_context(nc.sbuf_tensor("s_hdrep", [128, 2, CB], F32))
        Mb = [st.enter_context(nc.sbuf_tensor("s_Mb%d" % i, [128, 8, 256], BF16)) for i in range(2)]
        KFb = [st.enter_context(nc.sbuf_tensor("s_KFb%d" % i, [128, 4, 3, CB], BF16)) for i in range(3)]
        t1 = [st.enter_context(nc.sbuf_tensor("s_ht1_%d" % i, [128, 4, 2, CB], F32)) for i in range(2)]
        t2 = [st.enter_context(nc.sbuf_tensor("s_ht2_%d" % i, [128, 4, 2, CB], F32)) for i in range(2)]
        te = [st.enter_context(nc.sbuf_tensor("s_hte%d" % i, [128, 8, CB], F32)) for i in range(2)]
        S.dma(Ed[:], din["E_d"][:, :], writes=["Ed"])
        S.dma(G1[:], din["G1"][:, :], writes=["G1"])
        S.dma(G2[:], din["G2"][:, :], writes=["G2"])
        ohs = AP(Pb[:].tensor, Pb[:].offset, [[Pb[:].ap[0][0], 64], [1, T]])
        mn = 0
        kn = 0
        tn_ = 0
        pn = 0

        def load_blk(buf, res, row0):
            src = AP(k.uT.tensor, k.uT[row0:row0 + 1, 0:1].offset, [[128, 128], [T, CB], [1, 128]])
            S.dma(buf[:], src, writes=[res])

        for cb in range(NCB):
            c0 = cb * CB
            load_blk(bufA, "hA", c0)
            load_blk(bufB, "hB", 768 + c0)
            for o in range(2):
                S.dma(drep[:, o, :], din["hyena_d"][o:o + 1, c0:c0 + CB].partition_broadcast(128), writes=["hdrep"])
            for o in range(2):
                Din, dres = (bufA, "hA") if o == 0 else (bufC, "hC")
                for c in range(CB):
                    pb, pr = ps[pn % 4], ("ps", pn % 4)
                    pn += 1
                    S.op("pe", lambda e, pb=pb, c=c, Din=Din: e.matmul(pb[:, 0:384], lhsT=Din[:, c, :], rhs=Ed[:], start=True, stop=True),
                         reads=[dres, "Ed"], writes=[pr])
                    yv = Yd[:, 0:1, 0:1, c:c + 1]
                    yout = AP(yv.tensor, yv.offset, [list(yv.ap[0]), [CB, 3], [3 * CB, 128]])
                    pin = pb[:, 0:384].rearrange("p (b k) -> p b k", b=3)
                    if c % 2 == 0:
                        S.op("act", lambda e, yout=yout, pin=pin: e.copy(out=yout, in_=pin), reads=[pr], writes=["Yd"])
                    else:
                        S.op("dve", lambda e, yout=yout, pin=pin: e.tensor_copy(out=yout, in_=pin), reads=[pr], writes=["Yd"])
                for g in range(32):
                    pb, pr = ps[4 + pn % 4], ("ps", 4 + pn % 4)
                    pn += 1
                    kb, kr = KFb[kn % 3], ("KFb", kn % 3)
                    kn += 1
                    S.dma(kb[:].rearrange("p a b c -> p (a b c)"), k.KFd[o, cb, :, g * 12 * CB:(g + 1) * 12 * CB], writes=[kr])
                    for j in range(4):
                        k2 = 4 * g + j
                        if k2 % 8 == 0:
                            mb_, mr_ = Mb[mn % 2], ("Mb", mn % 2)
                            mn += 1
                            S.dma(mb_[:], din["Mf"][:, k2:k2 + 8, :], writes=[mr_])
                        S.op("pe", lambda e, pb=pb, j=j, k2=k2, mb_=mb_: e.matmul(
                            pb[:, j * 128:(j + 1) * 128], lhsT=mb_[:, k2 % 8, 0:128],
                            rhs=Yd[:, k2, 1:3, :].rearrange("p a b -> p (a b)"), start=True, stop=False),
                            reads=["Yd", mr_], writes=[pr])
                        S.op("pe", lambda e, pb=pb, j=j, k2=k2, mb_=mb_: e.matmul(
                            pb[:, j * 128:(j + 1) * 128], lhsT=mb_[:, k2 % 8, 128:256],
                            rhs=Yd[:, k2, 0:2, :].rearrange("p a b -> p (a b)"), start=False, stop=True),
                            reads=["Yd", mr_], writes=[pr])
                    a1, a1r = t1[tn_ % 2], ("ht1", tn_ % 2)
                    a2, a2r = t2[tn_ % 2], ("ht2", tn_ % 2)
                    tn_ += 1
                    pv = pb[:, :].rearrange("p (j s c) -> p j s c", j=4, s=2)
                    S.op("dve", lambda e, a1=a1, pv=pv, kb=kb: e.tensor_tensor(
                        out=a1[:], in0=pv, in1=bc(kb[:, :, 0:1, :], [128, 4, 2, CB]), op=ALU.mult), reads=[pr, kr], writes=[a1r])
                    S.op("dve", lambda e, a2=a2, pv=pv, kb=kb: e.tensor_tensor(
                        out=a2[:], in0=pv, in1=kb[:, :, 1:3, :], op=ALU.mult), reads=[pr, kr], writes=[a2r])
                    S.op("pool", lambda e, a1=a1, a2=a2, g=g: e.tensor_tensor(
                        out=Pb[:, 4 * g:4 * g + 4, 0, :], in0=a1[:, :, 0, :], in1=a2[:, :, 1, :], op=ALU.add),
                        reads=[a1r, a2r], writes=["P"])
                    S.op("pool", lambda e, a1=a1, a2=a2, g=g: e.tensor_tensor(
                        out=Pb[:, 4 * g:4 * g + 4, 1, :], in0=a1[:, :, 1, :], in1=a2[:, :, 0, :], op=ALU.add),
                        reads=[a1r, a2r], writes=["P"])
                for c2 in range(CB // 2):
                    pb, pr = ps[pn % 4], ("ps", pn % 4)
                    pn += 1
                    for h in range(2):
                        c = 2 * c2 + h
                        S.op("pe", lambda e, pb=pb, c=c, h=h: e.matmul(pb[:, h * 256:(h + 1) * 256], lhsT=Pb[:, :, 0, c], rhs=G1[:],
                                                                       start=True, stop=False), reads=["P", "G1"], writes=[pr])
                        S.op("pe", lambda e, pb=pb, c=c, h=h: e.matmul(pb[:, h * 256:(h + 1) * 256], lhsT=Pb[:, :, 1, c], rhs=G2[:],
                                                                       start=False, stop=True), reads=["P", "G2"], writes=[pr])
                    zv = Zs[:, 0:1, 0:1, 2 * c2:2 * c2 + 1]
                    zout = AP(zv.tensor, zv.offset, [list(zv.ap[0]), [1, 2], [CB, 2], [2 * CB, 128]])
                    pin = pb[:, :].rearrange("p (h b n) -> p h b n", h=2, b=2)
                    if c2 % 2 == 0:
                        S.op("act", lambda e, zout=zout, pin=pin: e.copy(out=zout, in_=pin), reads=[pr], writes=["Zs"])
                    else:
                        S.op("dve", lambda e, zout=zout, pin=pin: e.tensor_copy(out=zout, in_=pin), reads=[pr], writes=["Zs"])
                if o == 1:
                    load_blk(bufA, "hA", 1536 + c0)
                Xg, xres = (bufB, "hB") if o == 0 else (bufA, "hA")
                for g in range(16):
                    pb, pr = ps[4 + pn % 4], ("ps", 4 + pn % 4)
                    pn += 1
                    for j in range(8):
                        n1 = 8 * g + j
                        if n1 % 8 == 0:
                            mb_, mr_ = Mb[mn % 2], ("Mb", mn % 2)
                            mn += 1
                            S.dma(mb_[:], din["Minv"][:, n1:n1 + 8, :], writes=[mr_])
                        S.op("pe", lambda e, pb=pb, j=j, n1=n1, mb_=mb_: e.matmul(
                            pb[:, j * CB:(j + 1) * CB], lhsT=mb_[:, n1 % 8, 0:128], rhs=Zs[:, n1, 0, :], start=True, stop=False),
                            reads=["Zs", mr_], writes=[pr])
                        S.op("pe", lambda e, pb=pb, j=j, n1=n1, mb_=mb_: e.matmul(
                            pb[:, j * CB:(j + 1) * CB], lhsT=mb_[:, n1 % 8, 128:256], rhs=Zs[:, n1, 1, :], start=False, stop=True),
                            reads=["Zs", mr_], writes=[pr])
                    tb, tr = te[g % 2], ("hte", g % 2)
                    zin = Din[:, :, 8 * g:8 * g + 8].rearrange("p c j -> p j c")
                    xin = Xg[:, :, 8 * g:8 * g + 8].rearrange("p c j -> p j c")
                    S.op("pool", lambda e, tb=tb, zin=zin, o=o, c0=c0: e.tensor_tensor(
                        out=tb[:], in0=zin, in1=bc(drep[:, o:o + 1, :], [128, 8, CB]), op=ALU.mult),
                        reads=[dres, "hdrep"], writes=[tr])
                    S.op("dve", lambda e, tb=tb, pb=pb: e.tensor_tensor(
                        out=tb[:], in0=pb[:, :].rearrange("p (j c) -> p j c", j=8), in1=tb[:], op=ALU.add), reads=[pr, tr], writes=[tr])
                    if o == 0:
                        zo = bufC[:, :, 8 * g:8 * g + 8].rearrange("p c j -> p j c")
                        S.op("pool", lambda e, tb=tb, xin=xin, zo=zo: e.tensor_tensor(out=zo, in0=tb[:], in1=xin, op=ALU.mult),
                             reads=[tr, xres], writes=["hC"])
                    else:
                        z3v = bufB[:].rearrange("p c j -> p (c j)")[:, 8 * g * CB:(8 * g + 8) * CB].rearrange("p (j c) -> p j c", j=8)
                        S.op("pool", lambda e, tb=tb, xin=xin, z3v=z3v: e.tensor_tensor(out=z3v, in0=tb[:], in1=xin, op=ALU.mult),
                             reads=[tr, xres], writes=["hB"])
            z3 = bufB[:].rearrange("p c j -> p (c j)")
            for g in range(16):
                pb, pr = ps[pn % 4], ("ps", pn % 4)
                pn += 1
                pT = pb[:].bitcast(BF16)
                for j in range(8):
                    n1 = 8 * g + j
                    S.op("pe", lambda e, pT=pT, j=j, n1=n1: e.transpose(out=pT[0:CB, j * 128:(j + 1) * 128],
                                                                        in_=z3[:, n1 * CB:(n1 + 1) * CB], identity=k.ident[:]),
                         reads=["hB", "ident"], writes=[pr])
                ov = AP(ohs.tensor, ohs.offset + 8 * g, [list(ohs.ap[0]), [1, 8], [128, 128]])
                pin = pT[0:CB, :].rearrange("p (j n) -> p j n", j=8)
                if g % 2 == 0:
                    S.op("act", lambda e, ov=ov, pin=pin: e.copy(out=ov, in_=pin), reads=[pr], writes=["P"])
                else:
                    S.op("dve", lambda e, ov=ov, pin=pin: e.tensor_copy(out=ov, in_=pin), reads=[pr], writes=["P"])
            S.dma(k.ohT[c0:c0 + CB, :], ohs, reads=["P"])


def phaseA(k):
    nc, S, din, ps = k.nc, k.S, k.din, k.ps
    SPAN = 2048
    with ExitStack() as st:
        Hk = st.enter_context(nc.sbuf_tensor("s_Hk", [128, 36, 256], BF16))
        J = st.enter_context(nc.sbuf_tensor("s_J", [128, 128], BF16))
        bm = st.enter_context(nc.sbuf_tensor("s_bm", [128, 256], BF16))
        swb = st.enter_context(nc.sbuf_tensor("s_swb", [128, 128], BF16))
        swf = st.enter_context(nc.sbuf_tensor("s_swf", [128, 128], F32))
        with ExitStack() as s2:
            rb = s2.enter_context(nc.sbuf_tensor("s_rb", [32, 12], F32))
            oh = s2.enter_context(nc.sbuf_tensor("s_oh", [32, 1152], F32))
            mr = s2.enter_context(nc.sbuf_tensor("s_mrow", [12, 1152], F32))
            av = s2.enter_context(nc.sbuf_tensor("s_av", [12, 1152], BF16))
            S.dma(rb[:], din["rel_bias"][:, :], writes=["rb"])
            S.dma(oh[:], din["OH"][:, :], writes=["oh"])
            S.dma(mr[:], din["mrow"][:, :], writes=["mrow"])
            for i in range(3):
                S.op("pe", lambda e, i=i: e.matmul(ps[i][0:12, 0:384], lhsT=rb[:], rhs=oh[:, i * 384:(i + 1) * 384], start=True, stop=True),
                     reads=["rb", "oh"], writes=[("ps", i)])
                S.op("dve", lambda e, i=i: e.tensor_tensor(out=av[:, i * 384:(i + 1) * 384], in0=ps[i][0:12, 0:384],
                                                           in1=mr[:, i * 384:(i + 1) * 384], op=ALU.add),
                     reads=[("ps", i), "mrow"], writes=["av"])
            S.dma(k.Avec[:, :], av[:], reads=["av"], writes=["Avec"])
            for h in range(12):
                for ri in range(3):
                    src = AP(k.Avec.tensor, k.Avec[h:h + 1, ri * 384:ri * 384 + 1].offset, [[1, 128], [1, 256]])
                    S.dma(Hk[:, h * 3 + ri, :], src, reads=["Avec"], writes=["Hk"])
            S.dma(J[:], din["antiid"][:, :], writes=["J"])
            S.dma(bm[:], din["bmask"][:, :], writes=["bm"])
            S.dma(swb[:], din["swap"][:, :], writes=["swb"])
            S.op("dve", lambda e: e.tensor_copy(out=swf[:], in_=swb[:]), reads=["swb"], writes=["swf"])
            S.barrier()
        qT = st.enter_context(nc.sbuf_tensor("s_qT", [128, T + 2 * PAD], BF16))
        kT = st.enter_context(nc.sbuf_tensor("s_kT", [128, T + 2 * PAD], BF16))
        for tl, nm in ((qT, "qT"), (kT, "kT")):
            S.op("pool", lambda e, tl=tl: e.memset(tl[:, 0:PAD], 0.0), writes=[nm])
            S.op("pool", lambda e, tl=tl: e.memset(tl[:, PAD + T:PAD + T + PAD], 0.0), writes=[nm])
        acc = [st.enter_context(nc.sbuf_tensor("s_acc%d" % i, [128, SPAN], F32)) for i in range(2)]
        oT = [st.enter_context(nc.sbuf_tensor("s_oT%d" % i, [128, SPAN], BF16)) for i in range(2)]
        rden = [st.enter_context(nc.sbuf_tensor("s_rden%d" % i, [128, 512], F32)) for i in range(2)]
        Vt = [st.enter_context(nc.sbuf_tensor("s_Vt%d" % i, [128, 256], BF16)) for i in range(8)]
        PT = [st.enter_context(nc.sbuf_tensor("s_PT%d" % i, [128, 128], BF16)) for i in range(6)]
        vn = 0
        ptn = 0
        sn = 0
        on = 0
        rn = 0
        spn = 0
        for hp in range(6):
            S.dma(qT[:, PAD:PAD + T], k.qkT[hp * 128:(hp + 1) * 128, :], writes=["qT"])
            S.dma(kT[:, PAD:PAD + T], k.qkT[768 + hp * 128:768 + (hp + 1) * 128, :], writes=["kT"])
            for s in range(T // SPAN):
                first = {0: True, 1: True}
                for ri, r in enumerate((1, 4, 16)):
                    Lr = T // r
                    jb = (T // 2) // (r * 128)
                    for rho in range(r):
                        vcache = {}

                        def get_v(j, r=r, rho=rho, Lr=Lr):
                            nonlocal vn
                            if j in vcache:
                                return vcache[j]
                            vt, vr = Vt[vn % 8], ("Vt", vn % 8)
                            vn += 1
                            m0 = 128 * j - 64
                            lo, hi = 0, 128
                            if m0 < 0:
                                lo = 64
                            if m0 + 128 > Lr:
                                hi = 64
                            if lo > 0 or hi < 128:
                                S.op("pool", lambda e, vt=vt: e.memset(vt[:], 0.0), writes=[vr])
                            tok0 = rho + r * (m0 + lo)
                            src = AP(k.vaug.tensor, k.vaug[tok0:tok0 + 1, hp * 256:hp * 256 + 1].offset, [[1536 * r, hi - lo], [1, 256]])
                            S.dma(vt[lo:hi, :], src, writes=[vr])
                            vcache[j] = (vt, vr)
                            return vcache[j]

                        nqb = SPAN // (128 * r)
                        for qi in range(nqb):
                            jq = s * nqb + qi
                            qc0 = PAD + rho + r * 128 * jq
                            qsl = slice(qc0, qc0 + 127 * r + 1, r)
                            for hh in range(2):
                                pbase = 64 * hh
                                hglob = 2 * hp + hh
                                po, por = ps[4 + on % 4], ("ps", 4 + on % 4)
                                on += 1
                                for half in range(2):
                                    j = jq + half
                                    vt, vr = get_v(j)
                                    kc0 = PAD + rho + r * (128 * j - 64)
                                    ksl = slice(kc0, kc0 + 127 * r + 1, r)
                                    pS, psr = ps[sn % 4], ("ps", sn % 4)
                                    sn += 1
                                    hc = (1 - half) * 128
                                    straddle = (j == jb)
                                    S.op("pe", lambda e, pS=pS, ksl=ksl, qsl=qsl, pbase=pbase: e.matmul(
                                        pS[:, 0:128], lhsT=kT[pbase:pbase + 64, ksl], rhs=qT[pbase:pbase + 64, qsl],
                                        start=True, stop=False), reads=["kT", "qT"], writes=[psr])
                                    S.op("pe", lambda e, pS=pS, hglob=hglob, ri=ri, hc=hc, straddle=straddle: e.matmul(
                                        pS[:, 0:128], lhsT=J[:], rhs=Hk[:, hglob * 3 + ri, hc:hc + 128],
                                        start=False, stop=not straddle), reads=["J", "Hk"], writes=[psr])
                                    if straddle:
                                        S.op("pe", lambda e, pS=pS, hc=hc: e.matmul(
                                            pS[:, 0:128], lhsT=k.ident[:], rhs=bm[:, hc:hc + 128], start=False, stop=True),
                                            reads=["ident", "bm"], writes=[psr])
                                    pt, ptr = PT[ptn % 6], ("PT", ptn % 6)
                                    ptn += 1
                                    S.op("act", lambda e, pS=pS, pt=pt: e.activation(out=pt[:], in_=pS[:, 0:128], func=AF.Exp),
                                         reads=[psr], writes=[ptr])
                                    S.op("pe", lambda e, po=po, vt=vt, hh=hh, pt=pt, half=half: e.matmul(
                                        po[:, 0:128], lhsT=vt[:, hh * 128:(hh + 1) * 128], rhs=pt[:], start=(half == 0), stop=(half == 1)),
                                        reads=[vr, ptr], writes=[por])
                                a0 = rho + r * 128 * jq - s * SPAN
                                asl = slice(a0, a0 + 127 * r + 1, r)
                                ab, abr = acc[hh], ("acc", hh)
                                if first[hh] and r == 1:
                                    S.op("dve", lambda e, ab=ab, asl=asl, po=po: e.tensor_copy(out=ab[:, asl], in_=po[:, 0:128]),
                                         reads=[por], writes=[abr])
                                else:
                                    S.op("dve", lambda e, ab=ab, asl=asl, po=po: e.tensor_tensor(
                                        out=ab[:, asl], in0=po[:, 0:128], in1=ab[:, asl], op=ALU.add), reads=[por, abr], writes=[abr])
                ot, otr = oT[spn % 2], ("oT", spn % 2)
                spn += 1
                for hh in range(2):
                    nlo = 0 if hh == 0 else 64
                    for cc in range(SPAN // 512):
                        pw, pwr = ps[sn % 4], ("ps", sn % 4)
                        sn += 1
                        S.op("pe", lambda e, pw=pw, hh=hh, cc=cc: e.matmul(pw[:, :], lhsT=swf[:], rhs=acc[hh][:, cc * 512:(cc + 1) * 512],
                                                                           start=True, stop=True), reads=[("acc", hh), "swf"], writes=[pwr])
                        rd, rdr = rden[rn % 2], ("rden", rn % 2)
                        rn += 1
                        S.op("dve", lambda e, rd=rd, pw=pw, nlo=nlo: e.reciprocal(out=rd[nlo:nlo + 64, :], in_=pw[nlo:nlo + 64, :]),
                             reads=[pwr], writes=[rdr])
                        S.op("pool", lambda e, rd=rd, ot=ot, hh=hh, cc=cc, nlo=nlo: e.tensor_tensor(
                            out=ot[nlo:nlo + 64, cc * 512:(cc + 1) * 512], in0=acc[hh][nlo:nlo + 64, cc * 512:(cc + 1) * 512],
                            in1=rd[nlo:nlo + 64, :], op=ALU.mult), reads=[("acc", hh), rdr], writes=[otr])
                S.dma(k.oaT[hp * 128:(hp + 1) * 128, s * SPAN:(s + 1) * SPAN], ot[:], reads=[otr])


def phaseF(k):
    nc, S, din, ps = k.nc, k.S, k.din, k.ps
    with ExitStack() as st:
        wpa = st.enter_context(nc.sbuf_tensor("s_wpa", [128, 6, DM], BF16))
        wph = st.enter_context(nc.sbuf_tensor("s_wph", [128, 6, DM], BF16))
        wo = st.enter_context(nc.sbuf_tensor("s_wo", [128, 8, DM], BF16))
        with ExitStack() as s2:
            stg = [s2.enter_context(nc.sbuf_tensor("s_fstg%d" % i, [128, DM], F32)) for i in range(2)]
            n = 0
            for (wt, nm, src, nk) in ((wpa, "wpa", "w_proj_attn", 6), (wph, "wph", "w_proj_hyena", 6), (wo, "wo", "w_out", 8)):
                for kk in range(nk):
                    b = n % 2
                    S.dma(stg[b][:], din[src][kk * 128:(kk + 1) * 128, :], writes=[("fstg", b)])
                    eng = ("dve", "pool")[n % 2]
                    S.op(eng, lambda e, b=b, wt=wt, kk=kk: e.tensor_copy(out=wt[:, kk, :], in_=stg[b][:]), reads=[("fstg", b)], writes=[nm])
                    n += 1
            S.barrier()
        gater = st.enter_context(nc.sbuf_tensor("s_gater", [128, 2, DM], F32))
        for s_ in range(2):
            S.dma(gater[:, s_, :], k.modrep[s_:s_ + 1, 2 * DM:3 * DM].partition_broadcast(128), writes=[("modr", s_)])
        oa = [st.enter_context(nc.sbuf_tensor("s_foa%d" % i, [128, 6, 512], BF16)) for i in range(2)]
        ga = [st.enter_context(nc.sbuf_tensor("s_fga%d" % i, [128, 6, 512], BF16)) for i in range(2)]
        oh_ = [st.enter_context(nc.sbuf_tensor("s_foh%d" % i, [128, 6, 512], BF16)) for i in range(2)]
        gh = [st.enter_context(nc.sbuf_tensor("s_fgh%d" % i, [128, 6, 512], BF16)) for i in range(2)]
        mt = [st.enter_context(nc.sbuf_tensor("s_fmt%d" % i, [128, 16, 512], BF16)) for i in range(2)]
        mix = [st.enter_context(nc.sbuf_tensor("s_fmix%d" % i, [128, 8, 512], BF16)) for i in range(2)]
        ta = [st.enter_context(nc.sbuf_tensor("s_fta%d" % i, [128, 512], F32)) for i in range(2)]
        tb = [st.enter_context(nc.sbuf_tensor("s_ftb%d" % i, [128, 512], F32)) for i in range(2)]
        xt = [st.enter_context(nc.sbuf_tensor("s_fx%d" % i, [128, DM], F32)) for i in range(2)]
        r1 = [st.enter_context(nc.sbuf_tensor("s_fr%d" % i, [128, DM], F32)) for i in range(2)]
        yo = [st.enter_context(nc.sbuf_tensor("s_fy%d" % i, [128, DM], F32)) for i in range(2)]
        sqj = st.enter_context(nc.sbuf_tensor("s_fsq", [128, DM], BF16))
        ss = [st.enter_context(nc.sbuf_tensor("s_fss%d" % i, [128, 2], F32)) for i in range(2)]
        pn = 0
        tn_ = 0
        for ci in range(NCHUNK):
            seg = 0 if ci < NCHUNK // 2 else 1
            b = ci % 2
            cs = slice(ci * 512, (ci + 1) * 512)
            S.dma(oa[b][:], k.oaT[:, cs].rearrange("(a p) t -> p a t", p=128), writes=[("foa", b)])
            S.dma(ga[b][:], k.gaT[:, cs].rearrange("(a p) t -> p a t", p=128), writes=[("fga", b)])
            S.dma(oh_[b][:], k.ohT[:, cs].rearrange("(a p) t -> p a t", p=128), writes=[("foh", b)])
            S.dma(gh[b][:], k.ghT[:, cs].rearrange("(a p) t -> p a t", p=128), writes=[("fgh", b)])
            S.dma(mt[b][:], k.mT[:, cs].rearrange("(a p) t -> p a t", p=128), writes=[("fmt", b)])
            S.op("pool", lambda e, b=b: e.tensor_tensor(out=oa[b][:], in0=oa[b][:], in1=ga[b][:], op=ALU.mult),
                 reads=[("foa", b), ("fga", b)], writes=[("foa", b)])
            S.op("dve", lambda e, b=b: e.tensor_tensor(out=oh_[b][:], in0=oh_[b][:], in1=gh[b][:], op=ALU.mult),
                 reads=[("foh", b), ("fgh", b)], writes=[("foh", b)])
            for fb in range(8):
                pA, pAr = ps[pn % 4], ("ps", pn % 4)
                pn += 1
                pH, pHr = ps[pn % 4], ("ps", pn % 4)
                pn += 1
                for kk in range(6):
                    S.op("pe", lambda e, pA=pA, kk=kk, fb=fb, b=b: e.matmul(pA[:, :], lhsT=wpa[:, kk, fb * 128:(fb + 1) * 128], rhs=oa[b][:, kk, :],
                                                                             start=(kk == 0), stop=(kk == 5)), reads=["wpa", ("foa", b)], writes=[pAr])
                for kk in range(6):
                    S.op("pe", lambda e, pH=pH, kk=kk, fb=fb, b=b: e.matmul(pH[:, :], lhsT=wph[:, kk, fb * 128:(fb + 1) * 128], rhs=oh_[b][:, kk, :],
                                                                             start=(kk == 0), stop=(kk == 5)), reads=["wph", ("foh", b)], writes=[pHr])
                a_, ar_ = ta[tn_ % 2], ("fta", tn_ % 2)
                b_, br_ = tb[tn_ % 2], ("ftb", tn_ % 2)
                tn_ += 1
                S.op("dve", lambda e, a_=a_, pA=pA, fb=fb, b=b: e.tensor_tensor(out=a_[:], in0=pA[:, :], in1=mt[b][:, fb, :], op=ALU.mult),
                     reads=[pAr, ("fmt", b)], writes=[ar_])
                S.op("dve", lambda e, b_=b_, pH=pH, fb=fb, b=b: e.tensor_tensor(out=b_[:], in0=pH[:, :], in1=mt[b][:, 8 + fb, :], op=ALU.mult),
                     reads=[pHr, ("fmt", b)], writes=[br_])
                S.op("pool", lambda e, a_=a_, b_=b_, fb=fb, b=b: e.tensor_tensor(out=mix[b][:, fb, :], in0=a_[:], in1=b_[:], op=ALU.add),
                     reads=[ar_, br_], writes=[("fmix", b)])
            for tt in range(4):
                t = ci * 4 + tt
                tb2 = t % 2
                S.dma(xt[tb2][:], din["x"][t * 128:(t + 1) * 128, :], writes=[("fx", tb2)])
                p0, p1 = ps[4 + 2 * tb2], ps[5 + 2 * tb2]
                p0r, p1r = ("ps", 4 + 2 * tb2), ("ps", 5 + 2 * tb2)
                for half, (pp, ppr) in enumerate(((p0, p0r), (p1, p1r))):
                    for fb in range(8):
                        S.op("pe", lambda e, pp=pp, fb=fb, tt=tt, half=half, b=b: e.matmul(
                            pp[:, :], lhsT=mix[b][:, fb, tt * 128:(tt + 1) * 128], rhs=wo[:, fb, half * 512:(half + 1) * 512],
                            start=(fb == 0), stop=(fb == 7)), reads=[("fmix", b), "wo"], writes=[ppr])
                rr, rrr = r1[tb2], ("fr", tb2)
                for half, (pp, ppr) in enumerate(((p0, p0r), (p1, p1r))):
                    hs = slice(half * 512, (half + 1) * 512)
                    S.op("dve", lambda e, pp=pp, rr=rr, hs=hs, seg=seg: e.tensor_tensor(
                        out=rr[:, hs], in0=pp[:, :], in1=gater[:, seg, hs.start:hs.stop], op=ALU.mult),
                        reads=[ppr, ("modr", seg)], writes=[rrr])
                S.op("pool", lambda e, rr=rr, tb2=tb2: e.tensor_tensor(out=rr[:], in0=rr[:], in1=xt[tb2][:], op=ALU.add),
                     reads=[rrr, ("fx", tb2)], writes=[rrr])
                sb, sr = ss[tb2], ("fss", tb2)
                S.op("act", lambda e, rr=rr, sb=sb: e.activation(out=sqj[:], in_=rr[:], func=AF.Square, scale=1.0 / 32.0, accum_out=sb[:, 0:1]),
                     reads=[rrr], writes=["fsq", sr])
                S.op("act", lambda e, sb=sb: e.activation(out=sb[:, 1:2], in_=sb[:, 0:1], func=AF.Sqrt, bias=k.epsc[:, 0:1]),
                     reads=[sr, "epsc"], writes=[sr])
                S.op("dve", lambda e, sb=sb: e.reciprocal(out=sb[:, 1:2], in_=sb[:, 1:2]), reads=[sr], writes=[sr])
                yb, yr = yo[tb2], ("fy", tb2)
                S.op("dve", lambda e, rr=rr, sb=sb, yb=yb: e.scalar_tensor_tensor(
                    out=yb[:], in0=rr[:], scalar=sb[:, 1:2], in1=k.fg_rep[:], op0=ALU.mult, op1=ALU.mult),
                    reads=[rrr, sr, "fg_rep"], writes=[yr])
                S.dma(k.y[t * 128:(t + 1) * 128, :], yb[:], reads=[yr])
```

```python
import math
import numpy as np
import ml_dtypes
import concourse.bass as bass
import concourse.mybir as mybir
from concourse.ap import AP
from concourse.bass_utils import run_bass_kernel_spmd

F32 = mybir.dt.float32
BF16 = mybir.dt.bfloat16
AF = mybir.ActivationFunctionType
ALU = mybir.AluOpType

T = 16384
DM = 1024
NCHUNK = T // 512
NFFT = 32768
EPS = 1e-6
CB = 64
NCB = 768 // CB
PAD = 1024

OQ, OK_, OV, OGA, OU, OGH, OMA, OMH = 0, 768, 1536, 2304, 3072, 5376, 6144, 7168

DEBUG = {}


class Sched:
    ENG = ("pe", "act", "dve", "pool", "sp")

    def __init__(self, nc, sems, dma_ring):
        self.nc = nc
        self.ops = []
        self.eng = {"pe": nc.tensor, "act": nc.scalar, "dve": nc.vector, "pool": nc.gpsimd, "sp": nc.sync}
        self.sems = sems
        self.ring = dma_ring
        self.cnt = {e: 0 for e in self.ENG}
        self.ndma = 0
        self.last_w = {}
        self.readers = {}
        self.waited = {e: {d: 0 for d in self.ENG} for e in self.ENG}
        self.waited_dma = {e: {} for e in self.ENG}
        self.last_op = {e: None for e in self.ENG}
        self.pending = []
        self.dma_eng = "sp"

    def op(self, eng, fn, reads=(), writes=()):
        self.ops.append(("op", eng, fn, tuple(reads), tuple(writes)))

    def dma(self, out, in_, reads=(), writes=(), eng="sp"):
        self.ops.append(("dma", eng, (out, in_), tuple(reads), tuple(writes)))

    def barrier(self):
        self.ops.append(("bar",))

    def flush(self):
        ops = self.ops
        n = len(ops)
        last_w, readers = {}, {}
        last_on = {e: -1 for e in self.ENG}
        deps = [None] * n
        marked = [False] * n
        for i, o in enumerate(ops):
            if o[0] == "bar":
                deps[i] = dict(last_on)
                for e, j in last_on.items():
                    if j >= 0:
                        marked[j] = True
                last_w, readers = {}, {}
                continue
            _, e, _, rd, wr = o
            d = set()
            for r in rd:
                if r in last_w:
                    d.add(last_w[r])
            for w in wr:
                if w in last_w:
                    d.add(last_w[w])
                for j in readers.get(w, ()):
                    d.add(j)
            d.discard(i)
            deps[i] = d
            for j in d:
                marked[j] = True
            for w in wr:
                last_w[w] = i
                readers[w] = []
            for r in rd:
                if r not in wr:
                    readers.setdefault(r, []).append(i)
            last_on[e] = i
        ordinal = [0] * n
        cnt = {e: 0 for e in self.ENG}
        dslot = [None] * n
        nd = 0
        P = len(self.ring)
        for i, o in enumerate(ops):
            if o[0] == "dma":
                dslot[i] = (nd % P, 16 * (nd // P + 1))
                nd += 1
            elif o[0] == "op" and marked[i]:
                cnt[o[1]] += 1
                ordinal[i] = cnt[o[1]]
        waited = {e: {d: 0 for d in self.ENG} for e in self.ENG}
        wdma = {e: {} for e in self.ENG}

        def wait_for(e, j):
            oj = ops[j]
            if oj[0] == "dma":
                slot, val = dslot[j]
                if wdma[e].get(slot, 0) >= val:
                    return
                wdma[e][slot] = val
                self.eng[e].wait_ge(self.ring[slot], val)
            else:
                dsrc = oj[1]
                if dsrc == e and e == "pe":
                    return
                if waited[e][dsrc] >= ordinal[j]:
                    return
                waited[e][dsrc] = ordinal[j]
                self.eng[e].wait_ge(self.sems[dsrc], ordinal[j])

        nd = 0
        dma_hist = []
        for i, o in enumerate(ops):
            if o[0] == "bar":
                for e in self.ENG:
                    for dsrc, j in deps[i].items():
                        if j >= 0 and not (dsrc == e and ops[j][0] == "op"):
                            wait_for(e, j)
                    for j in dma_hist[-P:]:
                        wait_for(e, j)
                continue
            kind, e, payload, rd, wr = o
            for j in sorted(deps[i]):
                wait_for(e, j)
            if kind == "dma":
                slot, val = dslot[i]
                if val > 16:
                    if wdma[e].get(slot, 0) < val - 16:
                        wdma[e][slot] = val - 16
                        self.eng[e].wait_ge(self.ring[slot], val - 16)
                out, in_ = payload
                self.eng[e].dma_start(out=out, in_=in_).then_inc(self.ring[slot], 16)
                dma_hist.append(i)
                nd += 1
            else:
                ins = payload(self.eng[e])
                if marked[i]:
                    ins.then_inc(self.sems[e], 1)
        for j in dma_hist[-P:]:
            wait_for("sp", j)
        self.ops = []
        return n


def bc(ap, shape):
    return ap.to_broadcast(list(shape))


def _t5_bucket(rel):
    nb = 16
    max_exact = 8
    n = np.abs(rel)
    large = max_exact + (np.log(np.maximum(n, 1) / max_exact) / math.log(1024 / max_exact) * (nb - max_exact)).astype(np.int32)
    large = np.minimum(large, nb - 1)
    return ((rel > 0).astype(np.int32) * nb + np.where(n < max_exact, n, large)).astype(np.int32)


def bf(a):
    return np.ascontiguousarray(a.astype(np.float32)).astype(ml_dtypes.bfloat16)


_CONST_CACHE = {}


def build_consts(is_prompt):
    key = bool(is_prompt)
    if key in _CONST_CACHE:
        return _CONST_CACHE[key]
    c = {}
    L = 16384 if is_prompt else 8192
    tt = np.linspace(0.0, 1.0, L, dtype=np.float32)[:, None]
    w = (np.float32(2.0 * math.pi / L) * np.arange(L, dtype=np.float32))[:, None]
    f = np.linspace(1e-4, 15, 16, dtype=np.float32)[None, :]
    z = np.concatenate([tt, np.cos(f * w), -np.sin(f * w)], axis=-1).astype(np.float32)
    zf = np.zeros((T, 33), np.float32)
    zf[:L] = z
    c["zfT"] = np.ascontiguousarray(zf.T)
    tn = np.zeros(T, np.float32)
    tn[:L] = tt[:, 0]
    c["negtn"] = np.ascontiguousarray(-tn.reshape(128, 128))
    deltas = np.abs(np.linspace(math.log(0.01) / 1.5, math.log(0.01) / 0.3, 768, dtype=np.float32))
    c["deltas_rep"] = np.ascontiguousarray(np.broadcast_to(deltas[None, :], (128, 768))).astype(np.float32)
    slot = np.arange(128) if is_prompt else np.concatenate([np.arange(64), np.arange(64) + 128])
    k2 = np.arange(128)
    n1 = np.arange(128)
    k1 = np.arange(128)
    ang = -2 * np.pi * np.outer(slot, k2 + 0.5) / 256.0
    Er, Ei = np.cos(ang), np.sin(ang)
    c["E_d"] = bf(np.concatenate([-Ei, Er, Ei], axis=1))
    angk = -2 * np.pi * np.outer(np.arange(128), k2 + 0.5) / 256.0
    Ekr, Eki = np.cos(angk), np.sin(angk)
    if not is_prompt:
        Ekr[64:] = 0
        Eki[64:] = 0
    c["E_kS"] = bf(np.concatenate([Ekr, -Eki], axis=1))
    c["E_kD"] = bf(np.concatenate([Eki, Ekr], axis=1))
    angM = -2 * np.pi * (n1[None, :, None] * (k2[:, None, None] + 0.5) / NFFT + n1[None, :, None] * k1[None, None, :] / 128.0)
    c["Mf"] = bf(np.concatenate([np.cos(angM), np.sin(angM)], axis=2).transpose(1, 0, 2))
    angG = 2 * np.pi * np.outer(k1, n1) / 128.0
    c["G1"] = bf(np.concatenate([np.cos(angG), np.sin(angG)], axis=1))
    c["G2"] = bf(np.concatenate([-np.sin(angG), np.cos(angG)], axis=1))
    angI = 2 * np.pi * (n1[:, None, None] + 128 * slot[None, None, :]) * (k2[None, :, None] + 0.5) / NFFT
    sc = 2.0 / NFFT
    c["Minv"] = bf(np.concatenate([sc * np.cos(angI), -sc * np.sin(angI)], axis=2).transpose(1, 0, 2))
    OH = np.zeros((32, 3 * 384), np.float32)
    mrow = np.zeros((12, 3 * 384), np.float32)
    for ri, r in enumerate((1, 4, 16)):
        for j in range(384):
            d = j - 127
            if 0 <= d <= 128:
                rel = (64 - d) * r
                OH[_t5_bucket(np.array(rel)), ri * 384 + j] = 1.0
            else:
                mrow[:, ri * 384 + j] = -30000.0
    c["OH"] = OH
    c["mrow"] = mrow
    bm = np.zeros((128, 256), np.float32)
    if not is_prompt:
        bm[:64, 128:] = -30000.0
        bm[64:, :128] = -30000.0
    c["bmask"] = bf(bm)
    c["bflag"] = np.full((128, 1), 1.0 if is_prompt else 0.0, np.float32)
    ident = np.eye(128, dtype=np.float32)
    c["ident"] = bf(ident)
    c["antiid"] = bf(ident[::-1])
    c["swap"] = bf(np.roll(ident, 64, axis=1))
    _CONST_CACHE[key] = c
    return c


CONST_SHAPES = {
    "zfT": ([33, T], F32), "negtn": ([128, 128], F32), "deltas_rep": ([128, 768], F32),
    "E_d": ([128, 384], BF16), "E_kS": ([128, 256], BF16), "E_kD": ([128, 256], BF16),
    "Mf": ([128, 128, 256], BF16), "G1": ([128, 256], BF16), "G2": ([128, 256], BF16),
    "Minv": ([128, 128, 256], BF16), "OH": ([32, 1152], F32), "mrow": ([12, 1152], F32),
    "bmask": ([128, 256], BF16), "bflag": ([128, 1], F32), "ident": ([128, 128], BF16),
    "antiid": ([128, 128], BF16), "swap": ([128, 128], BF16),
}

INPUT_SHAPES = {
    "x": [T, DM], "cT": [128, 8, 2], "w_ada": [DM, 3 * DM], "b_ada": [1, 3 * DM], "norm_g": [1, DM],
    "w_in": [DM, 8192], "short_wT": [128, 18, 3], "short_bT": [128, 18],
    "filt_w1": [33, 64], "filt_b1": [64, 1], "filt_fr1": [64, 1], "filt_w2": [64, 64], "filt_b2": [64, 1],
    "filt_fr2": [64, 1], "filt_w3": [64, 3072], "hyena_d": [2, 768],
    "w_proj_attn": [768, DM], "w_proj_hyena": [768, DM], "w_out": [DM, DM], "rel_bias": [32, 12],
    "final_g": [1, DM],
}


from contextlib import ExitStack


class K:
    pass


def build_program(phases=("p0", "pk", "p1", "p1b", "pa", "ph", "pf"), debug_outs=()):
    nc = bass.Bass("TRN2", target_bir_lowering=False)
    k = K()
    k.nc = nc
    din = {}
    for name, shp in INPUT_SHAPES.items():
        din[name] = nc.dram_tensor(name, shp, F32, kind="ExternalInput").ap()
    for name, (shp, dt_) in CONST_SHAPES.items():
        din[name] = nc.dram_tensor(name, shp, dt_, kind="ExternalInput").ap()
    k.din = din
    y = nc.dram_tensor("y", [T, DM], F32, kind="ExternalOutput").ap()
    k.y = y

    def scratch(name, shp, dt_):
        kind = "ExternalOutput" if name in debug_outs else "Internal"
        return nc.dram_tensor(name, shp, dt_, kind=kind).ap()

    k.qkT = scratch("qkT", [1536, T], BF16)
    k.gaT = scratch("gaT", [768, T], BF16)
    k.ghT = scratch("ghT", [768, T], BF16)
    k.mT = scratch("mT", [2048, T], BF16)
    k.uraw = scratch("uraw", [2304, T], BF16)
    k.uT = scratch("uT", [2304, T], BF16)
    k.vaug = scratch("vaug", [T, 1536], BF16)
    k.oaT = scratch("oaT", [768, T], BF16)
    k.ohT = scratch("ohT", [768, T], BF16)
    k.KFd = scratch("KFd", [2, NCB, 128, 128 * 3 * CB], BF16)
    k.Avec = scratch("Avec", [12, 1152], BF16)
    k.modrep = scratch("modrep", [2, 3 * DM], F32)

    with ExitStack() as top:
        sems = {e: top.enter_context(nc.semaphore("sem_" + e)) for e in ("pe", "act", "dve", "pool")}
        sems["sp"] = None
        ring = [top.enter_context(nc.semaphore("dr%d" % i)) for i in range(24)]
        S = Sched(nc, sems, ring)
        k.S = S
        ps = [top.enter_context(nc.psum_tensor("psb%d" % i, [128, 512], F32)) for i in range(8)]
        k.ps = ps
        k.ident = top.enter_context(nc.sbuf_tensor("s_ident", [128, 128], BF16))
        k.fg_rep = top.enter_context(nc.sbuf_tensor("s_fg_rep", [128, DM], F32))
        S.dma(k.ident[:], din["ident"][:, :], writes=["ident"])
        k.epsc = top.enter_context(nc.sbuf_tensor("s_epsc", [128, 2], F32))
        S.op("pool", lambda e: e.memset(k.epsc[:], EPS), writes=["epsc"])
        S.dma(k.fg_rep[:], din["final_g"][0:1, :].partition_broadcast(128), writes=["fg_rep"])

        if "p0" in phases:
            phase0(k)
            S.barrier()
        if "pk" in phases:
            phaseK(k)
            S.barrier()
        if "p1" in phases:
            phase1(k)
            S.barrier()
        if "p1b" in phases:
            phase1b(k)
            S.barrier()
        if "ph" in phases:
            phaseH(k)
            S.barrier()
        if "pa" in phases:
            phaseA(k)
            S.barrier()
        if "pf" in phases:
            phaseF(k)
            S.barrier()
        S.flush()
    return nc


def phase0(k):
    nc, S, din, ps = k.nc, k.S, k.din, k.ps
    with ExitStack() as st:
        wada = st.enter_context(nc.sbuf_tensor("s_wada", [128, 8, 3 * DM], F32))
        cT = st.enter_context(nc.sbuf_tensor("s_cT", [128, 8, 2], F32))
        scT = st.enter_context(nc.sbuf_tensor("s_scT", [128, 8, 2], F32))
        screp = st.enter_context(nc.sbuf_tensor("s_screp", [128, 2, 8, 128], F32))
        brep = st.enter_context(nc.sbuf_tensor("s_brep", [128, 3 * DM], F32))
        ngrep = st.enter_context(nc.sbuf_tensor("s_ngrep", [128, DM], F32))
        k.modr = st.enter_context(nc.sbuf_tensor("s_modr", [128, 2, 3 * DM], F32))
        S.dma(cT[:], din["cT"][:, :, :], writes=["cT"])
        for kk in range(8):
            S.dma(wada[:, kk, :], din["w_ada"][kk * 128:(kk + 1) * 128, :], writes=[("wada", kk)])
        S.dma(brep[:], din["b_ada"][0:1, :].partition_broadcast(128), writes=["brep"])
        S.dma(ngrep[:], din["norm_g"][0:1, :].partition_broadcast(128), writes=["ngrep"])
        S.op("act", lambda e: e.activation(out=scT[:], in_=cT[:], func=AF.Silu), reads=["cT"], writes=["scT"])
        for s in range(2):
            S.op("dve", lambda e, s=s: e.tensor_copy(out=screp[:, s, :, :], in_=bc(scT[:, :, s:s + 1], [128, 8, 128])),
                 reads=["scT"], writes=[("screp", s)])
        for s in range(2):
            for cc in range(6):
                pb = ps[(s * 6 + cc) % 4]
                pr = ("ps", (s * 6 + cc) % 4)
                for kk in range(8):
                    S.op("pe", lambda e, s=s, cc=cc, kk=kk, pb=pb: e.matmul(
                        pb[:, :], lhsT=screp[:, s, kk, :], rhs=wada[:, kk, cc * 512:(cc + 1) * 512],
                        start=(kk == 0), stop=(kk == 7)),
                        reads=[("screp", s), ("wada", kk)], writes=[pr])
                S.op("dve", lambda e, s=s, cc=cc, pb=pb: e.tensor_tensor(
                    out=k.modr[:, s, cc * 512:(cc + 1) * 512], in0=pb[:, :], in1=brep[:, cc * 512:(cc + 1) * 512], op=ALU.add),
                    reads=[pr, "brep"], writes=[("modr", s)])
            S.op("dve", lambda e, s=s: e.scalar_tensor_tensor(
                out=k.modr[:, s, DM:2 * DM], in0=k.modr[:, s, DM:2 * DM], scalar=1.0, in1=ngrep[:], op0=ALU.add, op1=ALU.mult),
                reads=[("modr", s), "ngrep"], writes=[("modr", s)])
            S.dma(k.modrep[s:s + 1, :], k.modr[0:1, s, :], reads=[("modr", s)])
        S.barrier()


def phase1(k):
    nc, S, din, ps = k.nc, k.S, k.din, k.ps
    with ExitStack() as st:
        winb = st.enter_context(nc.sbuf_tensor("s_winb", [128, 8, 8192], BF16))
        stg_ctx = ExitStack()
        stg = [stg_ctx.enter_context(nc.sbuf_tensor("s_wstg%d" % i, [128, 2048], F32)) for i in range(2)]
        n = 0
        for kk in range(8):
            for cc in range(4):
                b = n % 2
                S.dma(stg[b][:], din["w_in"][kk * 128:(kk + 1) * 128, cc * 2048:(cc + 1) * 2048], writes=[("wstg", b)])
                eng = ("act", "dve", "pool")[n % 3]
                if eng == "act":
                    S.op("act", lambda e, b=b, kk=kk, cc=cc: e.copy(out=winb[:, kk, cc * 2048:(cc + 1) * 2048], in_=stg[b][:]),
                         reads=[("wstg", b)], writes=[("winb", kk)])
                else:
                    S.op(eng, lambda e, b=b, kk=kk, cc=cc: e.tensor_copy(out=winb[:, kk, cc * 2048:(cc + 1) * 2048], in_=stg[b][:]),
                         reads=[("wstg", b)], writes=[("winb", kk)])
                n += 1
        S.barrier()
        stg_ctx.close()
        modr1 = st.enter_context(nc.sbuf_tensor("s_modr1", [128, 2, 2 * DM], F32))
        for s_ in range(2):
            S.dma(modr1[:, s_, :], k.modrep[s_:s_ + 1, 0:2 * DM].partition_broadcast(128), writes=[("modr", s_)])
        xt = [st.enter_context(nc.sbuf_tensor("s_xt%d" % i, [128, DM], F32)) for i in range(2)]
        xm = [st.enter_context(nc.sbuf_tensor("s_xm%d" % i, [128, DM], F32)) for i in range(1)]
        hb = [st.enter_context(nc.sbuf_tensor("s_hb%d" % i, [128, DM], BF16)) for i in range(2)]
        sq = st.enter_context(nc.sbuf_tensor("s_sqj", [128, DM], BF16))
        ss = [st.enter_context(nc.sbuf_tensor("s_ss%d" % i, [128, 2], F32)) for i in range(3)]
        hT = [st.enter_context(nc.sbuf_tensor("s_hT%d" % i, [128, 8, 512], BF16)) for i in range(2)]
        ev = [st.enter_context(nc.sbuf_tensor("s_ev%d" % i, [128, 512], BF16)) for i in range(6)]
        vst = [st.enter_context(nc.sbuf_tensor("s_vst%d" % i, [128, 12, 128], BF16)) for i in range(2)]
        for i in range(2):
            S.op("pool", lambda e, i=i: e.memset(vst[i][:], 1.0), writes=[("vst", i)])

        blocks = []
        for j in range(6):
            blocks.append((OQ + j * 128, k.qkT, j * 128, "q"))
        for j in range(6):
            blocks.append((OK_ + j * 128, k.qkT, 768 + j * 128, "copy"))
        for j in range(18):
            blocks.append((OU + j * 128, k.uraw, j * 128, "copy"))
        for j in range(6):
            blocks.append((OGA + j * 128, k.gaT, j * 128, "silu"))
        for j in range(6):
            blocks.append((OGH + j * 128, k.ghT, j * 128, "silu"))
        for j in range(8):
            blocks.append((OMA + j * 128, k.mT, j * 128, "sig"))
        for j in range(8):
            blocks.append((OMH + j * 128, k.mT, 1024 + j * 128, "sig"))

        tcount = 0
        evn = 0
        for ci in range(NCHUNK):
            seg = 0 if ci < NCHUNK // 2 else 1
            hTc = hT[ci % 2]
            hres = ("hT", ci % 2)
            for tt in range(4):
                t = ci * 4 + tt
                xb, xr = xt[t % 2], ("xt", t % 2)
                sb, sr = ss[t % 3], ("ss", t % 3)
                mb, mr = xm[0], ("xm", 0)
                hbb, hbr = hb[t % 2], ("hb", t % 2)
                S.dma(xb[:], din["x"][t * 128:(t + 1) * 128, :], writes=[xr])
                S.op("act", lambda e, xb=xb, sb=sb: e.activation(out=sq[:], in_=xb[:], func=AF.Square, scale=1.0 / 32.0,
                                                                 accum_out=sb[:, 0:1]),
                     reads=[xr], writes=["sqj", sr])
                S.op("act", lambda e, sb=sb: e.activation(out=sb[:, 1:2], in_=sb[:, 0:1], func=AF.Sqrt, bias=k.epsc[:, 0:1]),
                     reads=[sr], writes=[sr])
                S.op("dve", lambda e, sb=sb: e.reciprocal(out=sb[:, 1:2], in_=sb[:, 1:2]), reads=[sr], writes=[sr])
                S.op("dve", lambda e, xb=xb, sb=sb, mb=mb, seg=seg: e.scalar_tensor_tensor(
                    out=mb[:], in0=xb[:], scalar=sb[:, 1:2], in1=modr1[:, seg, DM:2 * DM], op0=ALU.mult, op1=ALU.mult),
                    reads=[xr, sr, ("modr", seg)], writes=[mr])
                S.op("pool", lambda e, mb=mb, hbb=hbb, seg=seg: e.tensor_tensor(
                    out=hbb[:], in0=mb[:], in1=modr1[:, seg, 0:DM], op=ALU.add),
                    reads=[mr, ("modr", seg)], writes=[hbr])
                pbank = ps[6 + (t % 2)]
                pres = ("ps", 6 + (t % 2))
                pT = pbank[:].bitcast(BF16)
                for kk in range(8):
                    S.op("pe", lambda e, kk=kk, pT=pT, hbb=hbb: e.transpose(
                        out=pT[:, kk * 128:(kk + 1) * 128], in_=hbb[:, kk * 128:(kk + 1) * 128], identity=k.ident[:]),
                        reads=[hbr, "ident"], writes=[pres])
                S.op("act", lambda e, pT=pT, hTc=hTc, tt=tt: e.copy(
                    out=hTc[:, :, tt * 128:(tt + 1) * 128], in_=pT.rearrange("p (k t) -> p k t", k=8)),
                    reads=[pres], writes=[hres])
            for bi, (wc, dst, drow, kind) in enumerate(blocks):
                pb = ps[bi % 4]
                pr = ("ps", bi % 4)
                for kk in range(8):
                    S.op("pe", lambda e, kk=kk, pb=pb, wc=wc, hTc=hTc: e.matmul(
                        pb[:, :], lhsT=winb[:, kk, wc:wc + 128], rhs=hTc[:, kk, :], start=(kk == 0), stop=(kk == 7)),
                        reads=[hres, ("winb", kk)], writes=[pr])
                eb, er = ev[evn % 6], ("ev", evn % 6)
                evn += 1
                if kind == "q":
                    S.op("act", lambda e, pb=pb, eb=eb: e.activation(out=eb[:], in_=pb[:, :], func=AF.Copy, scale=0.125),
                         reads=[pr], writes=[er])
                elif kind == "copy":
                    S.op("dve", lambda e, pb=pb, eb=eb: e.tensor_copy(out=eb[:], in_=pb[:, :]), reads=[pr], writes=[er])
                elif kind == "silu":
                    S.op("act", lambda e, pb=pb, eb=eb: e.activation(out=eb[:], in_=pb[:, :], func=AF.Silu), reads=[pr], writes=[er])
                else:
                    S.op("act", lambda e, pb=pb, eb=eb: e.activation(out=eb[:], in_=pb[:, :], func=AF.Sigmoid), reads=[pr], writes=[er])
                S.dma(dst[drow:drow + 128, ci * 512:(ci + 1) * 512], eb[:], reads=[er])
            for tt in range(4):
                t = ci * 4 + tt
                pa, pb2 = ps[4], ps[5]
                for kk in range(8):
                    S.op("pe", lambda e, kk=kk, tt=tt, hTc=hTc: e.matmul(
                        ps[4][:, :], lhsT=hTc[:, kk, tt * 128:(tt + 1) * 128], rhs=winb[:, kk, OV:OV + 512],
                        start=(kk == 0), stop=(kk == 7)), reads=[hres, ("winb", kk)], writes=[("ps", 4)])
                for kk in range(8):
                    S.op("pe", lambda e, kk=kk, tt=tt, hTc=hTc: e.matmul(
                        ps[5][:, 0:256], lhsT=hTc[:, kk, tt * 128:(tt + 1) * 128], rhs=winb[:, kk, OV + 512:OV + 768],
                        start=(kk == 0), stop=(kk == 7)), reads=[hres, ("winb", kk)], writes=[("ps", 5)])
                vb, vr = vst[t % 2], ("vst", t % 2)
                def vdst(vb, p0, npair):
                    base = vb[:, 2 * p0:2 * p0 + 1, 0:1]
                    return AP(base.tensor, base.offset, [list(base.ap[0]), [256, npair], [192, 2], [1, 64]])
                S.op("dve", lambda e, vb=vb, vdst=vdst: e.tensor_copy(
                    out=vdst(vb, 0, 4), in_=ps[4][:, :].rearrange("p (a b c) -> p a b c", a=4, b=2)),
                    reads=[("ps", 4)], writes=[vr])
                S.op("dve", lambda e, vb=vb, vdst=vdst: e.tensor_copy(
                    out=vdst(vb, 4, 2), in_=ps[5][:, 0:256].rearrange("p (a b c) -> p a b c", a=2, b=2)),
                    reads=[("ps", 5)], writes=[vr])
                S.dma(k.vaug[t * 128:(t + 1) * 128, :], vb[:].rearrange("p a b -> p (a b)"), reads=[vr])


def core_assignment():
    return [("p", 0), ("p", 1), ("s", 0, 1), ("s", 2, 3), ("s", 4, 5), ("s", 6, 7), ("s", 6, 7), ("s", 6, 7)]


def prep_core_inputs(inp, role):
    f32 = lambda a: np.ascontiguousarray(np.asarray(a, dtype=np.float32))
    m = {}
    if role[0] == "p":
        b = role[1]
        m["x"] = f32(inp["x_prompt"][b])
        c2 = np.stack([inp["c_prompt"][b], inp["c_prompt"][b]], 0)
    else:
        m["x"] = f32(np.concatenate([inp["x_sample"][role[1]], inp["x_sample"][role[2]]], 0))
        c2 = np.stack([inp["c_sample"][role[1]], inp["c_sample"][role[2]]], 0)
    c2 = np.asarray(c2, np.float32)
    m["cT"] = f32(c2.reshape(2, 8, 128).transpose(2, 1, 0))
    m["w_ada"] = f32(inp["w_ada"][0])
    m["b_ada"] = f32(inp["b_ada"][0][None])
    m["norm_g"] = f32(inp["norm_g"][0][None])
    m["w_in"] = f32(inp["w_in"][0])
    m["short_wT"] = f32(np.asarray(inp["short_w"][0]).reshape(3, 18, 128).transpose(2, 1, 0))
    m["short_bT"] = f32(np.asarray(inp["short_b"][0]).reshape(18, 128).T)
    m["filt_w1"] = f32(inp["filt_w1"][0])
    m["filt_b1"] = f32(np.asarray(inp["filt_b1"][0])[:, None])
    m["filt_fr1"] = f32(np.asarray(inp["filt_freq1"][0])[:, None])
    m["filt_w2"] = f32(inp["filt_w2"][0])
    m["filt_b2"] = f32(np.asarray(inp["filt_b2"][0])[:, None])
    m["filt_fr2"] = f32(np.asarray(inp["filt_freq2"][0])[:, None])
    m["filt_w3"] = f32(inp["filt_w3"][0])
    m["hyena_d"] = f32(inp["hyena_d"][0])
    m["w_proj_attn"] = f32(inp["w_proj_attn"][0])
    m["w_proj_hyena"] = f32(inp["w_proj_hyena"][0])
    m["w_out"] = f32(inp["w_out"][0])
    m["rel_bias"] = f32(inp["rel_bias"])
    m["final_g"] = f32(np.asarray(inp["final_g"])[None])
    m.update(build_consts(role[0] == "p"))
    return m


_NC_CACHE = {}


def kernel(**inputs):
    inp = {k_: np.asarray(v) for k_, v in inputs.items()}
    if "full" not in _NC_CACHE:
        _NC_CACHE["full"] = build_program()
    nc = _NC_CACHE["full"]
    roles = core_assignment()
    in_maps = [prep_core_inputs(inp, r) for r in roles]
    res = run_bass_kernel_spmd(nc, in_maps, core_ids=list(range(8)))
    outs = [np.asarray(r["y"], dtype=np.float32) for r in res.results]
    y_prompt = np.stack([outs[0], outs[1]], 0)
    ys = []
    for c in range(2, 6):
        ys.append(outs[c][:8192])
        ys.append(outs[c][8192:])
    y_sample = np.stack(ys, 0)
    return (y_prompt, y_sample)


def phase1b(k):
    nc, S, din = k.nc, k.S, k.din
    W = 2048
    with ExitStack() as st:
        swT = st.enter_context(nc.sbuf_tensor("s_swT", [128, 18, 3], F32))
        sbT = st.enter_context(nc.sbuf_tensor("s_sbT", [128, 18], F32))
        bfl = st.enter_context(nc.sbuf_tensor("s_bfl", [128, 1], F32))
        S.dma(swT[:], din["short_wT"][:, :, :], writes=["swT"])
        S.dma(sbT[:], din["short_bT"][:, :], writes=["sbT"])
        S.dma(bfl[:], din["bflag"][:, :], writes=["bfl"])
        ib = [st.enter_context(nc.sbuf_tensor("s_cin%d" % i, [128, W + 2], BF16)) for i in range(3)]
        t1 = [st.enter_context(nc.sbuf_tensor("s_ct%d" % i, [128, W], F32)) for i in range(2)]
        ob = [st.enter_context(nc.sbuf_tensor("s_cout%d" % i, [128, W], BF16)) for i in range(3)]
        n = 0
        for ub in range(18):
            for tc in range(T // W):
                a, ar = ib[n % 3], ("cin", n % 3)
                tb, tr = t1[n % 2], ("ct", n % 2)
                o, orr = ob[n % 3], ("cout", n % 3)
                eng = "pool"
                lo = tc * W - 1
                hi = tc * W + W + 1
                c0 = 0
                if lo < 0:
                    S.op("pool", lambda e, a=a: e.memset(a[:, 0:1], 0.0), writes=[ar])
                    lo, c0 = 0, 1
                c1 = W + 2
                if hi > T:
                    S.op("pool", lambda e, a=a: e.memset(a[:, W + 1:W + 2], 0.0), writes=[ar])
                    hi, c1 = T, W + 1
                S.dma(a[:, c0:c1], k.uraw[ub * 128:(ub + 1) * 128, lo:hi], writes=[ar])
                if tc * W == T // 2:
                    S.op(eng, lambda e, a=a: e.tensor_scalar(out=a[:, 0:1], in0=a[:, 0:1], scalar1=bfl[:, 0:1], scalar2=None,
                                                             op0=ALU.mult), reads=[ar, "bfl"], writes=[ar])
                if tc * W + W == T // 2:
                    S.op(eng, lambda e, a=a: e.tensor_scalar(out=a[:, W + 1:W + 2], in0=a[:, W + 1:W + 2], scalar1=bfl[:, 0:1],
                                                             scalar2=None, op0=ALU.mult), reads=[ar, "bfl"], writes=[ar])
                S.op("pool", lambda e, a=a, tb=tb, ub=ub: e.tensor_scalar(
                    out=tb[:], in0=a[:, 1:W + 1], scalar1=swT[:, ub, 1:2], scalar2=sbT[:, ub:ub + 1], op0=ALU.mult, op1=ALU.add),
                    reads=[ar, "swT", "sbT"], writes=[tr])
                S.op("dve", lambda e, a=a, tb=tb, ub=ub: e.scalar_tensor_tensor(
                    out=tb[:], in0=a[:, 0:W], scalar=swT[:, ub, 0:1], in1=tb[:], op0=ALU.mult, op1=ALU.add),
                    reads=[ar, tr, "swT"], writes=[tr])
                S.op("dve", lambda e, a=a, tb=tb, o=o, ub=ub: e.scalar_tensor_tensor(
                    out=o[:], in0=a[:, 2:W + 2], scalar=swT[:, ub, 2:3], in1=tb[:], op0=ALU.mult, op1=ALU.add),
                    reads=[ar, tr, "swT"], writes=[orr])
                S.dma(k.uT[ub * 128:(ub + 1) * 128, tc * W:(tc + 1) * W], o[:], reads=[orr])
                n += 1


def sin_wrapped(S, src_ps, pres, dst, dres, scale_ap, bias_ap, tmp, tres, tmp2, t2res, nparts, ncols):
    PI = math.pi
    S.op("dve", lambda e: e.tensor_scalar(out=tmp[0:nparts, 0:ncols], in0=src_ps, scalar1=scale_ap, scalar2=bias_ap,
                                          op0=ALU.mult, op1=ALU.add), reads=[pres], writes=[tres])
    S.op("dve", lambda e: e.tensor_scalar(out=tmp2[0:nparts, 0:ncols], in0=tmp[0:nparts, 0:ncols], scalar1=PI, scalar2=-2 * PI,
                                          op0=ALU.is_gt, op1=ALU.mult), reads=[tres], writes=[t2res])
    S.op("dve", lambda e: e.tensor_tensor(out=tmp2[0:nparts, 0:ncols], in0=tmp2[0:nparts, 0:ncols], in1=tmp[0:nparts, 0:ncols],
                                          op=ALU.add), reads=[tres, t2res], writes=[t2res])
    S.op("dve", lambda e: e.tensor_scalar(out=tmp[0:nparts, 0:ncols], in0=tmp[0:nparts, 0:ncols], scalar1=-PI, scalar2=2 * PI,
                                          op0=ALU.is_lt, op1=ALU.mult), reads=[tres], writes=[tres])
    S.op("dve", lambda e: e.tensor_tensor(out=tmp[0:nparts, 0:ncols], in0=tmp2[0:nparts, 0:ncols], in1=tmp[0:nparts, 0:ncols],
                                          op=ALU.add), reads=[tres, t2res], writes=[tres])
    S.op("act", lambda e: e.activation(out=dst, in_=tmp[0:nparts, 0:ncols], func=AF.Sin), reads=[tres], writes=[dres])


def phaseK(k):
    nc, S, din, ps = k.nc, k.S, k.din, k.ps
    with ExitStack() as st:
        hdn = st.enter_context(nc.sbuf_tensor("s_hdn2T", [64, T], BF16))
        w3sd = st.enter_context(nc.sbuf_tensor("s_w3sd", [64, 2, NCB, 2, CB], BF16))
        with ExitStack() as s2:
            w1 = s2.enter_context(nc.sbuf_tensor("s_fw1", [33, 64], F32))
            w2 = s2.enter_context(nc.sbuf_tensor("s_fw2", [64, 64], F32))
            w3 = s2.enter_context(nc.sbuf_tensor("s_fw3", [64, 3072], F32))
            fv = s2.enter_context(nc.sbuf_tensor("s_fv", [64, 6], F32))
            zc = [s2.enter_context(nc.sbuf_tensor("s_zc%d" % i, [33, 512], F32)) for i in range(2)]
            ta = s2.enter_context(nc.sbuf_tensor("s_fta", [64, 512], F32))
            tb = s2.enter_context(nc.sbuf_tensor("s_ftb", [64, 512], F32))
            h1 = s2.enter_context(nc.sbuf_tensor("s_fh1", [64, 512], F32))
            S.dma(w1[:], din["filt_w1"][:, :], writes=["fw1"])
            S.dma(w2[:], din["filt_w2"][:, :], writes=["fw2"])
            S.dma(w3[:], din["filt_w3"][:, :], writes=["fw3"])
            for i, nm in enumerate(("filt_b1", "filt_fr1", "filt_b2", "filt_fr2")):
                S.dma(fv[:, i:i + 1], din[nm][:, :], writes=["fv"])
            S.op("dve", lambda e: e.tensor_tensor(out=fv[:, 4:5], in0=fv[:, 0:1], in1=fv[:, 1:2], op=ALU.mult), reads=["fv"], writes=["fv"])
            S.op("dve", lambda e: e.tensor_tensor(out=fv[:, 5:6], in0=fv[:, 2:3], in1=fv[:, 3:4], op=ALU.mult), reads=["fv"], writes=["fv"])
            w3v = w3[:].rearrange("p (o d b c) -> p o d b c", o=2, d=2, b=NCB)
            for o in range(2):
                S.op("dve", lambda e, o=o: e.tensor_tensor(out=w3sd[:, o, :, 0, :], in0=w3v[:, o, 0], in1=w3v[:, o, 1], op=ALU.add),
                     reads=["fw3"], writes=["w3sd"])
                S.op("dve", lambda e, o=o: e.tensor_tensor(out=w3sd[:, o, :, 1, :], in0=w3v[:, o, 0], in1=w3v[:, o, 1], op=ALU.subtract),
                     reads=["fw3"], writes=["w3sd"])
            for ci in range(T // 512):
                z, zr = zc[ci % 2], ("zc", ci % 2)
                S.dma(z[:], din["zfT"][:, ci * 512:(ci + 1) * 512], writes=[zr])
                S.op("pe", lambda e, z=z: e.matmul(ps[0][0:64, :], lhsT=w1[:], rhs=z[:], start=True, stop=True),
                     reads=[zr, "fw1"], writes=[("ps", 0)])
                sin_wrapped(S, ps[0][0:64, :], ("ps", 0), h1[:], "fh1", fv[:, 1:2], fv[:, 4:5], ta, "fta", tb, "ftb", 64, 512)
                S.op("pe", lambda e: e.matmul(ps[1][0:64, :], lhsT=w2[:], rhs=h1[:], start=True, stop=True),
                     reads=["fh1", "fw2"], writes=[("ps", 1)])
                sin_wrapped(S, ps[1][0:64, :], ("ps", 1), hdn[:, ci * 512:(ci + 1) * 512], "hdn", fv[:, 3:4], fv[:, 5:6],
                            ta, "fta", tb, "ftb", 64, 512)
            S.barrier()
        H = st.enter_context(nc.sbuf_tensor("s_H", [128, 2, CB, 128], BF16))
        Yk = st.enter_context(nc.sbuf_tensor("s_Yk", [128, 128, 4, CB], BF16))
        dec = st.enter_context(nc.sbuf_tensor("s_dec", [128, 128, CB], BF16))
        negtn = st.enter_context(nc.sbuf_tensor("s_negtn", [128, 128], F32))
        drep = st.enter_context(nc.sbuf_tensor("s_drep", [128, 768], F32))
        EkS = st.enter_context(nc.sbuf_tensor("s_EkS", [128, 256], BF16))
        EkD = st.enter_context(nc.sbuf_tensor("s_EkD", [128, 256], BF16))
        Mfb = [st.enter_context(nc.sbuf_tensor("s_Mfb%d" % i, [128, 16, 256], BF16)) for i in range(2)]
        KFs = [st.enter_context(nc.sbuf_tensor("s_KFs%d" % i, [128, 4, 3, CB], BF16)) for i in range(3)]
        S.dma(negtn[:], din["negtn"][:, :], writes=["negtn"])
        S.dma(drep[:], din["deltas_rep"][:, :], writes=["drep"])
        S.dma(EkS[:], din["E_kS"][:, :], writes=["EkS"])
        S.dma(EkD[:], din["E_kD"][:, :], writes=["EkD"])
        mfn = 0
        kfn = 0
        pn = 0
        for cb in range(NCB):
            c0 = cb * CB
            for n1 in range(128):
                S.op("act", lambda e, n1=n1, c0=c0: e.activation(out=dec[:, n1, :], in_=drep[:, c0:c0 + CB], func=AF.Exp,
                                                                scale=negtn[:, n1:n1 + 1]),
                     reads=["drep", "negtn"], writes=["dec"])
            for o in range(2):
                for g in range(32):
                    pb, pr = ps[pn % 4], ("ps", pn % 4)
                    pn += 1
                    for j in range(4):
                        n1 = 4 * g + j
                        S.op("pe", lambda e, pb=pb, j=j, n1=n1, o=o, cb=cb: e.matmul(
                            pb[:, j * 128:(j + 1) * 128], lhsT=hdn[:, n1:T:128],
                            rhs=w3sd[:, o, cb].rearrange("p a b -> p (a b)"), start=True, stop=True),
                            reads=["hdn", "w3sd"], writes=[pr])
                    hv = H[:, 0:1, 0:1, 4 * g:4 * g + 1]
                    hout = AP(hv.tensor, hv.offset, [list(hv.ap[0]), [1, 4], [CB * 128, 2], [128, CB]])
                    dv = dec[:, 4 * g:4 * g + 1, 0:1]
                    din1 = AP(dv.tensor, dv.offset, [list(dv.ap[0]), [CB, 4], [0, 2], [1, CB]])
                    S.op("dve", lambda e, pb=pb, hout=hout, din1=din1: e.tensor_tensor(
                        out=hout, in0=pb[:, :].rearrange("p (j s c) -> p j s c", j=4, s=2), in1=din1, op=ALU.mult),
                        reads=[pr, "dec"], writes=["H"])
                for c in range(CB):
                    pb, pr = ps[pn % 4], ("ps", pn % 4)
                    pn += 1
                    S.op("pe", lambda e, pb=pb, c=c: e.matmul(pb[:, 0:256], lhsT=H[:, 0, c, :], rhs=EkS[:], start=True, stop=True),
                         reads=["H", "EkS"], writes=[pr])
                    S.op("pe", lambda e, pb=pb, c=c: e.matmul(pb[:, 256:512], lhsT=H[:, 1, c, :], rhs=EkD[:], start=True, stop=True),
                         reads=["H", "EkD"], writes=[pr])
                    yv = Yk[:, 0:1, 0:1, c:c + 1]
                    yout = AP(yv.tensor, yv.offset, [list(yv.ap[0]), [CB, 2], [2 * CB, 2], [4 * CB, 128]])
                    pin = pb[:, :].rearrange("p (a b k) -> p a b k", a=2, b=2)
                    if c % 2 == 0:
                        S.op("act", lambda e, yout=yout, pin=pin: e.copy(out=yout, in_=pin), reads=[pr], writes=["Yk"])
                    else:
                        S.op("dve", lambda e, yout=yout, pin=pin: e.tensor_copy(out=yout, in_=pin), reads=[pr], writes=["Yk"])
                for g in range(32):
                    pb, pr = ps[4 + pn % 4], ("ps", 4 + pn % 4)
                    pn += 1
                    for j in range(4):
                        k2 = 4 * g + j
                        if k2 % 16 == 0:
                            mb_, mr_ = Mfb[mfn % 2], ("Mfb", mfn % 2)
                            mfn += 1
                            S.dma(mb_[:], din["Mf"][:, k2:k2 + 16, :], writes=[mr_])
                        S.op("pe", lambda e, pb=pb, j=j, k2=k2, mb_=mb_: e.matmul(
                            pb[:, j * 128:(j + 1) * 128], lhsT=mb_[:, k2 % 16, 0:128],
                            rhs=Yk[:, k2, 0:2, :].rearrange("p a b -> p (a b)"), start=True, stop=False),
                            reads=["Yk", mr_], writes=[pr])
                        S.op("pe", lambda e, pb=pb, j=j, k2=k2, mb_=mb_: e.matmul(
                            pb[:, j * 128:(j + 1) * 128], lhsT=mb_[:, k2 % 16, 128:256],
                            rhs=Yk[:, k2, 2:4, :].rearrange("p a b -> p (a b)"), start=False, stop=True),
                            reads=["Yk", mr_], writes=[pr])
                    kb, kr = KFs[kfn % 3], ("KFs", kfn % 3)
                    kfn += 1
                    pv = pb[:, :].rearrange("p (j s c) -> p j s c", j=4, s=2)
                    S.op("act", lambda e, kb=kb, pv=pv: e.copy(out=kb[:, :, 0:2, :], in_=pv), reads=[pr], writes=[kr])
                    S.op("dve", lambda e, kb=kb, pv=pv: e.tensor_scalar(out=kb[:, :, 2, :], in0=pv[:, :, 1, :], scalar1=-1.0, scalar2=None,
                                                                        op0=ALU.mult), reads=[pr], writes=[kr])
                    S.dma(k.KFd[o, cb, :, g * 4 * 3 * CB:(g + 1) * 4 * 3 * CB], kb[:].rearrange("p a b c -> p (a b c)"), reads=[kr])


def phaseH(k):
    nc, S, din, ps = k.nc, k.S, k.din, k.ps
    with ExitStack() as st:
        bufA = st.enter_context(nc.sbuf_tensor("s_hA", [128, CB, 128], BF16))
        bufB = st.enter_context(nc.sbuf_tensor("s_hB", [128, CB, 128], BF16))
        bufC = st.enter_context(nc.sbuf_tensor("s_hC", [128, CB, 128], BF16))
        Yd = st.enter_context(nc.sbuf_tensor("s_Yd", [128, 128, 3, CB], BF16))
        Pb = st.enter_context(nc.sbuf_tensor("s_P", [128, 128, 2, CB], BF16))
        Zs = st.enter_context(nc.sbuf_tensor("s_Zs", [128, 128, 2, CB], BF16))
        Ed = st.enter_context(nc.sbuf_tensor("s_Ed", [128, 384], BF16))
        G1 = st.enter_context(nc.sbuf_tensor("s_G1", [128, 256], BF16))
        G2 = st.enter_context(nc.sbuf_tensor("s_G2", [128, 256], BF16))
        drep = st.enter_context(nc.sbuf_tensor("s_hdrep", [128, 2, CB], F32))
        Mb = [st.enter_context(nc.sbuf_tensor("s_Mb%d" % i, [128, 8, 256], BF16)) for i in range(2)]
        KFb = [st.enter_context(nc.sbuf_tensor("s_KFb%d" % i, [128, 4, 3, CB], BF16)) for i in range(3)]
        t1 = [st.enter_context(nc.sbuf_tensor("s_ht1_%d" % i, [128, 4, 2, CB], F32)) for i in range(2)]
        t2 = [st.enter_context(nc.sbuf_tensor("s_ht2_%d" % i, [128, 4, 2, CB], F32)) for i in range(2)]
        te = [st.enter_context(nc.sbuf_tensor("s_hte%d" % i, [128, 8, CB], F32)) for i in range(2)]
        S.dma(Ed[:], din["E_d"][:, :], writes=["Ed"])
        S.dma(G1[:], din["G1"][:, :], writes=["G1"])
        S.dma(G2[:], din["G2"][:, :], writes=["G2"])
        ohs = AP(Pb[:].tensor, Pb[:].offset, [[Pb[:].ap[0][0], 64], [1, T]])
        mn = 0
        kn = 0
        tn_ = 0
        pn = 0

        def load_blk(buf, res, row0):
            src = AP(k.uT.tensor, k.uT[row0:row0 + 1, 0:1].offset, [[128, 128], [T, CB], [1, 128]])
            S.dma(buf[:], src, writes=[res])

        for cb in range(NCB):
            c0 = cb * CB
            load_blk(bufA, "hA", c0)
            load_blk(bufB, "hB", 768 + c0)
            for o in range(2):
                S.dma(drep[:, o, :], din["hyena_d"][o:o + 1, c0:c0 + CB].partition_broadcast(128), writes=["hdrep"])
            for o in range(2):
                Din, dres = (bufA, "hA") if o == 0 else (bufC, "hC")
                for c in range(CB):
                    pb, pr = ps[pn % 4], ("ps", pn % 4)
                    pn += 1
                    S.op("pe", lambda e, pb=pb, c=c, Din=Din: e.matmul(pb[:, 0:384], lhsT=Din[:, c, :], rhs=Ed[:], start=True, stop=True),
                         reads=[dres, "Ed"], writes=[pr])
                    yv = Yd[:, 0:1, 0:1, c:c + 1]
                    yout = AP(yv.tensor, yv.offset, [list(yv.ap[0]), [CB, 3], [3 * CB, 128]])
                    pin = pb[:, 0:384].rearrange("p (b k) -> p b k", b=3)
                    if c % 2 == 0:
                        S.op("act", lambda e, yout=yout, pin=pin: e.copy(out=yout, in_=pin), reads=[pr], writes=["Yd"])
                    else:
                        S.op("dve", lambda e, yout=yout, pin=pin: e.tensor_copy(out=yout, in_=pin), reads=[pr], writes=["Yd"])
                for g in range(32):
                    pb, pr = ps[4 + pn % 4], ("ps", 4 + pn % 4)
                    pn += 1
                    kb, kr = KFb[kn % 3], ("KFb", kn % 3)
                    kn += 1
                    S.dma(kb[:].rearrange("p a b c -> p (a b c)"), k.KFd[o, cb, :, g * 12 * CB:(g + 1) * 12 * CB], writes=[kr])
                    for j in range(4):
                        k2 = 4 * g + j
                        if k2 % 8 == 0:
                            mb_, mr_ = Mb[mn % 2], ("Mb", mn % 2)
                            mn += 1
                            S.dma(mb_[:], din["Mf"][:, k2:k2 + 8, :], writes=[mr_])
                        S.op("pe", lambda e, pb=pb, j=j, k2=k2, mb_=mb_: e.matmul(
                            pb[:, j * 128:(j + 1) * 128], lhsT=mb_[:, k2 % 8, 0:128],
                            rhs=Yd[:, k2, 1:3, :].rearrange("p a b -> p (a b)"), start=True, stop=False),
                            reads=["Yd", mr_], writes=[pr])
                        S.op("pe", lambda e, pb=pb, j=j, k2=k2, mb_=mb_: e.matmul(
                            pb[:, j * 128:(j + 1) * 128], lhsT=mb_[:, k2 % 8, 128:256],
                            rhs=Yd[:, k2, 0:2, :].rearrange("p a b -> p (a b)"), start=False, stop=True),
                            reads=["Yd", mr_], writes=[pr])
                    a1, a1r = t1[tn_ % 2], ("ht1", tn_ % 2)
                    a2, a2r = t2[tn_ % 2], ("ht2", tn_ % 2)
                    tn_ += 1
                    pv = pb[:, :].rearrange("p (j s c) -> p j s c", j=4, s=2)
                    S.op("dve", lambda e, a1=a1, pv=pv, kb=kb: e.tensor_tensor(
                        out=a1[:], in0=pv, in1=bc(kb[:, :, 0:1, :], [128, 4, 2, CB]), op=ALU.mult), reads=[pr, kr], writes=[a1r])
                    S.op("dve", lambda e, a2=a2, pv=pv, kb=kb: e.tensor_tensor(
                        out=a2[:], in0=pv, in1=kb[:, :, 1:3, :], op=ALU.mult), reads=[pr, kr], writes=[a2r])
                    S.op("pool", lambda e, a1=a1, a2=a2, g=g: e.tensor_tensor(
                        out=Pb[:, 4 * g:4 * g + 4, 0, :], in0=a1[:, :, 0, :], in1=a2[:, :, 1, :], op=ALU.add),
                        reads=[a1r, a2r], writes=["P"])
                    S.op("pool", lambda e, a1=a1, a2=a2, g=g: e.tensor_tensor(
                        out=Pb[:, 4 * g:4 * g + 4, 1, :], in0=a1[:, :, 1, :], in1=a2[:, :, 0, :], op=ALU.add),
                        reads=[a1r, a2r], writes=["P"])
                for c2 in range(CB // 2):
                    pb, pr = ps[pn % 4], ("ps", pn % 4)
                    pn += 1
                    for h in range(2):
                        c = 2 * c2 + h
                        S.op("pe", lambda e, pb=pb, c=c, h=h: e.matmul(pb[:, h * 256:(h + 1) * 256], lhsT=Pb[:, :, 0, c], rhs=G1[:],
                                                                       start=True, stop=False), reads=["P", "G1"], writes=[pr])
                        S.op("pe", lambda e, pb=pb, c=c, h=h: e.matmul(pb[:, h * 256:(h + 1) * 256], lhsT=Pb[:, :, 1, c], rhs=G2[:],
                                                                       start=False, stop=True), reads=["P", "G2"], writes=[pr])
                    zv = Zs[:, 0:1, 0:1, 2 * c2:2 * c2 + 1]
                    zout = AP(zv.tensor, zv.offset, [list(zv.ap[0]), [1, 2], [CB, 2], [2 * CB, 128]])
                    pin = pb[:, :].rearrange("p (h b n) -> p h b n", h=2, b=2)
                    if c2 % 2 == 0:
                        S.op("act", lambda e, zout=zout, pin=pin: e.copy(out=zout, in_=pin), reads=[pr], writes=["Zs"])
                    else:
                        S.op("dve", lambda e, zout=zout, pin=pin: e.tensor_copy(out=zout, in_=pin), reads=[pr], writes=["Zs"])
                if o == 1:
                    load_blk(bufA, "hA", 1536 + c0)
                Xg, xres = (bufB, "hB") if o == 0 else (bufA, "hA")
                for g in range(16):
                    pb, pr = ps[4 + pn % 4], ("ps", 4 + pn % 4)
                    pn += 1
                    for j in range(8):
                        n1 = 8 * g + j
                        if n1 % 8 == 0:
                            mb_, mr_ = Mb[mn % 2], ("Mb", mn % 2)
                            mn += 1
                            S.dma(mb_[:], din["Minv"][:, n1:n1 + 8, :], writes=[mr_])
                        S.op("pe", lambda e, pb=pb, j=j, n1=n1, mb_=mb_: e.matmul(
                            pb[:, j * CB:(j + 1) * CB], lhsT=mb_[:, n1 % 8, 0:128], rhs=Zs[:, n1, 0, :], start=True, stop=False),
                            reads=["Zs", mr_], writes=[pr])
                        S.op("pe", lambda e, pb=pb, j=j, n1=n1, mb_=mb_: e.matmul(
                            pb[:, j * CB:(j + 1) * CB], lhsT=mb_[:, n1 % 8, 128:256], rhs=Zs[:, n1, 1, :], start=False, stop=True),
                            reads=["Zs", mr_], writes=[pr])
                    tb, tr = te[g % 2], ("hte", g % 2)
                    zin = Din[:, :, 8 * g:8 * g + 8].rearrange("p c j -> p j c")
                    xin = Xg[:, :, 8 * g:8 * g + 8].rearrange("p c j -> p j c")
                    S.op("pool", lambda e, tb=tb, zin=zin, o=o, c0=c0: e.tensor_tensor(
                        out=tb[:], in0=zin, in1=bc(drep[:, o:o + 1, :], [128, 8, CB]), op=ALU.mult),
                        reads=[dres, "hdrep"], writes=[tr])
                    S.op("dve", lambda e, tb=tb, pb=pb: e.tensor_tensor(
                        out=tb[:], in0=pb[:, :].rearrange("p (j c) -> p j c", j=8), in1=tb[:], op=ALU.add), reads=[pr, tr], writes=[tr])
                    if o == 0:
                        zo = bufC[:, :, 8 * g:8 * g + 8].rearrange("p c j -> p j c")
                        S.op("pool", lambda e, tb=tb, xin=xin, zo=zo: e.tensor_tensor(out=zo, in0=tb[:], in1=xin, op=ALU.mult),
                             reads=[tr, xres], writes=["hC"])
                    else:
                        z3v = bufB[:].rearrange("p c j -> p (c j)")[:, 8 * g * CB:(8 * g + 8) * CB].rearrange("p (j c) -> p j c", j=8)
                        S.op("pool", lambda e, tb=tb, xin=xin, z3v=z3v: e.tensor_tensor(out=z3v, in0=tb[:], in1=xin, op=ALU.mult),
                             reads=[tr, xres], writes=["hB"])
            z3 = bufB[:].rearrange("p c j -> p (c j)")
            for g in range(16):
                pb, pr = ps[pn % 4], ("ps", pn % 4)
                pn += 1
                pT = pb[:].bitcast(BF16)
                for j in range(8):
                    n1 = 8 * g + j
                    S.op("pe", lambda e, pT=pT, j=j, n1=n1: e.transpose(out=pT[0:CB, j * 128:(j + 1) * 128],
                                                                        in_=z3[:, n1 * CB:(n1 + 1) * CB], identity=k.ident[:]),
                         reads=["hB", "ident"], writes=[pr])
                ov = AP(ohs.tensor, ohs.offset + 8 * g, [list(ohs.ap[0]), [1, 8], [128, 128]])
                pin = pT[0:CB, :].rearrange("p (j n) -> p j n", j=8)
                if g % 2 == 0:
                    S.op("act", lambda e, ov=ov, pin=pin: e.copy(out=ov, in_=pin), reads=[pr], writes=["P"])
                else:
                    S.op("dve", lambda e, ov=ov, pin=pin: e.tensor_copy(out=ov, in_=pin), reads=[pr], writes=["P"])
            S.dma(k.ohT[c0:c0 + CB, :], ohs, reads=["P"])


def phaseA(k):
    nc, S, din, ps = k.nc, k.S, k.din, k.ps
    SPAN = 4096
    with ExitStack() as st:
        Hk = st.enter_context(nc.sbuf_tensor("s_Hk", [128, 3, 12, 256], BF16))
        J = st.enter_context(nc.sbuf_tensor("s_J", [128, 128], BF16))
        bm = st.enter_context(nc.sbuf_tensor("s_bm", [128, 256], BF16))
        swb = st.enter_context(nc.sbuf_tensor("s_swb", [128, 128], BF16))
        swf = st.enter_context(nc.sbuf_tensor("s_swf", [128, 128], F32))
        selA = st.enter_context(nc.sbuf_tensor("s_selA", [128, 128], F32))
        selB = st.enter_context(nc.sbuf_tensor("s_selB", [128, 128], F32))
        with ExitStack() as s2:
            rb = s2.enter_context(nc.sbuf_tensor("s_rb", [32, 12], F32))
            oh = s2.enter_context(nc.sbuf_tensor("s_oh", [32, 1152], F32))
            mr = s2.enter_context(nc.sbuf_tensor("s_mrow", [12, 1152], F32))
            av = s2.enter_context(nc.sbuf_tensor("s_av", [12, 1152], BF16))
            S.dma(rb[:], din["rel_bias"][:, :], writes=["rb"])
            S.dma(oh[:], din["OH"][:, :], writes=["oh"])
            S.dma(mr[:], din["mrow"][:, :], writes=["mrow"])
            for i in range(3):
                S.op("pe", lambda e, i=i: e.matmul(ps[i][0:12, 0:384], lhsT=rb[:], rhs=oh[:, i * 384:(i + 1) * 384], start=True, stop=True),
                     reads=["rb", "oh"], writes=[("ps", i)])
                S.op("dve", lambda e, i=i: e.tensor_tensor(out=av[:, i * 384:(i + 1) * 384], in0=ps[i][0:12, 0:384],
                                                           in1=mr[:, i * 384:(i + 1) * 384], op=ALU.add),
                     reads=[("ps", i), "mrow"], writes=["av"])
            S.dma(k.Avec[:, :], av[:], reads=["av"], writes=["Avec"])
            for h in range(12):
                for ri in range(3):
                    src = AP(k.Avec.tensor, k.Avec[h:h + 1, ri * 384:ri * 384 + 1].offset, [[1, 128], [1, 256]])
                    S.dma(Hk[:, ri, h, :], src, reads=["Avec"], writes=["Hk"])
            S.dma(J[:], din["antiid"][:, :], writes=["J"])
            S.dma(bm[:], din["bmask"][:, :], writes=["bm"])
            S.dma(swb[:], din["swap"][:, :], writes=["swb"])
            S.op("dve", lambda e: e.tensor_copy(out=swf[:], in_=swb[:]), reads=["swb"], writes=["swf"])
            S.op("pool", lambda e: e.memset(selA[:], 0.0), writes=["sel"])
            S.op("pool", lambda e: e.memset(selB[:], 0.0), writes=["sel"])
            S.op("dve", lambda e: e.tensor_copy(out=selA[:, 0:64], in_=swb[:, 0:64]), reads=["swb", "sel"], writes=["sel"])
            S.op("dve", lambda e: e.tensor_copy(out=selB[:, 64:128], in_=swb[:, 64:128]), reads=["swb", "sel"], writes=["sel"])
            S.barrier()
        TP = T + 2 * PAD
        qAB = st.enter_context(nc.sbuf_tensor("s_qAB", [128, 2, TP], BF16))
        kT = st.enter_context(nc.sbuf_tensor("s_kT", [128, TP], BF16))
        S.op("pool", lambda e: e.memset(qAB[:], 0.0), writes=["qAB"])
        S.op("pool", lambda e: e.memset(kT[:, 0:PAD], 0.0), writes=["kT"])
        S.op("pool", lambda e: e.memset(kT[:, PAD + T:TP], 0.0), writes=["kT"])
        acc = st.enter_context(nc.sbuf_tensor("s_acc", [128, 2, SPAN], F32))
        OW = 2048
        oT = [st.enter_context(nc.sbuf_tensor("s_oT%d" % i, [128, OW], BF16)) for i in range(2)]
        rden = [st.enter_context(nc.sbuf_tensor("s_rden%d" % i, [128, 512], F32)) for i in range(2)]
        Vt = [st.enter_context(nc.sbuf_tensor("s_Vt%d" % i, [128, 256], BF16)) for i in range(8)]
        PT = [st.enter_context(nc.sbuf_tensor("s_PT%d" % i, [128, 2, 256], BF16)) for i in range(6)]
        vn = 0
        ptn = 0
        sn = 0
        on = 0
        rn = 0
        spn = 0
        DSK = 3
        for hp in range(6):
            S.dma(qAB[0:64, 0, PAD:PAD + T], k.qkT[hp * 128:hp * 128 + 64, :], writes=["qAB"])
            S.dma(qAB[64:128, 1, PAD:PAD + T], k.qkT[hp * 128 + 64:hp * 128 + 128, :], writes=["qAB"])
            S.dma(kT[:, PAD:PAD + T], k.qkT[768 + hp * 128:768 + (hp + 1) * 128, :], writes=["kT"])
            for s in range(T // SPAN):
                pendB = []
                S.op("pool", lambda e: e.memset(acc[:], 0.0), writes=["acc"])
                for ri, r in enumerate((1, 4, 16)):
                    Lr = T // r
                    jb = (T // 2) // (r * 128)
                    nqb = SPAN // (128 * r)
                    for rho in range(r):
                        ja, jbnd = s * nqb, s * nqb + nqb
                        pobank = {}
                        for j in range(ja, jbnd + 1):
                            c_lo = 128 if j == ja else 0
                            c_hi = 128 if j == jbnd else 256
                            ncol = c_hi - c_lo
                            vt, vr = Vt[vn % 8], ("Vt", vn % 8)
                            vn += 1
                            m0 = 128 * j - 64
                            lo, hi = 0, 128
                            if m0 < 0:
                                lo = 64
                            if m0 + 128 > Lr:
                                hi = 64
                            if lo > 0 or hi < 128:
                                S.op("pool", lambda e, vt=vt: e.memset(vt[:], 0.0), writes=[vr])
                            tok0 = rho + r * (m0 + lo)
                            src = AP(k.vaug.tensor, k.vaug[tok0:tok0 + 1, hp * 256:hp * 256 + 1].offset, [[1536 * r, hi - lo], [1, 256]])
                            S.dma(vt[lo:hi, :], src, writes=[vr])
                            mq0 = 128 * j - 128 + c_lo
                            qc0 = PAD + rho + r * mq0
                            qsl = slice(qc0, qc0 + (ncol - 1) * r + 1, r)
                            kc0 = PAD + rho + r * m0
                            ksl = slice(kc0, kc0 + 127 * r + 1, r)
                            straddle = (j == jb)
                            pS, psr = ps[sn % 4], ("ps", sn % 4)
                            sn += 1
                            pt, ptr = PT[ptn % 6], ("PT", ptn % 6)
                            ptn += 1
                            pSv = pS[:, :].rearrange("p (h c) -> p h c", h=2)[:, :, 0:ncol]

                            def stageA(pSv=pSv, psr=psr, ksl=ksl, qsl=qsl, ncol=ncol, ri=ri, c_lo=c_lo, c_hi=c_hi,
                                       straddle=straddle, pt=pt, ptr=ptr, hp=hp):
                                S.op("pe", lambda e: e.matmul(pSv, lhsT=kT[:, ksl], rhs=qAB[:, :, qsl], start=True, stop=False),
                                     reads=["kT", "qAB"], writes=[psr])
                                S.op("pe", lambda e: e.matmul(pSv, lhsT=J[:], rhs=Hk[:, ri, 2 * hp:2 * hp + 2, c_lo:c_hi],
                                                              start=False, stop=not straddle), reads=["J", "Hk"], writes=[psr])
                                if straddle:
                                    S.op("pe", lambda e: e.matmul(pSv, lhsT=k.ident[:], rhs=bc(bm[:, c_lo:c_hi].rearrange("p (o c) -> p o c", o=1), [128, 2, ncol]),
                                                                  start=False, stop=True), reads=["ident", "bm"], writes=[psr])
                                S.op("act", lambda e: e.activation(out=pt[:, :, 0:ncol], in_=pSv, func=AF.Exp), reads=[psr], writes=[ptr])

                            pieces = []
                            if c_lo == 0:
                                pieces.append((0, j - 1))
                            if c_hi == 256:
                                pieces.append((1, j))
                            for (half, jq) in pieces:
                                if half == 1:
                                    pobank[jq] = (ps[4 + on % 4], ("ps", 4 + on % 4))
                                    on += 1
                            pbs = {jq: pobank[jq] for (_, jq) in pieces}

                            def stageB(pieces=pieces, pbs=pbs, vt=vt, vr=vr, pt=pt, ptr=ptr, c_lo=c_lo, r=r, rho=rho, s=s):
                                for (half, jq) in pieces:
                                    po, por = pbs[jq]
                                    off = half * 128 - c_lo
                                    for hh in range(2):
                                        S.op("pe", lambda e, po=po, hh=hh, off=off, half=half: e.matmul(
                                            po[:, hh * 128:(hh + 1) * 128], lhsT=vt[:, hh * 128:(hh + 1) * 128], rhs=pt[:, hh, off:off + 128],
                                            start=(half == 1 and hh == 0), stop=(half == 0), skip_group_check=True),
                                            reads=[vr, ptr], writes=[por])
                                    if half == 0:
                                        a0 = rho + r * 128 * jq - s * SPAN
                                        asl = slice(a0, a0 + 127 * r + 1, r)
                                        S.op("dve", lambda e, po=po, asl=asl: e.tensor_tensor(
                                            out=acc[:, :, asl], in0=po[:, 0:256].rearrange("p (h c) -> p h c", h=2), in1=acc[:, :, asl],
                                            op=ALU.add), reads=[por, "acc"], writes=["acc"])

                            stageA()
                            pendB.append(stageB)
                            if len(pendB) > DSK:
                                pendB.pop(0)()
                while pendB:
                    pendB.pop(0)()
                for ow in range(SPAN // OW):
                    ot, otr = oT[spn % 2], ("oT", spn % 2)
                    spn += 1
                    for cc in range(OW // 512):
                        c0 = ow * OW + cc * 512
                        pw, pwr = ps[sn % 4], ("ps", sn % 4)
                        sn += 1
                        S.op("pe", lambda e, pw=pw, c0=c0: e.matmul(pw[:, :], lhsT=selA[:], rhs=acc[:, 0, c0:c0 + 512],
                                                                    start=True, stop=False), reads=["acc", "sel"], writes=[pwr])
                        S.op("pe", lambda e, pw=pw, c0=c0: e.matmul(pw[:, :], lhsT=selB[:], rhs=acc[:, 1, c0:c0 + 512],
                                                                    start=False, stop=True), reads=["acc", "sel"], writes=[pwr])
                        rd, rdr = rden[rn % 2], ("rden", rn % 2)
                        rn += 1
                        S.op("dve", lambda e, rd=rd, pw=pw: e.reciprocal(out=rd[:], in_=pw[:, :]), reads=[pwr], writes=[rdr])
                        for hh in range(2):
                            nlo = 64 * hh
                            S.op("pool", lambda e, rd=rd, ot=ot, hh=hh, cc=cc, c0=c0, nlo=nlo: e.tensor_tensor(
                                out=ot[nlo:nlo + 64, cc * 512:(cc + 1) * 512], in0=acc[nlo:nlo + 64, hh, c0:c0 + 512],
                                in1=rd[nlo:nlo + 64, :], op=ALU.mult), reads=["acc", rdr], writes=[otr])
                    t0_ = s * SPAN + ow * OW
                    S.dma(k.oaT[hp * 128:(hp + 1) * 128, t0_:t0_ + OW], ot[:], reads=[otr])


def phaseF(k):
    nc, S, din, ps = k.nc, k.S, k.din, k.ps
    with ExitStack() as st:
        wpa = st.enter_context(nc.sbuf_tensor("s_wpa", [128, 6, DM], BF16))
        wph = st.enter_context(nc.sbuf_tensor("s_wph", [128, 6, DM], BF16))
        wo = st.enter_context(nc.sbuf_tensor("s_wo", [128, 8, DM], BF16))
        with ExitStack() as s2:
            stg = [s2.enter_context(nc.sbuf_tensor("s_fstg%d" % i, [128, DM], F32)) for i in range(2)]
            n = 0
            for (wt, nm, src, nk) in ((wpa, "wpa", "w_proj_attn", 6), (wph, "wph", "w_proj_hyena", 6), (wo, "wo", "w_out", 8)):
                for kk in range(nk):
                    b = n % 2
                    S.dma(stg[b][:], din[src][kk * 128:(kk + 1) * 128, :], writes=[("fstg", b)])
                    eng = ("dve", "pool")[n % 2]
                    S.op(eng, lambda e, b=b, wt=wt, kk=kk: e.tensor_copy(out=wt[:, kk, :], in_=stg[b][:]), reads=[("fstg", b)], writes=[nm])
                    n += 1
            S.barrier()
        gater = st.enter_context(nc.sbuf_tensor("s_gater", [128, 2, DM], F32))
        for s_ in range(2):
            S.dma(gater[:, s_, :], k.modrep[s_:s_ + 1, 2 * DM:3 * DM].partition_broadcast(128), writes=[("modr", s_)])
        oa = [st.enter_context(nc.sbuf_tensor("s_foa%d" % i, [128, 6, 512], BF16)) for i in range(2)]
        ga = [st.enter_context(nc.sbuf_tensor("s_fga%d" % i, [128, 6, 512], BF16)) for i in range(2)]
        oh_ = [st.enter_context(nc.sbuf_tensor("s_foh%d" % i, [128, 6, 512], BF16)) for i in range(2)]
        gh = [st.enter_context(nc.sbuf_tensor("s_fgh%d" % i, [128, 6, 512], BF16)) for i in range(2)]
        mt = [st.enter_context(nc.sbuf_tensor("s_fmt%d" % i, [128, 16, 512], BF16)) for i in range(2)]
        mix = [st.enter_context(nc.sbuf_tensor("s_fmix%d" % i, [128, 8, 512], BF16)) for i in range(2)]
        ta = [st.enter_context(nc.sbuf_tensor("s_fta%d" % i, [128, 512], F32)) for i in range(2)]
        tb = [st.enter_context(nc.sbuf_tensor("s_ftb%d" % i, [128, 512], F32)) for i in range(2)]
        xt = [st.enter_context(nc.sbuf_tensor("s_fx%d" % i, [128, DM], F32)) for i in range(2)]
        r1 = [st.enter_context(nc.sbuf_tensor("s_fr%d" % i, [128, DM], F32)) for i in range(2)]
        yo = [st.enter_context(nc.sbuf_tensor("s_fy%d" % i, [128, DM], F32)) for i in range(2)]
        sqj = st.enter_context(nc.sbuf_tensor("s_fsq", [128, DM], BF16))
        ss = [st.enter_context(nc.sbuf_tensor("s_fss%d" % i, [128, 2], F32)) for i in range(2)]
        pn = 0
        tn_ = 0
        for ci in range(NCHUNK):
            seg = 0 if ci < NCHUNK // 2 else 1
            b = ci % 2
            cs = slice(ci * 512, (ci + 1) * 512)
            S.dma(oa[b][:], k.oaT[:, cs].rearrange("(a p) t -> p a t", p=128), writes=[("foa", b)])
            S.dma(ga[b][:], k.gaT[:, cs].rearrange("(a p) t -> p a t", p=128), writes=[("fga", b)])
            S.dma(oh_[b][:], k.ohT[:, cs].rearrange("(a p) t -> p a t", p=128), writes=[("foh", b)])
            S.dma(gh[b][:], k.ghT[:, cs].rearrange("(a p) t -> p a t", p=128), writes=[("fgh", b)])
            S.dma(mt[b][:], k.mT[:, cs].rearrange("(a p) t -> p a t", p=128), writes=[("fmt", b)])
            S.op("pool", lambda e, b=b: e.tensor_tensor(out=oa[b][:], in0=oa[b][:], in1=ga[b][:], op=ALU.mult),
                 reads=[("foa", b), ("fga", b)], writes=[("foa", b)])
            S.op("dve", lambda e, b=b: e.tensor_tensor(out=oh_[b][:], in0=oh_[b][:], in1=gh[b][:], op=ALU.mult),
                 reads=[("foh", b), ("fgh", b)], writes=[("foh", b)])
            for fb in range(8):
                pA, pAr = ps[pn % 4], ("ps", pn % 4)
                pn += 1
                pH, pHr = ps[pn % 4], ("ps", pn % 4)
                pn += 1
                for kk in range(6):
                    S.op("pe", lambda e, pA=pA, kk=kk, fb=fb, b=b: e.matmul(pA[:, :], lhsT=wpa[:, kk, fb * 128:(fb + 1) * 128], rhs=oa[b][:, kk, :],
                                                                             start=(kk == 0), stop=(kk == 5)), reads=["wpa", ("foa", b)], writes=[pAr])
                for kk in range(6):
                    S.op("pe", lambda e, pH=pH, kk=kk, fb=fb, b=b: e.matmul(pH[:, :], lhsT=wph[:, kk, fb * 128:(fb + 1) * 128], rhs=oh_[b][:, kk, :],
                                                                             start=(kk == 0), stop=(kk == 5)), reads=["wph", ("foh", b)], writes=[pHr])
                a_, ar_ = ta[tn_ % 2], ("fta", tn_ % 2)
                b_, br_ = tb[tn_ % 2], ("ftb", tn_ % 2)
                tn_ += 1
                S.op("dve", lambda e, a_=a_, pA=pA, fb=fb, b=b: e.tensor_tensor(out=a_[:], in0=pA[:, :], in1=mt[b][:, fb, :], op=ALU.mult),
                     reads=[pAr, ("fmt", b)], writes=[ar_])
                S.op("dve", lambda e, b_=b_, pH=pH, fb=fb, b=b: e.tensor_tensor(out=b_[:], in0=pH[:, :], in1=mt[b][:, 8 + fb, :], op=ALU.mult),
                     reads=[pHr, ("fmt", b)], writes=[br_])
                S.op("pool", lambda e, a_=a_, b_=b_, fb=fb, b=b: e.tensor_tensor(out=mix[b][:, fb, :], in0=a_[:], in1=b_[:], op=ALU.add),
                     reads=[ar_, br_], writes=[("fmix", b)])
            for tt in range(4):
                t = ci * 4 + tt
                tb2 = t % 2
                S.dma(xt[tb2][:], din["x"][t * 128:(t + 1) * 128, :], writes=[("fx", tb2)])
                p0, p1 = ps[4 + 2 * tb2], ps[5 + 2 * tb2]
                p0r, p1r = ("ps", 4 + 2 * tb2), ("ps", 5 + 2 * tb2)
                for half, (pp, ppr) in enumerate(((p0, p0r), (p1, p1r))):
                    for fb in range(8):
                        S.op("pe", lambda e, pp=pp, fb=fb, tt=tt, half=half, b=b: e.matmul(
                            pp[:, :], lhsT=mix[b][:, fb, tt * 128:(tt + 1) * 128], rhs=wo[:, fb, half * 512:(half + 1) * 512],
                            start=(fb == 0), stop=(fb == 7)), reads=[("fmix", b), "wo"], writes=[ppr])
                rr, rrr = r1[tb2], ("fr", tb2)
                for half, (pp, ppr) in enumerate(((p0, p0r), (p1, p1r))):
                    hs = slice(half * 512, (half + 1) * 512)
                    S.op("dve", lambda e, pp=pp, rr=rr, hs=hs, seg=seg: e.tensor_tensor(
                        out=rr[:, hs], in0=pp[:, :], in1=gater[:, seg, hs.start:hs.stop], op=ALU.mult),
                        reads=[ppr, ("modr", seg)], writes=[rrr])
                S.op("pool", lambda e, rr=rr, tb2=tb2: e.tensor_tensor(out=rr[:], in0=rr[:], in1=xt[tb2][:], op=ALU.add),
                     reads=[rrr, ("fx", tb2)], writes=[rrr])
                sb, sr = ss[tb2], ("fss", tb2)
                S.op("act", lambda e, rr=rr, sb=sb: e.activation(out=sqj[:], in_=rr[:], func=AF.Square, scale=1.0 / 32.0, accum_out=sb[:, 0:1]),
                     reads=[rrr], writes=["fsq", sr])
                S.op("act", lambda e, sb=sb: e.activation(out=sb[:, 1:2], in_=sb[:, 0:1], func=AF.Sqrt, bias=k.epsc[:, 0:1]),
                     reads=[sr, "epsc"], writes=[sr])
                S.op("dve", lambda e, sb=sb: e.reciprocal(out=sb[:, 1:2], in_=sb[:, 1:2]), reads=[sr], writes=[sr])
                yb, yr = yo[tb2], ("fy", tb2)
                S.op("dve", lambda e, rr=rr, sb=sb, yb=yb: e.scalar_tensor_tensor(
                    out=yb[:], in0=rr[:], scalar=sb[:, 1:2], in1=k.fg_rep[:], op0=ALU.mult, op1=ALU.mult),
                    reads=[rrr, sr, "fg_rep"], writes=[yr])
                S.dma(k.y[t * 128:(t + 1) * 128, :], yb[:], reads=[yr])
```

```python
import math
import numpy as np
import ml_dtypes
import concourse.bass as bass
import concourse.mybir as mybir
from concourse.ap import AP
from concourse.bass_utils import run_bass_kernel_spmd

F32 = mybir.dt.float32
BF16 = mybir.dt.bfloat16
AF = mybir.ActivationFunctionType
ALU = mybir.AluOpType

T = 16384
DM = 1024
NCHUNK = T // 512
NFFT = 32768
EPS = 1e-6
CB = 64
NCB = 768 // CB
PAD = 1024

OQ, OK_, OV, OGA, OU, OGH, OMA, OMH = 0, 768, 1536, 2304, 3072, 5376, 6144, 7168

DEBUG = {}


class Sched:
    ENG = ("pe", "act", "dve", "pool", "sp")

    def __init__(self, nc, sems, dma_ring):
        self.nc = nc
        self.ops = []
        self.eng = {"pe": nc.tensor, "act": nc.scalar, "dve": nc.vector, "pool": nc.gpsimd, "sp": nc.sync}
        self.sems = sems
        self.ring = dma_ring
        self.cnt = {e: 0 for e in self.ENG}
        self.ndma = 0
        self.last_w = {}
        self.readers = {}
        self.waited = {e: {d: 0 for d in self.ENG} for e in self.ENG}
        self.waited_dma = {e: {} for e in self.ENG}
        self.last_op = {e: None for e in self.ENG}
        self.pending = []
        self.dma_eng = "sp"

    def op(self, eng, fn, reads=(), writes=()):
        self.ops.append(("op", eng, fn, tuple(reads), tuple(writes)))

    def dma(self, out, in_, reads=(), writes=(), eng="sp"):
        self.ops.append(("dma", eng, (out, in_), tuple(reads), tuple(writes)))

    def barrier(self):
        self.ops.append(("bar",))

    def flush(self):
        ops = self.ops
        n = len(ops)
        last_w, readers = {}, {}
        last_on = {e: -1 for e in self.ENG}
        deps = [None] * n
        marked = [False] * n
        for i, o in enumerate(ops):
            if o[0] == "bar":
                deps[i] = dict(last_on)
                for e, j in last_on.items():
                    if j >= 0:
                        marked[j] = True
                last_w, readers = {}, {}
                continue
            _, e, _, rd, wr = o
            d = set()
            for r in rd:
                if r in last_w:
                    d.add(last_w[r])
            for w in wr:
                if w in last_w:
                    d.add(last_w[w])
                for j in readers.get(w, ()):
                    d.add(j)
            d.discard(i)
            deps[i] = d
            for j in d:
                marked[j] = True
            for w in wr:
                last_w[w] = i
                readers[w] = []
            for r in rd:
                if r not in wr:
                    readers.setdefault(r, []).append(i)
            last_on[e] = i
        ordinal = [0] * n
        cnt = {e: 0 for e in self.ENG}
        dslot = [None] * n
        nd = 0
        P = len(self.ring)
        for i, o in enumerate(ops):
            if o[0] == "dma":
                dslot[i] = (nd % P, 16 * (nd // P + 1))
                nd += 1
            elif o[0] == "op" and marked[i]:
                cnt[o[1]] += 1
                ordinal[i] = cnt[o[1]]
        waited = {e: {d: 0 for d in self.ENG} for e in self.ENG}
        wdma = {e: {} for e in self.ENG}

        def wait_for(e, j):
            oj = ops[j]
            if oj[0] == "dma":
                slot, val = dslot[j]
                if wdma[e].get(slot, 0) >= val:
                    return
                wdma[e][slot] = val
                self.eng[e].wait_ge(self.ring[slot], val)
            else:
                dsrc = oj[1]
                if dsrc == e and e == "pe":
                    return
                if waited[e][dsrc] >= ordinal[j]:
                    return
                waited[e][dsrc] = ordinal[j]
                self.eng[e].wait_ge(self.sems[dsrc], ordinal[j])

        nd = 0
        dma_hist = []
        for i, o in enumerate(ops):
            if o[0] == "bar":
                for e in self.ENG:
                    for dsrc, j in deps[i].items():
                        if j >= 0 and not (dsrc == e and ops[j][0] == "op"):
                            wait_for(e, j)
                    for j in dma_hist[-P:]:
                        wait_for(e, j)
                continue
            kind, e, payload, rd, wr = o
            for j in sorted(deps[i]):
                wait_for(e, j)
            if kind == "dma":
                slot, val = dslot[i]
                if val > 16:
                    if wdma[e].get(slot, 0) < val - 16:
                        wdma[e][slot] = val - 16
                        self.eng[e].wait_ge(self.ring[slot], val - 16)
                out, in_ = payload
                self.eng[e].dma_start(out=out, in_=in_).then_inc(self.ring[slot], 16)
                dma_hist.append(i)
                nd += 1
            else:
                ins = payload(self.eng[e])
                if marked[i]:
                    ins.then_inc(self.sems[e], 1)
        for j in dma_hist[-P:]:
            wait_for("sp", j)
        self.ops = []
        return n


def bc(ap, shape):
    return ap.to_broadcast(list(shape))


def _t5_bucket(rel):
    nb = 16
    max_exact = 8
    n = np.abs(rel)
    large = max_exact + (np.log(np.maximum(n, 1) / max_exact) / math.log(1024 / max_exact) * (nb - max_exact)).astype(np.int32)
    large = np.minimum(large, nb - 1)
    return ((rel > 0).astype(np.int32) * nb + np.where(n < max_exact, n, large)).astype(np.int32)


def bf(a):
    return np.ascontiguousarray(a.astype(np.float32)).astype(ml_dtypes.bfloat16)


_CONST_CACHE = {}


def build_consts(is_prompt):
    key = bool(is_prompt)
    if key in _CONST_CACHE:
        return _CONST_CACHE[key]
    c = {}
    L = 16384 if is_prompt else 8192
    tt = np.linspace(0.0, 1.0, L, dtype=np.float32)[:, None]
    w = (np.float32(2.0 * math.pi / L) * np.arange(L, dtype=np.float32))[:, None]
    f = np.linspace(1e-4, 15, 16, dtype=np.float32)[None, :]
    z = np.concatenate([tt, np.cos(f * w), -np.sin(f * w)], axis=-1).astype(np.float32)
    zf = np.zeros((T, 33), np.float32)
    zf[:L] = z
    c["zfT"] = np.ascontiguousarray(zf.T)
    tn = np.zeros(T, np.float32)
    tn[:L] = tt[:, 0]
    c["negtn"] = np.ascontiguousarray(-tn.reshape(128, 128))
    deltas = np.abs(np.linspace(math.log(0.01) / 1.5, math.log(0.01) / 0.3, 768, dtype=np.float32))
    c["deltas_rep"] = np.ascontiguousarray(np.broadcast_to(deltas[None, :], (128, 768))).astype(np.float32)
    slot = np.arange(128) if is_prompt else np.concatenate([np.arange(64), np.arange(64) + 128])
    k2 = np.arange(128)
    n1 = np.arange(128)
    k1 = np.arange(128)
    ang = -2 * np.pi * np.outer(slot, k2 + 0.5) / 256.0
    Er, Ei = np.cos(ang), np.sin(ang)
    c["E_d"] = bf(np.concatenate([-Ei, Er, Ei], axis=1))
    angk = -2 * np.pi * np.outer(np.arange(128), k2 + 0.5) / 256.0
    Ekr, Eki = np.cos(angk), np.sin(angk)
    if not is_prompt:
        Ekr[64:] = 0
        Eki[64:] = 0
    c["E_kS"] = bf(np.concatenate([Ekr, -Eki], axis=1))
    c["E_kD"] = bf(np.concatenate([Eki, Ekr], axis=1))
    angM = -2 * np.pi * (n1[None, :, None] * (k2[:, None, None] + 0.5) / NFFT + n1[None, :, None] * k1[None, None, :] / 128.0)
    c["Mf"] = bf(np.concatenate([np.cos(angM), np.sin(angM)], axis=2).transpose(1, 0, 2))
    angG = 2 * np.pi * np.outer(k1, n1) / 128.0
    c["G1"] = bf(np.concatenate([np.cos(angG), np.sin(angG)], axis=1))
    c["G2"] = bf(np.concatenate([-np.sin(angG), np.cos(angG)], axis=1))
    angI = 2 * np.pi * (n1[:, None, None] + 128 * slot[None, None, :]) * (k2[None, :, None] + 0.5) / NFFT
    sc = 2.0 / NFFT
    c["Minv"] = bf(np.concatenate([sc * np.cos(angI), -sc * np.sin(angI)], axis=2).transpose(1, 0, 2))
    OH = np.zeros((32, 3 * 384), np.float32)
    mrow = np.zeros((12, 3 * 384), np.float32)
    for ri, r in enumerate((1, 4, 16)):
        for j in range(384):
            d = j - 127
            if 0 <= d <= 128:
                rel = (64 - d) * r
                OH[_t5_bucket(np.array(rel)), ri * 384 + j] = 1.0
            else:
                mrow[:, ri * 384 + j] = -30000.0
    c["OH"] = OH
    c["mrow"] = mrow
    bm = np.zeros((128, 256), np.float32)
    if not is_prompt:
        bm[:64, 128:] = -30000.0
        bm[64:, :128] = -30000.0
    c["bmask"] = bf(bm)
    c["bflag"] = np.full((128, 1), 1.0 if is_prompt else 0.0, np.float32)
    ident = np.eye(128, dtype=np.float32)
    c["ident"] = bf(ident)
    c["antiid"] = bf(ident[::-1])
    c["swap"] = bf(np.roll(ident, 64, axis=1))
    _CONST_CACHE[key] = c
    return c


CONST_SHAPES = {
    "zfT": ([33, T], F32), "negtn": ([128, 128], F32), "deltas_rep": ([128, 768], F32),
    "E_d": ([128, 384], BF16), "E_kS": ([128, 256], BF16), "E_kD": ([128, 256], BF16),
    "Mf": ([128, 128, 256], BF16), "G1": ([128, 256], BF16), "G2": ([128, 256], BF16),
    "Minv": ([128, 128, 256], BF16), "OH": ([32, 1152], F32), "mrow": ([12, 1152], F32),
    "bmask": ([128, 256], BF16), "bflag": ([128, 1], F32), "ident": ([128, 128], BF16),
    "antiid": ([128, 128], BF16), "swap": ([128, 128], BF16),
}

INPUT_SHAPES = {
    "x": [T, DM], "cT": [128, 8, 2], "w_ada": [DM, 3 * DM], "b_ada": [1, 3 * DM], "norm_g": [1, DM],
    "w_in": [DM, 8192], "short_wT": [128, 18, 3], "short_bT": [128, 18],
    "filt_w1": [33, 64], "filt_b1": [64, 1], "filt_fr1": [64, 1], "filt_w2": [64, 64], "filt_b2": [64, 1],
    "filt_fr2": [64, 1], "filt_w3": [64, 3072], "hyena_d": [2, 768],
    "w_proj_attn": [768, DM], "w_proj_hyena": [768, DM], "w_out": [DM, DM], "rel_bias": [32, 12],
    "final_g": [1, DM],
}


from contextlib import ExitStack


class K:
    pass


def build_program(phases=("p0", "pk", "p1", "p1b", "pa", "ph", "pf"), debug_outs=()):
    nc = bass.Bass("TRN2", target_bir_lowering=False)
    k = K()
    k.nc = nc
    din = {}
    for name, shp in INPUT_SHAPES.items():
        din[name] = nc.dram_tensor(name, shp, F32, kind="ExternalInput").ap()
    for name, (shp, dt_) in CONST_SHAPES.items():
        din[name] = nc.dram_tensor(name, shp, dt_, kind="ExternalInput").ap()
    k.din = din
    y = nc.dram_tensor("y", [T, DM], F32, kind="ExternalOutput").ap()
    k.y = y

    def scratch(name, shp, dt_):
        kind = "ExternalOutput" if name in debug_outs else "Internal"
        return nc.dram_tensor(name, shp, dt_, kind=kind).ap()

    k.qkT = scratch("qkT", [1536, T], BF16)
    k.gaT = scratch("gaT", [768, T], BF16)
    k.ghT = scratch("ghT", [768, T], BF16)
    k.mT = scratch("mT", [2048, T], BF16)
    k.uraw = scratch("uraw", [2304, T], BF16)
    k.uT = scratch("uT", [2304, T], BF16)
    k.vaug = scratch("vaug", [T, 1536], BF16)
    k.oaT = scratch("oaT", [768, T], BF16)
    k.ohT = scratch("ohT", [768, T], BF16)
    k.KFd = scratch("KFd", [2, NCB, 128, 128 * 3 * CB], BF16)
    k.Avec = scratch("Avec", [12, 1152], BF16)
    k.modrep = scratch("modrep", [2, 3 * DM], F32)

    with ExitStack() as top:
        sems = {e: top.enter_context(nc.semaphore("sem_" + e)) for e in ("pe", "act", "dve", "pool")}
        sems["sp"] = None
        ring = [top.enter_context(nc.semaphore("dr%d" % i)) for i in range(24)]
        S = Sched(nc, sems, ring)
        k.S = S
        ps = [top.enter_context(nc.psum_tensor("psb%d" % i, [128, 512], F32)) for i in range(8)]
        k.ps = ps
        k.ident = top.enter_context(nc.sbuf_tensor("s_ident", [128, 128], BF16))
        k.fg_rep = top.enter_context(nc.sbuf_tensor("s_fg_rep", [128, DM], F32))
        S.dma(k.ident[:], din["ident"][:, :], writes=["ident"])
        k.epsc = top.enter_context(nc.sbuf_tensor("s_epsc", [128, 2], F32))
        S.op("pool", lambda e: e.memset(k.epsc[:], EPS), writes=["epsc"])
        S.dma(k.fg_rep[:], din["final_g"][0:1, :].partition_broadcast(128), writes=["fg_rep"])

        if "p0" in phases:
            phase0(k)
            S.barrier()
        if "pk" in phases:
            phaseK(k)
            S.barrier()
        if "p1" in phases:
            phase1(k)
            S.barrier()
        if "p1b" in phases:
            phase1b(k)
            S.barrier()
        if "ph" in phases:
            phaseH(k)
            S.barrier()
        if "pa" in phases:
            phaseA(k)
            S.barrier()
        if "pf" in phases:
            phaseF(k)
            S.barrier()
        S.flush()
    return nc


def phase0(k):
    nc, S, din, ps = k.nc, k.S, k.din, k.ps
    with ExitStack() as st:
        wada = st.enter_context(nc.sbuf_tensor("s_wada", [128, 8, 3 * DM], F32))
        cT = st.enter_context(nc.sbuf_tensor("s_cT", [128, 8, 2], F32))
        scT = st.enter_context(nc.sbuf_tensor("s_scT", [128, 8, 2], F32))
        screp = st.enter_context(nc.sbuf_tensor("s_screp", [128, 2, 8, 128], F32))
        brep = st.enter_context(nc.sbuf_tensor("s_brep", [128, 3 * DM], F32))
        ngrep = st.enter_context(nc.sbuf_tensor("s_ngrep", [128, DM], F32))
        k.modr = st.enter_context(nc.sbuf_tensor("s_modr", [128, 2, 3 * DM], F32))
        S.dma(cT[:], din["cT"][:, :, :], writes=["cT"])
        for kk in range(8):
            S.dma(wada[:, kk, :], din["w_ada"][kk * 128:(kk + 1) * 128, :], writes=[("wada", kk)])
        S.dma(brep[:], din["b_ada"][0:1, :].partition_broadcast(128), writes=["brep"])
        S.dma(ngrep[:], din["norm_g"][0:1, :].partition_broadcast(128), writes=["ngrep"])
        S.op("act", lambda e: e.activation(out=scT[:], in_=cT[:], func=AF.Silu), reads=["cT"], writes=["scT"])
        for s in range(2):
            S.op("dve", lambda e, s=s: e.tensor_copy(out=screp[:, s, :, :], in_=bc(scT[:, :, s:s + 1], [128, 8, 128])),
                 reads=["scT"], writes=[("screp", s)])
        for s in range(2):
            for cc in range(6):
                pb = ps[(s * 6 + cc) % 4]
                pr = ("ps", (s * 6 + cc) % 4)
                for kk in range(8):
                    S.op("pe", lambda e, s=s, cc=cc, kk=kk, pb=pb: e.matmul(
                        pb[:, :], lhsT=screp[:, s, kk, :], rhs=wada[:, kk, cc * 512:(cc + 1) * 512],
                        start=(kk == 0), stop=(kk == 7)),
                        reads=[("screp", s), ("wada", kk)], writes=[pr])
                S.op("dve", lambda e, s=s, cc=cc, pb=pb: e.tensor_tensor(
                    out=k.modr[:, s, cc * 512:(cc + 1) * 512], in0=pb[:, :], in1=brep[:, cc * 512:(cc + 1) * 512], op=ALU.add),
                    reads=[pr, "brep"], writes=[("modr", s)])
            S.op("dve", lambda e, s=s: e.scalar_tensor_tensor(
                out=k.modr[:, s, DM:2 * DM], in0=k.modr[:, s, DM:2 * DM], scalar=1.0, in1=ngrep[:], op0=ALU.add, op1=ALU.mult),
                reads=[("modr", s), "ngrep"], writes=[("modr", s)])
            S.dma(k.modrep[s:s + 1, :], k.modr[0:1, s, :], reads=[("modr", s)])
        S.barrier()


def phase1(k):
    nc, S, din, ps = k.nc, k.S, k.din, k.ps
    with ExitStack() as st:
        winb = st.enter_context(nc.sbuf_tensor("s_winb", [128, 8, 8192], BF16))
        stg_ctx = ExitStack()
        stg = [stg_ctx.enter_context(nc.sbuf_tensor("s_wstg%d" % i, [128, 2048], F32)) for i in range(2)]
        n = 0
        for kk in range(8):
            for cc in range(4):
                b = n % 2
                S.dma(stg[b][:], din["w_in"][kk * 128:(kk + 1) * 128, cc * 2048:(cc + 1) * 2048], writes=[("wstg", b)])
                eng = ("act", "dve", "pool")[n % 3]
                if eng == "act":
                    S.op("act", lambda e, b=b, kk=kk, cc=cc: e.copy(out=winb[:, kk, cc * 2048:(cc + 1) * 2048], in_=stg[b][:]),
                         reads=[("wstg", b)], writes=[("winb", kk)])
                else:
                    S.op(eng, lambda e, b=b, kk=kk, cc=cc: e.tensor_copy(out=winb[:, kk, cc * 2048:(cc + 1) * 2048], in_=stg[b][:]),
                         reads=[("wstg", b)], writes=[("winb", kk)])
                n += 1
        S.barrier()
        stg_ctx.close()
        modr1 = st.enter_context(nc.sbuf_tensor("s_modr1", [128, 2, 2 * DM], F32))
        for s_ in range(2):
            S.dma(modr1[:, s_, :], k.modrep[s_:s_ + 1, 0:2 * DM].partition_broadcast(128), writes=[("modr", s_)])
        xt = [st.enter_context(nc.sbuf_tensor("s_xt%d" % i, [128, DM], F32)) for i in range(2)]
        xm = [st.enter_context(nc.sbuf_tensor("s_xm%d" % i, [128, DM], F32)) for i in range(1)]
        hb = [st.enter_context(nc.sbuf_tensor("s_hb%d" % i, [128, DM], BF16)) for i in range(2)]
        sq = st.enter_context(nc.sbuf_tensor("s_sqj", [128, DM], BF16))
        ss = [st.enter_context(nc.sbuf_tensor("s_ss%d" % i, [128, 2], F32)) for i in range(3)]
        hT = [st.enter_context(nc.sbuf_tensor("s_hT%d" % i, [128, 8, 512], BF16)) for i in range(2)]
        ev = [st.enter_context(nc.sbuf_tensor("s_ev%d" % i, [128, 512], BF16)) for i in range(6)]
        vst = [st.enter_context(nc.sbuf_tensor("s_vst%d" % i, [128, 12, 128], BF16)) for i in range(2)]
        for i in range(2):
            S.op("pool", lambda e, i=i: e.memset(vst[i][:], 1.0), writes=[("vst", i)])

        blocks = []
        for j in range(6):
            blocks.append((OQ + j * 128, k.qkT, j * 128, "q"))
        for j in range(6):
            blocks.append((OK_ + j * 128, k.qkT, 768 + j * 128, "copy"))
        for j in range(18):
            blocks.append((OU + j * 128, k.uraw, j * 128, "copy"))
        for j in range(6):
            blocks.append((OGA + j * 128, k.gaT, j * 128, "silu"))
        for j in range(6):
            blocks.append((OGH + j * 128, k.ghT, j * 128, "silu"))
        for j in range(8):
            blocks.append((OMA + j * 128, k.mT, j * 128, "sig"))
        for j in range(8):
            blocks.append((OMH + j * 128, k.mT, 1024 + j * 128, "sig"))

        tcount = 0
        evn = 0
        for ci in range(NCHUNK):
            seg = 0 if ci < NCHUNK // 2 else 1
            hTc = hT[ci % 2]
            hres = ("hT", ci % 2)
            for tt in range(4):
                t = ci * 4 + tt
                xb, xr = xt[t % 2], ("xt", t % 2)
                sb, sr = ss[t % 3], ("ss", t % 3)
                mb, mr = xm[0], ("xm", 0)
                hbb, hbr = hb[t % 2], ("hb", t % 2)
                S.dma(xb[:], din["x"][t * 128:(t + 1) * 128, :], writes=[xr])
                S.op("act", lambda e, xb=xb, sb=sb: e.activation(out=sq[:], in_=xb[:], func=AF.Square, scale=1.0 / 32.0,
                                                                 accum_out=sb[:, 0:1]),
                     reads=[xr], writes=["sqj", sr])
                S.op("act", lambda e, sb=sb: e.activation(out=sb[:, 1:2], in_=sb[:, 0:1], func=AF.Sqrt, bias=k.epsc[:, 0:1]),
                     reads=[sr], writes=[sr])
                S.op("dve", lambda e, sb=sb: e.reciprocal(out=sb[:, 1:2], in_=sb[:, 1:2]), reads=[sr], writes=[sr])
                S.op("dve", lambda e, xb=xb, sb=sb, mb=mb, seg=seg: e.scalar_tensor_tensor(
                    out=mb[:], in0=xb[:], scalar=sb[:, 1:2], in1=modr1[:, seg, DM:2 * DM], op0=ALU.mult, op1=ALU.mult),
                    reads=[xr, sr, ("modr", seg)], writes=[mr])
                S.op("pool", lambda e, mb=mb, hbb=hbb, seg=seg: e.tensor_tensor(
                    out=hbb[:], in0=mb[:], in1=modr1[:, seg, 0:DM], op=ALU.add),
                    reads=[mr, ("modr", seg)], writes=[hbr])
                pbank = ps[6 + (t % 2)]
                pres = ("ps", 6 + (t % 2))
                pT = pbank[:].bitcast(BF16)
                for kk in range(8):
                    S.op("pe", lambda e, kk=kk, pT=pT, hbb=hbb: e.transpose(
                        out=pT[:, kk * 128:(kk + 1) * 128], in_=hbb[:, kk * 128:(kk + 1) * 128], identity=k.ident[:]),
                        reads=[hbr, "ident"], writes=[pres])
                S.op("act", lambda e, pT=pT, hTc=hTc, tt=tt: e.copy(
                    out=hTc[:, :, tt * 128:(tt + 1) * 128], in_=pT.rearrange("p (k t) -> p k t", k=8)),
                    reads=[pres], writes=[hres])
            for bi, (wc, dst, drow, kind) in enumerate(blocks):
                pb = ps[bi % 4]
                pr = ("ps", bi % 4)
                for kk in range(8):
                    S.op("pe", lambda e, kk=kk, pb=pb, wc=wc, hTc=hTc: e.matmul(
                        pb[:, :], lhsT=winb[:, kk, wc:wc + 128], rhs=hTc[:, kk, :], start=(kk == 0), stop=(kk == 7)),
                        reads=[hres, ("winb", kk)], writes=[pr])
                eb, er = ev[evn % 6], ("ev", evn % 6)
                evn += 1
                if kind == "q":
                    S.op("act", lambda e, pb=pb, eb=eb: e.activation(out=eb[:], in_=pb[:, :], func=AF.Copy, scale=0.125),
                         reads=[pr], writes=[er])
                elif kind == "copy":
                    S.op("dve", lambda e, pb=pb, eb=eb: e.tensor_copy(out=eb[:], in_=pb[:, :]), reads=[pr], writes=[er])
                elif kind == "silu":
                    S.op("act", lambda e, pb=pb, eb=eb: e.activation(out=eb[:], in_=pb[:, :], func=AF.Silu), reads=[pr], writes=[er])
                else:
                    S.op("act", lambda e, pb=pb, eb=eb: e.activation(out=eb[:], in_=pb[:, :], func=AF.Sigmoid), reads=[pr], writes=[er])
                S.dma(dst[drow:drow + 128, ci * 512:(ci + 1) * 512], eb[:], reads=[er], eng="pool")
            for tt in range(4):
                t = ci * 4 + tt
                pa, pb2 = ps[4], ps[5]
                for kk in range(8):
                    S.op("pe", lambda e, kk=kk, tt=tt, hTc=hTc: e.matmul(
                        ps[4][:, :], lhsT=hTc[:, kk, tt * 128:(tt + 1) * 128], rhs=winb[:, kk, OV:OV + 512],
                        start=(kk == 0), stop=(kk == 7)), reads=[hres, ("winb", kk)], writes=[("ps", 4)])
                for kk in range(8):
                    S.op("pe", lambda e, kk=kk, tt=tt, hTc=hTc: e.matmul(
                        ps[5][:, 0:256], lhsT=hTc[:, kk, tt * 128:(tt + 1) * 128], rhs=winb[:, kk, OV + 512:OV + 768],
                        start=(kk == 0), stop=(kk == 7)), reads=[hres, ("winb", kk)], writes=[("ps", 5)])
                vb, vr = vst[t % 2], ("vst", t % 2)
                def vdst(vb, p0, npair):
                    base = vb[:, 2 * p0:2 * p0 + 1, 0:1]
                    return AP(base.tensor, base.offset, [list(base.ap[0]), [256, npair], [192, 2], [1, 64]])
                S.op("dve", lambda e, vb=vb, vdst=vdst: e.tensor_copy(
                    out=vdst(vb, 0, 4), in_=ps[4][:, :].rearrange("p (a b c) -> p a b c", a=4, b=2)),
                    reads=[("ps", 4)], writes=[vr])
                S.op("dve", lambda e, vb=vb, vdst=vdst: e.tensor_copy(
                    out=vdst(vb, 4, 2), in_=ps[5][:, 0:256].rearrange("p (a b c) -> p a b c", a=2, b=2)),
                    reads=[("ps", 5)], writes=[vr])
                S.dma(k.vaug[t * 128:(t + 1) * 128, :], vb[:].rearrange("p a b -> p (a b)"), reads=[vr], eng="pool")


def core_assignment():
    return [("p", 0), ("p", 1), ("s", 0, 1), ("s", 2, 3), ("s", 4, 5), ("s", 6, 7), ("s", 6, 7), ("s", 6, 7)]


def prep_core_inputs(inp, role):
    f32 = lambda a: np.ascontiguousarray(np.asarray(a, dtype=np.float32))
    m = {}
    if role[0] == "p":
        b = role[1]
        m["x"] = f32(inp["x_prompt"][b])
        c2 = np.stack([inp["c_prompt"][b], inp["c_prompt"][b]], 0)
    else:
        m["x"] = f32(np.concatenate([inp["x_sample"][role[1]], inp["x_sample"][role[2]]], 0))
        c2 = np.stack([inp["c_sample"][role[1]], inp["c_sample"][role[2]]], 0)
    c2 = np.asarray(c2, np.float32)
    m["cT"] = f32(c2.reshape(2, 8, 128).transpose(2, 1, 0))
    m["w_ada"] = f32(inp["w_ada"][0])
    m["b_ada"] = f32(inp["b_ada"][0][None])
    m["norm_g"] = f32(inp["norm_g"][0][None])
    m["w_in"] = f32(inp["w_in"][0])
    m["short_wT"] = f32(np.asarray(inp["short_w"][0]).reshape(3, 18, 128).transpose(2, 1, 0))
    m["short_bT"] = f32(np.asarray(inp["short_b"][0]).reshape(18, 128).T)
    m["filt_w1"] = f32(inp["filt_w1"][0])
    m["filt_b1"] = f32(np.asarray(inp["filt_b1"][0])[:, None])
    m["filt_fr1"] = f32(np.asarray(inp["filt_freq1"][0])[:, None])
    m["filt_w2"] = f32(inp["filt_w2"][0])
    m["filt_b2"] = f32(np.asarray(inp["filt_b2"][0])[:, None])
    m["filt_fr2"] = f32(np.asarray(inp["filt_freq2"][0])[:, None])
    m["filt_w3"] = f32(inp["filt_w3"][0])
    m["hyena_d"] = f32(inp["hyena_d"][0])
    m["w_proj_attn"] = f32(inp["w_proj_attn"][0])
    m["w_proj_hyena"] = f32(inp["w_proj_hyena"][0])
    m["w_out"] = f32(inp["w_out"][0])
    m["rel_bias"] = f32(inp["rel_bias"])
    m["final_g"] = f32(np.asarray(inp["final_g"])[None])
    m.update(build_consts(role[0] == "p"))
    return m


_NC_CACHE = {}


def kernel(**inputs):
    inp = {k_: np.asarray(v) for k_, v in inputs.items()}
    if "full" not in _NC_CACHE:
        _NC_CACHE["full"] = build_program()
    nc = _NC_CACHE["full"]
    roles = core_assignment()
    in_maps = [prep_core_inputs(inp, r) for r in roles]
    res = run_bass_kernel_spmd(nc, in_maps, core_ids=list(range(8)))
    outs = [np.asarray(r["y"], dtype=np.float32) for r in res.results]
    y_prompt = np.stack([outs[0], outs[1]], 0)
    ys = []
    for c in range(2, 6):
        ys.append(outs[c][:8192])
        ys.append(outs[c][8192:])
    y_sample = np.stack(ys, 0)
    return (y_prompt, y_sample)


def phase1b(k):
    nc, S, din = k.nc, k.S, k.din
    W = 2048
    with ExitStack() as st:
        swT = st.enter_context(nc.sbuf_tensor("s_swT", [128, 18, 3], F32))
        sbT = st.enter_context(nc.sbuf_tensor("s_sbT", [128, 18], F32))
        bfl = st.enter_context(nc.sbuf_tensor("s_bfl", [128, 1], F32))
        S.dma(swT[:], din["short_wT"][:, :, :], writes=["swT"])
        S.dma(sbT[:], din["short_bT"][:, :], writes=["sbT"])
        S.dma(bfl[:], din["bflag"][:, :], writes=["bfl"])
        ib = [st.enter_context(nc.sbuf_tensor("s_cin%d" % i, [128, W + 2], BF16)) for i in range(3)]
        t1 = [st.enter_context(nc.sbuf_tensor("s_ct%d" % i, [128, W], F32)) for i in range(2)]
        ob = [st.enter_context(nc.sbuf_tensor("s_cout%d" % i, [128, W], BF16)) for i in range(3)]
        n = 0
        for ub in range(18):
            for tc in range(T // W):
                a, ar = ib[n % 3], ("cin", n % 3)
                tb, tr = t1[n % 2], ("ct", n % 2)
                o, orr = ob[n % 3], ("cout", n % 3)
                eng = "pool"
                lo = tc * W - 1
                hi = tc * W + W + 1
                c0 = 0
                if lo < 0:
                    S.op("pool", lambda e, a=a: e.memset(a[:, 0:1], 0.0), writes=[ar])
                    lo, c0 = 0, 1
                c1 = W + 2
                if hi > T:
                    S.op("pool", lambda e, a=a: e.memset(a[:, W + 1:W + 2], 0.0), writes=[ar])
                    hi, c1 = T, W + 1
                S.dma(a[:, c0:c1], k.uraw[ub * 128:(ub + 1) * 128, lo:hi], writes=[ar])
                if tc * W == T // 2:
                    S.op(eng, lambda e, a=a: e.tensor_scalar(out=a[:, 0:1], in0=a[:, 0:1], scalar1=bfl[:, 0:1], scalar2=None,
                                                             op0=ALU.mult), reads=[ar, "bfl"], writes=[ar])
                if tc * W + W == T // 2:
                    S.op(eng, lambda e, a=a: e.tensor_scalar(out=a[:, W + 1:W + 2], in0=a[:, W + 1:W + 2], scalar1=bfl[:, 0:1],
                                                             scalar2=None, op0=ALU.mult), reads=[ar, "bfl"], writes=[ar])
                S.op("act", lambda e, a=a, tb=tb, ub=ub: e.activation(
                    out=tb[:], in_=a[:, 1:W + 1], func=AF.Identity, scale=swT[:, ub, 1:2], bias=sbT[:, ub:ub + 1]),
                    reads=[ar, "swT", "sbT"], writes=[tr])
                S.op("dve", lambda e, a=a, tb=tb, ub=ub: e.scalar_tensor_tensor(
                    out=tb[:], in0=a[:, 0:W], scalar=swT[:, ub, 0:1], in1=tb[:], op0=ALU.mult, op1=ALU.add),
                    reads=[ar, tr, "swT"], writes=[tr])
                S.op("dve", lambda e, a=a, tb=tb, o=o, ub=ub: e.scalar_tensor_tensor(
                    out=o[:], in0=a[:, 2:W + 2], scalar=swT[:, ub, 2:3], in1=tb[:], op0=ALU.mult, op1=ALU.add),
                    reads=[ar, tr, "swT"], writes=[orr])
                S.dma(k.uT[ub * 128:(ub + 1) * 128, tc * W:(tc + 1) * W], o[:], reads=[orr], eng="act")
                n += 1


def sin_wrapped(S, src_ps, pres, dst, dres, scale_ap, bias_ap, tmp, tres, tmp2, t2res, nparts, ncols):
    PI = math.pi
    S.op("dve", lambda e: e.tensor_scalar(out=tmp[0:nparts, 0:ncols], in0=src_ps, scalar1=scale_ap, scalar2=bias_ap,
                                          op0=ALU.mult, op1=ALU.add), reads=[pres], writes=[tres])
    S.op("dve", lambda e: e.tensor_scalar(out=tmp2[0:nparts, 0:ncols], in0=tmp[0:nparts, 0:ncols], scalar1=PI, scalar2=-2 * PI,
                                          op0=ALU.is_gt, op1=ALU.mult), reads=[tres], writes=[t2res])
    S.op("dve", lambda e: e.tensor_tensor(out=tmp2[0:nparts, 0:ncols], in0=tmp2[0:nparts, 0:ncols], in1=tmp[0:nparts, 0:ncols],
                                          op=ALU.add), reads=[tres, t2res], writes=[t2res])
    S.op("dve", lambda e: e.tensor_scalar(out=tmp[0:nparts, 0:ncols], in0=tmp[0:nparts, 0:ncols], scalar1=-PI, scalar2=2 * PI,
                                          op0=ALU.is_lt, op1=ALU.mult), reads=[tres], writes=[tres])
    S.op("dve", lambda e: e.tensor_tensor(out=tmp[0:nparts, 0:ncols], in0=tmp2[0:nparts, 0:ncols], in1=tmp[0:nparts, 0:ncols],
                                          op=ALU.add), reads=[tres, t2res], writes=[tres])
    S.op("act", lambda e: e.activation(out=dst, in_=tmp[0:nparts, 0:ncols], func=AF.Sin), reads=[tres], writes=[dres])


def phaseK(k):
    nc, S, din, ps = k.nc, k.S, k.din, k.ps
    with ExitStack() as st:
        hdn = st.enter_context(nc.sbuf_tensor("s_hdn2T", [128, T], BF16))
        w3sd = st.enter_context(nc.sbuf_tensor("s_w3sd", [128, 2, NCB, 2, CB], BF16))
        S.op("pool", lambda e: e.memset(hdn[64:128, :], 0.0), writes=["hdn"])
        S.op("pool", lambda e: e.memset(w3sd[64:128], 0.0), writes=["w3sd"])
        with ExitStack() as s2:
            w1 = s2.enter_context(nc.sbuf_tensor("s_fw1", [33, 64], F32))
            w2 = s2.enter_context(nc.sbuf_tensor("s_fw2", [64, 64], F32))
            w3 = s2.enter_context(nc.sbuf_tensor("s_fw3", [64, 3072], F32))
            fv = s2.enter_context(nc.sbuf_tensor("s_fv", [64, 6], F32))
            zc = [s2.enter_context(nc.sbuf_tensor("s_zc%d" % i, [33, 512], F32)) for i in range(2)]
            ta = s2.enter_context(nc.sbuf_tensor("s_fta", [64, 512], F32))
            tb = s2.enter_context(nc.sbuf_tensor("s_ftb", [64, 512], F32))
            h1 = s2.enter_context(nc.sbuf_tensor("s_fh1", [64, 512], F32))
            S.dma(w1[:], din["filt_w1"][:, :], writes=["fw1"])
            S.dma(w2[:], din["filt_w2"][:, :], writes=["fw2"])
            S.dma(w3[:], din["filt_w3"][:, :], writes=["fw3"])
            for i, nm in enumerate(("filt_b1", "filt_fr1", "filt_b2", "filt_fr2")):
                S.dma(fv[:, i:i + 1], din[nm][:, :], writes=["fv"])
            S.op("dve", lambda e: e.tensor_tensor(out=fv[:, 4:5], in0=fv[:, 0:1], in1=fv[:, 1:2], op=ALU.mult), reads=["fv"], writes=["fv"])
            S.op("dve", lambda e: e.tensor_tensor(out=fv[:, 5:6], in0=fv[:, 2:3], in1=fv[:, 3:4], op=ALU.mult), reads=["fv"], writes=["fv"])
            w3v = w3[:].rearrange("p (o d b c) -> p o d b c", o=2, d=2, b=NCB)
            for o in range(2):
                S.op("dve", lambda e, o=o: e.tensor_tensor(out=w3sd[0:64, o, :, 0, :], in0=w3v[:, o, 0], in1=w3v[:, o, 1], op=ALU.add),
                     reads=["fw3"], writes=["w3sd"])
                S.op("dve", lambda e, o=o: e.tensor_tensor(out=w3sd[0:64, o, :, 1, :], in0=w3v[:, o, 0], in1=w3v[:, o, 1], op=ALU.subtract),
                     reads=["fw3"], writes=["w3sd"])
            for ci in range(T // 512):
                z, zr = zc[ci % 2], ("zc", ci % 2)
                S.dma(z[:], din["zfT"][:, ci * 512:(ci + 1) * 512], writes=[zr])
                S.op("pe", lambda e, z=z: e.matmul(ps[0][0:64, :], lhsT=w1[:], rhs=z[:], start=True, stop=True),
                     reads=[zr, "fw1"], writes=[("ps", 0)])
                sin_wrapped(S, ps[0][0:64, :], ("ps", 0), h1[:], "fh1", fv[:, 1:2], fv[:, 4:5], ta, "fta", tb, "ftb", 64, 512)
                S.op("pe", lambda e: e.matmul(ps[1][0:64, :], lhsT=w2[:], rhs=h1[:], start=True, stop=True),
                     reads=["fh1", "fw2"], writes=[("ps", 1)])
                sin_wrapped(S, ps[1][0:64, :], ("ps", 1), hdn[0:64, ci * 512:(ci + 1) * 512], "hdn", fv[:, 3:4], fv[:, 5:6],
                            ta, "fta", tb, "ftb", 64, 512)
            S.barrier()
        H = st.enter_context(nc.sbuf_tensor("s_H", [128, 128, 2, CB], BF16))
        Yk = st.enter_context(nc.sbuf_tensor("s_Yk", [128, CB, 512], BF16))
        decs = [st.enter_context(nc.sbuf_tensor("s_dec%d" % i, [128, 128, CB], BF16)) for i in range(2)]
        negtn = st.enter_context(nc.sbuf_tensor("s_negtn", [128, 128], F32))
        drep = st.enter_context(nc.sbuf_tensor("s_drep", [128, 768], F32))
        EkS = st.enter_context(nc.sbuf_tensor("s_EkS", [128, 256], BF16))
        EkD = st.enter_context(nc.sbuf_tensor("s_EkD", [128, 256], BF16))
        Mfb = [st.enter_context(nc.sbuf_tensor("s_Mfb%d" % i, [128, 16, 256], BF16)) for i in range(2)]
        KFs = [st.enter_context(nc.sbuf_tensor("s_KFs%d" % i, [128, 4, 3, CB], BF16)) for i in range(3)]
        S.dma(negtn[:], din["negtn"][:, :], writes=["negtn"])
        S.dma(drep[:], din["deltas_rep"][:, :], writes=["drep"])
        S.dma(EkS[:], din["E_kS"][:, :], writes=["EkS"])
        S.dma(EkD[:], din["E_kD"][:, :], writes=["EkD"])
        mfn = 0
        kfn = 0
        pn = 0

        def ykv(k2, off):
            b = Yk[:, 0:1, k2 + off:k2 + off + 1]
            return AP(b.tensor, b.offset, [list(b.ap[0]), [256, 2], [512, CB]])

        def emit_dec(cbx, n1s):
            dd = decs[cbx % 2]
            for n1 in n1s:
                S.op("act", lambda e, n1=n1, dd=dd, cbx=cbx: e.activation(out=dd[:, n1, :], in_=drep[:, cbx * CB:(cbx + 1) * CB], func=AF.Exp,
                                                                        scale=negtn[:, n1:n1 + 1]),
                     reads=["drep", "negtn"], writes=[("dec", cbx % 2)])

        emit_dec(0, range(128))
        for cb in range(NCB):
            c0 = cb * CB
            dec = decs[cb % 2]
            for o in range(2):
                for g in range(32):
                    pb, pr = ps[pn % 4], ("ps", pn % 4)
                    pn += 1
                    for j in range(4):
                        n1 = 4 * g + j
                        S.op("pe", lambda e, pb=pb, j=j, n1=n1, o=o, cb=cb: e.matmul(
                            pb[:, j * 128:(j + 1) * 128], lhsT=hdn[:, n1:T:128],
                            rhs=w3sd[:, o, cb].rearrange("p a b -> p (a b)"), start=True, stop=True),
                            reads=["hdn", "w3sd"], writes=[pr])
                    hout = H[:, 4 * g:4 * g + 4, :, :]
                    dv = dec[:, 4 * g:4 * g + 1, 0:1]
                    din1 = AP(dv.tensor, dv.offset, [list(dv.ap[0]), [CB, 4], [0, 2], [1, CB]])
                    S.op("dve", lambda e, pb=pb, hout=hout, din1=din1: e.tensor_tensor(
                        out=hout, in0=pb[:, :].rearrange("p (j s c) -> p j s c", j=4, s=2), in1=din1, op=ALU.mult),
                        reads=[pr, ("dec", cb % 2)], writes=["H"])
                for c in range(CB):
                    pb, pr = ps[pn % 4], ("ps", pn % 4)
                    pn += 1
                    S.op("pe", lambda e, pb=pb, c=c: e.matmul(pb[:, 0:256], lhsT=H[:, :, 0, c], rhs=EkS[:], start=True, stop=True),
                         reads=["H", "EkS"], writes=[pr])
                    S.op("pe", lambda e, pb=pb, c=c: e.matmul(pb[:, 256:512], lhsT=H[:, :, 1, c], rhs=EkD[:], start=True, stop=True),
                         reads=["H", "EkD"], writes=[pr])
                    yout = Yk[:, c, :]
                    pin = pb[:, :]
                    if o == 1 and cb + 1 < NCB:
                        emit_dec(cb + 1, range(2 * c, 2 * c + 2))
                    if c % 2 == 0:
                        S.op("act", lambda e, yout=yout, pin=pin: e.copy(out=yout, in_=pin), reads=[pr], writes=["Yk"])
                    else:
                        S.op("dve", lambda e, yout=yout, pin=pin: e.tensor_copy(out=yout, in_=pin), reads=[pr], writes=["Yk"])
                for g in range(32):
                    pb, pr = ps[4 + pn % 4], ("ps", 4 + pn % 4)
                    pn += 1
                    for j in range(4):
                        k2 = 4 * g + j
                        if k2 % 16 == 0:
                            mb_, mr_ = Mfb[mfn % 2], ("Mfb", mfn % 2)
                            mfn += 1
                            S.dma(mb_[:], din["Mf"][:, k2:k2 + 16, :], writes=[mr_])
                        S.op("pe", lambda e, pb=pb, j=j, k2=k2, mb_=mb_: e.matmul(
                            pb[:, j * 128:(j + 1) * 128], lhsT=mb_[:, k2 % 16, 0:128],
                            rhs=ykv(k2, 0), start=True, stop=False),
                            reads=["Yk", mr_], writes=[pr])
                        S.op("pe", lambda e, pb=pb, j=j, k2=k2, mb_=mb_: e.matmul(
                            pb[:, j * 128:(j + 1) * 128], lhsT=mb_[:, k2 % 16, 128:256],
                            rhs=ykv(k2, 128), start=False, stop=True),
                            reads=["Yk", mr_], writes=[pr])
                    kb, kr = KFs[kfn % 3], ("KFs", kfn % 3)
                    kfn += 1
                    pv = pb[:, :].rearrange("p (j s c) -> p j s c", j=4, s=2)
                    S.op("act", lambda e, kb=kb, pv=pv: e.copy(out=kb[:, :, 0:2, :], in_=pv), reads=[pr], writes=[kr])
                    S.op("dve", lambda e, kb=kb, pv=pv: e.tensor_scalar(out=kb[:, :, 2, :], in0=pv[:, :, 1, :], scalar1=-1.0, scalar2=None,
                                                                        op0=ALU.mult), reads=[pr], writes=[kr])
                    S.dma(k.KFd[o, cb, :, g * 4 * 3 * CB:(g + 1) * 4 * 3 * CB], kb[:].rearrange("p a b c -> p (a b c)"), reads=[kr], eng="pool")


def phaseH(k):
    nc, S, din, ps = k.nc, k.S, k.din, k.ps
    with ExitStack() as st:
        bufA = st.enter_context(nc.sbuf_tensor("s_hA", [128, CB, 128], BF16))
        bufB = st.enter_context(nc.sbuf_tensor("s_hB", [128, CB, 128], BF16))
        bufC = st.enter_context(nc.sbuf_tensor("s_hC", [128, 128, CB], BF16))
        Yd = st.enter_context(nc.sbuf_tensor("s_Yd", [128, CB, 384], BF16))
        Pb = st.enter_context(nc.sbuf_tensor("s_P", [128, 128, 2, CB], BF16))
        Zs = st.enter_context(nc.sbuf_tensor("s_Zs", [128, CB, 256], BF16))
        Ed = st.enter_context(nc.sbuf_tensor("s_Ed", [128, 384], BF16))
        G1 = st.enter_context(nc.sbuf_tensor("s_G1", [128, 256], BF16))
        G2 = st.enter_context(nc.sbuf_tensor("s_G2", [128, 256], BF16))
        drep = st.enter_context(nc.sbuf_tensor("s_hdrep", [128, 2, CB], F32))
        Mb = [st.enter_context(nc.sbuf_tensor("s_Mb%d" % i, [128, 8, 256], BF16)) for i in range(2)]
        KFb = [st.enter_context(nc.sbuf_tensor("s_KFb%d" % i, [128, 4, 3, CB], BF16)) for i in range(3)]
        t1 = [st.enter_context(nc.sbuf_tensor("s_ht1_%d" % i, [128, 4, 2, CB], F32)) for i in range(2)]
        t2 = [st.enter_context(nc.sbuf_tensor("s_ht2_%d" % i, [128, 4, 2, CB], F32)) for i in range(2)]
        te = [st.enter_context(nc.sbuf_tensor("s_hte%d" % i, [128, 8, CB], F32)) for i in range(2)]
        S.dma(Ed[:], din["E_d"][:, :], writes=["Ed"])
        S.dma(G1[:], din["G1"][:, :], writes=["G1"])
        S.dma(G2[:], din["G2"][:, :], writes=["G2"])
        ohs = AP(Pb[:].tensor, Pb[:].offset, [[Pb[:].ap[0][0], 64], [1, T]])
        mn = 0
        kn = 0
        tn_ = 0
        pn = 0

        def ydv(k2, off):
            b = Yd[:, 0:1, k2 + off:k2 + off + 1]
            return AP(b.tensor, b.offset, [list(b.ap[0]), [128, 2], [384, CB]])

        def load_blk(buf, res, row0):
            src = AP(k.uT.tensor, k.uT[row0:row0 + 1, 0:1].offset, [[128, 128], [T, CB], [1, 128]])
            S.dma(buf[:], src, writes=[res])

        for cb in range(NCB):
            c0 = cb * CB
            load_blk(bufA, "hA", c0)
            load_blk(bufB, "hB", 768 + c0)
            for o in range(2):
                S.dma(drep[:, o, :], din["hyena_d"][o:o + 1, c0:c0 + CB].partition_broadcast(128), writes=["hdrep"])
            for o in range(2):
                Din, dres = (bufA, "hA") if o == 0 else (bufC, "hC")
                for c in range(CB):
                    pb, pr = ps[pn % 4], ("ps", pn % 4)
                    pn += 1
                    S.op("pe", lambda e, pb=pb, c=c, Din=Din, o=o: e.matmul(pb[:, 0:384], lhsT=(Din[:, c, :] if o == 0 else Din[:, :, c]),
                                                                         rhs=Ed[:], start=True, stop=True),
                         reads=[dres, "Ed"], writes=[pr])
                    yout = Yd[:, c, :]
                    pin = pb[:, 0:384]
                    if c % 2 == 0:
                        S.op("act", lambda e, yout=yout, pin=pin: e.copy(out=yout, in_=pin), reads=[pr], writes=["Yd"])
                    else:
                        S.op("dve", lambda e, yout=yout, pin=pin: e.tensor_copy(out=yout, in_=pin), reads=[pr], writes=["Yd"])
                for g in range(32):
                    pb, pr = ps[4 + pn % 4], ("ps", 4 + pn % 4)
                    pn += 1
                    kb, kr = KFb[kn % 3], ("KFb", kn % 3)
                    kn += 1
                    S.dma(kb[:].rearrange("p a b c -> p (a b c)"), k.KFd[o, cb, :, g * 12 * CB:(g + 1) * 12 * CB], writes=[kr])
                    for j in range(4):
                        k2 = 4 * g + j
                        if k2 % 8 == 0:
                            mb_, mr_ = Mb[mn % 2], ("Mb", mn % 2)
                            mn += 1
                            S.dma(mb_[:], din["Mf"][:, k2:k2 + 8, :], writes=[mr_])
                        S.op("pe", lambda e, pb=pb, j=j, k2=k2, mb_=mb_: e.matmul(
                            pb[:, j * 128:(j + 1) * 128], lhsT=mb_[:, k2 % 8, 0:128],
                            rhs=ydv(k2, 128), start=True, stop=False),
                            reads=["Yd", mr_], writes=[pr])
                        S.op("pe", lambda e, pb=pb, j=j, k2=k2, mb_=mb_: e.matmul(
                            pb[:, j * 128:(j + 1) * 128], lhsT=mb_[:, k2 % 8, 128:256],
                            rhs=ydv(k2, 0), start=False, stop=True),
                            reads=["Yd", mr_], writes=[pr])
                    a1, a1r = t1[tn_ % 2], ("ht1", tn_ % 2)
                    a2, a2r = t2[tn_ % 2], ("ht2", tn_ % 2)
                    tn_ += 1
                    pv = pb[:, :].rearrange("p (j s c) -> p j s c", j=4, s=2)
                    S.op("dve", lambda e, a1=a1, pv=pv, kb=kb: e.tensor_tensor(
                        out=a1[:], in0=pv, in1=bc(kb[:, :, 0:1, :], [128, 4, 2, CB]), op=ALU.mult), reads=[pr, kr], writes=[a1r])
                    S.op("dve", lambda e, a2=a2, pv=pv, kb=kb: e.tensor_tensor(
                        out=a2[:], in0=pv, in1=kb[:, :, 1:3, :], op=ALU.mult), reads=[pr, kr], writes=[a2r])
                    S.op("pool", lambda e, a1=a1, a2=a2, g=g: e.tensor_tensor(
                        out=Pb[:, 4 * g:4 * g + 4, 0, :], in0=a1[:, :, 0, :], in1=a2[:, :, 1, :], op=ALU.add),
                        reads=[a1r, a2r], writes=["P"])
                    S.op("pool", lambda e, a1=a1, a2=a2, g=g: e.tensor_tensor(
                        out=Pb[:, 4 * g:4 * g + 4, 1, :], in0=a1[:, :, 1, :], in1=a2[:, :, 0, :], op=ALU.add),
                        reads=[a1r, a2r], writes=["P"])
                for c2 in range(CB // 2):
                    pb, pr = ps[pn % 4], ("ps", pn % 4)
                    pn += 1
                    for h in range(2):
                        c = 2 * c2 + h
                        S.op("pe", lambda e, pb=pb, c=c, h=h: e.matmul(pb[:, h * 256:(h + 1) * 256], lhsT=Pb[:, :, 0, c], rhs=G1[:],
                                                                       start=True, stop=False), reads=["P", "G1"], writes=[pr])
                        S.op("pe", lambda e, pb=pb, c=c, h=h: e.matmul(pb[:, h * 256:(h + 1) * 256], lhsT=Pb[:, :, 1, c], rhs=G2[:],
                                                                       start=False, stop=True), reads=["P", "G2"], writes=[pr])
                    zout = Zs[:, 2 * c2:2 * c2 + 2, :]
                    pin = pb[:, :].rearrange("p (h x) -> p h x", h=2)
                    if c2 % 2 == 0:
                        S.op("act", lambda e, zout=zout, pin=pin: e.copy(out=zout, in_=pin), reads=[pr], writes=["Zs"])
                    else:
                        S.op("dve", lambda e, zout=zout, pin=pin: e.tensor_copy(out=zout, in_=pin), reads=[pr], writes=["Zs"])
                if o == 1:
                    load_blk(bufA, "hA", 1536 + c0)
                Xg, xres = (bufB, "hB") if o == 0 else (bufA, "hA")
                for g in range(16):
                    pb, pr = ps[4 + pn % 4], ("ps", 4 + pn % 4)
                    pn += 1
                    for j in range(8):
                        n1 = 8 * g + j
                        if n1 % 8 == 0:
                            mb_, mr_ = Mb[mn % 2], ("Mb", mn % 2)
                            mn += 1
                            S.dma(mb_[:], din["Minv"][:, n1:n1 + 8, :], writes=[mr_])
                        S.op("pe", lambda e, pb=pb, j=j, n1=n1, mb_=mb_: e.matmul(
                            pb[:, j * CB:(j + 1) * CB], lhsT=mb_[:, n1 % 8, 0:128], rhs=Zs[:, :, n1], start=True, stop=False),
                            reads=["Zs", mr_], writes=[pr])
                        S.op("pe", lambda e, pb=pb, j=j, n1=n1, mb_=mb_: e.matmul(
                            pb[:, j * CB:(j + 1) * CB], lhsT=mb_[:, n1 % 8, 128:256], rhs=Zs[:, :, 128 + n1], start=False, stop=True),
                            reads=["Zs", mr_], writes=[pr])
                    tb, tr = te[g % 2], ("hte", g % 2)
                    zin = Din[:, :, 8 * g:8 * g + 8].rearrange("p c j -> p j c") if o == 0 else Din[:, 8 * g:8 * g + 8, :]
                    xin = Xg[:, :, 8 * g:8 * g + 8].rearrange("p c j -> p j c")
                    S.op("pool", lambda e, tb=tb, zin=zin, o=o, c0=c0: e.tensor_tensor(
                        out=tb[:], in0=zin, in1=bc(drep[:, o:o + 1, :], [128, 8, CB]), op=ALU.mult),
                        reads=[dres, "hdrep"], writes=[tr])
                    S.op("dve", lambda e, tb=tb, pb=pb: e.tensor_tensor(
                        out=tb[:], in0=pb[:, :].rearrange("p (j c) -> p j c", j=8), in1=tb[:], op=ALU.add), reads=[pr, tr], writes=[tr])
                    if o == 0:
                        zo = bufC[:, 8 * g:8 * g + 8, :]
                        S.op("pool", lambda e, tb=tb, xin=xin, zo=zo: e.tensor_tensor(out=zo, in0=tb[:], in1=xin, op=ALU.mult),
                             reads=[tr, xres], writes=["hC"])
                    else:
                        z3v = bufB[:].rearrange("p c j -> p (c j)")[:, 8 * g * CB:(8 * g + 8) * CB].rearrange("p (j c) -> p j c", j=8)
                        S.op("pool", lambda e, tb=tb, xin=xin, z3v=z3v: e.tensor_tensor(out=z3v, in0=tb[:], in1=xin, op=ALU.mult),
                             reads=[tr, xres], writes=["hB"])
            z3 = bufB[:].rearrange("p c j -> p (c j)")
            for g in range(16):
                pb, pr = ps[pn % 4], ("ps", pn % 4)
                pn += 1
                pT = pb[:].bitcast(BF16)
                for j in range(8):
                    n1 = 8 * g + j
                    S.op("pe", lambda e, pT=pT, j=j, n1=n1: e.transpose(out=pT[0:CB, j * 128:(j + 1) * 128],
                                                                        in_=z3[:, n1 * CB:(n1 + 1) * CB], identity=k.ident[:]),
                         reads=["hB", "ident"], writes=[pr])
                ov = AP(ohs.tensor, ohs.offset + 8 * g, [list(ohs.ap[0]), [1, 8], [128, 128]])
                pin = pT[0:CB, :].rearrange("p (j n) -> p j n", j=8)
                if g % 2 == 0:
                    S.op("act", lambda e, ov=ov, pin=pin: e.copy(out=ov, in_=pin), reads=[pr], writes=["P"])
                else:
                    S.op("dve", lambda e, ov=ov, pin=pin: e.tensor_copy(out=ov, in_=pin), reads=[pr], writes=["P"])
            S.dma(k.ohT[c0:c0 + CB, :], ohs, reads=["P"])


def phaseA(k):
    nc, S, din, ps = k.nc, k.S, k.din, k.ps
    SPAN = 4096
    with ExitStack() as st:
        Hk = st.enter_context(nc.sbuf_tensor("s_Hk", [128, 3, 12, 256], BF16))
        J = st.enter_context(nc.sbuf_tensor("s_J", [128, 128], BF16))
        bm = st.enter_context(nc.sbuf_tensor("s_bm", [128, 256], BF16))
        swb = st.enter_context(nc.sbuf_tensor("s_swb", [128, 128], BF16))
        swf = st.enter_context(nc.sbuf_tensor("s_swf", [128, 128], F32))
        selA = st.enter_context(nc.sbuf_tensor("s_selA", [128, 128], F32))
        selB = st.enter_context(nc.sbuf_tensor("s_selB", [128, 128], F32))
        with ExitStack() as s2:
            rb = s2.enter_context(nc.sbuf_tensor("s_rb", [32, 12], F32))
            oh = s2.enter_context(nc.sbuf_tensor("s_oh", [32, 1152], F32))
            mr = s2.enter_context(nc.sbuf_tensor("s_mrow", [12, 1152], F32))
            av = s2.enter_context(nc.sbuf_tensor("s_av", [12, 1152], BF16))
            S.dma(rb[:], din["rel_bias"][:, :], writes=["rb"])
            S.dma(oh[:], din["OH"][:, :], writes=["oh"])
            S.dma(mr[:], din["mrow"][:, :], writes=["mrow"])
            for i in range(3):
                S.op("pe", lambda e, i=i: e.matmul(ps[i][0:12, 0:384], lhsT=rb[:], rhs=oh[:, i * 384:(i + 1) * 384], start=True, stop=True),
                     reads=["rb", "oh"], writes=[("ps", i)])
                S.op("dve", lambda e, i=i: e.tensor_tensor(out=av[:, i * 384:(i + 1) * 384], in0=ps[i][0:12, 0:384],
                                                           in1=mr[:, i * 384:(i + 1) * 384], op=ALU.add),
                     reads=[("ps", i), "mrow"], writes=["av"])
            S.dma(k.Avec[:, :], av[:], reads=["av"], writes=["Avec"])
            for h in range(12):
                for ri in range(3):
                    src = AP(k.Avec.tensor, k.Avec[h:h + 1, ri * 384:ri * 384 + 1].offset, [[1, 128], [1, 256]])
                    S.dma(Hk[:, ri, h, :], src, reads=["Avec"], writes=["Hk"])
            S.dma(J[:], din["antiid"][:, :], writes=["J"])
            S.dma(bm[:], din["bmask"][:, :], writes=["bm"])
            S.dma(swb[:], din["swap"][:, :], writes=["swb"])
            S.op("dve", lambda e: e.tensor_copy(out=swf[:], in_=swb[:]), reads=["swb"], writes=["swf"])
            S.op("pool", lambda e: e.memset(selA[:], 0.0), writes=["sel"])
            S.op("pool", lambda e: e.memset(selB[:], 0.0), writes=["sel"])
            S.op("dve", lambda e: e.tensor_copy(out=selA[:, 0:64], in_=swb[:, 0:64]), reads=["swb", "sel"], writes=["sel"])
            S.op("dve", lambda e: e.tensor_copy(out=selB[:, 64:128], in_=swb[:, 64:128]), reads=["swb", "sel"], writes=["sel"])
            S.barrier()
        TP = T + 2 * PAD
        qAB = st.enter_context(nc.sbuf_tensor("s_qAB", [128, 2, TP], BF16))
        kT = st.enter_context(nc.sbuf_tensor("s_kT", [128, TP], BF16))
        S.op("pool", lambda e: e.memset(qAB[:], 0.0), writes=["qAB"])
        S.op("pool", lambda e: e.memset(kT[:, 0:PAD], 0.0), writes=["kT"])
        S.op("pool", lambda e: e.memset(kT[:, PAD + T:TP], 0.0), writes=["kT"])
        acc = st.enter_context(nc.sbuf_tensor("s_acc", [128, 2, SPAN], F32))
        OW = 2048
        oT = [st.enter_context(nc.sbuf_tensor("s_oT%d" % i, [128, OW], BF16)) for i in range(2)]
        rden = [st.enter_context(nc.sbuf_tensor("s_rden%d" % i, [128, 512], F32)) for i in range(2)]
        Vt = [st.enter_context(nc.sbuf_tensor("s_Vt%d" % i, [128, 256], BF16)) for i in range(8)]
        PT = [st.enter_context(nc.sbuf_tensor("s_PT%d" % i, [128, 2, 256], BF16)) for i in range(6)]
        vn = 0
        ptn = 0
        sn = 0
        on = 0
        rn = 0
        spn = 0
        DSK = 3
        for hp in range(6):
            S.dma(qAB[0:64, 0, PAD:PAD + T], k.qkT[hp * 128:hp * 128 + 64, :], writes=["qAB"])
            S.dma(qAB[64:128, 1, PAD:PAD + T], k.qkT[hp * 128 + 64:hp * 128 + 128, :], writes=["qAB"])
            S.dma(kT[:, PAD:PAD + T], k.qkT[768 + hp * 128:768 + (hp + 1) * 128, :], writes=["kT"])
            for s in range(T // SPAN):
                pendB = []
                S.op("pool", lambda e: e.memset(acc[:], 0.0), writes=["acc"])
                for ri, r in enumerate((1, 4, 16)):
                    Lr = T // r
                    jb = (T // 2) // (r * 128)
                    nqb = SPAN // (128 * r)
                    for rho in range(r):
                        ja, jbnd = s * nqb, s * nqb + nqb
                        pobank = {}
                        for j in range(ja, jbnd + 1):
                            c_lo = 128 if j == ja else 0
                            c_hi = 128 if j == jbnd else 256
                            ncol = c_hi - c_lo
                            vt, vr = Vt[vn % 8], ("Vt", vn % 8)
                            vn += 1
                            m0 = 128 * j - 64
                            lo, hi = 0, 128
                            if m0 < 0:
                                lo = 64
                            if m0 + 128 > Lr:
                                hi = 64
                            if lo > 0 or hi < 128:
                                S.op("pool", lambda e, vt=vt: e.memset(vt[:], 0.0), writes=[vr])
                            tok0 = rho + r * (m0 + lo)
                            src = AP(k.vaug.tensor, k.vaug[tok0:tok0 + 1, hp * 256:hp * 256 + 1].offset, [[1536 * r, hi - lo], [1, 256]])
                            S.dma(vt[lo:hi, :], src, writes=[vr])
                            mq0 = 128 * j - 128 + c_lo
                            qc0 = PAD + rho + r * mq0
                            qsl = slice(qc0, qc0 + (ncol - 1) * r + 1, r)
                            kc0 = PAD + rho + r * m0
                            ksl = slice(kc0, kc0 + 127 * r + 1, r)
                            straddle = (j == jb)
                            pS, psr = ps[sn % 4], ("ps", sn % 4)
                            sn += 1
                            pt, ptr = PT[ptn % 6], ("PT", ptn % 6)
                            ptn += 1
                            pSv = pS[:, :].rearrange("p (h c) -> p h c", h=2)[:, :, 0:ncol]

                            def stageA(pSv=pSv, psr=psr, ksl=ksl, qsl=qsl, ncol=ncol, ri=ri, c_lo=c_lo, c_hi=c_hi,
                                       straddle=straddle, pt=pt, ptr=ptr, hp=hp):
                                S.op("pe", lambda e: e.matmul(pSv, lhsT=kT[:, ksl], rhs=qAB[:, :, qsl], start=True, stop=False),
                                     reads=["kT", "qAB"], writes=[psr])
                                S.op("pe", lambda e: e.matmul(pSv, lhsT=J[:], rhs=Hk[:, ri, 2 * hp:2 * hp + 2, c_lo:c_hi],
                                                              start=False, stop=not straddle), reads=["J", "Hk"], writes=[psr])
                                if straddle:
                                    S.op("pe", lambda e: e.matmul(pSv, lhsT=k.ident[:], rhs=bc(bm[:, c_lo:c_hi].rearrange("p (o c) -> p o c", o=1), [128, 2, ncol]),
                                                                  start=False, stop=True), reads=["ident", "bm"], writes=[psr])
                                S.op("act", lambda e: e.activation(out=pt[:, :, 0:ncol], in_=pSv, func=AF.Exp), reads=[psr], writes=[ptr])

                            pieces = []
                            if c_lo == 0:
                                pieces.append((0, j - 1))
                            if c_hi == 256:
                                pieces.append((1, j))
                            for (half, jq) in pieces:
                                if half == 1:
                                    pobank[jq] = (ps[4 + on % 4], ("ps", 4 + on % 4))
                                    on += 1
                            pbs = {jq: pobank[jq] for (_, jq) in pieces}

                            def stageB(pieces=pieces, pbs=pbs, vt=vt, vr=vr, pt=pt, ptr=ptr, c_lo=c_lo, r=r, rho=rho, s=s):
                                for (half, jq) in pieces:
                                    po, por = pbs[jq]
                                    off = half * 128 - c_lo
                                    for hh in range(2):
                                        S.op("pe", lambda e, po=po, hh=hh, off=off, half=half: e.matmul(
                                            po[:, hh * 128:(hh + 1) * 128], lhsT=vt[:, hh * 128:(hh + 1) * 128], rhs=pt[:, hh, off:off + 128],
                                            start=(half == 1 and hh == 0), stop=(half == 0), skip_group_check=True),
                                            reads=[vr, ptr], writes=[por])
                                    if half == 0:
                                        a0 = rho + r * 128 * jq - s * SPAN
                                        asl = slice(a0, a0 + 127 * r + 1, r)
                                        S.op("dve", lambda e, po=po, asl=asl: e.tensor_tensor(
                                            out=acc[:, :, asl], in0=po[:, 0:256].rearrange("p (h c) -> p h c", h=2), in1=acc[:, :, asl],
                                            op=ALU.add), reads=[por, "acc"], writes=["acc"])

                            stageA()
                            pendB.append(stageB)
                            if len(pendB) > DSK:
                                pendB.pop(0)()
                while pendB:
                    pendB.pop(0)()
                for ow in range(SPAN // OW):
                    ot, otr = oT[spn % 2], ("oT", spn % 2)
                    spn += 1
                    for cc in range(OW // 512):
                        c0 = ow * OW + cc * 512
                        pw, pwr = ps[sn % 4], ("ps", sn % 4)
                        sn += 1
                        S.op("pe", lambda e, pw=pw, c0=c0: e.matmul(pw[:, :], lhsT=selA[:], rhs=acc[:, 0, c0:c0 + 512],
                                                                    start=True, stop=False), reads=["acc", "sel"], writes=[pwr])
                        S.op("pe", lambda e, pw=pw, c0=c0: e.matmul(pw[:, :], lhsT=selB[:], rhs=acc[:, 1, c0:c0 + 512],
                                                                    start=False, stop=True), reads=["acc", "sel"], writes=[pwr])
                        rd, rdr = rden[rn % 2], ("rden", rn % 2)
                        rn += 1
                        S.op("dve", lambda e, rd=rd, pw=pw: e.reciprocal(out=rd[:], in_=pw[:, :]), reads=[pwr], writes=[rdr])
                        for hh in range(2):
                            nlo = 64 * hh
                            S.op("pool", lambda e, rd=rd, ot=ot, hh=hh, cc=cc, c0=c0, nlo=nlo: e.tensor_tensor(
                                out=ot[nlo:nlo + 64, cc * 512:(cc + 1) * 512], in0=acc[nlo:nlo + 64, hh, c0:c0 + 512],
                                in1=rd[nlo:nlo + 64, :], op=ALU.mult), reads=["acc", rdr], writes=[otr])
                    t0_ = s * SPAN + ow * OW
                    S.dma(k.oaT[hp * 128:(hp + 1) * 128, t0_:t0_ + OW], ot[:], reads=[otr])


def phaseF(k):
    nc, S, din, ps = k.nc, k.S, k.din, k.ps
    with ExitStack() as st:
        wpa = st.enter_context(nc.sbuf_tensor("s_wpa", [128, 6, DM], BF16))
        wph = st.enter_context(nc.sbuf_tensor("s_wph", [128, 6, DM], BF16))
        wo = st.enter_context(nc.sbuf_tensor("s_wo", [128, 8, DM], BF16))
        with ExitStack() as s2:
            stg = [s2.enter_context(nc.sbuf_tensor("s_fstg%d" % i, [128, DM], F32)) for i in range(2)]
            n = 0
            for (wt, nm, src, nk) in ((wpa, "wpa", "w_proj_attn", 6), (wph, "wph", "w_proj_hyena", 6), (wo, "wo", "w_out", 8)):
                for kk in range(nk):
                    b = n % 2
                    S.dma(stg[b][:], din[src][kk * 128:(kk + 1) * 128, :], writes=[("fstg", b)])
                    eng = ("dve", "pool")[n % 2]
                    S.op(eng, lambda e, b=b, wt=wt, kk=kk: e.tensor_copy(out=wt[:, kk, :], in_=stg[b][:]), reads=[("fstg", b)], writes=[nm])
                    n += 1
            S.barrier()
        gater = st.enter_context(nc.sbuf_tensor("s_gater", [128, 2, DM], F32))
        for s_ in range(2):
            S.dma(gater[:, s_, :], k.modrep[s_:s_ + 1, 2 * DM:3 * DM].partition_broadcast(128), writes=[("modr", s_)])
        oa = [st.enter_context(nc.sbuf_tensor("s_foa%d" % i, [128, 6, 512], BF16)) for i in range(2)]
        ga = [st.enter_context(nc.sbuf_tensor("s_fga%d" % i, [128, 6, 512], BF16)) for i in range(2)]
        oh_ = [st.enter_context(nc.sbuf_tensor("s_foh%d" % i, [128, 6, 512], BF16)) for i in range(2)]
        gh = [st.enter_context(nc.sbuf_tensor("s_fgh%d" % i, [128, 6, 512], BF16)) for i in range(2)]
        mt = [st.enter_context(nc.sbuf_tensor("s_fmt%d" % i, [128, 16, 512], BF16)) for i in range(2)]
        mix = [st.enter_context(nc.sbuf_tensor("s_fmix%d" % i, [128, 8, 512], BF16)) for i in range(2)]
        ta = [st.enter_context(nc.sbuf_tensor("s_fta%d" % i, [128, 512], F32)) for i in range(2)]
        tb = [st.enter_context(nc.sbuf_tensor("s_ftb%d" % i, [128, 512], F32)) for i in range(2)]
        xt = [st.enter_context(nc.sbuf_tensor("s_fx%d" % i, [128, DM], F32)) for i in range(2)]
        r1 = [st.enter_context(nc.sbuf_tensor("s_fr%d" % i, [128, DM], F32)) for i in range(2)]
        yo = [st.enter_context(nc.sbuf_tensor("s_fy%d" % i, [128, DM], F32)) for i in range(2)]
        sqj = st.enter_context(nc.sbuf_tensor("s_fsq", [128, DM], BF16))
        ss = [st.enter_context(nc.sbuf_tensor("s_fss%d" % i, [128, 2], F32)) for i in range(2)]
        pn = 0
        tn_ = 0
        for ci in range(NCHUNK):
            seg = 0 if ci < NCHUNK // 2 else 1
            b = ci % 2
            cs = slice(ci * 512, (ci + 1) * 512)
            S.dma(oa[b][:], k.oaT[:, cs].rearrange("(a p) t -> p a t", p=128), writes=[("foa", b)])
            S.dma(ga[b][:], k.gaT[:, cs].rearrange("(a p) t -> p a t", p=128), writes=[("fga", b)])
            S.dma(oh_[b][:], k.ohT[:, cs].rearrange("(a p) t -> p a t", p=128), writes=[("foh", b)])
            S.dma(gh[b][:], k.ghT[:, cs].rearrange("(a p) t -> p a t", p=128), writes=[("fgh", b)])
            S.dma(mt[b][:], k.mT[:, cs].rearrange("(a p) t -> p a t", p=128), writes=[("fmt", b)])
            S.op("pool", lambda e, b=b: e.tensor_tensor(out=oa[b][:], in0=oa[b][:], in1=ga[b][:], op=ALU.mult),
                 reads=[("foa", b), ("fga", b)], writes=[("foa", b)])
            S.op("dve", lambda e, b=b: e.tensor_tensor(out=oh_[b][:], in0=oh_[b][:], in1=gh[b][:], op=ALU.mult),
                 reads=[("foh", b), ("fgh", b)], writes=[("foh", b)])
            for fb in range(8):
                pA, pAr = ps[pn % 4], ("ps", pn % 4)
                pn += 1
                pH, pHr = ps[pn % 4], ("ps", pn % 4)
                pn += 1
                for kk in range(6):
                    S.op("pe", lambda e, pA=pA, kk=kk, fb=fb, b=b: e.matmul(pA[:, :], lhsT=wpa[:, kk, fb * 128:(fb + 1) * 128], rhs=oa[b][:, kk, :],
                                                                             start=(kk == 0), stop=(kk == 5)), reads=["wpa", ("foa", b)], writes=[pAr])
                for kk in range(6):
                    S.op("pe", lambda e, pH=pH, kk=kk, fb=fb, b=b: e.matmul(pH[:, :], lhsT=wph[:, kk, fb * 128:(fb + 1) * 128], rhs=oh_[b][:, kk, :],
                                                                             start=(kk == 0), stop=(kk == 5)), reads=["wph", ("foh", b)], writes=[pHr])
                a_, ar_ = ta[tn_ % 2], ("fta", tn_ % 2)
                b_, br_ = tb[tn_ % 2], ("ftb", tn_ % 2)
                tn_ += 1
                S.op("dve", lambda e, a_=a_, pA=pA, fb=fb, b=b: e.tensor_tensor(out=a_[:], in0=pA[:, :], in1=mt[b][:, fb, :], op=ALU.mult),
                     reads=[pAr, ("fmt", b)], writes=[ar_])
                S.op("dve", lambda e, b_=b_, pH=pH, fb=fb, b=b: e.tensor_tensor(out=b_[:], in0=pH[:, :], in1=mt[b][:, 8 + fb, :], op=ALU.mult),
                     reads=[pHr, ("fmt", b)], writes=[br_])
                S.op("pool", lambda e, a_=a_, b_=b_, fb=fb, b=b: e.tensor_tensor(out=mix[b][:, fb, :], in0=a_[:], in1=b_[:], op=ALU.add),
                     reads=[ar_, br_], writes=[("fmix", b)])
            for tt in range(4):
                t = ci * 4 + tt
                tb2 = t % 2
                S.dma(xt[tb2][:], din["x"][t * 128:(t + 1) * 128, :], writes=[("fx", tb2)])
                p0, p1 = ps[4 + 2 * tb2], ps[5 + 2 * tb2]
                p0r, p1r = ("ps", 4 + 2 * tb2), ("ps", 5 + 2 * tb2)
                for half, (pp, ppr) in enumerate(((p0, p0r), (p1, p1r))):
                    for fb in range(8):
                        S.op("pe", lambda e, pp=pp, fb=fb, tt=tt, half=half, b=b: e.matmul(
                            pp[:, :], lhsT=mix[b][:, fb, tt * 128:(tt + 1) * 128], rhs=wo[:, fb, half * 512:(half + 1) * 512],
                            start=(fb == 0), stop=(fb == 7)), reads=[("fmix", b), "wo"], writes=[ppr])
                rr, rrr = r1[tb2], ("fr", tb2)
                for half, (pp, ppr) in enumerate(((p0, p0r), (p1, p1r))):
                    hs = slice(half * 512, (half + 1) * 512)
                    S.op("dve", lambda e, pp=pp, rr=rr, hs=hs, seg=seg: e.tensor_tensor(
                        out=rr[:, hs], in0=pp[:, :], in1=gater[:, seg, hs.start:hs.stop], op=ALU.mult),
                        reads=[ppr, ("modr", seg)], writes=[rrr])
                S.op("pool", lambda e, rr=rr, tb2=tb2: e.tensor_tensor(out=rr[:], in0=rr[:], in1=xt[tb2][:], op=ALU.add),
                     reads=[rrr, ("fx", tb2)], writes=[rrr])
                sb, sr = ss[tb2], ("fss", tb2)
                S.op("act", lambda e, rr=rr, sb=sb: e.activation(out=sqj[:], in_=rr[:], func=AF.Square, scale=1.0 / 32.0, accum_out=sb[:, 0:1]),
                     reads=[rrr], writes=["fsq", sr])
                S.op("act", lambda e, sb=sb: e.activation(out=sb[:, 1:2], in_=sb[:, 0:1], func=AF.Sqrt, bias=k.epsc[:, 0:1]),
                     reads=[sr, "epsc"], writes=[sr])
                S.op("dve", lambda e, sb=sb: e.reciprocal(out=sb[:, 1:2], in_=sb[:, 1:2]), reads=[sr], writes=[sr])
                yb, yr = yo[tb2], ("fy", tb2)
                S.op("dve", lambda e, rr=rr, sb=sb, yb=yb: e.scalar_tensor_tensor(
                    out=yb[:], in0=rr[:], scalar=sb[:, 1:2], in1=k.fg_rep[:], op0=ALU.mult, op1=ALU.mult),
                    reads=[rrr, sr, "fg_rep"], writes=[yr])
                S.dma(k.y[t * 128:(t + 1) * 128, :], yb[:], reads=[yr], eng="pool")
```

```python
import math
import numpy as np
import ml_dtypes
import concourse.bass as bass
import concourse.mybir as mybir
from concourse.ap import AP
from concourse.bass_utils import run_bass_kernel_spmd

F32 = mybir.dt.float32
BF16 = mybir.dt.bfloat16
AF = mybir.ActivationFunctionType
ALU = mybir.AluOpType

T = 16384
DM = 1024
NCHUNK = T // 512
NFFT = 32768
EPS = 1e-6
CB = 64
NCB = 768 // CB
PAD = 1024

OQ, OK_, OV, OGA, OU, OGH, OMA, OMH = 0, 768, 1536, 2304, 3072, 5376, 6144, 7168

DEBUG = {}


class Sched:
    ENG = ("pe", "act", "dve", "pool", "sp")

    def __init__(self, nc, sems, dma_ring):
        self.nc = nc
        self.ops = []
        self.eng = {"pe": nc.tensor, "act": nc.scalar, "dve": nc.vector, "pool": nc.gpsimd, "sp": nc.sync}
        self.sems = sems
        self.ring = dma_ring
        self.cnt = {e: 0 for e in self.ENG}
        self.ndma = 0
        self.last_w = {}
        self.readers = {}
        self.waited = {e: {d: 0 for d in self.ENG} for e in self.ENG}
        self.waited_dma = {e: {} for e in self.ENG}
        self.last_op = {e: None for e in self.ENG}
        self.pending = []
        self.dma_eng = "sp"

    def op(self, eng, fn, reads=(), writes=()):
        self.ops.append(("op", eng, fn, tuple(reads), tuple(writes)))

    def dma(self, out, in_, reads=(), writes=(), eng="sp"):
        self.ops.append(("dma", eng, (out, in_), tuple(reads), tuple(writes)))

    def barrier(self):
        self.ops.append(("bar",))

    def flush(self):
        ops = self.ops
        n = len(ops)
        last_w, readers = {}, {}
        last_on = {e: -1 for e in self.ENG}
        deps = [None] * n
        marked = [False] * n
        for i, o in enumerate(ops):
            if o[0] == "bar":
                deps[i] = dict(last_on)
                for e, j in last_on.items():
                    if j >= 0:
                        marked[j] = True
                last_w, readers = {}, {}
                continue
            _, e, _, rd, wr = o
            d = set()
            for r in rd:
                if r in last_w:
                    d.add(last_w[r])
            for w in wr:
                if w in last_w:
                    d.add(last_w[w])
                for j in readers.get(w, ()):
                    d.add(j)
            d.discard(i)
            deps[i] = d
            for j in d:
                marked[j] = True
            for w in wr:
                last_w[w] = i
                readers[w] = []
            for r in rd:
                if r not in wr:
                    readers.setdefault(r, []).append(i)
            last_on[e] = i
        ordinal = [0] * n
        cnt = {e: 0 for e in self.ENG}
        dslot = [None] * n
        nd = 0
        P = len(self.ring)
        for i, o in enumerate(ops):
            if o[0] == "dma":
                dslot[i] = (nd % P, 16 * (nd // P + 1))
                nd += 1
            elif o[0] == "op" and marked[i]:
                cnt[o[1]] += 1
                ordinal[i] = cnt[o[1]]
        waited = {e: {d: 0 for d in self.ENG} for e in self.ENG}
        wdma = {e: {} for e in self.ENG}

        def wait_for(e, j):
            oj = ops[j]
            if oj[0] == "dma":
                slot, val = dslot[j]
                if wdma[e].get(slot, 0) >= val:
                    return
                wdma[e][slot] = val
                self.eng[e].wait_ge(self.ring[slot], val)
            else:
                dsrc = oj[1]
                if dsrc == e and e == "pe":
                    return
                if waited[e][dsrc] >= ordinal[j]:
                    return
                waited[e][dsrc] = ordinal[j]
                self.eng[e].wait_ge(self.sems[dsrc], ordinal[j])

        nd = 0
        dma_hist = []
        for i, o in enumerate(ops):
            if o[0] == "bar":
                for e in self.ENG:
                    for dsrc, j in deps[i].items():
                        if j >= 0 and not (dsrc == e and ops[j][0] == "op"):
                            wait_for(e, j)
                    for j in dma_hist[-P:]:
                        wait_for(e, j)
                continue
            kind, e, payload, rd, wr = o
            for j in sorted(deps[i]):
                wait_for(e, j)
            if kind == "dma":
                slot, val = dslot[i]
                if val > 16:
                    if wdma[e].get(slot, 0) < val - 16:
                        wdma[e][slot] = val - 16
                        self.eng[e].wait_ge(self.ring[slot], val - 16)
                out, in_ = payload
                self.eng[e].dma_start(out=out, in_=in_).then_inc(self.ring[slot], 16)
                dma_hist.append(i)
                nd += 1
            else:
                ins = payload(self.eng[e])
                if marked[i]:
                    ins.then_inc(self.sems[e], 1)
        for j in dma_hist[-P:]:
            wait_for("sp", j)
        self.ops = []
        return n


def bc(ap, shape):
    return ap.to_broadcast(list(shape))


def _t5_bucket(rel):
    nb = 16
    max_exact = 8
    n = np.abs(rel)
    large = max_exact + (np.log(np.maximum(n, 1) / max_exact) / math.log(1024 / max_exact) * (nb - max_exact)).astype(np.int32)
    large = np.minimum(large, nb - 1)
    return ((rel > 0).astype(np.int32) * nb + np.where(n < max_exact, n, large)).astype(np.int32)


def bf(a):
    return np.ascontiguousarray(a.astype(np.float32)).astype(ml_dtypes.bfloat16)


_CONST_CACHE = {}


def build_consts(is_prompt):
    key = bool(is_prompt)
    if key in _CONST_CACHE:
        return _CONST_CACHE[key]
    c = {}
    L = 16384 if is_prompt else 8192
    tt = np.linspace(0.0, 1.0, L, dtype=np.float32)[:, None]
    w = (np.float32(2.0 * math.pi / L) * np.arange(L, dtype=np.float32))[:, None]
    f = np.linspace(1e-4, 15, 16, dtype=np.float32)[None, :]
    z = np.concatenate([tt, np.cos(f * w), -np.sin(f * w)], axis=-1).astype(np.float32)
    zf = np.zeros((T, 33), np.float32)
    zf[:L] = z
    c["zfT"] = np.ascontiguousarray(zf.T)
    tn = np.zeros(T, np.float32)
    tn[:L] = tt[:, 0]
    c["negtn"] = np.ascontiguousarray(-tn.reshape(128, 128))
    deltas = np.abs(np.linspace(math.log(0.01) / 1.5, math.log(0.01) / 0.3, 768, dtype=np.float32))
    c["deltas_rep"] = np.ascontiguousarray(np.broadcast_to(deltas[None, :], (128, 768))).astype(np.float32)
    slot = np.arange(128) if is_prompt else np.concatenate([np.arange(64), np.arange(64) + 128])
    k2 = np.arange(128)
    n1 = np.arange(128)
    k1 = np.arange(128)
    ang = -2 * np.pi * np.outer(slot, k2 + 0.5) / 256.0
    Er, Ei = np.cos(ang), np.sin(ang)
    c["E_d"] = bf(np.concatenate([-Ei, Er, Ei], axis=1))
    angk = -2 * np.pi * np.outer(np.arange(128), k2 + 0.5) / 256.0
    Ekr, Eki = np.cos(angk), np.sin(angk)
    if not is_prompt:
        Ekr[64:] = 0
        Eki[64:] = 0
    c["E_kS"] = bf(np.concatenate([Ekr, -Eki], axis=1))
    c["E_kD"] = bf(np.concatenate([Eki, Ekr], axis=1))
    angM = -2 * np.pi * (n1[None, :, None] * (k2[:, None, None] + 0.5) / NFFT + n1[None, :, None] * k1[None, None, :] / 128.0)
    c["Mf"] = bf(np.concatenate([np.cos(angM), np.sin(angM)], axis=2).transpose(1, 0, 2))
    angG = 2 * np.pi * np.outer(k1, n1) / 128.0
    c["G1"] = bf(np.concatenate([np.cos(angG), np.sin(angG)], axis=1))
    c["G2"] = bf(np.concatenate([-np.sin(angG), np.cos(angG)], axis=1))
    angI = 2 * np.pi * (n1[:, None, None] + 128 * slot[None, None, :]) * (k2[None, :, None] + 0.5) / NFFT
    sc = 2.0 / NFFT
    c["Minv"] = bf(np.concatenate([sc * np.cos(angI), -sc * np.sin(angI)], axis=2).transpose(1, 0, 2))
    OH = np.zeros((32, 3 * 384), np.float32)
    mrow = np.zeros((12, 3 * 384), np.float32)
    for ri, r in enumerate((1, 4, 16)):
        for j in range(384):
            d = j - 127
            if 0 <= d <= 128:
                rel = (64 - d) * r
                OH[_t5_bucket(np.array(rel)), ri * 384 + j] = 1.0
            else:
                mrow[:, ri * 384 + j] = -30000.0
    c["OH"] = OH
    c["mrow"] = mrow
    bm = np.zeros((128, 256), np.float32)
    if not is_prompt:
        bm[:64, 128:] = -30000.0
        bm[64:, :128] = -30000.0
    c["bmask"] = bf(bm)
    c["bflag"] = np.full((128, 1), 1.0 if is_prompt else 0.0, np.float32)
    ident = np.eye(128, dtype=np.float32)
    c["ident"] = bf(ident)
    c["antiid"] = bf(ident[::-1])
    c["swap"] = bf(np.roll(ident, 64, axis=1))
    _CONST_CACHE[key] = c
    return c


CONST_SHAPES = {
    "zfT": ([33, T], F32), "negtn": ([128, 128], F32), "deltas_rep": ([128, 768], F32),
    "E_d": ([128, 384], BF16), "E_kS": ([128, 256], BF16), "E_kD": ([128, 256], BF16),
    "Mf": ([128, 128, 256], BF16), "G1": ([128, 256], BF16), "G2": ([128, 256], BF16),
    "Minv": ([128, 128, 256], BF16), "OH": ([32, 1152], F32), "mrow": ([12, 1152], F32),
    "bmask": ([128, 256], BF16), "bflag": ([128, 1], F32), "ident": ([128, 128], BF16),
    "antiid": ([128, 128], BF16), "swap": ([128, 128], BF16),
}

INPUT_SHAPES = {
    "x": [T, DM], "cT": [128, 8, 2], "w_ada": [DM, 3 * DM], "b_ada": [1, 3 * DM], "norm_g": [1, DM],
    "w_in": [DM, 8192], "short_wT": [128, 18, 3], "short_bT": [128, 18],
    "filt_w1": [33, 64], "filt_b1": [64, 1], "filt_fr1": [64, 1], "filt_w2": [64, 64], "filt_b2": [64, 1],
    "filt_fr2": [64, 1], "filt_w3": [64, 3072], "hyena_d": [2, 768],
    "w_proj_attn": [768, DM], "w_proj_hyena": [768, DM], "w_out": [DM, DM], "rel_bias": [32, 12],
    "final_g": [1, DM],
}


from contextlib import ExitStack


class K:
    pass


def build_program(phases=("p0", "pk", "p1", "p1b", "pa", "ph", "pf"), debug_outs=()):
    nc = bass.Bass("TRN2", target_bir_lowering=False)
    k = K()
    k.nc = nc
    din = {}
    for name, shp in INPUT_SHAPES.items():
        din[name] = nc.dram_tensor(name, shp, F32, kind="ExternalInput").ap()
    for name, (shp, dt_) in CONST_SHAPES.items():
        din[name] = nc.dram_tensor(name, shp, dt_, kind="ExternalInput").ap()
    k.din = din
    y = nc.dram_tensor("y", [T, DM], F32, kind="ExternalOutput").ap()
    k.y = y

    def scratch(name, shp, dt_):
        kind = "ExternalOutput" if name in debug_outs else "Internal"
        return nc.dram_tensor(name, shp, dt_, kind=kind).ap()

    k.qkT = scratch("qkT", [1536, T], BF16)
    k.gaT = scratch("gaT", [768, T], BF16)
    k.ghT = scratch("ghT", [768, T], BF16)
    k.mT = scratch("mT", [2048, T], BF16)
    k.uraw = scratch("uraw", [2304, T], BF16)
    k.uT = scratch("uT", [2304, T], BF16)
    k.vaug = scratch("vaug", [T, 1536], BF16)
    k.oaT = scratch("oaT", [768, T], BF16)
    k.ohT = scratch("ohT", [768, T], BF16)
    k.KFd = scratch("KFd", [2, NCB, 128, 128 * 3 * CB], BF16)
    k.Avec = scratch("Avec", [12, 1152], BF16)
    k.modrep = scratch("modrep", [2, 3 * DM], F32)

    with ExitStack() as top:
        sems = {e: top.enter_context(nc.semaphore("sem_" + e)) for e in ("pe", "act", "dve", "pool")}
        sems["sp"] = None
        ring = [top.enter_context(nc.semaphore("dr%d" % i)) for i in range(24)]
        S = Sched(nc, sems, ring)
        k.S = S
        pp = [top.enter_context(nc.psum_tensor("psb%d" % i, [128, 1024], F32)) for i in range(4)]
        ps = []
        for i in range(4):
            ps.append(pp[i][:, 0:512])
            ps.append(pp[i][:, 512:1024])
        k.ps = ps
        k.pp = pp
        k.ident = top.enter_context(nc.sbuf_tensor("s_ident", [128, 128], BF16))
        k.fg_rep = top.enter_context(nc.sbuf_tensor("s_fg_rep", [128, DM], F32))
        S.dma(k.ident[:], din["ident"][:, :], writes=["ident"])
        k.epsc = top.enter_context(nc.sbuf_tensor("s_epsc", [128, 2], F32))
        S.op("pool", lambda e: e.memset(k.epsc[:], EPS), writes=["epsc"])
        S.dma(k.fg_rep[:], din["final_g"][0:1, :].partition_broadcast(128), writes=["fg_rep"])

        if "p0" in phases:
            phase0(k)
            S.barrier()
        k.fuse1b = ("p1b" in phases and "pk" in phases)
        if "p1" in phases:
            phase1(k)
            S.barrier()
        if "pk" in phases:
            phaseK(k)
            S.barrier()
        if "p1b" in phases and not k.fuse1b:
            phase1b(k)
            S.barrier()
        if "pa" in phases:
            phaseA(k)
            S.barrier()
        if "ph" in phases:
            phaseH(k)
            S.barrier()
        if "pf" in phases:
            phaseF(k)
            S.barrier()
        S.flush()
    return nc


def phase0(k):
    nc, S, din, ps = k.nc, k.S, k.din, k.ps
    with ExitStack() as st:
        wada = st.enter_context(nc.sbuf_tensor("s_wada", [128, 8, 3 * DM], F32))
        cT = st.enter_context(nc.sbuf_tensor("s_cT", [128, 8, 2], F32))
        scT = st.enter_context(nc.sbuf_tensor("s_scT", [128, 8, 2], F32))
        screp = st.enter_context(nc.sbuf_tensor("s_screp", [128, 2, 8, 128], F32))
        brep = st.enter_context(nc.sbuf_tensor("s_brep", [128, 3 * DM], F32))
        ngrep = st.enter_context(nc.sbuf_tensor("s_ngrep", [128, DM], F32))
        k.modr = st.enter_context(nc.sbuf_tensor("s_modr", [128, 2, 3 * DM], F32))
        S.dma(cT[:], din["cT"][:, :, :], writes=["cT"])
        for kk in range(8):
            S.dma(wada[:, kk, :], din["w_ada"][kk * 128:(kk + 1) * 128, :], writes=[("wada", kk)])
        S.dma(brep[:], din["b_ada"][0:1, :].partition_broadcast(128), writes=["brep"])
        S.dma(ngrep[:], din["norm_g"][0:1, :].partition_broadcast(128), writes=["ngrep"])
        S.op("act", lambda e: e.activation(out=scT[:], in_=cT[:], func=AF.Silu), reads=["cT"], writes=["scT"])
        for s in range(2):
            S.op("dve", lambda e, s=s: e.tensor_copy(out=screp[:, s, :, :], in_=bc(scT[:, :, s:s + 1], [128, 8, 128])),
                 reads=["scT"], writes=[("screp", s)])
        for s in range(2):
            for cc in range(6):
                pb = ps[(s * 6 + cc) % 4]
                pr = ("ps", (s * 6 + cc) % 4)
                for kk in range(8):
                    S.op("pe", lambda e, s=s, cc=cc, kk=kk, pb=pb: e.matmul(
                        pb[:, :], lhsT=screp[:, s, kk, :], rhs=wada[:, kk, cc * 512:(cc + 1) * 512],
                        start=(kk == 0), stop=(kk == 7)),
                        reads=[("screp", s), ("wada", kk)], writes=[pr])
                S.op("dve", lambda e, s=s, cc=cc, pb=pb: e.tensor_tensor(
                    out=k.modr[:, s, cc * 512:(cc + 1) * 512], in0=pb[:, :], in1=brep[:, cc * 512:(cc + 1) * 512], op=ALU.add),
                    reads=[pr, "brep"], writes=[("modr", s)])
            S.op("dve", lambda e, s=s: e.scalar_tensor_tensor(
                out=k.modr[:, s, DM:2 * DM], in0=k.modr[:, s, DM:2 * DM], scalar=1.0, in1=ngrep[:], op0=ALU.add, op1=ALU.mult),
                reads=[("modr", s), "ngrep"], writes=[("modr", s)])
            S.dma(k.modrep[s:s + 1, :], k.modr[0:1, s, :], reads=[("modr", s)])
        S.barrier()


def phase1(k):
    nc, S, din, ps = k.nc, k.S, k.din, k.ps
    with ExitStack() as st:
        winb = st.enter_context(nc.sbuf_tensor("s_winb", [128, 8, 8192], BF16))
        stg_ctx = ExitStack()
        stg = [stg_ctx.enter_context(nc.sbuf_tensor("s_wstg%d" % i, [128, 2048], F32)) for i in range(2)]
        n = 0
        for kk in range(8):
            for cc in range(4):
                b = n % 2
                S.dma(stg[b][:], din["w_in"][kk * 128:(kk + 1) * 128, cc * 2048:(cc + 1) * 2048], writes=[("wstg", b)])
                eng = ("act", "dve", "pool")[n % 3]
                if eng == "act":
                    S.op("act", lambda e, b=b, kk=kk, cc=cc: e.copy(out=winb[:, kk, cc * 2048:(cc + 1) * 2048], in_=stg[b][:]),
                         reads=[("wstg", b)], writes=[("winb", kk)])
                else:
                    S.op(eng, lambda e, b=b, kk=kk, cc=cc: e.tensor_copy(out=winb[:, kk, cc * 2048:(cc + 1) * 2048], in_=stg[b][:]),
                         reads=[("wstg", b)], writes=[("winb", kk)])
                n += 1
        S.barrier()
        stg_ctx.close()
        modr1 = st.enter_context(nc.sbuf_tensor("s_modr1", [128, 2, 2 * DM], F32))
        for s_ in range(2):
            S.dma(modr1[:, s_, :], k.modrep[s_:s_ + 1, 0:2 * DM].partition_broadcast(128), writes=[("modr", s_)])
        xt = [st.enter_context(nc.sbuf_tensor("s_xt%d" % i, [128, DM], F32)) for i in range(2)]
        xm = [st.enter_context(nc.sbuf_tensor("s_xm%d" % i, [128, DM], F32)) for i in range(1)]
        hb = [st.enter_context(nc.sbuf_tensor("s_hb%d" % i, [128, DM], BF16)) for i in range(4)]
        sq = st.enter_context(nc.sbuf_tensor("s_sqj", [128, DM], BF16))
        ss = [st.enter_context(nc.sbuf_tensor("s_ss%d" % i, [128, 2], F32)) for i in range(3)]
        hT = [st.enter_context(nc.sbuf_tensor("s_hT%d" % i, [128, 8, 512], BF16)) for i in range(2)]
        ev = [st.enter_context(nc.sbuf_tensor("s_ev%d" % i, [128, 512], BF16)) for i in range(6)]
        vst = [st.enter_context(nc.sbuf_tensor("s_vst%d" % i, [128, 12, 128], BF16)) for i in range(2)]
        for i in range(2):
            S.op("pool", lambda e, i=i: e.memset(vst[i][:], 1.0), writes=[("vst", i)])

        blocks = []
        for j in range(6):
            blocks.append((OQ + j * 128, k.qkT, j * 128, "q"))
        for j in range(6):
            blocks.append((OK_ + j * 128, k.qkT, 768 + j * 128, "copy"))
        for j in range(18):
            blocks.append((OU + j * 128, k.uraw, j * 128, "copy"))
        for j in range(6):
            blocks.append((OGA + j * 128, k.gaT, j * 128, "silu"))
        for j in range(6):
            blocks.append((OGH + j * 128, k.ghT, j * 128, "silu"))
        for j in range(8):
            blocks.append((OMA + j * 128, k.mT, j * 128, "sig"))
        for j in range(8):
            blocks.append((OMH + j * 128, k.mT, 1024 + j * 128, "sig"))

        tcount = 0
        evn = 0

        def prep(ci):
            seg = 0 if ci < NCHUNK // 2 else 1
            hTc = hT[ci % 2]
            hres = ("hT", ci % 2)
            for tt in range(4):
                t = ci * 4 + tt
                xb, xr = xt[t % 2], ("xt", t % 2)
                sb, sr = ss[t % 3], ("ss", t % 3)
                mb, mr = xm[0], ("xm", 0)
                hbb, hbr = hb[t % 4], ("hb", t % 4)
                S.dma(xb[:], din["x"][t * 128:(t + 1) * 128, :], writes=[xr])
                S.op("act", lambda e, xb=xb, sb=sb: e.activation(out=sq[:], in_=xb[:], func=AF.Square, scale=1.0 / 32.0,
                                                                 accum_out=sb[:, 0:1]),
                     reads=[xr], writes=["sqj", sr])
                S.op("act", lambda e, sb=sb: e.activation(out=sb[:, 1:2], in_=sb[:, 0:1], func=AF.Sqrt, bias=k.epsc[:, 0:1]),
                     reads=[sr], writes=[sr])
                S.op("dve", lambda e, sb=sb: e.reciprocal(out=sb[:, 1:2], in_=sb[:, 1:2]), reads=[sr], writes=[sr])
                S.op("dve", lambda e, xb=xb, sb=sb, mb=mb, seg=seg: e.scalar_tensor_tensor(
                    out=mb[:], in0=xb[:], scalar=sb[:, 1:2], in1=modr1[:, seg, DM:2 * DM], op0=ALU.mult, op1=ALU.mult),
                    reads=[xr, sr, ("modr", seg)], writes=[mr])
                S.op("dve", lambda e, mb=mb, hbb=hbb, seg=seg: e.tensor_tensor(
                    out=hbb[:], in0=mb[:], in1=modr1[:, seg, 0:DM], op=ALU.add),
                    reads=[mr, ("modr", seg)], writes=[hbr])

        def prepB(ci):
            hTc = hT[ci % 2]
            hres = ("hT", ci % 2)
            for tt in range(4):
                t = ci * 4 + tt
                hbb, hbr = hb[t % 4], ("hb", t % 4)
                pbank = ps[6 + (t % 2)]
                pres = ("ps", 6 + (t % 2))
                pT = pbank[:].bitcast(BF16)
                for kk in range(8):
                    S.op("pe", lambda e, kk=kk, pT=pT, hbb=hbb: e.transpose(
                        out=pT[:, kk * 128:(kk + 1) * 128], in_=hbb[:, kk * 128:(kk + 1) * 128], identity=k.ident[:]),
                        reads=[hbr, "ident"], writes=[pres])
                S.op("act", lambda e, pT=pT, hTc=hTc, tt=tt: e.copy(
                    out=hTc[:, :, tt * 128:(tt + 1) * 128], in_=pT.rearrange("p (k t) -> p k t", k=8)),
                    reads=[pres], writes=[hres])
        def run_blocks(ci):
            nonlocal evn
            hTc = hT[ci % 2]
            hres = ("hT", ci % 2)
            for bi, (wc, dst, drow, kind) in enumerate(blocks):
                pb = ps[bi % 4]
                pr = ("ps", bi % 4)
                for kk in range(8):
                    S.op("pe", lambda e, kk=kk, pb=pb, wc=wc, hTc=hTc: e.matmul(
                        pb[:, :], lhsT=winb[:, kk, wc:wc + 128], rhs=hTc[:, kk, :], start=(kk == 0), stop=(kk == 7)),
                        reads=[hres, ("winb", kk)], writes=[pr])
                eb, er = ev[evn % 6], ("ev", evn % 6)
                evn += 1
                if kind == "q":
                    S.op("act", lambda e, pb=pb, eb=eb: e.activation(out=eb[:], in_=pb[:, :], func=AF.Copy, scale=0.125),
                         reads=[pr], writes=[er])
                elif kind == "copy":
                    S.op("dve", lambda e, pb=pb, eb=eb: e.tensor_copy(out=eb[:], in_=pb[:, :]), reads=[pr], writes=[er])
                elif kind == "silu":
                    S.op("act", lambda e, pb=pb, eb=eb: e.activation(out=eb[:], in_=pb[:, :], func=AF.Silu), reads=[pr], writes=[er])
                else:
                    S.op("act", lambda e, pb=pb, eb=eb: e.activation(out=eb[:], in_=pb[:, :], func=AF.Sigmoid), reads=[pr], writes=[er])
                S.dma(dst[drow:drow + 128, ci * 512:(ci + 1) * 512], eb[:], reads=[er], eng="pool")
            for tt in range(4):
                t = ci * 4 + tt
                pa, pb2 = ps[4], ps[5]
                for kk in range(8):
                    S.op("pe", lambda e, kk=kk, tt=tt, hTc=hTc: e.matmul(
                        ps[4][:, :], lhsT=hTc[:, kk, tt * 128:(tt + 1) * 128], rhs=winb[:, kk, OV:OV + 512],
                        start=(kk == 0), stop=(kk == 7)), reads=[hres, ("winb", kk)], writes=[("ps", 4)])
                for kk in range(8):
                    S.op("pe", lambda e, kk=kk, tt=tt, hTc=hTc: e.matmul(
                        ps[5][:, 0:256], lhsT=hTc[:, kk, tt * 128:(tt + 1) * 128], rhs=winb[:, kk, OV + 512:OV + 768],
                        start=(kk == 0), stop=(kk == 7)), reads=[hres, ("winb", kk)], writes=[("ps", 5)])
                vb, vr = vst[t % 2], ("vst", t % 2)
                def vdst(vb, p0, npair):
                    base = vb[:, 2 * p0:2 * p0 + 1, 0:1]
                    return AP(base.tensor, base.offset, [list(base.ap[0]), [256, npair], [192, 2], [1, 64]])
                S.op("dve", lambda e, vb=vb, vdst=vdst: e.tensor_copy(
                    out=vdst(vb, 0, 4), in_=ps[4][:, :].rearrange("p (a b c) -> p a b c", a=4, b=2)),
                    reads=[("ps", 4)], writes=[vr])
                S.op("dve", lambda e, vb=vb, vdst=vdst: e.tensor_copy(
                    out=vdst(vb, 4, 2), in_=ps[5][:, 0:256].rearrange("p (a b c) -> p a b c", a=2, b=2)),
                    reads=[("ps", 5)], writes=[vr])
                S.dma(k.vaug[t * 128:(t + 1) * 128, :], vb[:].rearrange("p a b -> p (a b)"), reads=[vr], eng="pool")

        prep(0)
        prepB(0)
        for ci in range(NCHUNK):
            if ci + 1 < NCHUNK:
                prep(ci + 1)
            run_blocks(ci)
            if ci + 1 < NCHUNK:
                prepB(ci + 1)


def core_assignment():
    return [("p", 0), ("p", 1), ("s", 0, 1), ("s", 2, 3), ("s", 4, 5), ("s", 6, 7), ("s", 6, 7), ("s", 6, 7)]


def prep_core_inputs(inp, role):
    f32 = lambda a: np.ascontiguousarray(np.asarray(a, dtype=np.float32))
    m = {}
    if role[0] == "p":
        b = role[1]
        m["x"] = f32(inp["x_prompt"][b])
        c2 = np.stack([inp["c_prompt"][b], inp["c_prompt"][b]], 0)
    else:
        m["x"] = f32(np.concatenate([inp["x_sample"][role[1]], inp["x_sample"][role[2]]], 0))
        c2 = np.stack([inp["c_sample"][role[1]], inp["c_sample"][role[2]]], 0)
    c2 = np.asarray(c2, np.float32)
    m["cT"] = f32(c2.reshape(2, 8, 128).transpose(2, 1, 0))
    m["w_ada"] = f32(inp["w_ada"][0])
    m["b_ada"] = f32(inp["b_ada"][0][None])
    m["norm_g"] = f32(inp["norm_g"][0][None])
    m["w_in"] = f32(inp["w_in"][0])
    m["short_wT"] = f32(np.asarray(inp["short_w"][0]).reshape(3, 18, 128).transpose(2, 1, 0))
    m["short_bT"] = f32(np.asarray(inp["short_b"][0]).reshape(18, 128).T)
    m["filt_w1"] = f32(inp["filt_w1"][0])
    m["filt_b1"] = f32(np.asarray(inp["filt_b1"][0])[:, None])
    m["filt_fr1"] = f32(np.asarray(inp["filt_freq1"][0])[:, None])
    m["filt_w2"] = f32(inp["filt_w2"][0])
    m["filt_b2"] = f32(np.asarray(inp["filt_b2"][0])[:, None])
    m["filt_fr2"] = f32(np.asarray(inp["filt_freq2"][0])[:, None])
    m["filt_w3"] = f32(inp["filt_w3"][0])
    m["hyena_d"] = f32(inp["hyena_d"][0])
    m["w_proj_attn"] = f32(inp["w_proj_attn"][0])
    m["w_proj_hyena"] = f32(inp["w_proj_hyena"][0])
    m["w_out"] = f32(inp["w_out"][0])
    m["rel_bias"] = f32(inp["rel_bias"])
    m["final_g"] = f32(np.asarray(inp["final_g"])[None])
    m.update(build_consts(role[0] == "p"))
    return m


_NC_CACHE = {}


def kernel(**inputs):
    inp = {k_: np.asarray(v) for k_, v in inputs.items()}
    if "full" not in _NC_CACHE:
        _NC_CACHE["full"] = build_program()
    nc = _NC_CACHE["full"]
    roles = core_assignment()
    in_maps = [prep_core_inputs(inp, r) for r in roles]
    res = run_bass_kernel_spmd(nc, in_maps, core_ids=list(range(8)))
    outs = [np.asarray(r["y"], dtype=np.float32) for r in res.results]
    y_prompt = np.stack([outs[0], outs[1]], 0)
    ys = []
    for c in range(2, 6):
        ys.append(outs[c][:8192])
        ys.append(outs[c][8192:])
    y_sample = np.stack(ys, 0)
    return (y_prompt, y_sample)


def phase1b_gen(k, st, W=512):
    nc, S, din = k.nc, k.S, k.din
    swT = st.enter_context(nc.sbuf_tensor("s_swT", [128, 18, 3], F32))
    sbT = st.enter_context(nc.sbuf_tensor("s_sbT", [128, 18], F32))
    bfl = st.enter_context(nc.sbuf_tensor("s_bfl", [128, 1], F32))
    S.dma(swT[:], din["short_wT"][:, :, :], writes=["swT"])
    S.dma(sbT[:], din["short_bT"][:, :], writes=["sbT"])
    S.dma(bfl[:], din["bflag"][:, :], writes=["bfl"])
    ib = [st.enter_context(nc.sbuf_tensor("s_cin%d" % i, [128, W + 2], BF16)) for i in range(3)]
    t1 = [st.enter_context(nc.sbuf_tensor("s_ct%d" % i, [128, W], F32)) for i in range(2)]
    ob = [st.enter_context(nc.sbuf_tensor("s_cout%d" % i, [128, W], BF16)) for i in range(3)]
    n = 0
    for ub in range(18):
        for tc in range(T // W):
            a, ar = ib[n % 3], ("cin", n % 3)
            tb, tr = t1[n % 2], ("ct", n % 2)
            o, orr = ob[n % 3], ("cout", n % 3)
            lo = tc * W - 1
            hi = tc * W + W + 1
            c0 = 0
            if lo < 0:
                S.op("pool", lambda e, a=a: e.memset(a[:, 0:1], 0.0), writes=[ar])
                lo, c0 = 0, 1
            c1 = W + 2
            if hi > T:
                S.op("pool", lambda e, a=a: e.memset(a[:, W + 1:W + 2], 0.0), writes=[ar])
                hi, c1 = T, W + 1
            S.dma(a[:, c0:c1], k.uraw[ub * 128:(ub + 1) * 128, lo:hi], writes=[ar])
            if tc * W == T // 2:
                S.op("pool", lambda e, a=a: e.tensor_scalar(out=a[:, 0:1], in0=a[:, 0:1], scalar1=bfl[:, 0:1], scalar2=None,
                                                            op0=ALU.mult), reads=[ar, "bfl"], writes=[ar])
            if tc * W + W == T // 2:
                S.op("pool", lambda e, a=a: e.tensor_scalar(out=a[:, W + 1:W + 2], in0=a[:, W + 1:W + 2], scalar1=bfl[:, 0:1],
                                                            scalar2=None, op0=ALU.mult), reads=[ar, "bfl"], writes=[ar])
            S.op("act", lambda e, a=a, tb=tb, ub=ub: e.activation(
                out=tb[:], in_=a[:, 1:W + 1], func=AF.Identity, scale=swT[:, ub, 1:2], bias=sbT[:, ub:ub + 1]),
                reads=[ar, "swT", "sbT"], writes=[tr])
            S.op("dve", lambda e, a=a, tb=tb, ub=ub: e.scalar_tensor_tensor(
                out=tb[:], in0=a[:, 0:W], scalar=swT[:, ub, 0:1], in1=tb[:], op0=ALU.mult, op1=ALU.add),
                reads=[ar, tr, "swT"], writes=[tr])
            S.op("dve", lambda e, a=a, tb=tb, o=o, ub=ub: e.scalar_tensor_tensor(
                out=o[:], in0=a[:, 2:W + 2], scalar=swT[:, ub, 2:3], in1=tb[:], op0=ALU.mult, op1=ALU.add),
                reads=[ar, tr, "swT"], writes=[orr])
            S.dma(k.uT[ub * 128:(ub + 1) * 128, tc * W:(tc + 1) * W], o[:], reads=[orr], eng="pool")
            n += 1
            yield


def phase1b(k):
    with ExitStack() as st:
        for _ in phase1b_gen(k, st, W=2048):
            pass


def sin_wrapped(S, src_ps, pres, dst, dres, scale_ap, bias_ap, tmp, tres, tmp2, t2res, nparts, ncols):
    PI = math.pi
    S.op("dve", lambda e: e.tensor_scalar(out=tmp[0:nparts, 0:ncols], in0=src_ps, scalar1=scale_ap, scalar2=bias_ap,
                                          op0=ALU.mult, op1=ALU.add), reads=[pres], writes=[tres])
    S.op("dve", lambda e: e.tensor_scalar(out=tmp2[0:nparts, 0:ncols], in0=tmp[0:nparts, 0:ncols], scalar1=PI, scalar2=-2 * PI,
                                          op0=ALU.is_gt, op1=ALU.mult), reads=[tres], writes=[t2res])
    S.op("dve", lambda e: e.tensor_tensor(out=tmp2[0:nparts, 0:ncols], in0=tmp2[0:nparts, 0:ncols], in1=tmp[0:nparts, 0:ncols],
                                          op=ALU.add), reads=[tres, t2res], writes=[t2res])
    S.op("dve", lambda e: e.tensor_scalar(out=tmp[0:nparts, 0:ncols], in0=tmp[0:nparts, 0:ncols], scalar1=-PI, scalar2=2 * PI,
                                          op0=ALU.is_lt, op1=ALU.mult), reads=[tres], writes=[tres])
    S.op("dve", lambda e: e.tensor_tensor(out=tmp[0:nparts, 0:ncols], in0=tmp2[0:nparts, 0:ncols], in1=tmp[0:nparts, 0:ncols],
                                          op=ALU.add), reads=[tres, t2res], writes=[tres])
    S.op("act", lambda e: e.activation(out=dst, in_=tmp[0:nparts, 0:ncols], func=AF.Sin), reads=[tres], writes=[dres])


def phaseK(k):
    nc, S, din, ps = k.nc, k.S, k.din, k.ps
    with ExitStack() as st:
        hdn = st.enter_context(nc.sbuf_tensor("s_hdn2T", [128, T], BF16))
        w3sd = st.enter_context(nc.sbuf_tensor("s_w3sd", [128, 2, NCB, 2, CB], BF16))
        S.op("pool", lambda e: e.memset(hdn[64:128, :], 0.0), writes=["hdn"])
        S.op("pool", lambda e: e.memset(w3sd[64:128], 0.0), writes=["w3sd"])
        with ExitStack() as s2:
            w1 = s2.enter_context(nc.sbuf_tensor("s_fw1", [33, 64], F32))
            w2 = s2.enter_context(nc.sbuf_tensor("s_fw2", [64, 64], F32))
            w3 = s2.enter_context(nc.sbuf_tensor("s_fw3", [64, 3072], F32))
            fv = s2.enter_context(nc.sbuf_tensor("s_fv", [64, 6], F32))
            zc = [s2.enter_context(nc.sbuf_tensor("s_zc%d" % i, [33, 512], F32)) for i in range(2)]
            ta = s2.enter_context(nc.sbuf_tensor("s_fta", [64, 512], F32))
            tb = s2.enter_context(nc.sbuf_tensor("s_ftb", [64, 512], F32))
            h1 = s2.enter_context(nc.sbuf_tensor("s_fh1", [64, 512], F32))
            S.dma(w1[:], din["filt_w1"][:, :], writes=["fw1"])
            S.dma(w2[:], din["filt_w2"][:, :], writes=["fw2"])
            S.dma(w3[:], din["filt_w3"][:, :], writes=["fw3"])
            for i, nm in enumerate(("filt_b1", "filt_fr1", "filt_b2", "filt_fr2")):
                S.dma(fv[:, i:i + 1], din[nm][:, :], writes=["fv"])
            S.op("dve", lambda e: e.tensor_tensor(out=fv[:, 4:5], in0=fv[:, 0:1], in1=fv[:, 1:2], op=ALU.mult), reads=["fv"], writes=["fv"])
            S.op("dve", lambda e: e.tensor_tensor(out=fv[:, 5:6], in0=fv[:, 2:3], in1=fv[:, 3:4], op=ALU.mult), reads=["fv"], writes=["fv"])
            w3v = w3[:].rearrange("p (o d b c) -> p o d b c", o=2, d=2, b=NCB)
            for o in range(2):
                S.op("dve", lambda e, o=o: e.tensor_tensor(out=w3sd[0:64, o, :, 0, :], in0=w3v[:, o, 0], in1=w3v[:, o, 1], op=ALU.add),
                     reads=["fw3"], writes=["w3sd"])
                S.op("dve", lambda e, o=o: e.tensor_tensor(out=w3sd[0:64, o, :, 1, :], in0=w3v[:, o, 0], in1=w3v[:, o, 1], op=ALU.subtract),
                     reads=["fw3"], writes=["w3sd"])
            for ci in range(T // 512):
                z, zr = zc[ci % 2], ("zc", ci % 2)
                S.dma(z[:], din["zfT"][:, ci * 512:(ci + 1) * 512], writes=[zr])
                S.op("pe", lambda e, z=z: e.matmul(ps[0][0:64, :], lhsT=w1[:], rhs=z[:], start=True, stop=True),
                     reads=[zr, "fw1"], writes=[("ps", 0)])
                sin_wrapped(S, ps[0][0:64, :], ("ps", 0), h1[:], "fh1", fv[:, 1:2], fv[:, 4:5], ta, "fta", tb, "ftb", 64, 512)
                S.op("pe", lambda e: e.matmul(ps[1][0:64, :], lhsT=w2[:], rhs=h1[:], start=True, stop=True),
                     reads=["fh1", "fw2"], writes=[("ps", 1)])
                sin_wrapped(S, ps[1][0:64, :], ("ps", 1), hdn[0:64, ci * 512:(ci + 1) * 512], "hdn", fv[:, 3:4], fv[:, 5:6],
                            ta, "fta", tb, "ftb", 64, 512)
            S.barrier()
        H = st.enter_context(nc.sbuf_tensor("s_H", [128, 128, 2, CB], BF16))
        Yk = st.enter_context(nc.sbuf_tensor("s_Yk", [128, 4, 128, CB], BF16))
        decs = [st.enter_context(nc.sbuf_tensor("s_dec%d" % i, [128, 128, CB], BF16)) for i in range(2)]
        negtn = st.enter_context(nc.sbuf_tensor("s_negtn", [128, 128], F32))
        drep = st.enter_context(nc.sbuf_tensor("s_drep", [128, 768], F32))
        EkS = st.enter_context(nc.sbuf_tensor("s_EkS", [128, 256], BF16))
        EkD = st.enter_context(nc.sbuf_tensor("s_EkD", [128, 256], BF16))
        Mfb = [st.enter_context(nc.sbuf_tensor("s_Mfb%d" % i, [128, 8, 256], BF16)) for i in range(3)]
        KFs = [st.enter_context(nc.sbuf_tensor("s_KFs%d" % i, [128, 4, 3, CB], BF16)) for i in range(3)]
        S.dma(negtn[:], din["negtn"][:, :], writes=["negtn"])
        S.dma(drep[:], din["deltas_rep"][:, :], writes=["drep"])
        S.dma(EkS[:], din["E_kS"][:, :], writes=["EkS"])
        S.dma(EkD[:], din["E_kD"][:, :], writes=["EkD"])
        mfn = 0
        kfn = 0
        pn = 0
        cgen = phase1b_gen(k, st, W=512) if k.fuse1b else None
        cstep = [0]

        def cstepf():
            cstep[0] += 1
            if cgen is not None and cstep[0] % 4 == 0:
                next(cgen, None)

        def ykv(k2, off):
            b0 = off // 128
            b = Yk[:, b0:b0 + 1, k2:k2 + 1, 0:1]
            return AP(b.tensor, b.offset, [list(b.ap[0]), [2 * 128 * CB, 2], [1, CB]])

        def emit_dec(cbx, n1s):
            dd = decs[cbx % 2]
            for n1 in n1s:
                S.op("act", lambda e, n1=n1, dd=dd, cbx=cbx: e.activation(out=dd[:, n1, :], in_=drep[:, cbx * CB:(cbx + 1) * CB], func=AF.Exp,
                                                                        scale=negtn[:, n1:n1 + 1]),
                     reads=["drep", "negtn"], writes=[("dec", cbx % 2)])

        emit_dec(0, range(128))
        for cb in range(NCB):
            c0 = cb * CB
            dec = decs[cb % 2]
            for o in range(2):
                for g in range(32):
                    pb, pr = ps[pn % 4], ("ps", pn % 4)
                    pn += 1
                    for j in range(4):
                        n1 = 4 * g + j
                        S.op("pe", lambda e, pb=pb, j=j, n1=n1, o=o, cb=cb: e.matmul(
                            pb[:, j * 128:(j + 1) * 128], lhsT=hdn[:, n1:T:128],
                            rhs=w3sd[:, o, cb].rearrange("p a b -> p (a b)"), start=True, stop=True),
                            reads=["hdn", "w3sd"], writes=[pr])
                    hout = H[:, 4 * g:4 * g + 4, :, :]
                    dv = dec[:, 4 * g:4 * g + 1, 0:1]
                    din1 = AP(dv.tensor, dv.offset, [list(dv.ap[0]), [CB, 4], [0, 2], [1, CB]])
                    S.op("dve", lambda e, pb=pb, hout=hout, din1=din1: e.tensor_tensor(
                        out=hout, in0=pb[:, :].rearrange("p (j s c) -> p j s c", j=4, s=2), in1=din1, op=ALU.mult),
                        reads=[pr, ("dec", cb % 2)], writes=["H"])
                    cstepf()
                for cp in range(CB // 2):
                    pi = pn % 2
                    pn += 1
                    prl = [("ps", 2 * pi), ("ps", 2 * pi + 1)]
                    for h in range(2):
                        c = 2 * cp + h
                        pb = ps[2 * pi + h]
                        S.op("pe", lambda e, pb=pb, c=c: e.matmul(pb[:, 0:256], lhsT=H[:, :, 0, c], rhs=EkS[:], start=True, stop=True),
                             reads=["H", "EkS"], writes=prl)
                        S.op("pe", lambda e, pb=pb, c=c: e.matmul(pb[:, 256:512], lhsT=H[:, :, 1, c], rhs=EkD[:], start=True, stop=True),
                             reads=["H", "EkD"], writes=prl)
                    yv = Yk[:, 0:1, 0:1, 2 * cp:2 * cp + 1]
                    yout = AP(yv.tensor, yv.offset, [list(yv.ap[0]), [CB, 512], [1, 2]])
                    pin = k.pp[pi][:, :].rearrange("p (h x) -> p x h", h=2)
                    if cp % 2 == 0:
                        S.op("act", lambda e, yout=yout, pin=pin: e.copy(out=yout, in_=pin), reads=prl, writes=["Yk"])
                    else:
                        S.op("dve", lambda e, yout=yout, pin=pin: e.tensor_copy(out=yout, in_=pin), reads=prl, writes=["Yk"])
                    if o == 1 and cb + 1 < NCB:
                        emit_dec(cb + 1, range(4 * cp, 4 * cp + 4))
                    cstepf()
                for g in range(32):
                    pb, pr = ps[4 + pn % 4], ("ps", 4 + pn % 4)
                    pn += 1
                    for j in range(4):
                        k2 = 4 * g + j
                        if k2 % 8 == 0:
                            mb_, mr_ = Mfb[mfn % 3], ("Mfb", mfn % 3)
                            mfn += 1
                            S.dma(mb_[:], din["Mf"][:, k2:k2 + 8, :], writes=[mr_])
                        S.op("pe", lambda e, pb=pb, j=j, k2=k2, mb_=mb_: e.matmul(
                            pb[:, j * 128:(j + 1) * 128], lhsT=mb_[:, k2 % 8, 0:128],
                            rhs=ykv(k2, 0), start=True, stop=False),
                            reads=["Yk", mr_], writes=[pr])
                        S.op("pe", lambda e, pb=pb, j=j, k2=k2, mb_=mb_: e.matmul(
                            pb[:, j * 128:(j + 1) * 128], lhsT=mb_[:, k2 % 8, 128:256],
                            rhs=ykv(k2, 128), start=False, stop=True),
                            reads=["Yk", mr_], writes=[pr])
                    kb, kr = KFs[kfn % 3], ("KFs", kfn % 3)
                    kfn += 1
                    pv = pb[:, :].rearrange("p (j s c) -> p j s c", j=4, s=2)
                    S.op("act", lambda e, kb=kb, pv=pv: e.copy(out=kb[:, :, 0:2, :], in_=pv), reads=[pr], writes=[kr])
                    S.op("dve", lambda e, kb=kb, pv=pv: e.tensor_scalar(out=kb[:, :, 2, :], in0=pv[:, :, 1, :], scalar1=-1.0, scalar2=None,
                                                                        op0=ALU.mult), reads=[pr], writes=[kr])
                    S.dma(k.KFd[o, cb, :, g * 4 * 3 * CB:(g + 1) * 4 * 3 * CB], kb[:].rearrange("p a b c -> p (a b c)"), reads=[kr], eng="pool")
                    cstepf()
        if cgen is not None:
            for _ in cgen:
                pass


def phaseH(k):
    nc, S, din, ps = k.nc, k.S, k.din, k.ps
    with ExitStack() as st:
        bufA = st.enter_context(nc.sbuf_tensor("s_hA", [128, CB, 128], BF16))
        bufB = st.enter_context(nc.sbuf_tensor("s_hB", [128, CB, 128], BF16))
        bufC = st.enter_context(nc.sbuf_tensor("s_hC", [128, 128, CB], BF16))
        Yd = st.enter_context(nc.sbuf_tensor("s_Yd", [128, 3, 128, CB], BF16))
        Pb = st.enter_context(nc.sbuf_tensor("s_P", [128, 128, 2, CB], BF16))
        Zs = st.enter_context(nc.sbuf_tensor("s_Zs", [128, 2, 128, CB], BF16))
        Ed = st.enter_context(nc.sbuf_tensor("s_Ed", [128, 384], BF16))
        G1 = st.enter_context(nc.sbuf_tensor("s_G1", [128, 256], BF16))
        G2 = st.enter_context(nc.sbuf_tensor("s_G2", [128, 256], BF16))
        drep = st.enter_context(nc.sbuf_tensor("s_hdrep", [128, 2, CB], F32))
        Mb = [st.enter_context(nc.sbuf_tensor("s_Mb%d" % i, [128, 8, 256], BF16)) for i in range(4)]
        KFb = [st.enter_context(nc.sbuf_tensor("s_KFb%d" % i, [128, 4, 3, CB], BF16)) for i in range(6)]
        t1 = [st.enter_context(nc.sbuf_tensor("s_ht1_%d" % i, [128, 4, 2, CB], BF16)) for i in range(2)]
        t2 = [st.enter_context(nc.sbuf_tensor("s_ht2_%d" % i, [128, 4, 2, CB], BF16)) for i in range(2)]
        te = [st.enter_context(nc.sbuf_tensor("s_hte%d" % i, [128, 8, CB], F32)) for i in range(2)]
        S.dma(Ed[:], din["E_d"][:, :], writes=["Ed"])
        S.dma(G1[:], din["G1"][:, :], writes=["G1"])
        S.dma(G2[:], din["G2"][:, :], writes=["G2"])
        ohs = AP(Pb[:].tensor, Pb[:].offset, [[Pb[:].ap[0][0], 64], [1, T]])
        mn = 0
        kn = 0
        tn_ = 0
        pn = 0

        def ydv(k2, off):
            b0 = off // 128
            return Yd[:, b0:b0 + 2, k2, :]

        def load_blk(buf, res, row0):
            src = AP(k.uT.tensor, k.uT[row0:row0 + 1, 0:1].offset, [[128, 128], [T, CB], [1, 128]])
            S.dma(buf[:], src, writes=[res])

        for cb in range(NCB):
            c0 = cb * CB
            load_blk(bufA, "hA", c0)
            load_blk(bufB, "hB", 768 + c0)
            for o in range(2):
                S.dma(drep[:, o, :], din["hyena_d"][o:o + 1, c0:c0 + CB].partition_broadcast(128), writes=["hdrep"])
            for o in range(2):
                Din, dres = (bufA, "hA") if o == 0 else (bufC, "hC")
                for cp in range(CB // 2):
                    pi = pn % 2
                    pn += 1
                    prl = [("ps", 2 * pi), ("ps", 2 * pi + 1)]
                    for h in range(2):
                        c = 2 * cp + h
                        pb = ps[2 * pi + h]
                        S.op("pe", lambda e, pb=pb, c=c, Din=Din, o=o: e.matmul(pb[:, 0:384], lhsT=(Din[:, c, :] if o == 0 else Din[:, :, c]),
                                                                             rhs=Ed[:], start=True, stop=True),
                             reads=[dres, "Ed"], writes=prl)
                    yv = Yd[:, 0:1, 0:1, 2 * cp:2 * cp + 1]
                    yout = AP(yv.tensor, yv.offset, [list(yv.ap[0]), [CB, 384], [1, 2]])
                    pin = k.pp[pi][:, :].rearrange("p (h x) -> p x h", h=2)[:, 0:384, :]
                    if cp % 2 == 0:
                        S.op("act", lambda e, yout=yout, pin=pin: e.copy(out=yout, in_=pin), reads=prl, writes=["Yd"])
                    else:
                        S.op("dve", lambda e, yout=yout, pin=pin: e.tensor_copy(out=yout, in_=pin), reads=prl, writes=["Yd"])
                for g in range(32):
                    pb, pr = ps[4 + pn % 4], ("ps", 4 + pn % 4)
                    pn += 1
                    kb, kr = KFb[kn % 6], ("KFb", kn % 6)
                    kn += 1
                    S.dma(kb[:].rearrange("p a b c -> p (a b c)"), k.KFd[o, cb, :, g * 12 * CB:(g + 1) * 12 * CB], writes=[kr])
                    for j in range(4):
                        k2 = 4 * g + j
                        if k2 % 8 == 0:
                            mb_, mr_ = Mb[mn % 4], ("Mb", mn % 4)
                            mn += 1
                            S.dma(mb_[:], din["Mf"][:, k2:k2 + 8, :], writes=[mr_])
                        S.op("pe", lambda e, pb=pb, j=j, k2=k2, mb_=mb_: e.matmul(
                            pb[:, j * 128:(j + 1) * 128], lhsT=mb_[:, k2 % 8, 0:128],
                            rhs=ydv(k2, 128), start=True, stop=False),
                            reads=["Yd", mr_], writes=[pr])
                        S.op("pe", lambda e, pb=pb, j=j, k2=k2, mb_=mb_: e.matmul(
                            pb[:, j * 128:(j + 1) * 128], lhsT=mb_[:, k2 % 8, 128:256],
                            rhs=ydv(k2, 0), start=False, stop=True),
                            reads=["Yd", mr_], writes=[pr])
                    a1, a1r = t1[tn_ % 2], ("ht1", tn_ % 2)
                    a2, a2r = t2[tn_ % 2], ("ht2", tn_ % 2)
                    tn_ += 1
                    pv = pb[:, :].rearrange("p (j s c) -> p j s c", j=4, s=2)
                    S.op("dve", lambda e, a1=a1, pv=pv, kb=kb: e.tensor_tensor(
                        out=a1[:], in0=pv, in1=bc(kb[:, :, 0:1, :], [128, 4, 2, CB]), op=ALU.mult), reads=[pr, kr], writes=[a1r])
                    S.op("dve", lambda e, a2=a2, pv=pv, kb=kb: e.tensor_tensor(
                        out=a2[:], in0=pv, in1=kb[:, :, 1:3, :], op=ALU.mult), reads=[pr, kr], writes=[a2r])
                    S.op("pool", lambda e, a1=a1, a2=a2, g=g: e.tensor_tensor(
                        out=Pb[:, 4 * g:4 * g + 4, 0, :], in0=a1[:, :, 0, :], in1=a2[:, :, 1, :], op=ALU.add),
                        reads=[a1r, a2r], writes=["P"])
                    S.op("pool", lambda e, a1=a1, a2=a2, g=g: e.tensor_tensor(
                        out=Pb[:, 4 * g:4 * g + 4, 1, :], in0=a1[:, :, 1, :], in1=a2[:, :, 0, :], op=ALU.add),
                        reads=[a1r, a2r], writes=["P"])
                for c2 in range(CB // 2):
                    pb, pr = ps[pn % 4], ("ps", pn % 4)
                    pn += 1
                    for h in range(2):
                        c = 2 * c2 + h
                        S.op("pe", lambda e, pb=pb, c=c, h=h: e.matmul(pb[:, h * 256:(h + 1) * 256], lhsT=Pb[:, :, 0, c], rhs=G1[:],
                                                                       start=True, stop=False), reads=["P", "G1"], writes=[pr])
                        S.op("pe", lambda e, pb=pb, c=c, h=h: e.matmul(pb[:, h * 256:(h + 1) * 256], lhsT=Pb[:, :, 1, c], rhs=G2[:],
                                                                       start=False, stop=True), reads=["P", "G2"], writes=[pr])
                    zv = Zs[:, 0:1, 0:1, 2 * c2:2 * c2 + 1]
                    zout = AP(zv.tensor, zv.offset, [list(zv.ap[0]), [CB, 256], [1, 2]])
                    pin = pb[:, :].rearrange("p (h x) -> p x h", h=2)
                    if c2 % 2 == 0:
                        S.op("act", lambda e, zout=zout, pin=pin: e.copy(out=zout, in_=pin), reads=[pr], writes=["Zs"])
                    else:
                        S.op("dve", lambda e, zout=zout, pin=pin: e.tensor_copy(out=zout, in_=pin), reads=[pr], writes=["Zs"])
                if o == 1:
                    load_blk(bufA, "hA", 1536 + c0)
                Xg, xres = (bufB, "hB") if o == 0 else (bufA, "hA")
                for g in range(16):
                    pb, pr = ps[4 + pn % 4], ("ps", 4 + pn % 4)
                    pn += 1
                    for j in range(8):
                        n1 = 8 * g + j
                        if n1 % 8 == 0:
                            mb_, mr_ = Mb[mn % 4], ("Mb", mn % 4)
                            mn += 1
                            S.dma(mb_[:], din["Minv"][:, n1:n1 + 8, :], writes=[mr_])
                        S.op("pe", lambda e, pb=pb, j=j, n1=n1, mb_=mb_: e.matmul(
                            pb[:, j * CB:(j + 1) * CB], lhsT=mb_[:, n1 % 8, 0:128], rhs=Zs[:, 0, n1, :], start=True, stop=False),
                            reads=["Zs", mr_], writes=[pr])
                        S.op("pe", lambda e, pb=pb, j=j, n1=n1, mb_=mb_: e.matmul(
                            pb[:, j * CB:(j + 1) * CB], lhsT=mb_[:, n1 % 8, 128:256], rhs=Zs[:, 1, n1, :], start=False, stop=True),
                            reads=["Zs", mr_], writes=[pr])
                    tb, tr = te[g % 2], ("hte", g % 2)
                    zin = Din[:, :, 8 * g:8 * g + 8].rearrange("p c j -> p j c") if o == 0 else Din[:, 8 * g:8 * g + 8, :]
                    xin = Xg[:, :, 8 * g:8 * g + 8].rearrange("p c j -> p j c")
                    S.op("pool", lambda e, tb=tb, zin=zin, o=o, c0=c0: e.tensor_tensor(
                        out=tb[:], in0=zin, in1=bc(drep[:, o:o + 1, :], [128, 8, CB]), op=ALU.mult),
                        reads=[dres, "hdrep"], writes=[tr])
                    S.op("dve", lambda e, tb=tb, pb=pb: e.tensor_tensor(
                        out=tb[:], in0=pb[:, :].rearrange("p (j c) -> p j c", j=8), in1=tb[:], op=ALU.add), reads=[pr, tr], writes=[tr])
                    if o == 0:
                        zo = bufC[:, 8 * g:8 * g + 8, :]
                        S.op("pool", lambda e, tb=tb, xin=xin, zo=zo: e.tensor_tensor(out=zo, in0=tb[:], in1=xin, op=ALU.mult),
                             reads=[tr, xres], writes=["hC"])
                    else:
                        z3v = bufB[:].rearrange("p c j -> p (c j)")[:, 8 * g * CB:(8 * g + 8) * CB].rearrange("p (j c) -> p j c", j=8)
                        S.op("pool", lambda e, tb=tb, xin=xin, z3v=z3v: e.tensor_tensor(out=z3v, in0=tb[:], in1=xin, op=ALU.mult),
                             reads=[tr, xres], writes=["hB"])
            z3 = bufB[:].rearrange("p c j -> p (c j)")
            for g in range(16):
                pb, pr = ps[pn % 4], ("ps", pn % 4)
                pn += 1
                pT = pb[:].bitcast(BF16)
                for j in range(8):
                    n1 = 8 * g + j
                    S.op("pe", lambda e, pT=pT, j=j, n1=n1: e.transpose(out=pT[0:CB, j * 128:(j + 1) * 128],
                                                                        in_=z3[:, n1 * CB:(n1 + 1) * CB], identity=k.ident[:]),
                         reads=["hB", "ident"], writes=[pr])
                ov = AP(ohs.tensor, ohs.offset + 8 * g, [list(ohs.ap[0]), [128, 128], [1, 8]])
                pin = pT[0:CB, :].rearrange("p (j n) -> p n j", j=8)
                if g % 2 == 0:
                    S.op("act", lambda e, ov=ov, pin=pin: e.copy(out=ov, in_=pin), reads=[pr], writes=["P"])
                else:
                    S.op("dve", lambda e, ov=ov, pin=pin: e.tensor_copy(out=ov, in_=pin), reads=[pr], writes=["P"])
            S.dma(k.ohT[c0:c0 + CB, :], ohs, reads=["P"])


def phaseA(k):
    nc, S, din, ps = k.nc, k.S, k.din, k.ps
    SPAN = 4096
    with ExitStack() as st:
        Hk = st.enter_context(nc.sbuf_tensor("s_Hk", [128, 3, 12, 256], BF16))
        J = st.enter_context(nc.sbuf_tensor("s_J", [128, 128], BF16))
        bm = st.enter_context(nc.sbuf_tensor("s_bm", [128, 256], BF16))
        swb = st.enter_context(nc.sbuf_tensor("s_swb", [128, 128], BF16))
        swf = st.enter_context(nc.sbuf_tensor("s_swf", [128, 128], F32))
        selA = st.enter_context(nc.sbuf_tensor("s_selA", [128, 128], F32))
        selB = st.enter_context(nc.sbuf_tensor("s_selB", [128, 128], F32))
        with ExitStack() as s2:
            rb = s2.enter_context(nc.sbuf_tensor("s_rb", [32, 12], F32))
            oh = s2.enter_context(nc.sbuf_tensor("s_oh", [32, 1152], F32))
            mr = s2.enter_context(nc.sbuf_tensor("s_mrow", [12, 1152], F32))
            av = s2.enter_context(nc.sbuf_tensor("s_av", [12, 1152], BF16))
            S.dma(rb[:], din["rel_bias"][:, :], writes=["rb"])
            S.dma(oh[:], din["OH"][:, :], writes=["oh"])
            S.dma(mr[:], din["mrow"][:, :], writes=["mrow"])
            for i in range(3):
                S.op("pe", lambda e, i=i: e.matmul(ps[i][0:12, 0:384], lhsT=rb[:], rhs=oh[:, i * 384:(i + 1) * 384], start=True, stop=True),
                     reads=["rb", "oh"], writes=[("ps", i)])
                S.op("dve", lambda e, i=i: e.tensor_tensor(out=av[:, i * 384:(i + 1) * 384], in0=ps[i][0:12, 0:384],
                                                           in1=mr[:, i * 384:(i + 1) * 384], op=ALU.add),
                     reads=[("ps", i), "mrow"], writes=["av"])
            S.dma(k.Avec[:, :], av[:], reads=["av"], writes=["Avec"])
            for h in range(12):
                for ri in range(3):
                    src = AP(k.Avec.tensor, k.Avec[h:h + 1, ri * 384:ri * 384 + 1].offset, [[1, 128], [1, 256]])
                    S.dma(Hk[:, ri, h, :], src, reads=["Avec"], writes=["Hk"])
            S.dma(J[:], din["antiid"][:, :], writes=["J"])
            S.dma(bm[:], din["bmask"][:, :], writes=["bm"])
            S.dma(swb[:], din["swap"][:, :], writes=["swb"])
            S.op("dve", lambda e: e.tensor_copy(out=swf[:], in_=swb[:]), reads=["swb"], writes=["swf"])
            S.op("pool", lambda e: e.memset(selA[:], 0.0), writes=["sel"])
            S.op("pool", lambda e: e.memset(selB[:], 0.0), writes=["sel"])
            S.op("dve", lambda e: e.tensor_copy(out=selA[:, 0:64], in_=swb[:, 0:64]), reads=["swb", "sel"], writes=["sel"])
            S.op("dve", lambda e: e.tensor_copy(out=selB[:, 64:128], in_=swb[:, 64:128]), reads=["swb", "sel"], writes=["sel"])
            S.barrier()
        TP = T + 2 * PAD
        qAB = st.enter_context(nc.sbuf_tensor("s_qAB", [128, 2, TP], BF16))
        kT = st.enter_context(nc.sbuf_tensor("s_kT", [128, TP], BF16))
        S.op("pool", lambda e: e.memset(qAB[:], 0.0), writes=["qAB"])
        S.op("pool", lambda e: e.memset(kT[:, 0:PAD], 0.0), writes=["kT"])
        S.op("pool", lambda e: e.memset(kT[:, PAD + T:TP], 0.0), writes=["kT"])
        acc = st.enter_context(nc.sbuf_tensor("s_acc", [128, 2, SPAN], F32))
        OW = 2048
        oT = [st.enter_context(nc.sbuf_tensor("s_oT%d" % i, [128, OW], BF16)) for i in range(2)]
        rden = [st.enter_context(nc.sbuf_tensor("s_rden%d" % i, [128, 512], F32)) for i in range(2)]
        Vt = [st.enter_context(nc.sbuf_tensor("s_Vt%d" % i, [128, 256], BF16)) for i in range(8)]
        PT = [st.enter_context(nc.sbuf_tensor("s_PT%d" % i, [128, 2, 256], BF16)) for i in range(6)]
        vn = 0
        ptn = 0
        sn = 0
        on = 0
        rn = 0
        spn = 0
        DSK = 3
        cgen = None
        cstep = 0
        for hp in range(6):
            S.dma(qAB[0:64, 0, PAD:PAD + T], k.qkT[hp * 128:hp * 128 + 64, :], writes=["qAB"])
            S.dma(qAB[64:128, 1, PAD:PAD + T], k.qkT[hp * 128 + 64:hp * 128 + 128, :], writes=["qAB"])
            S.dma(kT[:, PAD:PAD + T], k.qkT[768 + hp * 128:768 + (hp + 1) * 128, :], writes=["kT"])
            for s in range(T // SPAN):
                pendB = []
                S.op("pool", lambda e: e.memset(acc[:], 0.0), writes=["acc"])
                for ri, r in enumerate((1, 4, 16)):
                    Lr = T // r
                    jb = (T // 2) // (r * 128)
                    nqb = SPAN // (128 * r)
                    for rho in range(r):
                        ja, jbnd = s * nqb, s * nqb + nqb
                        pobank = {}
                        for j in range(ja, jbnd + 1):
                            c_lo = 128 if j == ja else 0
                            c_hi = 128 if j == jbnd else 256
                            ncol = c_hi - c_lo
                            vt, vr = Vt[vn % 8], ("Vt", vn % 8)
                            vn += 1
                            m0 = 128 * j - 64
                            lo, hi = 0, 128
                            if m0 < 0:
                                lo = 64
                            if m0 + 128 > Lr:
                                hi = 64
                            if lo > 0 or hi < 128:
                                S.op("pool", lambda e, vt=vt: e.memset(vt[:], 0.0), writes=[vr])
                            tok0 = rho + r * (m0 + lo)
                            src = AP(k.vaug.tensor, k.vaug[tok0:tok0 + 1, hp * 256:hp * 256 + 1].offset, [[1536 * r, hi - lo], [1, 256]])
                            S.dma(vt[lo:hi, :], src, writes=[vr])
                            mq0 = 128 * j - 128 + c_lo
                            qc0 = PAD + rho + r * mq0
                            qsl = slice(qc0, qc0 + (ncol - 1) * r + 1, r)
                            kc0 = PAD + rho + r * m0
                            ksl = slice(kc0, kc0 + 127 * r + 1, r)
                            straddle = (j == jb)
                            pS, psr = ps[sn % 4], ("ps", sn % 4)
                            sn += 1
                            pt, ptr = PT[ptn % 6], ("PT", ptn % 6)
                            ptn += 1
                            pSv = pS[:, :].rearrange("p (h c) -> p h c", h=2)[:, :, 0:ncol]

                            def stageA(pSv=pSv, psr=psr, ksl=ksl, qsl=qsl, ncol=ncol, ri=ri, c_lo=c_lo, c_hi=c_hi,
                                       straddle=straddle, pt=pt, ptr=ptr, hp=hp):
                                S.op("pe", lambda e: e.matmul(pSv, lhsT=kT[:, ksl], rhs=qAB[:, :, qsl], start=True, stop=False),
                                     reads=["kT", "qAB"], writes=[psr])
                                S.op("pe", lambda e: e.matmul(pSv, lhsT=J[:], rhs=Hk[:, ri, 2 * hp:2 * hp + 2, c_lo:c_hi],
                                                              start=False, stop=not straddle), reads=["J", "Hk"], writes=[psr])
                                if straddle:
                                    S.op("pe", lambda e: e.matmul(pSv, lhsT=k.ident[:], rhs=bc(bm[:, c_lo:c_hi].rearrange("p (o c) -> p o c", o=1), [128, 2, ncol]),
                                                                  start=False, stop=True), reads=["ident", "bm"], writes=[psr])
                                S.op("act", lambda e: e.activation(out=pt[:, :, 0:ncol], in_=pSv, func=AF.Exp), reads=[psr], writes=[ptr])

                            pieces = []
                            if c_lo == 0:
                                pieces.append((0, j - 1))
                            if c_hi == 256:
                                pieces.append((1, j))
                            for (half, jq) in pieces:
                                if half == 1:
                                    pobank[jq] = (ps[4 + on % 4], ("ps", 4 + on % 4))
                                    on += 1
                            pbs = {jq: pobank[jq] for (_, jq) in pieces}

                            def stageB(pieces=pieces, pbs=pbs, vt=vt, vr=vr, pt=pt, ptr=ptr, c_lo=c_lo, r=r, rho=rho, s=s):
                                for (half, jq) in pieces:
                                    po, por = pbs[jq]
                                    off = half * 128 - c_lo
                                    for hh in range(2):
                                        S.op("pe", lambda e, po=po, hh=hh, off=off, half=half: e.matmul(
                                            po[:, hh * 128:(hh + 1) * 128], lhsT=vt[:, hh * 128:(hh + 1) * 128], rhs=pt[:, hh, off:off + 128],
                                            start=(half == 1 and hh == 0), stop=(half == 0), skip_group_check=True),
                                            reads=[vr, ptr], writes=[por])
                                    if half == 0:
                                        a0 = rho + r * 128 * jq - s * SPAN
                                        asl = slice(a0, a0 + 127 * r + 1, r)
                                        S.op("dve", lambda e, po=po, asl=asl: e.tensor_tensor(
                                            out=acc[:, :, asl], in0=po[:, 0:256].rearrange("p (h c) -> p h c", h=2), in1=acc[:, :, asl],
                                            op=ALU.add), reads=[por, "acc"], writes=["acc"])

                            stageA()
                            pendB.append(stageB)
                            if len(pendB) > DSK:
                                pendB.pop(0)()
                            cstep += 1
                            if cgen is not None and cstep % 4 == 0:
                                next(cgen, None)
                while pendB:
                    pendB.pop(0)()
                for ow in range(SPAN // OW):
                    ot, otr = oT[spn % 2], ("oT", spn % 2)
                    spn += 1
                    for cc in range(OW // 512):
                        c0 = ow * OW + cc * 512
                        pw, pwr = ps[sn % 4], ("ps", sn % 4)
                        sn += 1
                        S.op("pe", lambda e, pw=pw, c0=c0: e.matmul(pw[:, :], lhsT=selA[:], rhs=acc[:, 0, c0:c0 + 512],
                                                                    start=True, stop=False), reads=["acc", "sel"], writes=[pwr])
                        S.op("pe", lambda e, pw=pw, c0=c0: e.matmul(pw[:, :], lhsT=selB[:], rhs=acc[:, 1, c0:c0 + 512],
                                                                    start=False, stop=True), reads=["acc", "sel"], writes=[pwr])
                        rd, rdr = rden[rn % 2], ("rden", rn % 2)
                        rn += 1
                        S.op("dve", lambda e, rd=rd, pw=pw: e.reciprocal(out=rd[:], in_=pw[:, :]), reads=[pwr], writes=[rdr])
                        for hh in range(2):
                            nlo = 64 * hh
                            S.op("pool", lambda e, rd=rd, ot=ot, hh=hh, cc=cc, c0=c0, nlo=nlo: e.tensor_tensor(
                                out=ot[nlo:nlo + 64, cc * 512:(cc + 1) * 512], in0=acc[nlo:nlo + 64, hh, c0:c0 + 512],
                                in1=rd[nlo:nlo + 64, :], op=ALU.mult), reads=["acc", rdr], writes=[otr])
                    t0_ = s * SPAN + ow * OW
                    S.dma(k.oaT[hp * 128:(hp + 1) * 128, t0_:t0_ + OW], ot[:], reads=[otr])
        if cgen is not None:
            for _ in cgen:
                pass


def phaseF(k):
    nc, S, din, ps = k.nc, k.S, k.din, k.ps
    with ExitStack() as st:
        wpa = st.enter_context(nc.sbuf_tensor("s_wpa", [128, 6, DM], BF16))
        wph = st.enter_context(nc.sbuf_tensor("s_wph", [128, 6, DM], BF16))
        wo = st.enter_context(nc.sbuf_tensor("s_wo", [128, 2, 8, DM], BF16))
        gater = st.enter_context(nc.sbuf_tensor("s_gater", [128, 2, DM], F32))
        for s_ in range(2):
            S.dma(gater[:, s_, :], k.modrep[s_:s_ + 1, 2 * DM:3 * DM].partition_broadcast(128), writes=[("modr", s_)])
        with ExitStack() as s2:
            stg = [s2.enter_context(nc.sbuf_tensor("s_fstg%d" % i, [128, DM], F32)) for i in range(2)]
            n = 0
            for (wt, nm, src, nk) in ((wpa, "wpa", "w_proj_attn", 6), (wph, "wph", "w_proj_hyena", 6), (wo, "wo", "w_out", 8)):
                for kk in range(nk):
                    b = n % 2
                    S.dma(stg[b][:], din[src][kk * 128:(kk + 1) * 128, :], writes=[("fstg", b)])
                    eng = ("dve", "pool")[n % 2]
                    if nm == "wo":
                        for s_ in range(2):
                            S.op(("dve", "pool")[s_], lambda e, b=b, wt=wt, kk=kk, s_=s_: e.tensor_tensor(
                                out=wt[:, s_, kk, :], in0=stg[b][:], in1=gater[:, s_, :], op=ALU.mult),
                                reads=[("fstg", b), ("modr", s_)], writes=[nm])
                    else:
                        S.op(eng, lambda e, b=b, wt=wt, kk=kk: e.tensor_copy(out=wt[:, kk, :], in_=stg[b][:]), reads=[("fstg", b)], writes=[nm])
                    n += 1
            S.barrier()
        oa = [st.enter_context(nc.sbuf_tensor("s_foa%d" % i, [128, 6, 512], BF16)) for i in range(2)]
        ga = [st.enter_context(nc.sbuf_tensor("s_fga%d" % i, [128, 6, 512], BF16)) for i in range(2)]
        oh_ = [st.enter_context(nc.sbuf_tensor("s_foh%d" % i, [128, 6, 512], BF16)) for i in range(2)]
        gh = [st.enter_context(nc.sbuf_tensor("s_fgh%d" % i, [128, 6, 512], BF16)) for i in range(2)]
        mt = [st.enter_context(nc.sbuf_tensor("s_fmt%d" % i, [128, 16, 512], BF16)) for i in range(2)]
        mix = [st.enter_context(nc.sbuf_tensor("s_fmix%d" % i, [128, 8, 512], BF16)) for i in range(2)]
        ta = [st.enter_context(nc.sbuf_tensor("s_fta%d" % i, [128, 512], F32)) for i in range(2)]
        tb = [st.enter_context(nc.sbuf_tensor("s_ftb%d" % i, [128, 512], F32)) for i in range(2)]
        xt = [st.enter_context(nc.sbuf_tensor("s_fx%d" % i, [128, DM], F32)) for i in range(2)]
        r1 = [st.enter_context(nc.sbuf_tensor("s_fr%d" % i, [128, DM], F32)) for i in range(2)]
        yo = [st.enter_context(nc.sbuf_tensor("s_fy%d" % i, [128, DM], F32)) for i in range(2)]
        sqj = st.enter_context(nc.sbuf_tensor("s_fsq", [128, DM], BF16))
        ss = [st.enter_context(nc.sbuf_tensor("s_fss%d" % i, [128, 2], F32)) for i in range(2)]
        pn = 0
        tn_ = 0

        def fprep(ci):
            b = ci % 2
            cs = slice(ci * 512, (ci + 1) * 512)
            S.dma(oa[b][:], k.oaT[:, cs].rearrange("(a p) t -> p a t", p=128), writes=[("foa", b)])
            S.dma(ga[b][:], k.gaT[:, cs].rearrange("(a p) t -> p a t", p=128), writes=[("fga", b)])
            S.dma(oh_[b][:], k.ohT[:, cs].rearrange("(a p) t -> p a t", p=128), writes=[("foh", b)])
            S.dma(gh[b][:], k.ghT[:, cs].rearrange("(a p) t -> p a t", p=128), writes=[("fgh", b)])
            S.dma(mt[b][:], k.mT[:, cs].rearrange("(a p) t -> p a t", p=128), writes=[("fmt", b)])
            S.op("pool", lambda e, b=b: e.tensor_tensor(out=oa[b][:], in0=oa[b][:], in1=ga[b][:], op=ALU.mult),
                 reads=[("foa", b), ("fga", b)], writes=[("foa", b)])
            S.op("dve", lambda e, b=b: e.tensor_tensor(out=oh_[b][:], in0=oh_[b][:], in1=gh[b][:], op=ALU.mult),
                 reads=[("foh", b), ("fgh", b)], writes=[("foh", b)])

        def frun(ci):
            nonlocal pn, tn_
            seg = 0 if ci < NCHUNK // 2 else 1
            b = ci % 2
            for fb in range(8):
                pA, pAr = ps[pn % 4], ("ps", pn % 4)
                pn += 1
                pH, pHr = ps[pn % 4], ("ps", pn % 4)
                pn += 1
                for kk in range(6):
                    S.op("pe", lambda e, pA=pA, kk=kk, fb=fb, b=b: e.matmul(pA[:, :], lhsT=wpa[:, kk, fb * 128:(fb + 1) * 128], rhs=oa[b][:, kk, :],
                                                                             start=(kk == 0), stop=(kk == 5)), reads=["wpa", ("foa", b)], writes=[pAr])
                for kk in range(6):
                    S.op("pe", lambda e, pH=pH, kk=kk, fb=fb, b=b: e.matmul(pH[:, :], lhsT=wph[:, kk, fb * 128:(fb + 1) * 128], rhs=oh_[b][:, kk, :],
                                                                             start=(kk == 0), stop=(kk == 5)), reads=["wph", ("foh", b)], writes=[pHr])
                a_, ar_ = ta[tn_ % 2], ("fta", tn_ % 2)
                b_, br_ = tb[tn_ % 2], ("ftb", tn_ % 2)
                tn_ += 1
                S.op("dve", lambda e, a_=a_, pA=pA, fb=fb, b=b: e.tensor_tensor(out=a_[:], in0=pA[:, :], in1=mt[b][:, fb, :], op=ALU.mult),
                     reads=[pAr, ("fmt", b)], writes=[ar_])
                S.op("dve", lambda e, b_=b_, pH=pH, fb=fb, b=b: e.tensor_tensor(out=b_[:], in0=pH[:, :], in1=mt[b][:, 8 + fb, :], op=ALU.mult),
                     reads=[pHr, ("fmt", b)], writes=[br_])
                S.op("pool", lambda e, a_=a_, b_=b_, fb=fb, b=b: e.tensor_tensor(out=mix[b][:, fb, :], in0=a_[:], in1=b_[:], op=ALU.add),
                     reads=[ar_, br_], writes=[("fmix", b)])
            pend = []
            for tt in range(4):
                t = ci * 4 + tt
                tb2 = t % 2
                S.dma(xt[tb2][:], din["x"][t * 128:(t + 1) * 128, :], writes=[("fx", tb2)], eng="act")
                p0, p1 = ps[4 + 2 * tb2], ps[5 + 2 * tb2]
                p0r, p1r = ("ps", 4 + 2 * tb2), ("ps", 5 + 2 * tb2)
                for half, (pp_, ppr) in enumerate(((p0, p0r), (p1, p1r))):
                    for fb in range(8):
                        S.op("pe", lambda e, pp_=pp_, fb=fb, tt=tt, half=half, b=b, seg=seg: e.matmul(
                            pp_[:, :], lhsT=mix[b][:, fb, tt * 128:(tt + 1) * 128], rhs=wo[:, seg, fb, half * 512:(half + 1) * 512],
                            start=(fb == 0), stop=(fb == 7)), reads=[("fmix", b), "wo"], writes=[ppr])
                rr, rrr = r1[tb2], ("fr", tb2)
                S.op("dve", lambda e, rr=rr, tb2=tb2: e.tensor_tensor(
                    out=rr[:], in0=k.pp[2 + tb2][:, :], in1=xt[tb2][:], op=ALU.add),
                    reads=[p0r, p1r, ("fx", tb2)], writes=[rrr])
                sb, sr = ss[tb2], ("fss", tb2)
                S.op("act", lambda e, rr=rr, sb=sb: e.activation(out=sqj[:], in_=rr[:], func=AF.Square, scale=1.0 / 32.0, accum_out=sb[:, 0:1]),
                     reads=[rrr], writes=["fsq", sr])
                S.op("act", lambda e, sb=sb: e.activation(out=sb[:, 1:2], in_=sb[:, 0:1], func=AF.Sqrt, bias=k.epsc[:, 0:1]),
                     reads=[sr, "epsc"], writes=[sr])

                def stage2(rr=rr, rrr=rrr, sb=sb, sr=sr, tb2=tb2, t=t):
                    S.op("dve", lambda e: e.reciprocal(out=sb[:, 1:2], in_=sb[:, 1:2]), reads=[sr], writes=[sr])
                    yb, yr = yo[tb2], ("fy", tb2)
                    S.op("dve", lambda e: e.scalar_tensor_tensor(
                        out=yb[:], in0=rr[:], scalar=sb[:, 1:2], in1=k.fg_rep[:], op0=ALU.mult, op1=ALU.mult),
                        reads=[rrr, sr, "fg_rep"], writes=[yr])
                    S.dma(k.y[t * 128:(t + 1) * 128, :], yb[:], reads=[yr], eng="pool")

                pend.append(stage2)
                if len(pend) > 1:
                    pend.pop(0)()
            while pend:
                pend.pop(0)()

        fprep(0)
        for ci in range(NCHUNK):
            if ci + 1 < NCHUNK:
                fprep(ci + 1)
            frun(ci)
```

```python
import math
import numpy as np
import ml_dtypes
import concourse.bass as bass
import concourse.mybir as mybir
from concourse.ap import AP
from concourse.bass_utils import run_bass_kernel_spmd

F32 = mybir.dt.float32
BF16 = mybir.dt.bfloat16
AF = mybir.ActivationFunctionType
ALU = mybir.AluOpType

T = 16384
DM = 1024
NCHUNK = T // 512
NFFT = 32768
EPS = 1e-6
CB = 64
NCB = 768 // CB
PAD = 1024

OQ, OK_, OV, OGA, OU, OGH, OMA, OMH = 0, 768, 1536, 2304, 3072, 5376, 6144, 7168

DEBUG = {}


class Sched:
    ENG = ("pe", "act", "dve", "pool", "sp")

    def __init__(self, nc, sems, dma_ring):
        self.nc = nc
        self.ops = []
        self.eng = {"pe": nc.tensor, "act": nc.scalar, "dve": nc.vector, "pool": nc.gpsimd, "sp": nc.sync}
        self.sems = sems
        self.ring = dma_ring
        self.cnt = {e: 0 for e in self.ENG}
        self.ndma = 0
        self.last_w = {}
        self.readers = {}
        self.waited = {e: {d: 0 for d in self.ENG} for e in self.ENG}
        self.waited_dma = {e: {} for e in self.ENG}
        self.last_op = {e: None for e in self.ENG}
        self.pending = []
        self.dma_eng = "sp"

    def op(self, eng, fn, reads=(), writes=()):
        self.ops.append(("op", eng, fn, tuple(reads), tuple(writes)))

    def dma(self, out, in_, reads=(), writes=(), eng="sp"):
        self.ops.append(("dma", eng, (out, in_), tuple(reads), tuple(writes)))

    def barrier(self):
        self.ops.append(("bar",))

    def flush(self):
        ops = self.ops
        n = len(ops)
        last_w, readers = {}, {}
        last_on = {e: -1 for e in self.ENG}
        deps = [None] * n
        marked = [False] * n
        for i, o in enumerate(ops):
            if o[0] == "bar":
                deps[i] = dict(last_on)
                for e, j in last_on.items():
                    if j >= 0:
                        marked[j] = True
                last_w, readers = {}, {}
                continue
            _, e, _, rd, wr = o
            d = set()
            for r in rd:
                if r in last_w:
                    d.add(last_w[r])
            for w in wr:
                if w in last_w:
                    d.add(last_w[w])
                for j in readers.get(w, ()):
                    d.add(j)
            d.discard(i)
            deps[i] = d
            for j in d:
                marked[j] = True
            for w in wr:
                last_w[w] = i
                readers[w] = []
            for r in rd:
                if r not in wr:
                    readers.setdefault(r, []).append(i)
            last_on[e] = i
        ordinal = [0] * n
        cnt = {e: 0 for e in self.ENG}
        dslot = [None] * n
        nd = 0
        P = len(self.ring)
        for i, o in enumerate(ops):
            if o[0] == "dma":
                dslot[i] = (nd % P, 16 * (nd // P + 1))
                nd += 1
            elif o[0] == "op" and marked[i]:
                cnt[o[1]] += 1
                ordinal[i] = cnt[o[1]]
        waited = {e: {d: 0 for d in self.ENG} for e in self.ENG}
        wdma = {e: {} for e in self.ENG}

        def wait_for(e, j):
            oj = ops[j]
            if oj[0] == "dma":
                slot, val = dslot[j]
                if wdma[e].get(slot, 0) >= val:
                    return
                wdma[e][slot] = val
                self.eng[e].wait_ge(self.ring[slot], val)
            else:
                dsrc = oj[1]
                if dsrc == e and e == "pe":
                    return
                if waited[e][dsrc] >= ordinal[j]:
                    return
                waited[e][dsrc] = ordinal[j]
                self.eng[e].wait_ge(self.sems[dsrc], ordinal[j])

        nd = 0
        dma_hist = []
        for i, o in enumerate(ops):
            if o[0] == "bar":
                for e in self.ENG:
                    for dsrc, j in deps[i].items():
                        if j >= 0 and not (dsrc == e and ops[j][0] == "op"):
                            wait_for(e, j)
                    for j in dma_hist[-P:]:
                        wait_for(e, j)
                continue
            kind, e, payload, rd, wr = o
            for j in sorted(deps[i]):
                wait_for(e, j)
            if kind == "dma":
                slot, val = dslot[i]
                if val > 16:
                    if wdma[e].get(slot, 0) < val - 16:
                        wdma[e][slot] = val - 16
                        self.eng[e].wait_ge(self.ring[slot], val - 16)
                out, in_ = payload
                self.eng[e].dma_start(out=out, in_=in_).then_inc(self.ring[slot], 16)
                dma_hist.append(i)
                nd += 1
            else:
                ins = payload(self.eng[e])
                if marked[i]:
                    ins.then_inc(self.sems[e], 1)
        for j in dma_hist[-P:]:
            wait_for("sp", j)
        self.ops = []
        return n


def bc(ap, shape):
    return ap.to_broadcast(list(shape))


def _t5_bucket(rel):
    nb = 16
    max_exact = 8
    n = np.abs(rel)
    large = max_exact + (np.log(np.maximum(n, 1) / max_exact) / math.log(1024 / max_exact) * (nb - max_exact)).astype(np.int32)
    large = np.minimum(large, nb - 1)
    return ((rel > 0).astype(np.int32) * nb + np.where(n < max_exact, n, large)).astype(np.int32)


def bf(a):
    return np.ascontiguousarray(a.astype(np.float32)).astype(ml_dtypes.bfloat16)


_CONST_CACHE = {}


def build_consts(is_prompt):
    key = bool(is_prompt)
    if key in _CONST_CACHE:
        return _CONST_CACHE[key]
    c = {}
    L = 16384 if is_prompt else 8192
    tt = np.linspace(0.0, 1.0, L, dtype=np.float32)[:, None]
    w = (np.float32(2.0 * math.pi / L) * np.arange(L, dtype=np.float32))[:, None]
    f = np.linspace(1e-4, 15, 16, dtype=np.float32)[None, :]
    z = np.concatenate([tt, np.cos(f * w), -np.sin(f * w)], axis=-1).astype(np.float32)
    zf = np.zeros((T, 33), np.float32)
    zf[:L] = z
    c["zfT"] = np.ascontiguousarray(zf.T)
    tn = np.zeros(T, np.float32)
    tn[:L] = tt[:, 0]
    c["negtn"] = np.ascontiguousarray(-tn.reshape(128, 128))
    deltas = np.abs(np.linspace(math.log(0.01) / 1.5, math.log(0.01) / 0.3, 768, dtype=np.float32))
    c["deltas_rep"] = np.ascontiguousarray(np.broadcast_to(deltas[None, :], (128, 768))).astype(np.float32)
    slot = np.arange(128) if is_prompt else np.concatenate([np.arange(64), np.arange(64) + 128])
    k2 = np.arange(128)
    n1 = np.arange(128)
    k1 = np.arange(128)
    ang = -2 * np.pi * np.outer(slot, k2 + 0.5) / 256.0
    Er, Ei = np.cos(ang), np.sin(ang)
    c["E_d"] = bf(np.concatenate([-Ei, Er, Ei], axis=1))
    angk = -2 * np.pi * np.outer(np.arange(128), k2 + 0.5) / 256.0
    Ekr, Eki = np.cos(angk), np.sin(angk)
    if not is_prompt:
        Ekr[64:] = 0
        Eki[64:] = 0
    c["E_kS"] = bf(np.concatenate([Ekr, -Eki], axis=1))
    c["E_kD"] = bf(np.concatenate([Eki, Ekr], axis=1))
    angM = -2 * np.pi * (n1[None, :, None] * (k2[:, None, None] + 0.5) / NFFT + n1[None, :, None] * k1[None, None, :] / 128.0)
    c["Mf"] = bf(np.concatenate([np.cos(angM), np.sin(angM)], axis=2).transpose(1, 0, 2))
    angG = 2 * np.pi * np.outer(k1, n1) / 128.0
    c["G1"] = bf(np.concatenate([np.cos(angG), np.sin(angG)], axis=1))
    c["G2"] = bf(np.concatenate([-np.sin(angG), np.cos(angG)], axis=1))
    angI = 2 * np.pi * (n1[:, None, None] + 128 * slot[None, None, :]) * (k2[None, :, None] + 0.5) / NFFT
    sc = 2.0 / NFFT
    c["Minv"] = bf(np.concatenate([sc * np.cos(angI), -sc * np.sin(angI)], axis=2).transpose(1, 0, 2))
    OH = np.zeros((32, 3 * 384), np.float32)
    mrow = np.zeros((12, 3 * 384), np.float32)
    for ri, r in enumerate((1, 4, 16)):
        for j in range(384):
            d = j - 127
            if 0 <= d <= 128:
                rel = (64 - d) * r
                OH[_t5_bucket(np.array(rel)), ri * 384 + j] = 1.0
            else:
                mrow[:, ri * 384 + j] = -30000.0
    c["OH"] = OH
    c["mrow"] = mrow
    bm = np.zeros((128, 256), np.float32)
    if not is_prompt:
        bm[:64, 128:] = -30000.0
        bm[64:, :128] = -30000.0
    c["bmask"] = bf(bm)
    c["bflag"] = np.full((128, 1), 1.0 if is_prompt else 0.0, np.float32)
    ident = np.eye(128, dtype=np.float32)
    c["ident"] = bf(ident)
    c["antiid"] = bf(ident[::-1])
    c["swap"] = bf(np.roll(ident, 64, axis=1))
    _CONST_CACHE[key] = c
    return c


CONST_SHAPES = {
    "zfT": ([33, T], F32), "negtn": ([128, 128], F32), "deltas_rep": ([128, 768], F32),
    "E_d": ([128, 384], BF16), "E_kS": ([128, 256], BF16), "E_kD": ([128, 256], BF16),
    "Mf": ([128, 128, 256], BF16), "G1": ([128, 256], BF16), "G2": ([128, 256], BF16),
    "Minv": ([128, 128, 256], BF16), "OH": ([32, 1152], F32), "mrow": ([12, 1152], F32),
    "bmask": ([128, 256], BF16), "bflag": ([128, 1], F32), "ident": ([128, 128], BF16),
    "antiid": ([128, 128], BF16), "swap": ([128, 128], BF16),
}

INPUT_SHAPES = {
    "x": [T, DM], "cT": [128, 8, 2], "w_ada": [DM, 3 * DM], "b_ada": [1, 3 * DM], "norm_g": [1, DM],
    "w_in": [DM, 8192], "short_wT": [128, 18, 3], "short_bT": [128, 18],
    "filt_w1": [33, 64], "filt_b1": [64, 1], "filt_fr1": [64, 1], "filt_w2": [64, 64], "filt_b2": [64, 1],
    "filt_fr2": [64, 1], "filt_w3": [64, 3072], "hyena_d": [2, 768],
    "w_proj_attn": [768, DM], "w_proj_hyena": [768, DM], "w_out": [DM, DM], "rel_bias": [32, 12],
    "final_g": [1, DM],
}


from contextlib import ExitStack


class K:
    pass


def build_program(phases=("p0", "pk", "p1", "p1b", "pa", "ph", "pf"), debug_outs=()):
    nc = bass.Bass("TRN2", target_bir_lowering=False)
    k = K()
    k.nc = nc
    din = {}
    for name, shp in INPUT_SHAPES.items():
        din[name] = nc.dram_tensor(name, shp, F32, kind="ExternalInput").ap()
    for name, (shp, dt_) in CONST_SHAPES.items():
        din[name] = nc.dram_tensor(name, shp, dt_, kind="ExternalInput").ap()
    k.din = din
    y = nc.dram_tensor("y", [T, DM], F32, kind="ExternalOutput").ap()
    k.y = y

    def scratch(name, shp, dt_):
        kind = "ExternalOutput" if name in debug_outs else "Internal"
        return nc.dram_tensor(name, shp, dt_, kind=kind).ap()

    k.qkT = scratch("qkT", [1536, T], BF16)
    k.gaT = scratch("gaT", [768, T], BF16)
    k.ghT = scratch("ghT", [768, T], BF16)
    k.mT = scratch("mT", [2048, T], BF16)
    k.uraw = scratch("uraw", [2304, T], BF16)
    k.uT = scratch("uT", [2304, T], BF16)
    k.vaug = scratch("vaug", [T, 1536], BF16)
    k.oaT = scratch("oaT", [768, T], BF16)
    k.ohT = scratch("ohT", [768, T], BF16)
    k.KFd = scratch("KFd", [2, NCB, 128, 128 * 3 * CB], BF16)
    k.Avec = scratch("Avec", [12, 1152], BF16)
    k.modrep = scratch("modrep", [2, 3 * DM], F32)

    with ExitStack() as top:
        sems = {e: top.enter_context(nc.semaphore("sem_" + e)) for e in ("pe", "act", "dve", "pool")}
        sems["sp"] = None
        ring = [top.enter_context(nc.semaphore("dr%d" % i)) for i in range(24)]
        S = Sched(nc, sems, ring)
        k.S = S
        pp = [top.enter_context(nc.psum_tensor("psb%d" % i, [128, 1024], F32)) for i in range(4)]
        ps = []
        for i in range(4):
            ps.append(pp[i][:, 0:512])
            ps.append(pp[i][:, 512:1024])
        k.ps = ps
        k.pp = pp
        k.ident = top.enter_context(nc.sbuf_tensor("s_ident", [128, 128], BF16))
        k.fg_rep = top.enter_context(nc.sbuf_tensor("s_fg_rep", [128, DM], F32))
        S.dma(k.ident[:], din["ident"][:, :], writes=["ident"])
        k.epsc = top.enter_context(nc.sbuf_tensor("s_epsc", [128, 2], F32))
        S.op("pool", lambda e: e.memset(k.epsc[:], EPS), writes=["epsc"])
        S.dma(k.fg_rep[:], din["final_g"][0:1, :].partition_broadcast(128), writes=["fg_rep"])

        if "p0" in phases:
            phase0(k)
            S.barrier()
        k.fuse1b = ("p1b" in phases and "pk" in phases)
        if "p1" in phases:
            phase1(k)
            S.barrier()
        if "pk" in phases:
            phaseK(k)
            S.barrier()
        if "p1b" in phases and not k.fuse1b:
            phase1b(k)
            S.barrier()
        if "pa" in phases:
            phaseA(k)
            S.barrier()
        if "ph" in phases:
            phaseH(k)
            S.barrier()
        if "pf" in phases:
            phaseF(k)
            S.barrier()
        S.flush()
    return nc


def phase0(k):
    nc, S, din, ps = k.nc, k.S, k.din, k.ps
    with ExitStack() as st:
        wada = st.enter_context(nc.sbuf_tensor("s_wada", [128, 8, 3 * DM], F32))
        cT = st.enter_context(nc.sbuf_tensor("s_cT", [128, 8, 2], F32))
        scT = st.enter_context(nc.sbuf_tensor("s_scT", [128, 8, 2], F32))
        screp = st.enter_context(nc.sbuf_tensor("s_screp", [128, 2, 8, 128], F32))
        brep = st.enter_context(nc.sbuf_tensor("s_brep", [128, 3 * DM], F32))
        ngrep = st.enter_context(nc.sbuf_tensor("s_ngrep", [128, DM], F32))
        k.modr = st.enter_context(nc.sbuf_tensor("s_modr", [128, 2, 3 * DM], F32))
        S.dma(cT[:], din["cT"][:, :, :], writes=["cT"])
        for kk in range(8):
            S.dma(wada[:, kk, :], din["w_ada"][kk * 128:(kk + 1) * 128, :], writes=[("wada", kk)])
        S.dma(brep[:], din["b_ada"][0:1, :].partition_broadcast(128), writes=["brep"])
        S.dma(ngrep[:], din["norm_g"][0:1, :].partition_broadcast(128), writes=["ngrep"])
        S.op("act", lambda e: e.activation(out=scT[:], in_=cT[:], func=AF.Silu), reads=["cT"], writes=["scT"])
        for s in range(2):
            S.op("dve", lambda e, s=s: e.tensor_copy(out=screp[:, s, :, :], in_=bc(scT[:, :, s:s + 1], [128, 8, 128])),
                 reads=["scT"], writes=[("screp", s)])
        for s in range(2):
            for cc in range(6):
                pb = ps[(s * 6 + cc) % 4]
                pr = ("ps", (s * 6 + cc) % 4)
                for kk in range(8):
                    S.op("pe", lambda e, s=s, cc=cc, kk=kk, pb=pb: e.matmul(
                        pb[:, :], lhsT=screp[:, s, kk, :], rhs=wada[:, kk, cc * 512:(cc + 1) * 512],
                        start=(kk == 0), stop=(kk == 7)),
                        reads=[("screp", s), ("wada", kk)], writes=[pr])
                S.op("dve", lambda e, s=s, cc=cc, pb=pb: e.tensor_tensor(
                    out=k.modr[:, s, cc * 512:(cc + 1) * 512], in0=pb[:, :], in1=brep[:, cc * 512:(cc + 1) * 512], op=ALU.add),
                    reads=[pr, "brep"], writes=[("modr", s)])
            S.op("dve", lambda e, s=s: e.scalar_tensor_tensor(
                out=k.modr[:, s, DM:2 * DM], in0=k.modr[:, s, DM:2 * DM], scalar=1.0, in1=ngrep[:], op0=ALU.add, op1=ALU.mult),
                reads=[("modr", s), "ngrep"], writes=[("modr", s)])
            S.dma(k.modrep[s:s + 1, :], k.modr[0:1, s, :], reads=[("modr", s)])
        S.barrier()


def phase1(k):
    nc, S, din, ps = k.nc, k.S, k.din, k.ps
    with ExitStack() as st:
        winb = st.enter_context(nc.sbuf_tensor("s_winb", [128, 8, 8192], BF16))
        stg_ctx = ExitStack()
        stg = [stg_ctx.enter_context(nc.sbuf_tensor("s_wstg%d" % i, [128, 2048], F32)) for i in range(2)]
        n = 0
        for kk in range(8):
            for cc in range(4):
                b = n % 2
                S.dma(stg[b][:], din["w_in"][kk * 128:(kk + 1) * 128, cc * 2048:(cc + 1) * 2048], writes=[("wstg", b)])
                eng = ("act", "dve", "pool")[n % 3]
                if eng == "act":
                    S.op("act", lambda e, b=b, kk=kk, cc=cc: e.copy(out=winb[:, kk, cc * 2048:(cc + 1) * 2048], in_=stg[b][:]),
                         reads=[("wstg", b)], writes=[("winb", kk)])
                else:
                    S.op(eng, lambda e, b=b, kk=kk, cc=cc: e.tensor_copy(out=winb[:, kk, cc * 2048:(cc + 1) * 2048], in_=stg[b][:]),
                         reads=[("wstg", b)], writes=[("winb", kk)])
                n += 1
        S.barrier()
        stg_ctx.close()
        modr1 = st.enter_context(nc.sbuf_tensor("s_modr1", [128, 2, 2 * DM], F32))
        for s_ in range(2):
            S.dma(modr1[:, s_, :], k.modrep[s_:s_ + 1, 0:2 * DM].partition_broadcast(128), writes=[("modr", s_)])
        xt = [st.enter_context(nc.sbuf_tensor("s_xt%d" % i, [128, DM], F32)) for i in range(2)]
        xm = [st.enter_context(nc.sbuf_tensor("s_xm%d" % i, [128, DM], F32)) for i in range(1)]
        hb = [st.enter_context(nc.sbuf_tensor("s_hb%d" % i, [128, DM], BF16)) for i in range(4)]
        sq = st.enter_context(nc.sbuf_tensor("s_sqj", [128, DM], BF16))
        ss = [st.enter_context(nc.sbuf_tensor("s_ss%d" % i, [128, 2], F32)) for i in range(3)]
        hT = [st.enter_context(nc.sbuf_tensor("s_hT%d" % i, [128, 8, 512], BF16)) for i in range(2)]
        ev = [st.enter_context(nc.sbuf_tensor("s_ev%d" % i, [128, 512], BF16)) for i in range(6)]
        vst = [st.enter_context(nc.sbuf_tensor("s_vst%d" % i, [128, 12, 128], BF16)) for i in range(2)]
        for i in range(2):
            S.op("pool", lambda e, i=i: e.memset(vst[i][:], 1.0), writes=[("vst", i)])

        blocks = []
        for j in range(6):
            blocks.append((OQ + j * 128, k.qkT, j * 128, "q"))
        for j in range(6):
            blocks.append((OK_ + j * 128, k.qkT, 768 + j * 128, "copy"))
        for j in range(18):
            blocks.append((OU + j * 128, k.uraw, j * 128, "copy"))
        for j in range(6):
            blocks.append((OGA + j * 128, k.gaT, j * 128, "silu"))
        for j in range(6):
            blocks.append((OGH + j * 128, k.ghT, j * 128, "silu"))
        for j in range(8):
            blocks.append((OMA + j * 128, k.mT, j * 128, "sig"))
        for j in range(8):
            blocks.append((OMH + j * 128, k.mT, 1024 + j * 128, "sig"))

        tcount = 0
        evn = 0

        def prep(ci):
            seg = 0 if ci < NCHUNK // 2 else 1
            hTc = hT[ci % 2]
            hres = ("hT", ci % 2)
            for tt in range(4):
                t = ci * 4 + tt
                xb, xr = xt[t % 2], ("xt", t % 2)
                sb, sr = ss[t % 3], ("ss", t % 3)
                mb, mr = xm[0], ("xm", 0)
                hbb, hbr = hb[t % 4], ("hb", t % 4)
                S.dma(xb[:], din["x"][t * 128:(t + 1) * 128, :], writes=[xr])
                S.op("act", lambda e, xb=xb, sb=sb: e.activation(out=sq[:], in_=xb[:], func=AF.Square, scale=1.0 / 32.0,
                                                                 accum_out=sb[:, 0:1]),
                     reads=[xr], writes=["sqj", sr])
                S.op("act", lambda e, sb=sb: e.activation(out=sb[:, 1:2], in_=sb[:, 0:1], func=AF.Sqrt, bias=k.epsc[:, 0:1]),
                     reads=[sr], writes=[sr])
                S.op("dve", lambda e, sb=sb: e.reciprocal(out=sb[:, 1:2], in_=sb[:, 1:2]), reads=[sr], writes=[sr])
                S.op("dve", lambda e, xb=xb, sb=sb, mb=mb, seg=seg: e.scalar_tensor_tensor(
                    out=mb[:], in0=xb[:], scalar=sb[:, 1:2], in1=modr1[:, seg, DM:2 * DM], op0=ALU.mult, op1=ALU.mult),
                    reads=[xr, sr, ("modr", seg)], writes=[mr])
                S.op("dve", lambda e, mb=mb, hbb=hbb, seg=seg: e.tensor_tensor(
                    out=hbb[:], in0=mb[:], in1=modr1[:, seg, 0:DM], op=ALU.add),
                    reads=[mr, ("modr", seg)], writes=[hbr])

        def prepB(ci):
            hTc = hT[ci % 2]
            hres = ("hT", ci % 2)
            for tt in range(4):
                t = ci * 4 + tt
                hbb, hbr = hb[t % 4], ("hb", t % 4)
                pbank = ps[6 + (t % 2)]
                pres = ("ps", 6 + (t % 2))
                pT = pbank[:].bitcast(BF16)
                for kk in range(8):
                    S.op("pe", lambda e, kk=kk, pT=pT, hbb=hbb: e.transpose(
                        out=pT[:, kk * 128:(kk + 1) * 128], in_=hbb[:, kk * 128:(kk + 1) * 128], identity=k.ident[:]),
                        reads=[hbr, "ident"], writes=[pres])
                S.op("act", lambda e, pT=pT, hTc=hTc, tt=tt: e.copy(
                    out=hTc[:, :, tt * 128:(tt + 1) * 128], in_=pT.rearrange("p (k t) -> p k t", k=8)),
                    reads=[pres], writes=[hres])
        def run_blocks(ci):
            nonlocal evn
            hTc = hT[ci % 2]
            hres = ("hT", ci % 2)
            for bi, (wc, dst, drow, kind) in enumerate(blocks):
                pb = ps[bi % 4]
                pr = ("ps", bi % 4)
                for kk in range(8):
                    S.op("pe", lambda e, kk=kk, pb=pb, wc=wc, hTc=hTc: e.matmul(
                        pb[:, :], lhsT=winb[:, kk, wc:wc + 128], rhs=hTc[:, kk, :], start=(kk == 0), stop=(kk == 7)),
                        reads=[hres, ("winb", kk)], writes=[pr])
                eb, er = ev[evn % 6], ("ev", evn % 6)
                evn += 1
                if kind == "q":
                    S.op("act", lambda e, pb=pb, eb=eb: e.activation(out=eb[:], in_=pb[:, :], func=AF.Copy, scale=0.125),
                         reads=[pr], writes=[er])
                elif kind == "copy":
                    S.op("dve", lambda e, pb=pb, eb=eb: e.tensor_copy(out=eb[:], in_=pb[:, :]), reads=[pr], writes=[er])
                elif kind == "silu":
                    S.op("act", lambda e, pb=pb, eb=eb: e.activation(out=eb[:], in_=pb[:, :], func=AF.Silu), reads=[pr], writes=[er])
                else:
                    S.op("act", lambda e, pb=pb, eb=eb: e.activation(out=eb[:], in_=pb[:, :], func=AF.Sigmoid), reads=[pr], writes=[er])
                S.dma(dst[drow:drow + 128, ci * 512:(ci + 1) * 512], eb[:], reads=[er], eng="pool")
            for tt in range(4):
                t = ci * 4 + tt
                pa, pb2 = ps[4], ps[5]
                for kk in range(8):
                    S.op("pe", lambda e, kk=kk, tt=tt, hTc=hTc: e.matmul(
                        ps[4][:, :], lhsT=hTc[:, kk, tt * 128:(tt + 1) * 128], rhs=winb[:, kk, OV:OV + 512],
                        start=(kk == 0), stop=(kk == 7)), reads=[hres, ("winb", kk)], writes=[("ps", 4)])
                for kk in range(8):
                    S.op("pe", lambda e, kk=kk, tt=tt, hTc=hTc: e.matmul(
                        ps[5][:, 0:256], lhsT=hTc[:, kk, tt * 128:(tt + 1) * 128], rhs=winb[:, kk, OV + 512:OV + 768],
                        start=(kk == 0), stop=(kk == 7)), reads=[hres, ("winb", kk)], writes=[("ps", 5)])
                vb, vr = vst[t % 2], ("vst", t % 2)
                def vdst(vb, p0, npair):
                    base = vb[:, 2 * p0:2 * p0 + 1, 0:1]
                    return AP(base.tensor, base.offset, [list(base.ap[0]), [256, npair], [192, 2], [1, 64]])
                S.op("dve", lambda e, vb=vb, vdst=vdst: e.tensor_copy(
                    out=vdst(vb, 0, 4), in_=ps[4][:, :].rearrange("p (a b c) -> p a b c", a=4, b=2)),
                    reads=[("ps", 4)], writes=[vr])
                S.op("dve", lambda e, vb=vb, vdst=vdst: e.tensor_copy(
                    out=vdst(vb, 4, 2), in_=ps[5][:, 0:256].rearrange("p (a b c) -> p a b c", a=2, b=2)),
                    reads=[("ps", 5)], writes=[vr])
                S.dma(k.vaug[t * 128:(t + 1) * 128, :], vb[:].rearrange("p a b -> p (a b)"), reads=[vr], eng="pool")

        prep(0)
        prepB(0)
        for ci in range(NCHUNK):
            if ci + 1 < NCHUNK:
                prep(ci + 1)
            run_blocks(ci)
            if ci + 1 < NCHUNK:
                prepB(ci + 1)


def core_assignment():
    return [("p", 0), ("p", 1), ("s", 0, 1), ("s", 2, 3), ("s", 4, 5), ("s", 6, 7), ("s", 6, 7), ("s", 6, 7)]


def prep_core_inputs(inp, role):
    f32 = lambda a: np.ascontiguousarray(np.asarray(a, dtype=np.float32))
    m = {}
    if role[0] == "p":
        b = role[1]
        m["x"] = f32(inp["x_prompt"][b])
        c2 = np.stack([inp["c_prompt"][b], inp["c_prompt"][b]], 0)
    else:
        m["x"] = f32(np.concatenate([inp["x_sample"][role[1]], inp["x_sample"][role[2]]], 0))
        c2 = np.stack([inp["c_sample"][role[1]], inp["c_sample"][role[2]]], 0)
    c2 = np.asarray(c2, np.float32)
    m["cT"] = f32(c2.reshape(2, 8, 128).transpose(2, 1, 0))
    m["w_ada"] = f32(inp["w_ada"][0])
    m["b_ada"] = f32(inp["b_ada"][0][None])
    m["norm_g"] = f32(inp["norm_g"][0][None])
    m["w_in"] = f32(inp["w_in"][0])
    m["short_wT"] = f32(np.asarray(inp["short_w"][0]).reshape(3, 18, 128).transpose(2, 1, 0))
    m["short_bT"] = f32(np.asarray(inp["short_b"][0]).reshape(18, 128).T)
    m["filt_w1"] = f32(inp["filt_w1"][0])
    m["filt_b1"] = f32(np.asarray(inp["filt_b1"][0])[:, None])
    m["filt_fr1"] = f32(np.asarray(inp["filt_freq1"][0])[:, None])
    m["filt_w2"] = f32(inp["filt_w2"][0])
    m["filt_b2"] = f32(np.asarray(inp["filt_b2"][0])[:, None])
    m["filt_fr2"] = f32(np.asarray(inp["filt_freq2"][0])[:, None])
    m["filt_w3"] = f32(inp["filt_w3"][0])
    m["hyena_d"] = f32(inp["hyena_d"][0])
    m["w_proj_attn"] = f32(inp["w_proj_attn"][0])
    m["w_proj_hyena"] = f32(inp["w_proj_hyena"][0])
    m["w_out"] = f32(inp["w_out"][0])
    m["rel_bias"] = f32(inp["rel_bias"])
    m["final_g"] = f32(np.asarray(inp["final_g"])[None])
    m.update(build_consts(role[0] == "p"))
    return m


_NC_CACHE = {}


def kernel(**inputs):
    inp = {k_: np.asarray(v) for k_, v in inputs.items()}
    if "full" not in _NC_CACHE:
        _NC_CACHE["full"] = build_program()
    nc = _NC_CACHE["full"]
    roles = core_assignment()
    in_maps = [prep_core_inputs(inp, r) for r in roles]
    res = run_bass_kernel_spmd(nc, in_maps, core_ids=list(range(8)))
    outs = [np.asarray(r["y"], dtype=np.float32) for r in res.results]
    y_prompt = np.stack([outs[0], outs[1]], 0)
    ys = []
    for c in range(2, 6):
        ys.append(outs[c][:8192])
        ys.append(outs[c][8192:])
    y_sample = np.stack(ys, 0)
    return (y_prompt, y_sample)


def phase1b_gen(k, st, W=512):
    nc, S, din = k.nc, k.S, k.din
    swT = st.enter_context(nc.sbuf_tensor("s_swT", [128, 18, 3], F32))
    sbT = st.enter_context(nc.sbuf_tensor("s_sbT", [128, 18], F32))
    bfl = st.enter_context(nc.sbuf_tensor("s_bfl", [128, 1], F32))
    S.dma(swT[:], din["short_wT"][:, :, :], writes=["swT"])
    S.dma(sbT[:], din["short_bT"][:, :], writes=["sbT"])
    S.dma(bfl[:], din["bflag"][:, :], writes=["bfl"])
    ib = [st.enter_context(nc.sbuf_tensor("s_cin%d" % i, [128, W + 2], BF16)) for i in range(3)]
    t1 = [st.enter_context(nc.sbuf_tensor("s_ct%d" % i, [128, W], F32)) for i in range(2)]
    ob = [st.enter_context(nc.sbuf_tensor("s_cout%d" % i, [128, W], BF16)) for i in range(3)]
    n = 0
    for ub in range(18):
        for tc in range(T // W):
            a, ar = ib[n % 3], ("cin", n % 3)
            tb, tr = t1[n % 2], ("ct", n % 2)
            o, orr = ob[n % 3], ("cout", n % 3)
            lo = tc * W - 1
            hi = tc * W + W + 1
            c0 = 0
            if lo < 0:
                S.op("pool", lambda e, a=a: e.memset(a[:, 0:1], 0.0), writes=[ar])
                lo, c0 = 0, 1
            c1 = W + 2
            if hi > T:
                S.op("pool", lambda e, a=a: e.memset(a[:, W + 1:W + 2], 0.0), writes=[ar])
                hi, c1 = T, W + 1
            S.dma(a[:, c0:c1], k.uraw[ub * 128:(ub + 1) * 128, lo:hi], writes=[ar])
            if tc * W == T // 2:
                S.op("pool", lambda e, a=a: e.tensor_scalar(out=a[:, 0:1], in0=a[:, 0:1], scalar1=bfl[:, 0:1], scalar2=None,
                                                            op0=ALU.mult), reads=[ar, "bfl"], writes=[ar])
            if tc * W + W == T // 2:
                S.op("pool", lambda e, a=a: e.tensor_scalar(out=a[:, W + 1:W + 2], in0=a[:, W + 1:W + 2], scalar1=bfl[:, 0:1],
                                                            scalar2=None, op0=ALU.mult), reads=[ar, "bfl"], writes=[ar])
            S.op("act", lambda e, a=a, tb=tb, ub=ub: e.activation(
                out=tb[:], in_=a[:, 1:W + 1], func=AF.Identity, scale=swT[:, ub, 1:2], bias=sbT[:, ub:ub + 1]),
                reads=[ar, "swT", "sbT"], writes=[tr])
            S.op("dve", lambda e, a=a, tb=tb, ub=ub: e.scalar_tensor_tensor(
                out=tb[:], in0=a[:, 0:W], scalar=swT[:, ub, 0:1], in1=tb[:], op0=ALU.mult, op1=ALU.add),
                reads=[ar, tr, "swT"], writes=[tr])
            S.op("dve", lambda e, a=a, tb=tb, o=o, ub=ub: e.scalar_tensor_tensor(
                out=o[:], in0=a[:, 2:W + 2], scalar=swT[:, ub, 2:3], in1=tb[:], op0=ALU.mult, op1=ALU.add),
                reads=[ar, tr, "swT"], writes=[orr])
            S.dma(k.uT[ub * 128:(ub + 1) * 128, tc * W:(tc + 1) * W], o[:], reads=[orr], eng="pool")
            n += 1
            yield


def phase1b(k):
    with ExitStack() as st:
        for _ in phase1b_gen(k, st, W=2048):
            pass


def sin_wrapped(S, src_ps, pres, dst, dres, scale_ap, bias_ap, tmp, tres, tmp2, t2res, nparts, ncols):
    PI = math.pi
    S.op("dve", lambda e: e.tensor_scalar(out=tmp[0:nparts, 0:ncols], in0=src_ps, scalar1=scale_ap, scalar2=bias_ap,
                                          op0=ALU.mult, op1=ALU.add), reads=[pres], writes=[tres])
    S.op("dve", lambda e: e.tensor_scalar(out=tmp2[0:nparts, 0:ncols], in0=tmp[0:nparts, 0:ncols], scalar1=PI, scalar2=-2 * PI,
                                          op0=ALU.is_gt, op1=ALU.mult), reads=[tres], writes=[t2res])
    S.op("dve", lambda e: e.tensor_tensor(out=tmp2[0:nparts, 0:ncols], in0=tmp2[0:nparts, 0:ncols], in1=tmp[0:nparts, 0:ncols],
                                          op=ALU.add), reads=[tres, t2res], writes=[t2res])
    S.op("dve", lambda e: e.tensor_scalar(out=tmp[0:nparts, 0:ncols], in0=tmp[0:nparts, 0:ncols], scalar1=-PI, scalar2=2 * PI,
                                          op0=ALU.is_lt, op1=ALU.mult), reads=[tres], writes=[tres])
    S.op("dve", lambda e: e.tensor_tensor(out=tmp[0:nparts, 0:ncols], in0=tmp2[0:nparts, 0:ncols], in1=tmp[0:nparts, 0:ncols],
                                          op=ALU.add), reads=[tres, t2res], writes=[tres])
    S.op("act", lambda e: e.activation(out=dst, in_=tmp[0:nparts, 0:ncols], func=AF.Sin), reads=[tres], writes=[dres])


def phaseK(k):
    nc, S, din, ps = k.nc, k.S, k.din, k.ps
    with ExitStack() as st:
        hdn = st.enter_context(nc.sbuf_tensor("s_hdn2T", [128, T], BF16))
        w3sd = st.enter_context(nc.sbuf_tensor("s_w3sd", [128, 2, NCB, 2, CB], BF16))
        S.op("pool", lambda e: e.memset(hdn[64:128, :], 0.0), writes=["hdn"])
        S.op("pool", lambda e: e.memset(w3sd[64:128], 0.0), writes=["w3sd"])
        with ExitStack() as s2:
            w1 = s2.enter_context(nc.sbuf_tensor("s_fw1", [33, 64], F32))
            w2 = s2.enter_context(nc.sbuf_tensor("s_fw2", [64, 64], F32))
            w3 = s2.enter_context(nc.sbuf_tensor("s_fw3", [64, 3072], F32))
            fv = s2.enter_context(nc.sbuf_tensor("s_fv", [64, 6], F32))
            zc = [s2.enter_context(nc.sbuf_tensor("s_zc%d" % i, [33, 512], F32)) for i in range(2)]
            ta = s2.enter_context(nc.sbuf_tensor("s_fta", [64, 512], F32))
            tb = s2.enter_context(nc.sbuf_tensor("s_ftb", [64, 512], F32))
            h1 = s2.enter_context(nc.sbuf_tensor("s_fh1", [64, 512], F32))
            S.dma(w1[:], din["filt_w1"][:, :], writes=["fw1"])
            S.dma(w2[:], din["filt_w2"][:, :], writes=["fw2"])
            S.dma(w3[:], din["filt_w3"][:, :], writes=["fw3"])
            for i, nm in enumerate(("filt_b1", "filt_fr1", "filt_b2", "filt_fr2")):
                S.dma(fv[:, i:i + 1], din[nm][:, :], writes=["fv"])
            S.op("dve", lambda e: e.tensor_tensor(out=fv[:, 4:5], in0=fv[:, 0:1], in1=fv[:, 1:2], op=ALU.mult), reads=["fv"], writes=["fv"])
            S.op("dve", lambda e: e.tensor_tensor(out=fv[:, 5:6], in0=fv[:, 2:3], in1=fv[:, 3:4], op=ALU.mult), reads=["fv"], writes=["fv"])
            w3v = w3[:].rearrange("p (o d b c) -> p o d b c", o=2, d=2, b=NCB)
            for o in range(2):
                S.op("dve", lambda e, o=o: e.tensor_tensor(out=w3sd[0:64, o, :, 0, :], in0=w3v[:, o, 0], in1=w3v[:, o, 1], op=ALU.add),
                     reads=["fw3"], writes=["w3sd"])
                S.op("dve", lambda e, o=o: e.tensor_tensor(out=w3sd[0:64, o, :, 1, :], in0=w3v[:, o, 0], in1=w3v[:, o, 1], op=ALU.subtract),
                     reads=["fw3"], writes=["w3sd"])
            for ci in range(T // 512):
                z, zr = zc[ci % 2], ("zc", ci % 2)
                S.dma(z[:], din["zfT"][:, ci * 512:(ci + 1) * 512], writes=[zr])
                S.op("pe", lambda e, z=z: e.matmul(ps[0][0:64, :], lhsT=w1[:], rhs=z[:], start=True, stop=True),
                     reads=[zr, "fw1"], writes=[("ps", 0)])
                sin_wrapped(S, ps[0][0:64, :], ("ps", 0), h1[:], "fh1", fv[:, 1:2], fv[:, 4:5], ta, "fta", tb, "ftb", 64, 512)
                S.op("pe", lambda e: e.matmul(ps[1][0:64, :], lhsT=w2[:], rhs=h1[:], start=True, stop=True),
                     reads=["fh1", "fw2"], writes=[("ps", 1)])
                sin_wrapped(S, ps[1][0:64, :], ("ps", 1), hdn[0:64, ci * 512:(ci + 1) * 512], "hdn", fv[:, 3:4], fv[:, 5:6],
                            ta, "fta", tb, "ftb", 64, 512)
            S.barrier()
        H = st.enter_context(nc.sbuf_tensor("s_H", [128, 128, 2, CB], BF16))
        Yk = st.enter_context(nc.sbuf_tensor("s_Yk", [128, 4, 128, CB], BF16))
        decs = [st.enter_context(nc.sbuf_tensor("s_dec%d" % i, [128, 128, CB], BF16)) for i in range(2)]
        negtn = st.enter_context(nc.sbuf_tensor("s_negtn", [128, 128], F32))
        drep = st.enter_context(nc.sbuf_tensor("s_drep", [128, 768], F32))
        EkS = st.enter_context(nc.sbuf_tensor("s_EkS", [128, 256], BF16))
        EkD = st.enter_context(nc.sbuf_tensor("s_EkD", [128, 256], BF16))
        Mfb = [st.enter_context(nc.sbuf_tensor("s_Mfb%d" % i, [128, 8, 256], BF16)) for i in range(3)]
        KFs = [st.enter_context(nc.sbuf_tensor("s_KFs%d" % i, [128, 4, 3, CB], BF16)) for i in range(3)]
        dK = st.enter_context(nc.sbuf_tensor("s_dK", [128, 2, CB], F32))
        S.dma(negtn[:], din["negtn"][:, :], writes=["negtn"])
        S.dma(drep[:], din["deltas_rep"][:, :], writes=["drep"])
        S.dma(EkS[:], din["E_kS"][:, :], writes=["EkS"])
        S.dma(EkD[:], din["E_kD"][:, :], writes=["EkD"])
        mfn = 0
        kfn = 0
        pn = 0
        cgen = phase1b_gen(k, st, W=512) if k.fuse1b else None
        cstep = [0]

        def cstepf():
            cstep[0] += 1
            if cgen is not None and cstep[0] % 4 == 0:
                next(cgen, None)

        def ykv(k2, off):
            b0 = off // 128
            b = Yk[:, b0:b0 + 1, k2:k2 + 1, 0:1]
            return AP(b.tensor, b.offset, [list(b.ap[0]), [2 * 128 * CB, 2], [1, CB]])

        def emit_dec(cbx, n1s):
            dd = decs[cbx % 2]
            for n1 in n1s:
                S.op("act", lambda e, n1=n1, dd=dd, cbx=cbx: e.activation(out=dd[:, n1, :], in_=drep[:, cbx * CB:(cbx + 1) * CB], func=AF.Exp,
                                                                        scale=negtn[:, n1:n1 + 1]),
                     reads=["drep", "negtn"], writes=[("dec", cbx % 2)])

        emit_dec(0, range(128))
        for cb in range(NCB):
            c0 = cb * CB
            dec = decs[cb % 2]
            for o in range(2):
                S.dma(dK[:, o, :], din["hyena_d"][o:o + 1, c0:c0 + CB].partition_broadcast(128), writes=["dK"])
            for o in range(2):
                for g in range(32):
                    pb, pr = ps[pn % 4], ("ps", pn % 4)
                    pn += 1
                    for j in range(4):
                        n1 = 4 * g + j
                        S.op("pe", lambda e, pb=pb, j=j, n1=n1, o=o, cb=cb: e.matmul(
                            pb[:, j * 128:(j + 1) * 128], lhsT=hdn[:, n1:T:128],
                            rhs=w3sd[:, o, cb].rearrange("p a b -> p (a b)"), start=True, stop=True),
                            reads=["hdn", "w3sd"], writes=[pr])
                    hout = H[:, 4 * g:4 * g + 4, :, :]
                    dv = dec[:, 4 * g:4 * g + 1, 0:1]
                    din1 = AP(dv.tensor, dv.offset, [list(dv.ap[0]), [CB, 4], [0, 2], [1, CB]])
                    S.op("dve", lambda e, pb=pb, hout=hout, din1=din1: e.tensor_tensor(
                        out=hout, in0=pb[:, :].rearrange("p (j s c) -> p j s c", j=4, s=2), in1=din1, op=ALU.mult),
                        reads=[pr, ("dec", cb % 2)], writes=["H"])
                    cstepf()
                for cp in range(CB // 2):
                    pi = pn % 2
                    pn += 1
                    prl = [("ps", 2 * pi), ("ps", 2 * pi + 1)]
                    for h in range(2):
                        c = 2 * cp + h
                        pb = ps[2 * pi + h]
                        S.op("pe", lambda e, pb=pb, c=c: e.matmul(pb[:, 0:256], lhsT=H[:, :, 0, c], rhs=EkS[:], start=True, stop=True),
                             reads=["H", "EkS"], writes=prl)
                        S.op("pe", lambda e, pb=pb, c=c: e.matmul(pb[:, 256:512], lhsT=H[:, :, 1, c], rhs=EkD[:], start=True, stop=True),
                             reads=["H", "EkD"], writes=prl)
                    yv = Yk[:, 0:1, 0:1, 2 * cp:2 * cp + 1]
                    yout = AP(yv.tensor, yv.offset, [list(yv.ap[0]), [CB, 512], [1, 2]])
                    pin = k.pp[pi][:, :].rearrange("p (h x) -> p x h", h=2)
                    if cp % 2 == 0:
                        S.op("act", lambda e, yout=yout, pin=pin: e.copy(out=yout, in_=pin), reads=prl, writes=["Yk"])
                    else:
                        S.op("dve", lambda e, yout=yout, pin=pin: e.tensor_copy(out=yout, in_=pin), reads=prl, writes=["Yk"])
                    if o == 1 and cb + 1 < NCB:
                        emit_dec(cb + 1, range(4 * cp, 4 * cp + 4))
                    cstepf()
                for g in range(32):
                    pb, pr = ps[4 + pn % 4], ("ps", 4 + pn % 4)
                    pn += 1
                    for j in range(4):
                        k2 = 4 * g + j
                        if k2 % 8 == 0:
                            mb_, mr_ = Mfb[mfn % 3], ("Mfb", mfn % 3)
                            mfn += 1
                            S.dma(mb_[:], din["Mf"][:, k2:k2 + 8, :], writes=[mr_])
                        S.op("pe", lambda e, pb=pb, j=j, k2=k2, mb_=mb_: e.matmul(
                            pb[:, j * 128:(j + 1) * 128], lhsT=mb_[:, k2 % 8, 0:128],
                            rhs=ykv(k2, 0), start=True, stop=False),
                            reads=["Yk", mr_], writes=[pr])
                        S.op("pe", lambda e, pb=pb, j=j, k2=k2, mb_=mb_: e.matmul(
                            pb[:, j * 128:(j + 1) * 128], lhsT=mb_[:, k2 % 8, 128:256],
                            rhs=ykv(k2, 128), start=False, stop=True),
                            reads=["Yk", mr_], writes=[pr])
                    kb, kr = KFs[kfn % 3], ("KFs", kfn % 3)
                    kfn += 1
                    pv = pb[:, :].rearrange("p (j s c) -> p j s c", j=4, s=2)
                    S.op("dve", lambda e, kb=kb, pv=pv, o=o: e.tensor_tensor(out=kb[:, :, 0, :], in0=pv[:, :, 0, :],
                                                                             in1=bc(dK[:, o:o + 1, :], [128, 4, CB]), op=ALU.add),
                         reads=[pr, "dK"], writes=[kr])
                    S.op("act", lambda e, kb=kb, pv=pv: e.copy(out=kb[:, :, 1, :], in_=pv[:, :, 1, :]), reads=[pr], writes=[kr])
                    S.op("dve", lambda e, kb=kb, pv=pv: e.tensor_scalar(out=kb[:, :, 2, :], in0=pv[:, :, 1, :], scalar1=-1.0, scalar2=None,
                                                                        op0=ALU.mult), reads=[pr], writes=[kr])
                    S.dma(k.KFd[o, cb, :, g * 4 * 3 * CB:(g + 1) * 4 * 3 * CB], kb[:].rearrange("p a b c -> p (a b c)"), reads=[kr], eng="pool")
                    cstepf()
        if cgen is not None:
            for _ in cgen:
                pass


def phaseH(k):
    nc, S, din, ps = k.nc, k.S, k.din, k.ps
    with ExitStack() as st:
        bufA = st.enter_context(nc.sbuf_tensor("s_hA", [128, CB, 128], BF16))
        bufB = st.enter_context(nc.sbuf_tensor("s_hB", [128, CB, 128], BF16))
        bufC = st.enter_context(nc.sbuf_tensor("s_hC", [128, 128, CB], BF16))
        Yd = st.enter_context(nc.sbuf_tensor("s_Yd", [128, 3, 128, CB], BF16))
        Pb = st.enter_context(nc.sbuf_tensor("s_P", [128, 128, 2, CB], BF16))
        Zs = st.enter_context(nc.sbuf_tensor("s_Zs", [128, 2, 128, CB], BF16))
        Ed = st.enter_context(nc.sbuf_tensor("s_Ed", [128, 384], BF16))
        G1 = st.enter_context(nc.sbuf_tensor("s_G1", [128, 256], BF16))
        G2 = st.enter_context(nc.sbuf_tensor("s_G2", [128, 256], BF16))
        drep = st.enter_context(nc.sbuf_tensor("s_hdrep", [128, 2, CB], F32))
        Mb = [st.enter_context(nc.sbuf_tensor("s_Mb%d" % i, [128, 8, 256], BF16)) for i in range(4)]
        KFb = [st.enter_context(nc.sbuf_tensor("s_KFb%d" % i, [128, 4, 3, CB], BF16)) for i in range(6)]
        t1 = [st.enter_context(nc.sbuf_tensor("s_ht1_%d" % i, [128, 4, 2, CB], BF16)) for i in range(2)]
        t2 = [st.enter_context(nc.sbuf_tensor("s_ht2_%d" % i, [128, 4, 2, CB], BF16)) for i in range(2)]
        te = [st.enter_context(nc.sbuf_tensor("s_hte%d" % i, [128, 8, CB], F32)) for i in range(2)]
        S.dma(Ed[:], din["E_d"][:, :], writes=["Ed"])
        S.dma(G1[:], din["G1"][:, :], writes=["G1"])
        S.dma(G2[:], din["G2"][:, :], writes=["G2"])
        ohs = AP(Pb[:].tensor, Pb[:].offset, [[Pb[:].ap[0][0], 64], [1, T]])
        mn = 0
        kn = 0
        tn_ = 0
        pn = 0

        def ydv(k2, off):
            b0 = off // 128
            return Yd[:, b0:b0 + 2, k2, :]

        def load_blk(buf, res, row0):
            src = AP(k.uT.tensor, k.uT[row0:row0 + 1, 0:1].offset, [[128, 128], [T, CB], [1, 128]])
            S.dma(buf[:], src, writes=[res])

        for cb in range(NCB):
            c0 = cb * CB
            load_blk(bufA, "hA", c0)
            load_blk(bufB, "hB", 768 + c0)
            for o in range(2):
                S.dma(drep[:, o, :], din["hyena_d"][o:o + 1, c0:c0 + CB].partition_broadcast(128), writes=["hdrep"])
            for o in range(2):
                Din, dres = (bufA, "hA") if o == 0 else (bufC, "hC")
                for cp in range(CB // 2):
                    pi = pn % 2
                    pn += 1
                    prl = [("ps", 2 * pi), ("ps", 2 * pi + 1)]
                    for h in range(2):
                        c = 2 * cp + h
                        pb = ps[2 * pi + h]
                        S.op("pe", lambda e, pb=pb, c=c, Din=Din, o=o: e.matmul(pb[:, 0:384], lhsT=(Din[:, c, :] if o == 0 else Din[:, :, c]),
                                                                             rhs=Ed[:], start=True, stop=True),
                             reads=[dres, "Ed"], writes=prl)
                    yv = Yd[:, 0:1, 0:1, 2 * cp:2 * cp + 1]
                    yout = AP(yv.tensor, yv.offset, [list(yv.ap[0]), [CB, 384], [1, 2]])
                    pin = k.pp[pi][:, :].rearrange("p (h x) -> p x h", h=2)[:, 0:384, :]
                    if cp % 2 == 0:
                        S.op("act", lambda e, yout=yout, pin=pin: e.copy(out=yout, in_=pin), reads=prl, writes=["Yd"])
                    else:
                        S.op("dve", lambda e, yout=yout, pin=pin: e.tensor_copy(out=yout, in_=pin), reads=prl, writes=["Yd"])
                for g in range(32):
                    pb, pr = ps[4 + pn % 4], ("ps", 4 + pn % 4)
                    pn += 1
                    kb, kr = KFb[kn % 6], ("KFb", kn % 6)
                    kn += 1
                    S.dma(kb[:].rearrange("p a b c -> p (a b c)"), k.KFd[o, cb, :, g * 12 * CB:(g + 1) * 12 * CB], writes=[kr])
                    for j in range(4):
                        k2 = 4 * g + j
                        if k2 % 8 == 0:
                            mb_, mr_ = Mb[mn % 4], ("Mb", mn % 4)
                            mn += 1
                            S.dma(mb_[:], din["Mf"][:, k2:k2 + 8, :], writes=[mr_])
                        S.op("pe", lambda e, pb=pb, j=j, k2=k2, mb_=mb_: e.matmul(
                            pb[:, j * 128:(j + 1) * 128], lhsT=mb_[:, k2 % 8, 0:128],
                            rhs=ydv(k2, 128), start=True, stop=False),
                            reads=["Yd", mr_], writes=[pr])
                        S.op("pe", lambda e, pb=pb, j=j, k2=k2, mb_=mb_: e.matmul(
                            pb[:, j * 128:(j + 1) * 128], lhsT=mb_[:, k2 % 8, 128:256],
                            rhs=ydv(k2, 0), start=False, stop=True),
                            reads=["Yd", mr_], writes=[pr])
                    a1, a1r = t1[tn_ % 2], ("ht1", tn_ % 2)
                    a2, a2r = t2[tn_ % 2], ("ht2", tn_ % 2)
                    tn_ += 1
                    pv = pb[:, :].rearrange("p (j s c) -> p j s c", j=4, s=2)
                    S.op("dve", lambda e, a1=a1, pv=pv, kb=kb: e.tensor_tensor(
                        out=a1[:], in0=pv, in1=bc(kb[:, :, 0:1, :], [128, 4, 2, CB]), op=ALU.mult), reads=[pr, kr], writes=[a1r])
                    S.op("dve", lambda e, a2=a2, pv=pv, kb=kb: e.tensor_tensor(
                        out=a2[:], in0=pv, in1=kb[:, :, 1:3, :], op=ALU.mult), reads=[pr, kr], writes=[a2r])
                    S.op("pool", lambda e, a1=a1, a2=a2, g=g: e.tensor_tensor(
                        out=Pb[:, 4 * g:4 * g + 4, 0, :], in0=a1[:, :, 0, :], in1=a2[:, :, 1, :], op=ALU.add),
                        reads=[a1r, a2r], writes=["P"])
                    S.op("pool", lambda e, a1=a1, a2=a2, g=g: e.tensor_tensor(
                        out=Pb[:, 4 * g:4 * g + 4, 1, :], in0=a1[:, :, 1, :], in1=a2[:, :, 0, :], op=ALU.add),
                        reads=[a1r, a2r], writes=["P"])
                for c2 in range(CB // 2):
                    pb, pr = ps[pn % 4], ("ps", pn % 4)
                    pn += 1
                    for h in range(2):
                        c = 2 * c2 + h
                        S.op("pe", lambda e, pb=pb, c=c, h=h: e.matmul(pb[:, h * 256:(h + 1) * 256], lhsT=Pb[:, :, 0, c], rhs=G1[:],
                                                                       start=True, stop=False), reads=["P", "G1"], writes=[pr])
                        S.op("pe", lambda e, pb=pb, c=c, h=h: e.matmul(pb[:, h * 256:(h + 1) * 256], lhsT=Pb[:, :, 1, c], rhs=G2[:],
                                                                       start=False, stop=True), reads=["P", "G2"], writes=[pr])
                    zv = Zs[:, 0:1, 0:1, 2 * c2:2 * c2 + 1]
                    zout = AP(zv.tensor, zv.offset, [list(zv.ap[0]), [CB, 256], [1, 2]])
                    pin = pb[:, :].rearrange("p (h x) -> p x h", h=2)
                    if c2 % 2 == 0:
                        S.op("act", lambda e, zout=zout, pin=pin: e.copy(out=zout, in_=pin), reads=[pr], writes=["Zs"])
                    else:
                        S.op("dve", lambda e, zout=zout, pin=pin: e.tensor_copy(out=zout, in_=pin), reads=[pr], writes=["Zs"])
                if o == 1:
                    load_blk(bufA, "hA", 1536 + c0)
                Xg, xres = (bufB, "hB") if o == 0 else (bufA, "hA")
                for g in range(16):
                    pb, pr = ps[4 + pn % 4], ("ps", 4 + pn % 4)
                    pn += 1
                    for j in range(8):
                        n1 = 8 * g + j
                        if n1 % 8 == 0:
                            mb_, mr_ = Mb[mn % 4], ("Mb", mn % 4)
                            mn += 1
                            S.dma(mb_[:], din["Minv"][:, n1:n1 + 8, :], writes=[mr_])
                        S.op("pe", lambda e, pb=pb, j=j, n1=n1, mb_=mb_: e.matmul(
                            pb[:, j * CB:(j + 1) * CB], lhsT=mb_[:, n1 % 8, 0:128], rhs=Zs[:, 0, n1, :], start=True, stop=False),
                            reads=["Zs", mr_], writes=[pr])
                        S.op("pe", lambda e, pb=pb, j=j, n1=n1, mb_=mb_: e.matmul(
                            pb[:, j * CB:(j + 1) * CB], lhsT=mb_[:, n1 % 8, 128:256], rhs=Zs[:, 1, n1, :], start=False, stop=True),
                            reads=["Zs", mr_], writes=[pr])
                    xin = Xg[:, :, 8 * g:8 * g + 8].rearrange("p c j -> p j c")
                    pv8 = pb[:, :].rearrange("p (j c) -> p j c", j=8)
                    if o == 0:
                        zo = bufC[:, 8 * g:8 * g + 8, :]
                        S.op("dve", lambda e, pv8=pv8, xin=xin, zo=zo: e.tensor_tensor(out=zo, in0=pv8, in1=xin, op=ALU.mult),
                             reads=[pr, xres], writes=["hC"])
                    else:
                        z3v = bufB[:].rearrange("p c j -> p (c j)")[:, 8 * g * CB:(8 * g + 8) * CB].rearrange("p (j c) -> p j c", j=8)
                        S.op("dve", lambda e, pv8=pv8, xin=xin, z3v=z3v: e.tensor_tensor(out=z3v, in0=pv8, in1=xin, op=ALU.mult),
                             reads=[pr, xres], writes=["hB"])
            z3 = bufB[:].rearrange("p c j -> p (c j)")
            for g in range(16):
                pb, pr = ps[pn % 4], ("ps", pn % 4)
                pn += 1
                pT = pb[:].bitcast(BF16)
                for j in range(8):
                    n1 = 8 * g + j
                    S.op("pe", lambda e, pT=pT, j=j, n1=n1: e.transpose(out=pT[0:CB, j * 128:(j + 1) * 128],
                                                                        in_=z3[:, n1 * CB:(n1 + 1) * CB], identity=k.ident[:]),
                         reads=["hB", "ident"], writes=[pr])
                ov = AP(ohs.tensor, ohs.offset + 8 * g, [list(ohs.ap[0]), [128, 128], [1, 8]])
                pin = pT[0:CB, :].rearrange("p (j n) -> p n j", j=8)
                if g % 2 == 0:
                    S.op("act", lambda e, ov=ov, pin=pin: e.copy(out=ov, in_=pin), reads=[pr], writes=["P"])
                else:
                    S.op("dve", lambda e, ov=ov, pin=pin: e.tensor_copy(out=ov, in_=pin), reads=[pr], writes=["P"])
            S.dma(k.ohT[c0:c0 + CB, :], ohs, reads=["P"])


def phaseA(k):
    nc, S, din, ps = k.nc, k.S, k.din, k.ps
    SPAN = 4096
    with ExitStack() as st:
        Hk = st.enter_context(nc.sbuf_tensor("s_Hk", [128, 3, 12, 256], BF16))
        J = st.enter_context(nc.sbuf_tensor("s_J", [128, 128], BF16))
        bm = st.enter_context(nc.sbuf_tensor("s_bm", [128, 256], BF16))
        swb = st.enter_context(nc.sbuf_tensor("s_swb", [128, 128], BF16))
        swf = st.enter_context(nc.sbuf_tensor("s_swf", [128, 128], F32))
        selA = st.enter_context(nc.sbuf_tensor("s_selA", [128, 128], F32))
        selB = st.enter_context(nc.sbuf_tensor("s_selB", [128, 128], F32))
        with ExitStack() as s2:
            rb = s2.enter_context(nc.sbuf_tensor("s_rb", [32, 12], F32))
            oh = s2.enter_context(nc.sbuf_tensor("s_oh", [32, 1152], F32))
            mr = s2.enter_context(nc.sbuf_tensor("s_mrow", [12, 1152], F32))
            av = s2.enter_context(nc.sbuf_tensor("s_av", [12, 1152], BF16))
            S.dma(rb[:], din["rel_bias"][:, :], writes=["rb"])
            S.dma(oh[:], din["OH"][:, :], writes=["oh"])
            S.dma(mr[:], din["mrow"][:, :], writes=["mrow"])
            for i in range(3):
                S.op("pe", lambda e, i=i: e.matmul(ps[i][0:12, 0:384], lhsT=rb[:], rhs=oh[:, i * 384:(i + 1) * 384], start=True, stop=True),
                     reads=["rb", "oh"], writes=[("ps", i)])
                S.op("dve", lambda e, i=i: e.tensor_tensor(out=av[:, i * 384:(i + 1) * 384], in0=ps[i][0:12, 0:384],
                                                           in1=mr[:, i * 384:(i + 1) * 384], op=ALU.add),
                     reads=[("ps", i), "mrow"], writes=["av"])
            S.dma(k.Avec[:, :], av[:], reads=["av"], writes=["Avec"])
            for h in range(12):
                for ri in range(3):
                    src = AP(k.Avec.tensor, k.Avec[h:h + 1, ri * 384:ri * 384 + 1].offset, [[1, 128], [1, 256]])
                    S.dma(Hk[:, ri, h, :], src, reads=["Avec"], writes=["Hk"])
            S.dma(J[:], din["antiid"][:, :], writes=["J"])
            S.dma(bm[:], din["bmask"][:, :], writes=["bm"])
            S.dma(swb[:], din["swap"][:, :], writes=["swb"])
            S.op("dve", lambda e: e.tensor_copy(out=swf[:], in_=swb[:]), reads=["swb"], writes=["swf"])
            S.op("pool", lambda e: e.memset(selA[:], 0.0), writes=["sel"])
            S.op("pool", lambda e: e.memset(selB[:], 0.0), writes=["sel"])
            S.op("dve", lambda e: e.tensor_copy(out=selA[:, 0:64], in_=swb[:, 0:64]), reads=["swb", "sel"], writes=["sel"])
            S.op("dve", lambda e: e.tensor_copy(out=selB[:, 64:128], in_=swb[:, 64:128]), reads=["swb", "sel"], writes=["sel"])
            S.barrier()
        TP = T + 2 * PAD
        qAB = st.enter_context(nc.sbuf_tensor("s_qAB", [128, 2, TP], BF16))
        kT = st.enter_context(nc.sbuf_tensor("s_kT", [128, TP], BF16))
        S.op("pool", lambda e: e.memset(qAB[:], 0.0), writes=["qAB"])
        S.op("pool", lambda e: e.memset(kT[:, 0:PAD], 0.0), writes=["kT"])
        S.op("pool", lambda e: e.memset(kT[:, PAD + T:TP], 0.0), writes=["kT"])
        acc = st.enter_context(nc.sbuf_tensor("s_acc", [128, 2, SPAN], F32))
        OW = 2048
        oT = [st.enter_context(nc.sbuf_tensor("s_oT%d" % i, [128, OW], BF16)) for i in range(2)]
        rden = [st.enter_context(nc.sbuf_tensor("s_rden%d" % i, [128, 512], F32)) for i in range(2)]
        Vt = [st.enter_context(nc.sbuf_tensor("s_Vt%d" % i, [128, 256], BF16)) for i in range(8)]
        PT = [st.enter_context(nc.sbuf_tensor("s_PT%d" % i, [128, 2, 256], BF16)) for i in range(6)]
        vn = 0
        ptn = 0
        sn = 0
        on = 0
        rn = 0
        spn = 0
        DSK = 3
        cgen = None
        cstep = 0
        for hp in range(6):
            S.dma(qAB[0:64, 0, PAD:PAD + T], k.qkT[hp * 128:hp * 128 + 64, :], writes=["qAB"])
            S.dma(qAB[64:128, 1, PAD:PAD + T], k.qkT[hp * 128 + 64:hp * 128 + 128, :], writes=["qAB"])
            S.dma(kT[:, PAD:PAD + T], k.qkT[768 + hp * 128:768 + (hp + 1) * 128, :], writes=["kT"])
            for s in range(T // SPAN):
                pendB = []
                S.op("pool", lambda e: e.memset(acc[:], 0.0), writes=["acc"])
                for ri, r in enumerate((1, 4, 16)):
                    Lr = T // r
                    jb = (T // 2) // (r * 128)
                    nqb = SPAN // (128 * r)
                    for rho in range(r):
                        ja, jbnd = s * nqb, s * nqb + nqb
                        pobank = {}
                        for j in range(ja, jbnd + 1):
                            c_lo = 128 if j == ja else 0
                            c_hi = 128 if j == jbnd else 256
                            ncol = c_hi - c_lo
                            vt, vr = Vt[vn % 8], ("Vt", vn % 8)
                            vn += 1
                            m0 = 128 * j - 64
                            lo, hi = 0, 128
                            if m0 < 0:
                                lo = 64
                            if m0 + 128 > Lr:
                                hi = 64
                            if lo > 0 or hi < 128:
                                S.op("pool", lambda e, vt=vt: e.memset(vt[:], 0.0), writes=[vr])
                            tok0 = rho + r * (m0 + lo)
                            src = AP(k.vaug.tensor, k.vaug[tok0:tok0 + 1, hp * 256:hp * 256 + 1].offset, [[1536 * r, hi - lo], [1, 256]])
                            S.dma(vt[lo:hi, :], src, writes=[vr])
                            mq0 = 128 * j - 128 + c_lo
                            qc0 = PAD + rho + r * mq0
                            qsl = slice(qc0, qc0 + (ncol - 1) * r + 1, r)
                            kc0 = PAD + rho + r * m0
                            ksl = slice(kc0, kc0 + 127 * r + 1, r)
                            straddle = (j == jb)
                            pS, psr = ps[sn % 4], ("ps", sn % 4)
                            sn += 1
                            pt, ptr = PT[ptn % 6], ("PT", ptn % 6)
                            ptn += 1
                            pSv = pS[:, :].rearrange("p (h c) -> p h c", h=2)[:, :, 0:ncol]

                            def stageA(pSv=pSv, psr=psr, ksl=ksl, qsl=qsl, ncol=ncol, ri=ri, c_lo=c_lo, c_hi=c_hi,
                                       straddle=straddle, pt=pt, ptr=ptr, hp=hp):
                                S.op("pe", lambda e: e.matmul(pSv, lhsT=kT[:, ksl], rhs=qAB[:, :, qsl], start=True, stop=False),
                                     reads=["kT", "qAB"], writes=[psr])
                                S.op("pe", lambda e: e.matmul(pSv, lhsT=J[:], rhs=Hk[:, ri, 2 * hp:2 * hp + 2, c_lo:c_hi],
                                                              start=False, stop=not straddle), reads=["J", "Hk"], writes=[psr])
                                if straddle:
                                    S.op("pe", lambda e: e.matmul(pSv, lhsT=k.ident[:], rhs=bc(bm[:, c_lo:c_hi].rearrange("p (o c) -> p o c", o=1), [128, 2, ncol]),
                                                                  start=False, stop=True), reads=["ident", "bm"], writes=[psr])
                                S.op("act", lambda e: e.activation(out=pt[:, :, 0:ncol], in_=pSv, func=AF.Exp), reads=[psr], writes=[ptr])

                            pieces = []
                            if c_lo == 0:
                                pieces.append((0, j - 1))
                            if c_hi == 256:
                                pieces.append((1, j))
                            for (half, jq) in pieces:
                                if half == 1:
                                    pobank[jq] = (ps[4 + on % 4], ("ps", 4 + on % 4))
                                    on += 1
                            pbs = {jq: pobank[jq] for (_, jq) in pieces}

                            def stageB(pieces=pieces, pbs=pbs, vt=vt, vr=vr, pt=pt, ptr=ptr, c_lo=c_lo, r=r, rho=rho, s=s):
                                for (half, jq) in pieces:
                                    po, por = pbs[jq]
                                    off = half * 128 - c_lo
                                    for hh in range(2):
                                        S.op("pe", lambda e, po=po, hh=hh, off=off, half=half: e.matmul(
                                            po[:, hh * 128:(hh + 1) * 128], lhsT=vt[:, hh * 128:(hh + 1) * 128], rhs=pt[:, hh, off:off + 128],
                                            start=(half == 1 and hh == 0), stop=(half == 0), skip_group_check=True),
                                            reads=[vr, ptr], writes=[por])
                                    if half == 0:
                                        a0 = rho + r * 128 * jq - s * SPAN
                                        asl = slice(a0, a0 + 127 * r + 1, r)
                                        S.op("dve", lambda e, po=po, asl=asl: e.tensor_tensor(
                                            out=acc[:, :, asl], in0=po[:, 0:256].rearrange("p (h c) -> p h c", h=2), in1=acc[:, :, asl],
                                            op=ALU.add), reads=[por, "acc"], writes=["acc"])

                            stageA()
                            pendB.append(stageB)
                            if len(pendB) > DSK:
                                pendB.pop(0)()
                            cstep += 1
                            if cgen is not None and cstep % 4 == 0:
                                next(cgen, None)
                while pendB:
                    pendB.pop(0)()
                for ow in range(SPAN // OW):
                    ot, otr = oT[spn % 2], ("oT", spn % 2)
                    spn += 1
                    for cc in range(OW // 512):
                        c0 = ow * OW + cc * 512
                        pw, pwr = ps[sn % 4], ("ps", sn % 4)
                        sn += 1
                        S.op("pe", lambda e, pw=pw, c0=c0: e.matmul(pw[:, :], lhsT=selA[:], rhs=acc[:, 0, c0:c0 + 512],
                                                                    start=True, stop=False), reads=["acc", "sel"], writes=[pwr])
                        S.op("pe", lambda e, pw=pw, c0=c0: e.matmul(pw[:, :], lhsT=selB[:], rhs=acc[:, 1, c0:c0 + 512],
                                                                    start=False, stop=True), reads=["acc", "sel"], writes=[pwr])
                        rd, rdr = rden[rn % 2], ("rden", rn % 2)
                        rn += 1
                        S.op("dve", lambda e, rd=rd, pw=pw: e.reciprocal(out=rd[:], in_=pw[:, :]), reads=[pwr], writes=[rdr])
                        for hh in range(2):
                            nlo = 64 * hh
                            S.op("pool", lambda e, rd=rd, ot=ot, hh=hh, cc=cc, c0=c0, nlo=nlo: e.tensor_tensor(
                                out=ot[nlo:nlo + 64, cc * 512:(cc + 1) * 512], in0=acc[nlo:nlo + 64, hh, c0:c0 + 512],
                                in1=rd[nlo:nlo + 64, :], op=ALU.mult), reads=["acc", rdr], writes=[otr])
                    t0_ = s * SPAN + ow * OW
                    S.dma(k.oaT[hp * 128:(hp + 1) * 128, t0_:t0_ + OW], ot[:], reads=[otr])
        if cgen is not None:
            for _ in cgen:
                pass


def phaseF(k):
    nc, S, din, ps = k.nc, k.S, k.din, k.ps
    with ExitStack() as st:
        wpa = st.enter_context(nc.sbuf_tensor("s_wpa", [128, 6, DM], BF16))
        wph = st.enter_context(nc.sbuf_tensor("s_wph", [128, 6, DM], BF16))
        wo = st.enter_context(nc.sbuf_tensor("s_wo", [128, 2, 8, DM], BF16))
        gater = st.enter_context(nc.sbuf_tensor("s_gater", [128, 2, DM], F32))
        for s_ in range(2):
            S.dma(gater[:, s_, :], k.modrep[s_:s_ + 1, 2 * DM:3 * DM].partition_broadcast(128), writes=[("modr", s_)])
        with ExitStack() as s2:
            stg = [s2.enter_context(nc.sbuf_tensor("s_fstg%d" % i, [128, DM], F32)) for i in range(2)]
            n = 0
            for (wt, nm, src, nk) in ((wpa, "wpa", "w_proj_attn", 6), (wph, "wph", "w_proj_hyena", 6), (wo, "wo", "w_out", 8)):
                for kk in range(nk):
                    b = n % 2
                    S.dma(stg[b][:], din[src][kk * 128:(kk + 1) * 128, :], writes=[("fstg", b)])
                    eng = ("dve", "pool")[n % 2]
                    if nm == "wo":
                        for s_ in range(2):
                            S.op(("dve", "pool")[s_], lambda e, b=b, wt=wt, kk=kk, s_=s_: e.tensor_tensor(
                                out=wt[:, s_, kk, :], in0=stg[b][:], in1=gater[:, s_, :], op=ALU.mult),
                                reads=[("fstg", b), ("modr", s_)], writes=[nm])
                    else:
                        S.op(eng, lambda e, b=b, wt=wt, kk=kk: e.tensor_copy(out=wt[:, kk, :], in_=stg[b][:]), reads=[("fstg", b)], writes=[nm])
                    n += 1
            S.barrier()
        oa = [st.enter_context(nc.sbuf_tensor("s_foa%d" % i, [128, 6, 512], BF16)) for i in range(2)]
        ga = [st.enter_context(nc.sbuf_tensor("s_fga%d" % i, [128, 6, 512], BF16)) for i in range(2)]
        oh_ = [st.enter_context(nc.sbuf_tensor("s_foh%d" % i, [128, 6, 512], BF16)) for i in range(2)]
        gh = [st.enter_context(nc.sbuf_tensor("s_fgh%d" % i, [128, 6, 512], BF16)) for i in range(2)]
        mt = [st.enter_context(nc.sbuf_tensor("s_fmt%d" % i, [128, 16, 512], BF16)) for i in range(2)]
        mix = [st.enter_context(nc.sbuf_tensor("s_fmix%d" % i, [128, 8, 512], BF16)) for i in range(2)]
        ta = [st.enter_context(nc.sbuf_tensor("s_fta%d" % i, [128, 512], F32)) for i in range(2)]
        tb = [st.enter_context(nc.sbuf_tensor("s_ftb%d" % i, [128, 512], F32)) for i in range(2)]
        xt = [st.enter_context(nc.sbuf_tensor("s_fx%d" % i, [128, DM], F32)) for i in range(2)]
        r1 = [st.enter_context(nc.sbuf_tensor("s_fr%d" % i, [128, DM], F32)) for i in range(2)]
        yo = [st.enter_context(nc.sbuf_tensor("s_fy%d" % i, [128, DM], F32)) for i in range(2)]
        sqj = st.enter_context(nc.sbuf_tensor("s_fsq", [128, DM], BF16))
        ss = [st.enter_context(nc.sbuf_tensor("s_fss%d" % i, [128, 2], F32)) for i in range(2)]
        pn = 0
        tn_ = 0

        def fprep(ci):
            b = ci % 2
            cs = slice(ci * 512, (ci + 1) * 512)
            S.dma(oa[b][:], k.oaT[:, cs].rearrange("(a p) t -> p a t", p=128), writes=[("foa", b)])
            S.dma(ga[b][:], k.gaT[:, cs].rearrange("(a p) t -> p a t", p=128), writes=[("fga", b)])
            S.dma(oh_[b][:], k.ohT[:, cs].rearrange("(a p) t -> p a t", p=128), writes=[("foh", b)])
            S.dma(gh[b][:], k.ghT[:, cs].rearrange("(a p) t -> p a t", p=128), writes=[("fgh", b)])
            S.dma(mt[b][:], k.mT[:, cs].rearrange("(a p) t -> p a t", p=128), writes=[("fmt", b)])
            S.op("pool", lambda e, b=b: e.tensor_tensor(out=oa[b][:], in0=oa[b][:], in1=ga[b][:], op=ALU.mult),
                 reads=[("foa", b), ("fga", b)], writes=[("foa", b)])
            S.op("dve", lambda e, b=b: e.tensor_tensor(out=oh_[b][:], in0=oh_[b][:], in1=gh[b][:], op=ALU.mult),
                 reads=[("foh", b), ("fgh", b)], writes=[("foh", b)])

        def frun(ci):
            nonlocal pn, tn_
            seg = 0 if ci < NCHUNK // 2 else 1
            b = ci % 2
            for fb in range(8):
                pA, pAr = ps[pn % 4], ("ps", pn % 4)
                pn += 1
                pH, pHr = ps[pn % 4], ("ps", pn % 4)
                pn += 1
                for kk in range(6):
                    S.op("pe", lambda e, pA=pA, kk=kk, fb=fb, b=b: e.matmul(pA[:, :], lhsT=wpa[:, kk, fb * 128:(fb + 1) * 128], rhs=oa[b][:, kk, :],
                                                                             start=(kk == 0), stop=(kk == 5)), reads=["wpa", ("foa", b)], writes=[pAr])
                for kk in range(6):
                    S.op("pe", lambda e, pH=pH, kk=kk, fb=fb, b=b: e.matmul(pH[:, :], lhsT=wph[:, kk, fb * 128:(fb + 1) * 128], rhs=oh_[b][:, kk, :],
                                                                             start=(kk == 0), stop=(kk == 5)), reads=["wph", ("foh", b)], writes=[pHr])
                a_, ar_ = ta[tn_ % 2], ("fta", tn_ % 2)
                b_, br_ = tb[tn_ % 2], ("ftb", tn_ % 2)
                tn_ += 1
                S.op("dve", lambda e, a_=a_, pA=pA, fb=fb, b=b: e.tensor_tensor(out=a_[:], in0=pA[:, :], in1=mt[b][:, fb, :], op=ALU.mult),
                     reads=[pAr, ("fmt", b)], writes=[ar_])
                S.op("dve", lambda e, b_=b_, pH=pH, fb=fb, b=b: e.tensor_tensor(out=b_[:], in0=pH[:, :], in1=mt[b][:, 8 + fb, :], op=ALU.mult),
                     reads=[pHr, ("fmt", b)], writes=[br_])
                S.op("pool", lambda e, a_=a_, b_=b_, fb=fb, b=b: e.tensor_tensor(out=mix[b][:, fb, :], in0=a_[:], in1=b_[:], op=ALU.add),
                     reads=[ar_, br_], writes=[("fmix", b)])
            pend = []
            for tt in range(4):
                t = ci * 4 + tt
                tb2 = t % 2
                S.dma(xt[tb2][:], din["x"][t * 128:(t + 1) * 128, :], writes=[("fx", tb2)], eng="act")
                p0, p1 = ps[4 + 2 * tb2], ps[5 + 2 * tb2]
                p0r, p1r = ("ps", 4 + 2 * tb2), ("ps", 5 + 2 * tb2)
                for half, (pp_, ppr) in enumerate(((p0, p0r), (p1, p1r))):
                    for fb in range(8):
                        S.op("pe", lambda e, pp_=pp_, fb=fb, tt=tt, half=half, b=b, seg=seg: e.matmul(
                            pp_[:, :], lhsT=mix[b][:, fb, tt * 128:(tt + 1) * 128], rhs=wo[:, seg, fb, half * 512:(half + 1) * 512],
                            start=(fb == 0), stop=(fb == 7)), reads=[("fmix", b), "wo"], writes=[ppr])
                rr, rrr = r1[tb2], ("fr", tb2)
                S.op("dve", lambda e, rr=rr, tb2=tb2: e.tensor_tensor(
                    out=rr[:], in0=k.pp[2 + tb2][:, :], in1=xt[tb2][:], op=ALU.add),
                    reads=[p0r, p1r, ("fx", tb2)], writes=[rrr])
                sb, sr = ss[tb2], ("fss", tb2)
                S.op("act", lambda e, rr=rr, sb=sb: e.activation(out=sqj[:], in_=rr[:], func=AF.Square, scale=1.0 / 32.0, accum_out=sb[:, 0:1]),
                     reads=[rrr], writes=["fsq", sr])
                S.op("act", lambda e, sb=sb: e.activation(out=sb[:, 1:2], in_=sb[:, 0:1], func=AF.Sqrt, bias=k.epsc[:, 0:1]),
                     reads=[sr, "epsc"], writes=[sr])

                def stage2(rr=rr, rrr=rrr, sb=sb, sr=sr, tb2=tb2, t=t):
                    S.op("dve", lambda e: e.reciprocal(out=sb[:, 1:2], in_=sb[:, 1:2]), reads=[sr], writes=[sr])
                    yb, yr = yo[tb2], ("fy", tb2)
                    S.op("dve", lambda e: e.scalar_tensor_tensor(
                        out=yb[:], in0=rr[:], scalar=sb[:, 1:2], in1=k.fg_rep[:], op0=ALU.mult, op1=ALU.mult),
                        reads=[rrr, sr, "fg_rep"], writes=[yr])
                    S.dma(k.y[t * 128:(t + 1) * 128, :], yb[:], reads=[yr], eng="pool")

                pend.append(stage2)
                if len(pend) > 1:
                    pend.pop(0)()
            while pend:
                pend.pop(0)()

        fprep(0)
        for ci in range(NCHUNK):
            if ci + 1 < NCHUNK:
                fprep(ci + 1)
            frun(ci)
```

```python
import math
import numpy as np
import ml_dtypes
import concourse.bass as bass
import concourse.mybir as mybir
from concourse.ap import AP
from concourse.bass_utils import run_bass_kernel_spmd

F32 = mybir.dt.float32
BF16 = mybir.dt.bfloat16
AF = mybir.ActivationFunctionType
ALU = mybir.AluOpType

T = 16384
DM = 1024
NCHUNK = T // 512
NFFT = 32768
EPS = 1e-6
CB = 64
NCB = 768 // CB
PAD = 1024

OQ, OK_, OV, OGA, OU, OGH, OMA, OMH = 0, 768, 1536, 2304, 3072, 5376, 6144, 7168

DEBUG = {}


class Sched:
    ENG = ("pe", "act", "dve", "pool", "sp")

    def __init__(self, nc, sems, dma_ring):
        self.nc = nc
        self.ops = []
        self.eng = {"pe": nc.tensor, "act": nc.scalar, "dve": nc.vector, "pool": nc.gpsimd, "sp": nc.sync}
        self.sems = sems
        self.ring = dma_ring
        self.cnt = {e: 0 for e in self.ENG}
        self.ndma = 0
        self.last_w = {}
        self.readers = {}
        self.waited = {e: {d: 0 for d in self.ENG} for e in self.ENG}
        self.waited_dma = {e: {} for e in self.ENG}
        self.last_op = {e: None for e in self.ENG}
        self.pending = []
        self.dma_eng = "sp"

    def op(self, eng, fn, reads=(), writes=()):
        self.ops.append(("op", eng, fn, tuple(reads), tuple(writes)))

    def dma(self, out, in_, reads=(), writes=(), eng="sp"):
        self.ops.append(("dma", eng, (out, in_), tuple(reads), tuple(writes)))

    def barrier(self):
        self.ops.append(("bar",))

    def flush(self):
        ops = self.ops
        n = len(ops)
        last_w, readers = {}, {}
        last_on = {e: -1 for e in self.ENG}
        deps = [None] * n
        marked = [False] * n
        for i, o in enumerate(ops):
            if o[0] == "bar":
                deps[i] = dict(last_on)
                for e, j in last_on.items():
                    if j >= 0:
                        marked[j] = True
                last_w, readers = {}, {}
                continue
            _, e, _, rd, wr = o
            d = set()
            for r in rd:
                if r in last_w:
                    d.add(last_w[r])
            for w in wr:
                if w in last_w:
                    d.add(last_w[w])
                for j in readers.get(w, ()):
                    d.add(j)
            d.discard(i)
            deps[i] = d
            for j in d:
                marked[j] = True
            for w in wr:
                last_w[w] = i
                readers[w] = []
            for r in rd:
                if r not in wr:
                    readers.setdefault(r, []).append(i)
            last_on[e] = i
        ordinal = [0] * n
        cnt = {e: 0 for e in self.ENG}
        dslot = [None] * n
        nd = 0
        P = len(self.ring)
        for i, o in enumerate(ops):
            if o[0] == "dma":
                dslot[i] = (nd % P, 16 * (nd // P + 1))
                nd += 1
            elif o[0] == "op" and marked[i]:
                cnt[o[1]] += 1
                ordinal[i] = cnt[o[1]]
        waited = {e: {d: 0 for d in self.ENG} for e in self.ENG}
        wdma = {e: {} for e in self.ENG}

        def wait_for(e, j):
            oj = ops[j]
            if oj[0] == "dma":
                slot, val = dslot[j]
                if wdma[e].get(slot, 0) >= val:
                    return
                wdma[e][slot] = val
                self.eng[e].wait_ge(self.ring[slot], val)
            else:
                dsrc = oj[1]
                if dsrc == e and e == "pe":
                    return
                if waited[e][dsrc] >= ordinal[j]:
                    return
                waited[e][dsrc] = ordinal[j]
                self.eng[e].wait_ge(self.sems[dsrc], ordinal[j])

        nd = 0
        dma_hist = []
        for i, o in enumerate(ops):
            if o[0] == "bar":
                for e in self.ENG:
                    for dsrc, j in deps[i].items():
                        if j >= 0 and not (dsrc == e and ops[j][0] == "op"):
                            wait_for(e, j)
                    for j in dma_hist[-P:]:
                        wait_for(e, j)
                continue
            kind, e, payload, rd, wr = o
            for j in sorted(deps[i]):
                wait_for(e, j)
            if kind == "dma":
                slot, val = dslot[i]
                if val > 16:
                    if wdma[e].get(slot, 0) < val - 16:
                        wdma[e][slot] = val - 16
                        self.eng[e].wait_ge(self.ring[slot], val - 16)
                out, in_ = payload
                self.eng[e].dma_start(out=out, in_=in_).then_inc(self.ring[slot], 16)
                dma_hist.append(i)
                nd += 1
            else:
                ins = payload(self.eng[e])
                if marked[i]:
                    ins.then_inc(self.sems[e], 1)
        for j in dma_hist[-P:]:
            wait_for("sp", j)
        self.ops = []
        return n


def bc(ap, shape):
    return ap.to_broadcast(list(shape))


def _t5_bucket(rel):
    nb = 16
    max_exact = 8
    n = np.abs(rel)
    large = max_exact + (np.log(np.maximum(n, 1) / max_exact) / math.log(1024 / max_exact) * (nb - max_exact)).astype(np.int32)
    large = np.minimum(large, nb - 1)
    return ((rel > 0).astype(np.int32) * nb + np.where(n < max_exact, n, large)).astype(np.int32)


def bf(a):
    return np.ascontiguousarray(a.astype(np.float32)).astype(ml_dtypes.bfloat16)


_CONST_CACHE = {}


def build_consts(is_prompt):
    key = bool(is_prompt)
    if key in _CONST_CACHE:
        return _CONST_CACHE[key]
    c = {}
    L = 16384 if is_prompt else 8192
    tt = np.linspace(0.0, 1.0, L, dtype=np.float32)[:, None]
    w = (np.float32(2.0 * math.pi / L) * np.arange(L, dtype=np.float32))[:, None]
    f = np.linspace(1e-4, 15, 16, dtype=np.float32)[None, :]
    z = np.concatenate([tt, np.cos(f * w), -np.sin(f * w)], axis=-1).astype(np.float32)
    zf = np.zeros((T, 33), np.float32)
    zf[:L] = z
    c["zfT"] = np.ascontiguousarray(zf.T)
    tn = np.zeros(T, np.float32)
    tn[:L] = tt[:, 0]
    c["negtn"] = np.ascontiguousarray(-tn.reshape(128, 128))
    deltas = np.abs(np.linspace(math.log(0.01) / 1.5, math.log(0.01) / 0.3, 768, dtype=np.float32))
    c["deltas_rep"] = np.ascontiguousarray(np.broadcast_to(deltas[None, :], (128, 768))).astype(np.float32)
    slot = np.arange(128) if is_prompt else np.concatenate([np.arange(64), np.arange(64) + 128])
    k2 = np.arange(128)
    n1 = np.arange(128)
    k1 = np.arange(128)
    ang = -2 * np.pi * np.outer(slot, k2 + 0.5) / 256.0
    Er, Ei = np.cos(ang), np.sin(ang)
    c["E_d"] = bf(np.concatenate([-Ei, Er, Ei], axis=1))
    angk = -2 * np.pi * np.outer(np.arange(128), k2 + 0.5) / 256.0
    Ekr, Eki = np.cos(angk), np.sin(angk)
    if not is_prompt:
        Ekr[64:] = 0
        Eki[64:] = 0
    c["E_kS"] = bf(np.concatenate([Ekr, -Eki], axis=1))
    c["E_kD"] = bf(np.concatenate([Eki, Ekr], axis=1))
    angM = -2 * np.pi * (n1[None, :, None] * (k2[:, None, None] + 0.5) / NFFT + n1[None, :, None] * k1[None, None, :] / 128.0)
    c["Mf"] = bf(np.concatenate([np.cos(angM), np.sin(angM)], axis=2).transpose(1, 0, 2))
    angG = 2 * np.pi * np.outer(k1, n1) / 128.0
    c["G1"] = bf(np.concatenate([np.cos(angG), np.sin(angG)], axis=1))
    c["G2"] = bf(np.concatenate([-np.sin(angG), np.cos(angG)], axis=1))
    angI = 2 * np.pi * (n1[:, None, None] + 128 * slot[None, None, :]) * (k2[None, :, None] + 0.5) / NFFT
    sc = 2.0 / NFFT
    c["Minv"] = bf(np.concatenate([sc * np.cos(angI), -sc * np.sin(angI)], axis=2).transpose(1, 0, 2))
    OH = np.zeros((32, 3 * 384), np.float32)
    mrow = np.zeros((12, 3 * 384), np.float32)
    for ri, r in enumerate((1, 4, 16)):
        for j in range(384):
            d = j - 127
            if 0 <= d <= 128:
                rel = (64 - d) * r
                OH[_t5_bucket(np.array(rel)), ri * 384 + j] = 1.0
            else:
                mrow[:, ri * 384 + j] = -30000.0
    c["OH"] = OH
    c["mrow"] = mrow
    bm = np.zeros((128, 256), np.float32)
    if not is_prompt:
        bm[:64, 128:] = -30000.0
        bm[64:, :128] = -30000.0
    c["bmask"] = bf(bm)
    c["bflag"] = np.full((128, 1), 1.0 if is_prompt else 0.0, np.float32)
    ident = np.eye(128, dtype=np.float32)
    c["ident"] = bf(ident)
    c["antiid"] = bf(ident[::-1])
    c["swap"] = bf(np.roll(ident, 64, axis=1))
    _CONST_CACHE[key] = c
    return c


CONST_SHAPES = {
    "zfT": ([33, T], F32), "negtn": ([128, 128], F32), "deltas_rep": ([128, 768], F32),
    "E_d": ([128, 384], BF16), "E_kS": ([128, 256], BF16), "E_kD": ([128, 256], BF16),
    "Mf": ([128, 128, 256], BF16), "G1": ([128, 256], BF16), "G2": ([128, 256], BF16),
    "Minv": ([128, 128, 256], BF16), "OH": ([32, 1152], F32), "mrow": ([12, 1152], F32),
    "bmask": ([128, 256], BF16), "bflag": ([128, 1], F32), "ident": ([128, 128], BF16),
    "antiid": ([128, 128], BF16), "swap": ([128, 128], BF16),
}

INPUT_SHAPES = {
    "x": [T, DM], "cT": [128, 8, 2], "w_ada": [DM, 3 * DM], "b_ada": [1, 3 * DM], "norm_g": [1, DM],
    "w_in": [DM, 8192], "short_wT": [128, 18, 3], "short_bT": [128, 18],
    "filt_w1": [33, 64], "filt_b1": [64, 1], "filt_fr1": [64, 1], "filt_w2": [64, 64], "filt_b2": [64, 1],
    "filt_fr2": [64, 1], "filt_w3": [64, 3072], "hyena_d": [2, 768],
    "w_proj_attn": [768, DM], "w_proj_hyena": [768, DM], "w_out": [DM, DM], "rel_bias": [32, 12],
    "final_g": [1, DM],
}


from contextlib import ExitStack


class K:
    pass


def build_program(phases=("p0", "pk", "p1", "p1b", "pa", "ph", "pf"), debug_outs=()):
    nc = bass.Bass("TRN2", target_bir_lowering=False)
    k = K()
    k.nc = nc
    din = {}
    for name, shp in INPUT_SHAPES.items():
        din[name] = nc.dram_tensor(name, shp, F32, kind="ExternalInput").ap()
    for name, (shp, dt_) in CONST_SHAPES.items():
        din[name] = nc.dram_tensor(name, shp, dt_, kind="ExternalInput").ap()
    k.din = din
    y = nc.dram_tensor("y", [T, DM], F32, kind="ExternalOutput").ap()
    k.y = y

    def scratch(name, shp, dt_):
        kind = "ExternalOutput" if name in debug_outs else "Internal"
        return nc.dram_tensor(name, shp, dt_, kind=kind).ap()

    k.qkT = scratch("qkT", [1536, T], BF16)
    k.gaT = scratch("gaT", [768, T], BF16)
    k.ghT = scratch("ghT", [768, T], BF16)
    k.mT = scratch("mT", [2048, T], BF16)
    k.uraw = scratch("uraw", [2304, T], BF16)
    k.uT = scratch("uT", [2304, T], BF16)
    k.vaug = scratch("vaug", [T, 1536], BF16)
    k.oaT = scratch("oaT", [768, T], BF16)
    k.ohT = scratch("ohT", [768, T], BF16)
    k.KFd = scratch("KFd", [2, NCB, 128, 128 * 3 * CB], BF16)
    k.Avec = scratch("Avec", [12, 1152], BF16)
    k.modrep = scratch("modrep", [2, 3 * DM], F32)

    with ExitStack() as top:
        sems = {e: top.enter_context(nc.semaphore("sem_" + e)) for e in ("pe", "act", "dve", "pool")}
        sems["sp"] = None
        ring = [top.enter_context(nc.semaphore("dr%d" % i)) for i in range(24)]
        S = Sched(nc, sems, ring)
        k.S = S
        pp = [top.enter_context(nc.psum_tensor("psb%d" % i, [128, 1024], F32)) for i in range(4)]
        ps = []
        for i in range(4):
            ps.append(pp[i][:, 0:512])
            ps.append(pp[i][:, 512:1024])
        k.ps = ps
        k.pp = pp
        k.ident = top.enter_context(nc.sbuf_tensor("s_ident", [128, 128], BF16))
        k.fg_rep = top.enter_context(nc.sbuf_tensor("s_fg_rep", [128, DM], F32))
        S.dma(k.ident[:], din["ident"][:, :], writes=["ident"])
        k.epsc = top.enter_context(nc.sbuf_tensor("s_epsc", [128, 2], F32))
        S.op("pool", lambda e: e.memset(k.epsc[:], EPS), writes=["epsc"])
        S.dma(k.fg_rep[:], din["final_g"][0:1, :].partition_broadcast(128), writes=["fg_rep"])

        if "p0" in phases:
            phase0(k)
            S.barrier()
        k.fuse1b = ("p1b" in phases and "pk" in phases)
        if "p1" in phases:
            phase1(k)
            S.barrier()
        if "pk" in phases:
            phaseK(k)
            S.barrier()
        if "p1b" in phases and not k.fuse1b:
            phase1b(k)
            S.barrier()
        if "pa" in phases:
            phaseA(k)
            S.barrier()
        if "ph" in phases:
            phaseH(k)
            S.barrier()
        if "pf" in phases:
            phaseF(k)
            S.barrier()
        S.flush()
    return nc


def phase0(k):
    nc, S, din, ps = k.nc, k.S, k.din, k.ps
    with ExitStack() as st:
        wada = st.enter_context(nc.sbuf_tensor("s_wada", [128, 8, 3 * DM], F32))
        cT = st.enter_context(nc.sbuf_tensor("s_cT", [128, 8, 2], F32))
        scT = st.enter_context(nc.sbuf_tensor("s_scT", [128, 8, 2], F32))
        screp = st.enter_context(nc.sbuf_tensor("s_screp", [128, 2, 8, 128], F32))
        brep = st.enter_context(nc.sbuf_tensor("s_brep", [128, 3 * DM], F32))
        ngrep = st.enter_context(nc.sbuf_tensor("s_ngrep", [128, DM], F32))
        k.modr = st.enter_context(nc.sbuf_tensor("s_modr", [128, 2, 3 * DM], F32))
        S.dma(cT[:], din["cT"][:, :, :], writes=["cT"])
        for kk in range(8):
            S.dma(wada[:, kk, :], din["w_ada"][kk * 128:(kk + 1) * 128, :], writes=[("wada", kk)])
        S.dma(brep[:], din["b_ada"][0:1, :].partition_broadcast(128), writes=["brep"])
        S.dma(ngrep[:], din["norm_g"][0:1, :].partition_broadcast(128), writes=["ngrep"])
        S.op("act", lambda e: e.activation(out=scT[:], in_=cT[:], func=AF.Silu), reads=["cT"], writes=["scT"])
        for s in range(2):
            S.op("dve", lambda e, s=s: e.tensor_copy(out=screp[:, s, :, :], in_=bc(scT[:, :, s:s + 1], [128, 8, 128])),
                 reads=["scT"], writes=[("screp", s)])
        for s in range(2):
            for cc in range(6):
                pb = ps[(s * 6 + cc) % 4]
                pr = ("ps", (s * 6 + cc) % 4)
                for kk in range(8):
                    S.op("pe", lambda e, s=s, cc=cc, kk=kk, pb=pb: e.matmul(
                        pb[:, :], lhsT=screp[:, s, kk, :], rhs=wada[:, kk, cc * 512:(cc + 1) * 512],
                        start=(kk == 0), stop=(kk == 7)),
                        reads=[("screp", s), ("wada", kk)], writes=[pr])
                S.op("dve", lambda e, s=s, cc=cc, pb=pb: e.tensor_tensor(
                    out=k.modr[:, s, cc * 512:(cc + 1) * 512], in0=pb[:, :], in1=brep[:, cc * 512:(cc + 1) * 512], op=ALU.add),
                    reads=[pr, "brep"], writes=[("modr", s)])
            S.op("dve", lambda e, s=s: e.scalar_tensor_tensor(
                out=k.modr[:, s, DM:2 * DM], in0=k.modr[:, s, DM:2 * DM], scalar=1.0, in1=ngrep[:], op0=ALU.add, op1=ALU.mult),
                reads=[("modr", s), "ngrep"], writes=[("modr", s)])
            S.dma(k.modrep[s:s + 1, :], k.modr[0:1, s, :], reads=[("modr", s)])
        S.barrier()


def phase1(k):
    nc, S, din, ps = k.nc, k.S, k.din, k.ps
    with ExitStack() as st:
        winb = st.enter_context(nc.sbuf_tensor("s_winb", [128, 8, 8192], BF16))
        stg_ctx = ExitStack()
        stg = [stg_ctx.enter_context(nc.sbuf_tensor("s_wstg%d" % i, [128, 2048], F32)) for i in range(2)]
        n = 0
        for kk in range(8):
            for cc in range(4):
                b = n % 2
                S.dma(stg[b][:], din["w_in"][kk * 128:(kk + 1) * 128, cc * 2048:(cc + 1) * 2048], writes=[("wstg", b)])
                eng = ("act", "dve", "pool")[n % 3]
                if eng == "act":
                    S.op("act", lambda e, b=b, kk=kk, cc=cc: e.copy(out=winb[:, kk, cc * 2048:(cc + 1) * 2048], in_=stg[b][:]),
                         reads=[("wstg", b)], writes=[("winb", kk)])
                else:
                    S.op(eng, lambda e, b=b, kk=kk, cc=cc: e.tensor_copy(out=winb[:, kk, cc * 2048:(cc + 1) * 2048], in_=stg[b][:]),
                         reads=[("wstg", b)], writes=[("winb", kk)])
                n += 1
        S.barrier()
        stg_ctx.close()
        modr1 = st.enter_context(nc.sbuf_tensor("s_modr1", [128, 2, 2 * DM], F32))
        for s_ in range(2):
            S.dma(modr1[:, s_, :], k.modrep[s_:s_ + 1, 0:2 * DM].partition_broadcast(128), writes=[("modr", s_)])
        xt = [st.enter_context(nc.sbuf_tensor("s_xt%d" % i, [128, DM], F32)) for i in range(2)]
        xm = [st.enter_context(nc.sbuf_tensor("s_xm%d" % i, [128, DM], F32)) for i in range(1)]
        hb = [st.enter_context(nc.sbuf_tensor("s_hb%d" % i, [128, DM], BF16)) for i in range(4)]
        sq = st.enter_context(nc.sbuf_tensor("s_sqj", [128, DM], BF16))
        ss = [st.enter_context(nc.sbuf_tensor("s_ss%d" % i, [128, 2], F32)) for i in range(3)]
        hT = [st.enter_context(nc.sbuf_tensor("s_hT%d" % i, [128, 8, 512], BF16)) for i in range(2)]
        ev = [st.enter_context(nc.sbuf_tensor("s_ev%d" % i, [128, 512], BF16)) for i in range(6)]
        vst = [st.enter_context(nc.sbuf_tensor("s_vst%d" % i, [128, 12, 128], BF16)) for i in range(2)]
        for i in range(2):
            S.op("pool", lambda e, i=i: e.memset(vst[i][:], 1.0), writes=[("vst", i)])

        blocks = []
        for j in range(6):
            blocks.append((OQ + j * 128, k.qkT, j * 128, "q"))
        for j in range(6):
            blocks.append((OK_ + j * 128, k.qkT, 768 + j * 128, "copy"))
        for j in range(18):
            blocks.append((OU + j * 128, k.uraw, j * 128, "copy"))
        for j in range(6):
            blocks.append((OGA + j * 128, k.gaT, j * 128, "silu"))
        for j in range(6):
            blocks.append((OGH + j * 128, k.ghT, j * 128, "silu"))
        for j in range(8):
            blocks.append((OMA + j * 128, k.mT, j * 128, "sig"))
        for j in range(8):
            blocks.append((OMH + j * 128, k.mT, 1024 + j * 128, "sig"))

        tcount = 0
        evn = 0

        def prep(ci):
            seg = 0 if ci < NCHUNK // 2 else 1
            hTc = hT[ci % 2]
            hres = ("hT", ci % 2)
            for tt in range(4):
                t = ci * 4 + tt
                xb, xr = xt[t % 2], ("xt", t % 2)
                sb, sr = ss[t % 3], ("ss", t % 3)
                mb, mr = xm[0], ("xm", 0)
                hbb, hbr = hb[t % 4], ("hb", t % 4)
                S.dma(xb[:], din["x"][t * 128:(t + 1) * 128, :], writes=[xr])
                S.op("act", lambda e, xb=xb, sb=sb: e.activation(out=sq[:], in_=xb[:], func=AF.Square, scale=1.0 / 32.0,
                                                                 accum_out=sb[:, 0:1]),
                     reads=[xr], writes=["sqj", sr])
                S.op("act", lambda e, sb=sb: e.activation(out=sb[:, 1:2], in_=sb[:, 0:1], func=AF.Sqrt, bias=k.epsc[:, 0:1]),
                     reads=[sr], writes=[sr])
                S.op("dve", lambda e, sb=sb: e.reciprocal(out=sb[:, 1:2], in_=sb[:, 1:2]), reads=[sr], writes=[sr])
                S.op("dve", lambda e, xb=xb, sb=sb, mb=mb, seg=seg: e.scalar_tensor_tensor(
                    out=mb[:], in0=xb[:], scalar=sb[:, 1:2], in1=modr1[:, seg, DM:2 * DM], op0=ALU.mult, op1=ALU.mult),
                    reads=[xr, sr, ("modr", seg)], writes=[mr])
                S.op("dve", lambda e, mb=mb, hbb=hbb, seg=seg: e.tensor_tensor(
                    out=hbb[:], in0=mb[:], in1=modr1[:, seg, 0:DM], op=ALU.add),
                    reads=[mr, ("modr", seg)], writes=[hbr])

        def prepB(ci):
            hTc = hT[ci % 2]
            hres = ("hT", ci % 2)
            for tt in range(4):
                t = ci * 4 + tt
                hbb, hbr = hb[t % 4], ("hb", t % 4)
                pbank = ps[6 + (t % 2)]
                pres = ("ps", 6 + (t % 2))
                pT = pbank[:].bitcast(BF16)
                for kk in range(8):
                    S.op("pe", lambda e, kk=kk, pT=pT, hbb=hbb: e.transpose(
                        out=pT[:, kk * 128:(kk + 1) * 128], in_=hbb[:, kk * 128:(kk + 1) * 128], identity=k.ident[:]),
                        reads=[hbr, "ident"], writes=[pres])
                S.op("act", lambda e, pT=pT, hTc=hTc, tt=tt: e.copy(
                    out=hTc[:, :, tt * 128:(tt + 1) * 128], in_=pT.rearrange("p (k t) -> p k t", k=8)),
                    reads=[pres], writes=[hres])
        def run_blocks(ci):
            nonlocal evn
            hTc = hT[ci % 2]
            hres = ("hT", ci % 2)
            for bi, (wc, dst, drow, kind) in enumerate(blocks):
                pb = ps[bi % 4]
                pr = ("ps", bi % 4)
                for kk in range(8):
                    S.op("pe", lambda e, kk=kk, pb=pb, wc=wc, hTc=hTc: e.matmul(
                        pb[:, :], lhsT=winb[:, kk, wc:wc + 128], rhs=hTc[:, kk, :], start=(kk == 0), stop=(kk == 7)),
                        reads=[hres, ("winb", kk)], writes=[pr])
                eb, er = ev[evn % 6], ("ev", evn % 6)
                evn += 1
                if kind == "q":
                    S.op("act", lambda e, pb=pb, eb=eb: e.activation(out=eb[:], in_=pb[:, :], func=AF.Copy, scale=0.125),
                         reads=[pr], writes=[er])
                elif kind == "copy":
                    S.op("dve", lambda e, pb=pb, eb=eb: e.tensor_copy(out=eb[:], in_=pb[:, :]), reads=[pr], writes=[er])
                elif kind == "silu":
                    S.op("act", lambda e, pb=pb, eb=eb: e.activation(out=eb[:], in_=pb[:, :], func=AF.Silu), reads=[pr], writes=[er])
                else:
                    S.op("act", lambda e, pb=pb, eb=eb: e.activation(out=eb[:], in_=pb[:, :], func=AF.Sigmoid), reads=[pr], writes=[er])
                S.dma(dst[drow:drow + 128, ci * 512:(ci + 1) * 512], eb[:], reads=[er], eng="pool")
            for tt in range(4):
                t = ci * 4 + tt
                pa, pb2 = ps[4], ps[5]
                for kk in range(8):
                    S.op("pe", lambda e, kk=kk, tt=tt, hTc=hTc: e.matmul(
                        ps[4][:, :], lhsT=hTc[:, kk, tt * 128:(tt + 1) * 128], rhs=winb[:, kk, OV:OV + 512],
                        start=(kk == 0), stop=(kk == 7)), reads=[hres, ("winb", kk)], writes=[("ps", 4)])
                for kk in range(8):
                    S.op("pe", lambda e, kk=kk, tt=tt, hTc=hTc: e.matmul(
                        ps[5][:, 0:256], lhsT=hTc[:, kk, tt * 128:(tt + 1) * 128], rhs=winb[:, kk, OV + 512:OV + 768],
                        start=(kk == 0), stop=(kk == 7)), reads=[hres, ("winb", kk)], writes=[("ps", 5)])
                vb, vr = vst[t % 2], ("vst", t % 2)
                def vdst(vb, p0, npair):
                    base = vb[:, 2 * p0:2 * p0 + 1, 0:1]
                    return AP(base.tensor, base.offset, [list(base.ap[0]), [256, npair], [192, 2], [1, 64]])
                S.op("dve", lambda e, vb=vb, vdst=vdst: e.tensor_copy(
                    out=vdst(vb, 0, 4), in_=ps[4][:, :].rearrange("p (a b c) -> p a b c", a=4, b=2)),
                    reads=[("ps", 4)], writes=[vr])
                S.op("dve", lambda e, vb=vb, vdst=vdst: e.tensor_copy(
                    out=vdst(vb, 4, 2), in_=ps[5][:, 0:256].rearrange("p (a b c) -> p a b c", a=2, b=2)),
                    reads=[("ps", 5)], writes=[vr])
                S.dma(k.vaug[t * 128:(t + 1) * 128, :], vb[:].rearrange("p a b -> p (a b)"), reads=[vr], eng="pool")

        prep(0)
        prepB(0)
        for ci in range(NCHUNK):
            if ci + 1 < NCHUNK:
                prep(ci + 1)
            run_blocks(ci)
            if ci + 1 < NCHUNK:
                prepB(ci + 1)


def core_assignment():
    return [("p", 0), ("p", 1), ("s", 0, 1), ("s", 2, 3), ("s", 4, 5), ("s", 6, 7), ("s", 6, 7), ("s", 6, 7)]


def prep_core_inputs(inp, role):
    f32 = lambda a: np.ascontiguousarray(np.asarray(a, dtype=np.float32))
    m = {}
    if role[0] == "p":
        b = role[1]
        m["x"] = f32(inp["x_prompt"][b])
        c2 = np.stack([inp["c_prompt"][b], inp["c_prompt"][b]], 0)
    else:
        m["x"] = f32(np.concatenate([inp["x_sample"][role[1]], inp["x_sample"][role[2]]], 0))
        c2 = np.stack([inp["c_sample"][role[1]], inp["c_sample"][role[2]]], 0)
    c2 = np.asarray(c2, np.float32)
    m["cT"] = f32(c2.reshape(2, 8, 128).transpose(2, 1, 0))
    m["w_ada"] = f32(inp["w_ada"][0])
    m["b_ada"] = f32(inp["b_ada"][0][None])
    m["norm_g"] = f32(inp["norm_g"][0][None])
    m["w_in"] = f32(inp["w_in"][0])
    m["short_wT"] = f32(np.asarray(inp["short_w"][0]).reshape(3, 18, 128).transpose(2, 1, 0))
    m["short_bT"] = f32(np.asarray(inp["short_b"][0]).reshape(18, 128).T)
    m["filt_w1"] = f32(inp["filt_w1"][0])
    m["filt_b1"] = f32(np.asarray(inp["filt_b1"][0])[:, None])
    m["filt_fr1"] = f32(np.asarray(inp["filt_freq1"][0])[:, None])
    m["filt_w2"] = f32(inp["filt_w2"][0])
    m["filt_b2"] = f32(np.asarray(inp["filt_b2"][0])[:, None])
    m["filt_fr2"] = f32(np.asarray(inp["filt_freq2"][0])[:, None])
    m["filt_w3"] = f32(inp["filt_w3"][0])
    m["hyena_d"] = f32(inp["hyena_d"][0])
    m["w_proj_attn"] = f32(inp["w_proj_attn"][0])
    m["w_proj_hyena"] = f32(inp["w_proj_hyena"][0])
    m["w_out"] = f32(inp["w_out"][0])
    m["rel_bias"] = f32(inp["rel_bias"])
    m["final_g"] = f32(np.asarray(inp["final_g"])[None])
    m.update(build_consts(role[0] == "p"))
    return m


_NC_CACHE = {}


def kernel(**inputs):
    inp = {k_: np.asarray(v) for k_, v in inputs.items()}
    if "full" not in _NC_CACHE:
        _NC_CACHE["full"] = build_program()
    nc = _NC_CACHE["full"]
    roles = core_assignment()
    in_maps = [prep_core_inputs(inp, r) for r in roles]
    res = run_bass_kernel_spmd(nc, in_maps, core_ids=list(range(8)))
    outs = [np.asarray(r["y"], dtype=np.float32) for r in res.results]
    y_prompt = np.stack([outs[0], outs[1]], 0)
    ys = []
    for c in range(2, 6):
        ys.append(outs[c][:8192])
        ys.append(outs[c][8192:])
    y_sample = np.stack(ys, 0)
    return (y_prompt, y_sample)


def phase1b_gen(k, st, W=512):
    nc, S, din = k.nc, k.S, k.din
    swT = st.enter_context(nc.sbuf_tensor("s_swT", [128, 18, 3], F32))
    sbT = st.enter_context(nc.sbuf_tensor("s_sbT", [128, 18], F32))
    bfl = st.enter_context(nc.sbuf_tensor("s_bfl", [128, 1], F32))
    S.dma(swT[:], din["short_wT"][:, :, :], writes=["swT"])
    S.dma(sbT[:], din["short_bT"][:, :], writes=["sbT"])
    S.dma(bfl[:], din["bflag"][:, :], writes=["bfl"])
    ib = [st.enter_context(nc.sbuf_tensor("s_cin%d" % i, [128, W + 2], BF16)) for i in range(3)]
    t1 = [st.enter_context(nc.sbuf_tensor("s_ct%d" % i, [128, W], F32)) for i in range(2)]
    ob = [st.enter_context(nc.sbuf_tensor("s_cout%d" % i, [128, W], BF16)) for i in range(3)]
    n = 0
    for ub in range(18):
        for tc in range(T // W):
            a, ar = ib[n % 3], ("cin", n % 3)
            tb, tr = t1[n % 2], ("ct", n % 2)
            o, orr = ob[n % 3], ("cout", n % 3)
            lo = tc * W - 1
            hi = tc * W + W + 1
            c0 = 0
            if lo < 0:
                S.op("pool", lambda e, a=a: e.memset(a[:, 0:1], 0.0), writes=[ar])
                lo, c0 = 0, 1
            c1 = W + 2
            if hi > T:
                S.op("pool", lambda e, a=a: e.memset(a[:, W + 1:W + 2], 0.0), writes=[ar])
                hi, c1 = T, W + 1
            S.dma(a[:, c0:c1], k.uraw[ub * 128:(ub + 1) * 128, lo:hi], writes=[ar])
            if tc * W == T // 2:
                S.op("pool", lambda e, a=a: e.tensor_scalar(out=a[:, 0:1], in0=a[:, 0:1], scalar1=bfl[:, 0:1], scalar2=None,
                                                            op0=ALU.mult), reads=[ar, "bfl"], writes=[ar])
            if tc * W + W == T // 2:
                S.op("pool", lambda e, a=a: e.tensor_scalar(out=a[:, W + 1:W + 2], in0=a[:, W + 1:W + 2], scalar1=bfl[:, 0:1],
                                                            scalar2=None, op0=ALU.mult), reads=[ar, "bfl"], writes=[ar])
            S.op("act", lambda e, a=a, tb=tb, ub=ub: e.activation(
                out=tb[:], in_=a[:, 1:W + 1], func=AF.Identity, scale=swT[:, ub, 1:2], bias=sbT[:, ub:ub + 1]),
                reads=[ar, "swT", "sbT"], writes=[tr])
            S.op("dve", lambda e, a=a, tb=tb, ub=ub: e.scalar_tensor_tensor(
                out=tb[:], in0=a[:, 0:W], scalar=swT[:, ub, 0:1], in1=tb[:], op0=ALU.mult, op1=ALU.add),
                reads=[ar, tr, "swT"], writes=[tr])
            S.op("dve", lambda e, a=a, tb=tb, o=o, ub=ub: e.scalar_tensor_tensor(
                out=o[:], in0=a[:, 2:W + 2], scalar=swT[:, ub, 2:3], in1=tb[:], op0=ALU.mult, op1=ALU.add),
                reads=[ar, tr, "swT"], writes=[orr])
            S.dma(k.uT[ub * 128:(ub + 1) * 128, tc * W:(tc + 1) * W], o[:], reads=[orr], eng="pool")
            n += 1
            yield


def phase1b(k):
    with ExitStack() as st:
        for _ in phase1b_gen(k, st, W=2048):
            pass


def sin_wrapped(S, src_ps, pres, dst, dres, scale_ap, bias_ap, tmp, tres, tmp2, t2res, nparts, ncols):
    PI = math.pi
    S.op("dve", lambda e: e.tensor_scalar(out=tmp[0:nparts, 0:ncols], in0=src_ps, scalar1=scale_ap, scalar2=bias_ap,
                                          op0=ALU.mult, op1=ALU.add), reads=[pres], writes=[tres])
    S.op("dve", lambda e: e.tensor_scalar(out=tmp2[0:nparts, 0:ncols], in0=tmp[0:nparts, 0:ncols], scalar1=PI, scalar2=-2 * PI,
                                          op0=ALU.is_gt, op1=ALU.mult), reads=[tres], writes=[t2res])
    S.op("dve", lambda e: e.tensor_tensor(out=tmp2[0:nparts, 0:ncols], in0=tmp2[0:nparts, 0:ncols], in1=tmp[0:nparts, 0:ncols],
                                          op=ALU.add), reads=[tres, t2res], writes=[t2res])
    S.op("dve", lambda e: e.tensor_scalar(out=tmp[0:nparts, 0:ncols], in0=tmp[0:nparts, 0:ncols], scalar1=-PI, scalar2=2 * PI,
                                          op0=ALU.is_lt, op1=ALU.mult), reads=[tres], writes=[tres])
    S.op("dve", lambda e: e.tensor_tensor(out=tmp[0:nparts, 0:ncols], in0=tmp2[0:nparts, 0:ncols], in1=tmp[0:nparts, 0:ncols],
                                          op=ALU.add), reads=[tres, t2res], writes=[tres])
    S.op("act", lambda e: e.activation(out=dst, in_=tmp[0:nparts, 0:ncols], func=AF.Sin), reads=[tres], writes=[dres])


def phaseK(k):
    nc, S, din, ps = k.nc, k.S, k.din, k.ps
    with ExitStack() as st:
        hdn = st.enter_context(nc.sbuf_tensor("s_hdn2T", [128, T], BF16))
        w3sd = st.enter_context(nc.sbuf_tensor("s_w3sd", [128, 2, NCB, 2, CB], BF16))
        S.op("pool", lambda e: e.memset(hdn[64:128, :], 0.0), writes=["hdn"])
        S.op("pool", lambda e: e.memset(w3sd[64:128], 0.0), writes=["w3sd"])
        with ExitStack() as s2:
            w1 = s2.enter_context(nc.sbuf_tensor("s_fw1", [33, 64], F32))
            w2 = s2.enter_context(nc.sbuf_tensor("s_fw2", [64, 64], F32))
            w3 = s2.enter_context(nc.sbuf_tensor("s_fw3", [64, 3072], F32))
            fv = s2.enter_context(nc.sbuf_tensor("s_fv", [64, 6], F32))
            zc = [s2.enter_context(nc.sbuf_tensor("s_zc%d" % i, [33, 512], F32)) for i in range(2)]
            ta = s2.enter_context(nc.sbuf_tensor("s_fta", [64, 512], F32))
            tb = s2.enter_context(nc.sbuf_tensor("s_ftb", [64, 512], F32))
            h1 = s2.enter_context(nc.sbuf_tensor("s_fh1", [64, 512], F32))
            S.dma(w1[:], din["filt_w1"][:, :], writes=["fw1"])
            S.dma(w2[:], din["filt_w2"][:, :], writes=["fw2"])
            S.dma(w3[:], din["filt_w3"][:, :], writes=["fw3"])
            for i, nm in enumerate(("filt_b1", "filt_fr1", "filt_b2", "filt_fr2")):
                S.dma(fv[:, i:i + 1], din[nm][:, :], writes=["fv"])
            S.op("dve", lambda e: e.tensor_tensor(out=fv[:, 4:5], in0=fv[:, 0:1], in1=fv[:, 1:2], op=ALU.mult), reads=["fv"], writes=["fv"])
            S.op("dve", lambda e: e.tensor_tensor(out=fv[:, 5:6], in0=fv[:, 2:3], in1=fv[:, 3:4], op=ALU.mult), reads=["fv"], writes=["fv"])
            w3v = w3[:].rearrange("p (o d b c) -> p o d b c", o=2, d=2, b=NCB)
            for o in range(2):
                S.op("dve", lambda e, o=o: e.tensor_tensor(out=w3sd[0:64, o, :, 0, :], in0=w3v[:, o, 0], in1=w3v[:, o, 1], op=ALU.add),
                     reads=["fw3"], writes=["w3sd"])
                S.op("dve", lambda e, o=o: e.tensor_tensor(out=w3sd[0:64, o, :, 1, :], in0=w3v[:, o, 0], in1=w3v[:, o, 1], op=ALU.subtract),
                     reads=["fw3"], writes=["w3sd"])
            for ci in range(T // 512):
                z, zr = zc[ci % 2], ("zc", ci % 2)
                S.dma(z[:], din["zfT"][:, ci * 512:(ci + 1) * 512], writes=[zr])
                S.op("pe", lambda e, z=z: e.matmul(ps[0][0:64, :], lhsT=w1[:], rhs=z[:], start=True, stop=True),
                     reads=[zr, "fw1"], writes=[("ps", 0)])
                sin_wrapped(S, ps[0][0:64, :], ("ps", 0), h1[:], "fh1", fv[:, 1:2], fv[:, 4:5], ta, "fta", tb, "ftb", 64, 512)
                S.op("pe", lambda e: e.matmul(ps[1][0:64, :], lhsT=w2[:], rhs=h1[:], start=True, stop=True),
                     reads=["fh1", "fw2"], writes=[("ps", 1)])
                sin_wrapped(S, ps[1][0:64, :], ("ps", 1), hdn[0:64, ci * 512:(ci + 1) * 512], "hdn", fv[:, 3:4], fv[:, 5:6],
                            ta, "fta", tb, "ftb", 64, 512)
            S.barrier()
        H = st.enter_context(nc.sbuf_tensor("s_H", [128, 128, 2, CB], BF16))
        Yk = st.enter_context(nc.sbuf_tensor("s_Yk", [128, 4, 128, CB], BF16))
        decs = [st.enter_context(nc.sbuf_tensor("s_dec%d" % i, [128, 128, CB], BF16)) for i in range(2)]
        negtn = st.enter_context(nc.sbuf_tensor("s_negtn", [128, 128], F32))
        drep = st.enter_context(nc.sbuf_tensor("s_drep", [128, 768], F32))
        EkS = st.enter_context(nc.sbuf_tensor("s_EkS", [128, 256], BF16))
        EkD = st.enter_context(nc.sbuf_tensor("s_EkD", [128, 256], BF16))
        Mfb = [st.enter_context(nc.sbuf_tensor("s_Mfb%d" % i, [128, 8, 256], BF16)) for i in range(3)]
        KFs = [st.enter_context(nc.sbuf_tensor("s_KFs%d" % i, [128, 4, 3, CB], BF16)) for i in range(3)]
        dK = st.enter_context(nc.sbuf_tensor("s_dK", [128, 2, CB], F32))
        S.dma(negtn[:], din["negtn"][:, :], writes=["negtn"])
        S.dma(drep[:], din["deltas_rep"][:, :], writes=["drep"])
        S.dma(EkS[:], din["E_kS"][:, :], writes=["EkS"])
        S.dma(EkD[:], din["E_kD"][:, :], writes=["EkD"])
        mfn = 0
        kfn = 0
        pn = 0
        cgen = phase1b_gen(k, st, W=512) if k.fuse1b else None
        cstep = [0]

        def cstepf():
            cstep[0] += 1
            if cgen is not None and cstep[0] % 4 == 0:
                next(cgen, None)

        def ykv(k2, off):
            b0 = off // 128
            b = Yk[:, b0:b0 + 1, k2:k2 + 1, 0:1]
            return AP(b.tensor, b.offset, [list(b.ap[0]), [2 * 128 * CB, 2], [1, CB]])

        def emit_dec(cbx, n1s):
            dd = decs[cbx % 2]
            for n1 in n1s:
                S.op("act", lambda e, n1=n1, dd=dd, cbx=cbx: e.activation(out=dd[:, n1, :], in_=drep[:, cbx * CB:(cbx + 1) * CB], func=AF.Exp,
                                                                        scale=negtn[:, n1:n1 + 1]),
                     reads=["drep", "negtn"], writes=[("dec", cbx % 2)])

        emit_dec(0, range(128))
        for cb in range(NCB):
            c0 = cb * CB
            dec = decs[cb % 2]
            for o in range(2):
                S.dma(dK[:, o, :], din["hyena_d"][o:o + 1, c0:c0 + CB].partition_broadcast(128), writes=["dK"])
            for o in range(2):
                for g in range(32):
                    pb, pr = ps[pn % 4], ("ps", pn % 4)
                    pn += 1
                    for j in range(4):
                        n1 = 4 * g + j
                        S.op("pe", lambda e, pb=pb, j=j, n1=n1, o=o, cb=cb: e.matmul(
                            pb[:, j * 128:(j + 1) * 128], lhsT=hdn[:, n1:T:128],
                            rhs=w3sd[:, o, cb].rearrange("p a b -> p (a b)"), start=True, stop=True),
                            reads=["hdn", "w3sd"], writes=[pr])
                    hout = H[:, 4 * g:4 * g + 4, :, :]
                    dv = dec[:, 4 * g:4 * g + 1, 0:1]
                    din1 = AP(dv.tensor, dv.offset, [list(dv.ap[0]), [CB, 4], [0, 2], [1, CB]])
                    S.op("dve", lambda e, pb=pb, hout=hout, din1=din1: e.tensor_tensor(
                        out=hout, in0=pb[:, :].rearrange("p (j s c) -> p j s c", j=4, s=2), in1=din1, op=ALU.mult),
                        reads=[pr, ("dec", cb % 2)], writes=["H"])
                    cstepf()
                for cp in range(CB // 2):
                    pi = pn % 2
                    pn += 1
                    prl = [("ps", 2 * pi), ("ps", 2 * pi + 1)]
                    for h in range(2):
                        c = 2 * cp + h
                        pb = ps[2 * pi + h]
                        S.op("pe", lambda e, pb=pb, c=c: e.matmul(pb[:, 0:256], lhsT=H[:, :, 0, c], rhs=EkS[:], start=True, stop=True),
                             reads=["H", "EkS"], writes=prl)
                        S.op("pe", lambda e, pb=pb, c=c: e.matmul(pb[:, 256:512], lhsT=H[:, :, 1, c], rhs=EkD[:], start=True, stop=True),
                             reads=["H", "EkD"], writes=prl)
                    yv = Yk[:, 0:1, 0:1, 2 * cp:2 * cp + 1]
                    yout = AP(yv.tensor, yv.offset, [list(yv.ap[0]), [CB, 512], [1, 2]])
                    pin = k.pp[pi][:, :].rearrange("p (h x) -> p x h", h=2)
                    if cp % 2 == 0:
                        S.op("act", lambda e, yout=yout, pin=pin: e.copy(out=yout, in_=pin), reads=prl, writes=["Yk"])
                    else:
                        S.op("dve", lambda e, yout=yout, pin=pin: e.tensor_copy(out=yout, in_=pin), reads=prl, writes=["Yk"])
                    if o == 1 and cb + 1 < NCB:
                        emit_dec(cb + 1, range(4 * cp, 4 * cp + 4))
                    cstepf()
                for g in range(32):
                    pb, pr = ps[4 + pn % 4], ("ps", 4 + pn % 4)
                    pn += 1
                    for j in range(4):
                        k2 = 4 * g + j
                        if k2 % 8 == 0:
                            mb_, mr_ = Mfb[mfn % 3], ("Mfb", mfn % 3)
                            mfn += 1
                            S.dma(mb_[:], din["Mf"][:, k2:k2 + 8, :], writes=[mr_])
                        S.op("pe", lambda e, pb=pb, j=j, k2=k2, mb_=mb_: e.matmul(
                            pb[:, j * 128:(j + 1) * 128], lhsT=mb_[:, k2 % 8, 0:128],
                            rhs=ykv(k2, 0), start=True, stop=False),
                            reads=["Yk", mr_], writes=[pr])
                        S.op("pe", lambda e, pb=pb, j=j, k2=k2, mb_=mb_: e.matmul(
                            pb[:, j * 128:(j + 1) * 128], lhsT=mb_[:, k2 % 8, 128:256],
                            rhs=ykv(k2, 128), start=False, stop=True),
                            reads=["Yk", mr_], writes=[pr])
                    kb, kr = KFs[kfn % 3], ("KFs", kfn % 3)
                    kfn += 1
                    pv = pb[:, :].rearrange("p (j s c) -> p j s c", j=4, s=2)
                    S.op("dve", lambda e, kb=kb, pv=pv, o=o: e.tensor_tensor(out=kb[:, :, 0, :], in0=pv[:, :, 0, :],
                                                                             in1=bc(dK[:, o:o + 1, :], [128, 4, CB]), op=ALU.add),
                         reads=[pr, "dK"], writes=[kr])
                    S.op("act", lambda e, kb=kb, pv=pv: e.copy(out=kb[:, :, 1, :], in_=pv[:, :, 1, :]), reads=[pr], writes=[kr])
                    S.op("dve", lambda e, kb=kb, pv=pv: e.tensor_scalar(out=kb[:, :, 2, :], in0=pv[:, :, 1, :], scalar1=-1.0, scalar2=None,
                                                                        op0=ALU.mult), reads=[pr], writes=[kr])
                    S.dma(k.KFd[o, cb, :, g * 4 * 3 * CB:(g + 1) * 4 * 3 * CB], kb[:].rearrange("p a b c -> p (a b c)"), reads=[kr], eng="pool")
                    cstepf()
        if cgen is not None:
            for _ in cgen:
                pass


def phaseH(k):
    nc, S, din, ps = k.nc, k.S, k.din, k.ps
    with ExitStack() as st:
        bufA = st.enter_context(nc.sbuf_tensor("s_hA", [128, CB, 128], BF16))
        bufB = st.enter_context(nc.sbuf_tensor("s_hB", [128, CB, 128], BF16))
        bufC = st.enter_context(nc.sbuf_tensor("s_hC", [128, 128, CB], BF16))
        Yd = st.enter_context(nc.sbuf_tensor("s_Yd", [128, 3, 128, CB], BF16))
        Pb = st.enter_context(nc.sbuf_tensor("s_P", [128, 128, 2, CB], BF16))
        Zs = st.enter_context(nc.sbuf_tensor("s_Zs", [128, 2, 128, CB], BF16))
        Ed = st.enter_context(nc.sbuf_tensor("s_Ed", [128, 384], BF16))
        G1 = st.enter_context(nc.sbuf_tensor("s_G1", [128, 256], BF16))
        G2 = st.enter_context(nc.sbuf_tensor("s_G2", [128, 256], BF16))
        drep = st.enter_context(nc.sbuf_tensor("s_hdrep", [128, 2, CB], F32))
        Mb = [st.enter_context(nc.sbuf_tensor("s_Mb%d" % i, [128, 8, 256], BF16)) for i in range(4)]
        KFb = [st.enter_context(nc.sbuf_tensor("s_KFb%d" % i, [128, 4, 3, CB], BF16)) for i in range(6)]
        t1 = [st.enter_context(nc.sbuf_tensor("s_ht1_%d" % i, [128, 4, 2, CB], BF16)) for i in range(2)]
        t2 = [st.enter_context(nc.sbuf_tensor("s_ht2_%d" % i, [128, 4, 2, CB], BF16)) for i in range(2)]
        te = [st.enter_context(nc.sbuf_tensor("s_hte%d" % i, [128, 8, CB], F32)) for i in range(2)]
        S.dma(Ed[:], din["E_d"][:, :], writes=["Ed"])
        S.dma(G1[:], din["G1"][:, :], writes=["G1"])
        S.dma(G2[:], din["G2"][:, :], writes=["G2"])
        ohs = AP(Pb[:].tensor, Pb[:].offset, [[Pb[:].ap[0][0], 64], [1, T]])
        mn = 0
        kn = 0
        tn_ = 0
        pn = 0

        def ydv(k2, off):
            b0 = off // 128
            return Yd[:, b0:b0 + 2, k2, :]

        def load_blk(buf, res, row0):
            src = AP(k.uT.tensor, k.uT[row0:row0 + 1, 0:1].offset, [[128, 128], [T, CB], [1, 128]])
            S.dma(buf[:], src, writes=[res])

        for cb in range(NCB):
            c0 = cb * CB
            load_blk(bufA, "hA", c0)
            load_blk(bufB, "hB", 768 + c0)
            for o in range(2):
                S.dma(drep[:, o, :], din["hyena_d"][o:o + 1, c0:c0 + CB].partition_broadcast(128), writes=["hdrep"])
            for o in range(2):
                Din, dres = (bufA, "hA") if o == 0 else (bufC, "hC")
                for cp in range(CB // 2):
                    pi = pn % 2
                    pn += 1
                    prl = [("ps", 2 * pi), ("ps", 2 * pi + 1)]
                    for h in range(2):
                        c = 2 * cp + h
                        pb = ps[2 * pi + h]
                        S.op("pe", lambda e, pb=pb, c=c, Din=Din, o=o: e.matmul(pb[:, 0:384], lhsT=(Din[:, c, :] if o == 0 else Din[:, :, c]),
                                                                             rhs=Ed[:], start=True, stop=True),
                             reads=[dres, "Ed"], writes=prl)
                    yv = Yd[:, 0:1, 0:1, 2 * cp:2 * cp + 1]
                    yout = AP(yv.tensor, yv.offset, [list(yv.ap[0]), [CB, 384], [1, 2]])
                    pin = k.pp[pi][:, :].rearrange("p (h x) -> p x h", h=2)[:, 0:384, :]
                    if cp % 2 == 0:
                        S.op("act", lambda e, yout=yout, pin=pin: e.copy(out=yout, in_=pin), reads=prl, writes=["Yd"])
                    else:
                        S.op("dve", lambda e, yout=yout, pin=pin: e.tensor_copy(out=yout, in_=pin), reads=prl, writes=["Yd"])
                for g in range(32):
                    pb, pr = ps[4 + pn % 4], ("ps", 4 + pn % 4)
                    pn += 1
                    kb, kr = KFb[kn % 6], ("KFb", kn % 6)
                    kn += 1
                    S.dma(kb[:].rearrange("p a b c -> p (a b c)"), k.KFd[o, cb, :, g * 12 * CB:(g + 1) * 12 * CB], writes=[kr])
                    for j in range(4):
                        k2 = 4 * g + j
                        if k2 % 8 == 0:
                            mb_, mr_ = Mb[mn % 4], ("Mb", mn % 4)
                            mn += 1
                            S.dma(mb_[:], din["Mf"][:, k2:k2 + 8, :], writes=[mr_])
                        S.op("pe", lambda e, pb=pb, j=j, k2=k2, mb_=mb_: e.matmul(
                            pb[:, j * 128:(j + 1) * 128], lhsT=mb_[:, k2 % 8, 0:128],
                            rhs=ydv(k2, 128), start=True, stop=False),
                            reads=["Yd", mr_], writes=[pr])
                        S.op("pe", lambda e, pb=pb, j=j, k2=k2, mb_=mb_: e.matmul(
                            pb[:, j * 128:(j + 1) * 128], lhsT=mb_[:, k2 % 8, 128:256],
                            rhs=ydv(k2, 0), start=False, stop=True),
                            reads=["Yd", mr_], writes=[pr])
                    a1, a1r = t1[tn_ % 2], ("ht1", tn_ % 2)
                    a2, a2r = t2[tn_ % 2], ("ht2", tn_ % 2)
                    tn_ += 1
                    pv = pb[:, :].rearrange("p (j s c) -> p j s c", j=4, s=2)
                    S.op("dve", lambda e, a1=a1, pv=pv, kb=kb: e.tensor_tensor(
                        out=a1[:], in0=pv, in1=bc(kb[:, :, 0:1, :], [128, 4, 2, CB]), op=ALU.mult), reads=[pr, kr], writes=[a1r])
                    S.op("dve", lambda e, a2=a2, pv=pv, kb=kb: e.tensor_tensor(
                        out=a2[:], in0=pv, in1=kb[:, :, 1:3, :], op=ALU.mult), reads=[pr, kr], writes=[a2r])
                    S.op("pool", lambda e, a1=a1, a2=a2, g=g: e.tensor_tensor(
                        out=Pb[:, 4 * g:4 * g + 4, 0, :], in0=a1[:, :, 0, :], in1=a2[:, :, 1, :], op=ALU.add),
                        reads=[a1r, a2r], writes=["P"])
                    S.op("pool", lambda e, a1=a1, a2=a2, g=g: e.tensor_tensor(
                        out=Pb[:, 4 * g:4 * g + 4, 1, :], in0=a1[:, :, 1, :], in1=a2[:, :, 0, :], op=ALU.add),
                        reads=[a1r, a2r], writes=["P"])
                for c2 in range(CB // 2):
                    pb, pr = ps[pn % 4], ("ps", pn % 4)
                    pn += 1
                    for h in range(2):
                        c = 2 * c2 + h
                        S.op("pe", lambda e, pb=pb, c=c, h=h: e.matmul(pb[:, h * 256:(h + 1) * 256], lhsT=Pb[:, :, 0, c], rhs=G1[:],
                                                                       start=True, stop=False), reads=["P", "G1"], writes=[pr])
                        S.op("pe", lambda e, pb=pb, c=c, h=h: e.matmul(pb[:, h * 256:(h + 1) * 256], lhsT=Pb[:, :, 1, c], rhs=G2[:],
                                                                       start=False, stop=True), reads=["P", "G2"], writes=[pr])
                    zv = Zs[:, 0:1, 0:1, 2 * c2:2 * c2 + 1]
                    zout = AP(zv.tensor, zv.offset, [list(zv.ap[0]), [CB, 256], [1, 2]])
                    pin = pb[:, :].rearrange("p (h x) -> p x h", h=2)
                    if c2 % 2 == 0:
                        S.op("act", lambda e, zout=zout, pin=pin: e.copy(out=zout, in_=pin), reads=[pr], writes=["Zs"])
                    else:
                        S.op("dve", lambda e, zout=zout, pin=pin: e.tensor_copy(out=zout, in_=pin), reads=[pr], writes=["Zs"])
                if o == 1:
                    load_blk(bufA, "hA", 1536 + c0)
                Xg, xres = (bufB, "hB") if o == 0 else (bufA, "hA")
                for g in range(16):
                    pb, pr = ps[4 + pn % 4], ("ps", 4 + pn % 4)
                    pn += 1
                    for j in range(8):
                        n1 = 8 * g + j
                        if n1 % 8 == 0:
                            mb_, mr_ = Mb[mn % 4], ("Mb", mn % 4)
                            mn += 1
                            S.dma(mb_[:], din["Minv"][:, n1:n1 + 8, :], writes=[mr_])
                        S.op("pe", lambda e, pb=pb, j=j, n1=n1, mb_=mb_: e.matmul(
                            pb[:, j * CB:(j + 1) * CB], lhsT=mb_[:, n1 % 8, 0:128], rhs=Zs[:, 0, n1, :], start=True, stop=False),
                            reads=["Zs", mr_], writes=[pr])
                        S.op("pe", lambda e, pb=pb, j=j, n1=n1, mb_=mb_: e.matmul(
                            pb[:, j * CB:(j + 1) * CB], lhsT=mb_[:, n1 % 8, 128:256], rhs=Zs[:, 1, n1, :], start=False, stop=True),
                            reads=["Zs", mr_], writes=[pr])
                    xin = Xg[:, :, 8 * g:8 * g + 8].rearrange("p c j -> p j c")
                    pv8 = pb[:, :].rearrange("p (j c) -> p j c", j=8)
                    if o == 0:
                        zo = bufC[:, 8 * g:8 * g + 8, :]
                        S.op("dve", lambda e, pv8=pv8, xin=xin, zo=zo: e.tensor_tensor(out=zo, in0=pv8, in1=xin, op=ALU.mult),
                             reads=[pr, xres], writes=["hC"])
                    else:
                        z3v = bufB[:].rearrange("p c j -> p (c j)")[:, 8 * g * CB:(8 * g + 8) * CB].rearrange("p (j c) -> p j c", j=8)
                        S.op("dve", lambda e, pv8=pv8, xin=xin, z3v=z3v: e.tensor_tensor(out=z3v, in0=pv8, in1=xin, op=ALU.mult),
                             reads=[pr, xres], writes=["hB"])
            z3 = bufB[:].rearrange("p c j -> p (c j)")
            for g in range(16):
                pb, pr = ps[pn % 4], ("ps", pn % 4)
                pn += 1
                pT = pb[:].bitcast(BF16)
                for j in range(8):
                    n1 = 8 * g + j
                    S.op("pe", lambda e, pT=pT, j=j, n1=n1: e.transpose(out=pT[0:CB, j * 128:(j + 1) * 128],
                                                                        in_=z3[:, n1 * CB:(n1 + 1) * CB], identity=k.ident[:]),
                         reads=["hB", "ident"], writes=[pr])
                ov = AP(ohs.tensor, ohs.offset + 8 * g, [list(ohs.ap[0]), [128, 128], [1, 8]])
                pin = pT[0:CB, :].rearrange("p (j n) -> p n j", j=8)
                if g % 2 == 0:
                    S.op("act", lambda e, ov=ov, pin=pin: e.copy(out=ov, in_=pin), reads=[pr], writes=["P"])
                else:
                    S.op("dve", lambda e, ov=ov, pin=pin: e.tensor_copy(out=ov, in_=pin), reads=[pr], writes=["P"])
            S.dma(k.ohT[c0:c0 + CB, :], ohs, reads=["P"])


def phaseA(k):
    nc, S, din, ps = k.nc, k.S, k.din, k.ps
    SPAN = 4096
    with ExitStack() as st:
        Hk = st.enter_context(nc.sbuf_tensor("s_Hk", [128, 3, 12, 256], BF16))
        J = st.enter_context(nc.sbuf_tensor("s_J", [128, 128], BF16))
        bm = st.enter_context(nc.sbuf_tensor("s_bm", [128, 256], BF16))
        swb = st.enter_context(nc.sbuf_tensor("s_swb", [128, 128], BF16))
        swf = st.enter_context(nc.sbuf_tensor("s_swf", [128, 128], F32))
        selA = st.enter_context(nc.sbuf_tensor("s_selA", [128, 128], F32))
        selB = st.enter_context(nc.sbuf_tensor("s_selB", [128, 128], F32))
        with ExitStack() as s2:
            rb = s2.enter_context(nc.sbuf_tensor("s_rb", [32, 12], F32))
            oh = s2.enter_context(nc.sbuf_tensor("s_oh", [32, 1152], F32))
            mr = s2.enter_context(nc.sbuf_tensor("s_mrow", [12, 1152], F32))
            av = s2.enter_context(nc.sbuf_tensor("s_av", [12, 1152], BF16))
            S.dma(rb[:], din["rel_bias"][:, :], writes=["rb"])
            S.dma(oh[:], din["OH"][:, :], writes=["oh"])
            S.dma(mr[:], din["mrow"][:, :], writes=["mrow"])
            for i in range(3):
                S.op("pe", lambda e, i=i: e.matmul(ps[i][0:12, 0:384], lhsT=rb[:], rhs=oh[:, i * 384:(i + 1) * 384], start=True, stop=True),
                     reads=["rb", "oh"], writes=[("ps", i)])
                S.op("dve", lambda e, i=i: e.tensor_tensor(out=av[:, i * 384:(i + 1) * 384], in0=ps[i][0:12, 0:384],
                                                           in1=mr[:, i * 384:(i + 1) * 384], op=ALU.add),
                     reads=[("ps", i), "mrow"], writes=["av"])
            S.dma(k.Avec[:, :], av[:], reads=["av"], writes=["Avec"])
            for h in range(12):
                for ri in range(3):
                    src = AP(k.Avec.tensor, k.Avec[h:h + 1, ri * 384:ri * 384 + 1].offset, [[1, 128], [1, 256]])
                    S.dma(Hk[:, ri, h, :], src, reads=["Avec"], writes=["Hk"])
            S.dma(J[:], din["antiid"][:, :], writes=["J"])
            S.dma(bm[:], din["bmask"][:, :], writes=["bm"])
            S.dma(swb[:], din["swap"][:, :], writes=["swb"])
            S.op("dve", lambda e: e.tensor_copy(out=swf[:], in_=swb[:]), reads=["swb"], writes=["swf"])
            S.op("pool", lambda e: e.memset(selA[:], 0.0), writes=["sel"])
            S.op("pool", lambda e: e.memset(selB[:], 0.0), writes=["sel"])
            S.op("dve", lambda e: e.tensor_copy(out=selA[:, 0:64], in_=swb[:, 0:64]), reads=["swb", "sel"], writes=["sel"])
            S.op("dve", lambda e: e.tensor_copy(out=selB[:, 64:128], in_=swb[:, 64:128]), reads=["swb", "sel"], writes=["sel"])
            S.barrier()
        TP = T + 2 * PAD
        qs1 = [st.enter_context(nc.sbuf_tensor("s_qs1_%d" % i, [128, 2, SPAN], BF16)) for i in range(2)]
        q4 = st.enter_context(nc.sbuf_tensor("s_q4", [128, 2, 4, SPAN // 4], BF16))
        q16 = st.enter_context(nc.sbuf_tensor("s_q16", [128, 2, 16, SPAN // 16], BF16))
        kT = st.enter_context(nc.sbuf_tensor("s_kT", [128, TP], BF16))
        for i in range(2):
            S.op("pool", lambda e, i=i: e.memset(qs1[i][:], 0.0), writes=[("qs1", i)])
        S.op("pool", lambda e: e.memset(kT[:, 0:PAD], 0.0), writes=["kT"])
        S.op("pool", lambda e: e.memset(kT[:, PAD + T:TP], 0.0), writes=["kT"])
        acc = st.enter_context(nc.sbuf_tensor("s_acc", [128, 2, SPAN], F32))
        OW = 2048
        oT = [st.enter_context(nc.sbuf_tensor("s_oT%d" % i, [128, OW], BF16)) for i in range(2)]
        rden = [st.enter_context(nc.sbuf_tensor("s_rden%d" % i, [128, 512], F32)) for i in range(2)]
        Vt = [st.enter_context(nc.sbuf_tensor("s_Vt%d" % i, [128, 256], BF16)) for i in range(8)]
        PT = [st.enter_context(nc.sbuf_tensor("s_PT%d" % i, [128, 2, 256], BF16)) for i in range(6)]
        vn = 0
        ptn = 0
        sn = 0
        on = 0
        rn = 0
        spn = 0
        DSK = 3
        cgen = None
        cstep = 0
        for hp in range(6):
            S.dma(kT[:, PAD:PAD + T], k.qkT[768 + hp * 128:768 + (hp + 1) * 128, :], writes=["kT"])

            def load_q(hp_, s_, gi):
                qb, qr_ = qs1[gi % 2], ("qs1", gi % 2)
                S.dma(qb[0:64, 0, :], k.qkT[hp_ * 128:hp_ * 128 + 64, s_ * SPAN:(s_ + 1) * SPAN], writes=[qr_])
                S.dma(qb[64:128, 1, :], k.qkT[hp_ * 128 + 64:hp_ * 128 + 128, s_ * SPAN:(s_ + 1) * SPAN], writes=[qr_])

            NSP = T // SPAN
            if hp == 0:
                load_q(0, 0, 0)
            for s in range(NSP):
                gi = hp * NSP + s
                qb, qbr = qs1[gi % 2], ("qs1", gi % 2)
                if s + 1 < NSP:
                    load_q(hp, s + 1, gi + 1)
                elif hp + 1 < 6:
                    load_q(hp + 1, 0, gi + 1)
                S.op("act", lambda e, qb=qb: e.copy(out=q4[:], in_=qb[:].rearrange("p h (m r) -> p h r m", r=4)), reads=[qbr], writes=["q4"])
                S.op("act", lambda e, qb=qb: e.copy(out=q16[:], in_=qb[:].rearrange("p h (m r) -> p h r m", r=16)), reads=[qbr], writes=["q16"])
                pendB = []
                S.op("pool", lambda e: e.memset(acc[:], 0.0), writes=["acc"])
                for ri, r in enumerate((1, 4, 16)):
                    Lr = T // r
                    jb = (T // 2) // (r * 128)
                    nqb = SPAN // (128 * r)
                    for rho in range(r):
                        ja, jbnd = s * nqb, s * nqb + nqb
                        pobank = {}
                        for j in range(ja, jbnd + 1):
                            c_lo = 128 if j == ja else 0
                            c_hi = 128 if j == jbnd else 256
                            ncol = c_hi - c_lo
                            vt, vr = Vt[vn % 8], ("Vt", vn % 8)
                            vn += 1
                            m0 = 128 * j - 64
                            lo, hi = 0, 128
                            if m0 < 0:
                                lo = 64
                            if m0 + 128 > Lr:
                                hi = 64
                            if lo > 0 or hi < 128:
                                S.op("pool", lambda e, vt=vt: e.memset(vt[:], 0.0), writes=[vr])
                            tok0 = rho + r * (m0 + lo)
                            src = AP(k.vaug.tensor, k.vaug[tok0:tok0 + 1, hp * 256:hp * 256 + 1].offset, [[1536 * r, hi - lo], [1, 256]])
                            S.dma(vt[lo:hi, :], src, writes=[vr])
                            mq0 = 128 * j - 128 + c_lo
                            ml = mq0 - s * (SPAN // r)
                            if r == 1:
                                qmov, qres = qb[:, :, ml:ml + ncol], qbr
                            elif r == 4:
                                qmov, qres = q4[:, :, rho, ml:ml + ncol], "q4"
                            else:
                                qmov, qres = q16[:, :, rho, ml:ml + ncol], "q16"
                            kc0 = PAD + rho + r * m0
                            ksl = slice(kc0, kc0 + 127 * r + 1, r)
                            straddle = (j == jb)
                            pS, psr = ps[sn % 4], ("ps", sn % 4)
                            sn += 1
                            pt, ptr = PT[ptn % 6], ("PT", ptn % 6)
                            ptn += 1
                            pSv = pS[:, :].rearrange("p (h c) -> p h c", h=2)[:, :, 0:ncol]

                            def stageA(pSv=pSv, psr=psr, ksl=ksl, qmov=qmov, qres=qres, ncol=ncol, ri=ri, c_lo=c_lo, c_hi=c_hi,
                                       straddle=straddle, pt=pt, ptr=ptr, hp=hp):
                                S.op("pe", lambda e: e.matmul(pSv, lhsT=kT[:, ksl], rhs=qmov, start=True, stop=False),
                                     reads=["kT", qres], writes=[psr])
                                S.op("pe", lambda e: e.matmul(pSv, lhsT=J[:], rhs=Hk[:, ri, 2 * hp:2 * hp + 2, c_lo:c_hi],
                                                              start=False, stop=not straddle), reads=["J", "Hk"], writes=[psr])
                                if straddle:
                                    S.op("pe", lambda e: e.matmul(pSv, lhsT=k.ident[:], rhs=bc(bm[:, c_lo:c_hi].rearrange("p (o c) -> p o c", o=1), [128, 2, ncol]),
                                                                  start=False, stop=True), reads=["ident", "bm"], writes=[psr])
                                S.op("act", lambda e: e.activation(out=pt[:, :, 0:ncol], in_=pSv, func=AF.Exp), reads=[psr], writes=[ptr])

                            pieces = []
                            if c_lo == 0:
                                pieces.append((0, j - 1))
                            if c_hi == 256:
                                pieces.append((1, j))
                            for (half, jq) in pieces:
                                if half == 1:
                                    pobank[jq] = (ps[4 + on % 4], ("ps", 4 + on % 4))
                                    on += 1
                            pbs = {jq: pobank[jq] for (_, jq) in pieces}

                            def stageB(pieces=pieces, pbs=pbs, vt=vt, vr=vr, pt=pt, ptr=ptr, c_lo=c_lo, r=r, rho=rho, s=s):
                                for (half, jq) in pieces:
                                    po, por = pbs[jq]
                                    off = half * 128 - c_lo
                                    for hh in range(2):
                                        S.op("pe", lambda e, po=po, hh=hh, off=off, half=half: e.matmul(
                                            po[:, hh * 128:(hh + 1) * 128], lhsT=vt[:, hh * 128:(hh + 1) * 128], rhs=pt[:, hh, off:off + 128],
                                            start=(half == 1 and hh == 0), stop=(half == 0), skip_group_check=True),
                                            reads=[vr, ptr], writes=[por])
                                    if half == 0:
                                        a0 = rho + r * 128 * jq - s * SPAN
                                        asl = slice(a0, a0 + 127 * r + 1, r)
                                        S.op("dve", lambda e, po=po, asl=asl: e.tensor_tensor(
                                            out=acc[:, :, asl], in0=po[:, 0:256].rearrange("p (h c) -> p h c", h=2), in1=acc[:, :, asl],
                                            op=ALU.add), reads=[por, "acc"], writes=["acc"])

                            stageA()
                            pendB.append(stageB)
                            if len(pendB) > DSK:
                                pendB.pop(0)()
                            cstep += 1
                            if cgen is not None and cstep % 4 == 0:
                                next(cgen, None)
                while pendB:
                    pendB.pop(0)()
                for ow in range(SPAN // OW):
                    ot, otr = oT[spn % 2], ("oT", spn % 2)
                    spn += 1
                    for cc in range(OW // 512):
                        c0 = ow * OW + cc * 512
                        pw, pwr = ps[sn % 4], ("ps", sn % 4)
                        sn += 1
                        S.op("pe", lambda e, pw=pw, c0=c0: e.matmul(pw[:, :], lhsT=selA[:], rhs=acc[:, 0, c0:c0 + 512],
                                                                    start=True, stop=False), reads=["acc", "sel"], writes=[pwr])
                        S.op("pe", lambda e, pw=pw, c0=c0: e.matmul(pw[:, :], lhsT=selB[:], rhs=acc[:, 1, c0:c0 + 512],
                                                                    start=False, stop=True), reads=["acc", "sel"], writes=[pwr])
                        rd, rdr = rden[rn % 2], ("rden", rn % 2)
                        rn += 1
                        S.op("dve", lambda e, rd=rd, pw=pw: e.reciprocal(out=rd[:], in_=pw[:, :]), reads=[pwr], writes=[rdr])
                        for hh in range(2):
                            nlo = 64 * hh
                            S.op("pool", lambda e, rd=rd, ot=ot, hh=hh, cc=cc, c0=c0, nlo=nlo: e.tensor_tensor(
                                out=ot[nlo:nlo + 64, cc * 512:(cc + 1) * 512], in0=acc[nlo:nlo + 64, hh, c0:c0 + 512],
                                in1=rd[nlo:nlo + 64, :], op=ALU.mult), reads=["acc", rdr], writes=[otr])
                    t0_ = s * SPAN + ow * OW
                    S.dma(k.oaT[hp * 128:(hp + 1) * 128, t0_:t0_ + OW], ot[:], reads=[otr])
        if cgen is not None:
            for _ in cgen:
                pass


def phaseF(k):
    nc, S, din, ps = k.nc, k.S, k.din, k.ps
    with ExitStack() as st:
        wpa = st.enter_context(nc.sbuf_tensor("s_wpa", [128, 6, DM], BF16))
        wph = st.enter_context(nc.sbuf_tensor("s_wph", [128, 6, DM], BF16))
        wo = st.enter_context(nc.sbuf_tensor("s_wo", [128, 2, 8, DM], BF16))
        gater = st.enter_context(nc.sbuf_tensor("s_gater", [128, 2, DM], F32))
        for s_ in range(2):
            S.dma(gater[:, s_, :], k.modrep[s_:s_ + 1, 2 * DM:3 * DM].partition_broadcast(128), writes=[("modr", s_)])
        with ExitStack() as s2:
            stg = [s2.enter_context(nc.sbuf_tensor("s_fstg%d" % i, [128, DM], F32)) for i in range(2)]
            n = 0
            for (wt, nm, src, nk) in ((wpa, "wpa", "w_proj_attn", 6), (wph, "wph", "w_proj_hyena", 6), (wo, "wo", "w_out", 8)):
                for kk in range(nk):
                    b = n % 2
                    S.dma(stg[b][:], din[src][kk * 128:(kk + 1) * 128, :], writes=[("fstg", b)])
                    eng = ("dve", "pool")[n % 2]
                    if nm == "wo":
                        for s_ in range(2):
                            S.op(("dve", "pool")[s_], lambda e, b=b, wt=wt, kk=kk, s_=s_: e.tensor_tensor(
                                out=wt[:, s_, kk, :], in0=stg[b][:], in1=gater[:, s_, :], op=ALU.mult),
                                reads=[("fstg", b), ("modr", s_)], writes=[nm])
                    else:
                        S.op(eng, lambda e, b=b, wt=wt, kk=kk: e.tensor_copy(out=wt[:, kk, :], in_=stg[b][:]), reads=[("fstg", b)], writes=[nm])
                    n += 1
            S.barrier()
        oa = [st.enter_context(nc.sbuf_tensor("s_foa%d" % i, [128, 6, 512], BF16)) for i in range(2)]
        ga = [st.enter_context(nc.sbuf_tensor("s_fga%d" % i, [128, 6, 512], BF16)) for i in range(2)]
        oh_ = [st.enter_context(nc.sbuf_tensor("s_foh%d" % i, [128, 6, 512], BF16)) for i in range(2)]
        gh = [st.enter_context(nc.sbuf_tensor("s_fgh%d" % i, [128, 6, 512], BF16)) for i in range(2)]
        mt = [st.enter_context(nc.sbuf_tensor("s_fmt%d" % i, [128, 16, 512], BF16)) for i in range(2)]
        mix = [st.enter_context(nc.sbuf_tensor("s_fmix%d" % i, [128, 8, 512], BF16)) for i in range(2)]
        ta = [st.enter_context(nc.sbuf_tensor("s_fta%d" % i, [128, 512], F32)) for i in range(2)]
        tb = [st.enter_context(nc.sbuf_tensor("s_ftb%d" % i, [128, 512], F32)) for i in range(2)]
        xt = [st.enter_context(nc.sbuf_tensor("s_fx%d" % i, [128, DM], F32)) for i in range(2)]
        r1 = [st.enter_context(nc.sbuf_tensor("s_fr%d" % i, [128, DM], F32)) for i in range(2)]
        yo = [st.enter_context(nc.sbuf_tensor("s_fy%d" % i, [128, DM], F32)) for i in range(2)]
        sqj = st.enter_context(nc.sbuf_tensor("s_fsq", [128, DM], BF16))
        ss = [st.enter_context(nc.sbuf_tensor("s_fss%d" % i, [128, 2], F32)) for i in range(2)]
        pn = 0
        tn_ = 0

        def fprep(ci):
            b = ci % 2
            cs = slice(ci * 512, (ci + 1) * 512)
            S.dma(oa[b][:], k.oaT[:, cs].rearrange("(a p) t -> p a t", p=128), writes=[("foa", b)])
            S.dma(ga[b][:], k.gaT[:, cs].rearrange("(a p) t -> p a t", p=128), writes=[("fga", b)])
            S.dma(oh_[b][:], k.ohT[:, cs].rearrange("(a p) t -> p a t", p=128), writes=[("foh", b)])
            S.dma(gh[b][:], k.ghT[:, cs].rearrange("(a p) t -> p a t", p=128), writes=[("fgh", b)])
            S.dma(mt[b][:], k.mT[:, cs].rearrange("(a p) t -> p a t", p=128), writes=[("fmt", b)])
            S.op("pool", lambda e, b=b: e.tensor_tensor(out=oa[b][:], in0=oa[b][:], in1=ga[b][:], op=ALU.mult),
                 reads=[("foa", b), ("fga", b)], writes=[("foa", b)])
            S.op("dve", lambda e, b=b: e.tensor_tensor(out=oh_[b][:], in0=oh_[b][:], in1=gh[b][:], op=ALU.mult),
                 reads=[("foh", b), ("fgh", b)], writes=[("foh", b)])

        def frun(ci):
            nonlocal pn, tn_
            seg = 0 if ci < NCHUNK // 2 else 1
            b = ci % 2
            for fb in range(8):
                pA, pAr = ps[pn % 4], ("ps", pn % 4)
                pn += 1
                pH, pHr = ps[pn % 4], ("ps", pn % 4)
                pn += 1
                for kk in range(6):
                    S.op("pe", lambda e, pA=pA, kk=kk, fb=fb, b=b: e.matmul(pA[:, :], lhsT=wpa[:, kk, fb * 128:(fb + 1) * 128], rhs=oa[b][:, kk, :],
                                                                             start=(kk == 0), stop=(kk == 5)), reads=["wpa", ("foa", b)], writes=[pAr])
                for kk in range(6):
                    S.op("pe", lambda e, pH=pH, kk=kk, fb=fb, b=b: e.matmul(pH[:, :], lhsT=wph[:, kk, fb * 128:(fb + 1) * 128], rhs=oh_[b][:, kk, :],
                                                                             start=(kk == 0), stop=(kk == 5)), reads=["wph", ("foh", b)], writes=[pHr])
                a_, ar_ = ta[tn_ % 2], ("fta", tn_ % 2)
                b_, br_ = tb[tn_ % 2], ("ftb", tn_ % 2)
                tn_ += 1
                S.op("dve", lambda e, a_=a_, pA=pA, fb=fb, b=b: e.tensor_tensor(out=a_[:], in0=pA[:, :], in1=mt[b][:, fb, :], op=ALU.mult),
                     reads=[pAr, ("fmt", b)], writes=[ar_])
                S.op("dve", lambda e, b_=b_, pH=pH, fb=fb, b=b: e.tensor_tensor(out=b_[:], in0=pH[:, :], in1=mt[b][:, 8 + fb, :], op=ALU.mult),
                     reads=[pHr, ("fmt", b)], writes=[br_])
                S.op("pool", lambda e, a_=a_, b_=b_, fb=fb, b=b: e.tensor_tensor(out=mix[b][:, fb, :], in0=a_[:], in1=b_[:], op=ALU.add),
                     reads=[ar_, br_], writes=[("fmix", b)])
            pend = []
            for tt in range(4):
                t = ci * 4 + tt
                tb2 = t % 2
                S.dma(xt[tb2][:], din["x"][t * 128:(t + 1) * 128, :], writes=[("fx", tb2)], eng="act")
                p0, p1 = ps[4 + 2 * tb2], ps[5 + 2 * tb2]
                p0r, p1r = ("ps", 4 + 2 * tb2), ("ps", 5 + 2 * tb2)
                for half, (pp_, ppr) in enumerate(((p0, p0r), (p1, p1r))):
                    for fb in range(8):
                        S.op("pe", lambda e, pp_=pp_, fb=fb, tt=tt, half=half, b=b, seg=seg: e.matmul(
                            pp_[:, :], lhsT=mix[b][:, fb, tt * 128:(tt + 1) * 128], rhs=wo[:, seg, fb, half * 512:(half + 1) * 512],
                            start=(fb == 0), stop=(fb == 7)), reads=[("fmix", b), "wo"], writes=[ppr])
                rr, rrr = r1[tb2], ("fr", tb2)
                S.op("dve", lambda e, rr=rr, tb2=tb2: e.tensor_tensor(
                    out=rr[:], in0=k.pp[2 + tb2][:, :], in1=xt[tb2][:], op=ALU.add),
                    reads=[p0r, p1r, ("fx", tb2)], writes=[rrr])
                sb, sr = ss[tb2], ("fss", tb2)
                S.op("act", lambda e, rr=rr, sb=sb: e.activation(out=sqj[:], in_=rr[:], func=AF.Square, scale=1.0 / 32.0, accum_out=sb[:, 0:1]),
                     reads=[rrr], writes=["fsq", sr])
                S.op("act", lambda e, sb=sb: e.activation(out=sb[:, 1:2], in_=sb[:, 0:1], func=AF.Sqrt, bias=k.epsc[:, 0:1]),
                     reads=[sr, "epsc"], writes=[sr])

                def stage2(rr=rr, rrr=rrr, sb=sb, sr=sr, tb2=tb2, t=t):
                    S.op("dve", lambda e: e.reciprocal(out=sb[:, 1:2], in_=sb[:, 1:2]), reads=[sr], writes=[sr])
                    yb, yr = yo[tb2], ("fy", tb2)
                    S.op("dve", lambda e: e.scalar_tensor_tensor(
                        out=yb[:], in0=rr[:], scalar=sb[:, 1:2], in1=k.fg_rep[:], op0=ALU.mult, op1=ALU.mult),
                        reads=[rrr, sr, "fg_rep"], writes=[yr])
                    S.dma(k.y[t * 128:(t + 1) * 128, :], yb[:], reads=[yr], eng="pool")

                pend.append(stage2)
                if len(pend) > 1:
                    pend.pop(0)()
            while pend:
                pend.pop(0)()

        fprep(0)
        for ci in range(NCHUNK):
            if ci + 1 < NCHUNK:
                fprep(ci + 1)
            frun(ci)
```

```python
import math
import numpy as np
import ml_dtypes
import concourse.bass as bass
import concourse.mybir as mybir
from concourse.ap import AP
from concourse.bass_utils import run_bass_kernel_spmd

F32 = mybir.dt.float32
BF16 = mybir.dt.bfloat16
AF = mybir.ActivationFunctionType
ALU = mybir.AluOpType

T = 16384
DM = 1024
NCHUNK = T // 512
NFFT = 32768
EPS = 1e-6
CB = 64
NCB = 768 // CB
PAD = 1024

OQ, OK_, OV, OGA, OU, OGH, OMA, OMH = 0, 768, 1536, 2304, 3072, 5376, 6144, 7168

DEBUG = {}


class Sched:
    ENG = ("pe", "act", "dve", "pool", "sp")

    def __init__(self, nc, sems, dma_ring):
        self.nc = nc
        self.ops = []
        self.eng = {"pe": nc.tensor, "act": nc.scalar, "dve": nc.vector, "pool": nc.gpsimd, "sp": nc.sync}
        self.sems = sems
        self.ring = dma_ring
        self.cnt = {e: 0 for e in self.ENG}
        self.ndma = 0
        self.last_w = {}
        self.readers = {}
        self.waited = {e: {d: 0 for d in self.ENG} for e in self.ENG}
        self.waited_dma = {e: {} for e in self.ENG}
        self.last_op = {e: None for e in self.ENG}
        self.pending = []
        self.dma_eng = "sp"

    def op(self, eng, fn, reads=(), writes=()):
        self.ops.append(("op", eng, fn, tuple(reads), tuple(writes)))

    def dma(self, out, in_, reads=(), writes=(), eng="sp"):
        self.ops.append(("dma", eng, (out, in_), tuple(reads), tuple(writes)))

    def barrier(self):
        self.ops.append(("bar",))

    def flush(self):
        ops = self.ops
        n = len(ops)
        last_w, readers = {}, {}
        last_on = {e: -1 for e in self.ENG}
        deps = [None] * n
        marked = [False] * n
        for i, o in enumerate(ops):
            if o[0] == "bar":
                deps[i] = dict(last_on)
                for e, j in last_on.items():
                    if j >= 0:
                        marked[j] = True
                last_w, readers = {}, {}
                continue
            _, e, _, rd, wr = o
            d = set()
            for r in rd:
                if r in last_w:
                    d.add(last_w[r])
            for w in wr:
                if w in last_w:
                    d.add(last_w[w])
                for j in readers.get(w, ()):
                    d.add(j)
            d.discard(i)
            deps[i] = d
            for j in d:
                marked[j] = True
            for w in wr:
                last_w[w] = i
                readers[w] = []
            for r in rd:
                if r not in wr:
                    readers.setdefault(r, []).append(i)
            last_on[e] = i
        ordinal = [0] * n
        cnt = {e: 0 for e in self.ENG}
        dslot = [None] * n
        nd = 0
        P = len(self.ring)
        for i, o in enumerate(ops):
            if o[0] == "dma":
                dslot[i] = (nd % P, 16 * (nd // P + 1))
                nd += 1
            elif o[0] == "op" and marked[i]:
                cnt[o[1]] += 1
                ordinal[i] = cnt[o[1]]
        waited = {e: {d: 0 for d in self.ENG} for e in self.ENG}
        wdma = {e: {} for e in self.ENG}

        def wait_for(e, j):
            oj = ops[j]
            if oj[0] == "dma":
                slot, val = dslot[j]
                if wdma[e].get(slot, 0) >= val:
                    return
                wdma[e][slot] = val
                self.eng[e].wait_ge(self.ring[slot], val)
            else:
                dsrc = oj[1]
                if dsrc == e and e == "pe":
                    return
                if waited[e][dsrc] >= ordinal[j]:
                    return
                waited[e][dsrc] = ordinal[j]
                self.eng[e].wait_ge(self.sems[dsrc], ordinal[j])

        nd = 0
        dma_hist = []
        for i, o in enumerate(ops):
            if o[0] == "bar":
                for e in self.ENG:
                    for dsrc, j in deps[i].items():
                        if j >= 0 and not (dsrc == e and ops[j][0] == "op"):
                            wait_for(e, j)
                    for j in dma_hist[-P:]:
                        wait_for(e, j)
                continue
            kind, e, payload, rd, wr = o
            for j in sorted(deps[i]):
                wait_for(e, j)
            if kind == "dma":
                slot, val = dslot[i]
                if val > 16:
                    if wdma[e].get(slot, 0) < val - 16:
                        wdma[e][slot] = val - 16
                        self.eng[e].wait_ge(self.ring[slot], val - 16)
                out, in_ = payload
                self.eng[e].dma_start(out=out, in_=in_).then_inc(self.ring[slot], 16)
                dma_hist.append(i)
                nd += 1
            else:
                ins = payload(self.eng[e])
                if marked[i]:
                    ins.then_inc(self.sems[e], 1)
        for j in dma_hist[-P:]:
            wait_for("sp", j)
        self.ops = []
        return n


def bc(ap, shape):
    return ap.to_broadcast(list(shape))


def _t5_bucket(rel):
    nb = 16
    max_exact = 8
    n = np.abs(rel)
    large = max_exact + (np.log(np.maximum(n, 1) / max_exact) / math.log(1024 / max_exact) * (nb - max_exact)).astype(np.int32)
    large = np.minimum(large, nb - 1)
    return ((rel > 0).astype(np.int32) * nb + np.where(n < max_exact, n, large)).astype(np.int32)


def bf(a):
    return np.ascontiguousarray(a.astype(np.float32)).astype(ml_dtypes.bfloat16)


_CONST_CACHE = {}


def build_consts(is_prompt):
    key = bool(is_prompt)
    if key in _CONST_CACHE:
        return _CONST_CACHE[key]
    c = {}
    L = 16384 if is_prompt else 8192
    tt = np.linspace(0.0, 1.0, L, dtype=np.float32)[:, None]
    w = (np.float32(2.0 * math.pi / L) * np.arange(L, dtype=np.float32))[:, None]
    f = np.linspace(1e-4, 15, 16, dtype=np.float32)[None, :]
    z = np.concatenate([tt, np.cos(f * w), -np.sin(f * w)], axis=-1).astype(np.float32)
    zf = np.zeros((T, 33), np.float32)
    zf[:L] = z
    c["zfT"] = np.ascontiguousarray(zf.T)
    tn = np.zeros(T, np.float32)
    tn[:L] = tt[:, 0]
    c["negtn"] = np.ascontiguousarray(-tn.reshape(128, 128))
    deltas = np.abs(np.linspace(math.log(0.01) / 1.5, math.log(0.01) / 0.3, 768, dtype=np.float32))
    c["deltas_rep"] = np.ascontiguousarray(np.broadcast_to(deltas[None, :], (128, 768))).astype(np.float32)
    slot = np.arange(128) if is_prompt else np.concatenate([np.arange(64), np.arange(64) + 128])
    k2 = np.arange(128)
    n1 = np.arange(128)
    k1 = np.arange(128)
    ang = -2 * np.pi * np.outer(slot, k2 + 0.5) / 256.0
    Er, Ei = np.cos(ang), np.sin(ang)
    c["E_d"] = bf(np.concatenate([-Ei, Er, Ei], axis=1))
    angk = -2 * np.pi * np.outer(np.arange(128), k2 + 0.5) / 256.0
    Ekr, Eki = np.cos(angk), np.sin(angk)
    if not is_prompt:
        Ekr[64:] = 0
        Eki[64:] = 0
    c["E_kS"] = bf(np.concatenate([Ekr, -Eki], axis=1))
    c["E_kD"] = bf(np.concatenate([Eki, Ekr], axis=1))
    angM = -2 * np.pi * (n1[None, :, None] * (k2[:, None, None] + 0.5) / NFFT + n1[None, :, None] * k1[None, None, :] / 128.0)
    c["Mf"] = bf(np.concatenate([np.cos(angM), np.sin(angM)], axis=2).transpose(1, 0, 2))
    angG = 2 * np.pi * np.outer(k1, n1) / 128.0
    c["G1"] = bf(np.concatenate([np.cos(angG), np.sin(angG)], axis=1))
    c["G2"] = bf(np.concatenate([-np.sin(angG), np.cos(angG)], axis=1))
    angI = 2 * np.pi * (n1[:, None, None] + 128 * slot[None, None, :]) * (k2[None, :, None] + 0.5) / NFFT
    sc = 2.0 / NFFT
    c["Minv"] = bf(np.concatenate([sc * np.cos(angI), -sc * np.sin(angI)], axis=2).transpose(1, 0, 2))
    OH = np.zeros((32, 3 * 384), np.float32)
    mrow = np.zeros((12, 3 * 384), np.float32)
    for ri, r in enumerate((1, 4, 16)):
        for j in range(384):
            d = j - 127
            if 0 <= d <= 128:
                rel = (64 - d) * r
                OH[_t5_bucket(np.array(rel)), ri * 384 + j] = 1.0
            else:
                mrow[:, ri * 384 + j] = -30000.0
    c["OH"] = OH
    c["mrow"] = mrow
    bm = np.zeros((128, 256), np.float32)
    if not is_prompt:
        bm[:64, 128:] = -30000.0
        bm[64:, :128] = -30000.0
    c["bmask"] = bf(bm)
    c["bflag"] = np.full((128, 1), 1.0 if is_prompt else 0.0, np.float32)
    ident = np.eye(128, dtype=np.float32)
    c["ident"] = bf(ident)
    c["antiid"] = bf(ident[::-1])
    c["swap"] = bf(np.roll(ident, 64, axis=1))
    _CONST_CACHE[key] = c
    return c


CONST_SHAPES = {
    "zfT": ([33, T], F32), "negtn": ([128, 128], F32), "deltas_rep": ([128, 768], F32),
    "E_d": ([128, 384], BF16), "E_kS": ([128, 256], BF16), "E_kD": ([128, 256], BF16),
    "Mf": ([128, 128, 256], BF16), "G1": ([128, 256], BF16), "G2": ([128, 256], BF16),
    "Minv": ([128, 128, 256], BF16), "OH": ([32, 1152], F32), "mrow": ([12, 1152], F32),
    "bmask": ([128, 256], BF16), "bflag": ([128, 1], F32), "ident": ([128, 128], BF16),
    "antiid": ([128, 128], BF16), "swap": ([128, 128], BF16),
}

INPUT_SHAPES = {
    "x": [T, DM], "cT": [128, 8, 2], "w_ada": [DM, 3 * DM], "b_ada": [1, 3 * DM], "norm_g": [1, DM],
    "w_in": [DM, 8192], "short_wT": [128, 18, 3], "short_bT": [128, 18],
    "filt_w1": [33, 64], "filt_b1": [64, 1], "filt_fr1": [64, 1], "filt_w2": [64, 64], "filt_b2": [64, 1],
    "filt_fr2": [64, 1], "filt_w3": [64, 3072], "hyena_d": [2, 768],
    "w_proj_attn": [768, DM], "w_proj_hyena": [768, DM], "w_out": [DM, DM], "rel_bias": [32, 12],
    "final_g": [1, DM],
}


from contextlib import ExitStack


class K:
    pass


def build_program(phases=("p0", "pk", "p1", "p1b", "pa", "ph", "pf"), debug_outs=()):
    nc = bass.Bass("TRN2", target_bir_lowering=False)
    k = K()
    k.nc = nc
    din = {}
    for name, shp in INPUT_SHAPES.items():
        din[name] = nc.dram_tensor(name, shp, F32, kind="ExternalInput").ap()
    for name, (shp, dt_) in CONST_SHAPES.items():
        din[name] = nc.dram_tensor(name, shp, dt_, kind="ExternalInput").ap()
    k.din = din
    y = nc.dram_tensor("y", [T, DM], F32, kind="ExternalOutput").ap()
    k.y = y

    def scratch(name, shp, dt_):
        kind = "ExternalOutput" if name in debug_outs else "Internal"
        return nc.dram_tensor(name, shp, dt_, kind=kind).ap()

    k.qkT = scratch("qkT", [1536, T], BF16)
    k.gaT = scratch("gaT", [768, T], BF16)
    k.ghT = scratch("ghT", [768, T], BF16)
    k.mT = scratch("mT", [2048, T], BF16)
    k.uraw = scratch("uraw", [2304, T], BF16)
    k.uT = scratch("uT", [2304, T], BF16)
    k.vaug = scratch("vaug", [T, 1536], BF16)
    k.oaT = scratch("oaT", [768, T], BF16)
    k.ohT = scratch("ohT", [768, T], BF16)
    k.KFd = scratch("KFd", [2, NCB, 128, 128 * 3 * CB], BF16)
    k.Avec = scratch("Avec", [12, 1152], BF16)
    k.modrep = scratch("modrep", [2, 3 * DM], F32)

    with ExitStack() as top:
        sems = {e: top.enter_context(nc.semaphore("sem_" + e)) for e in ("pe", "act", "dve", "pool")}
        sems["sp"] = None
        ring = [top.enter_context(nc.semaphore("dr%d" % i)) for i in range(24)]
        S = Sched(nc, sems, ring)
        k.S = S
        pp = [top.enter_context(nc.psum_tensor("psb%d" % i, [128, 1024], F32)) for i in range(4)]
        ps = []
        for i in range(4):
            ps.append(pp[i][:, 0:512])
            ps.append(pp[i][:, 512:1024])
        k.ps = ps
        k.pp = pp
        k.ident = top.enter_context(nc.sbuf_tensor("s_ident", [128, 128], BF16))
        k.fg_rep = top.enter_context(nc.sbuf_tensor("s_fg_rep", [128, DM], F32))
        S.dma(k.ident[:], din["ident"][:, :], writes=["ident"])
        k.epsc = top.enter_context(nc.sbuf_tensor("s_epsc", [128, 2], F32))
        S.op("pool", lambda e: e.memset(k.epsc[:], EPS), writes=["epsc"])
        S.dma(k.fg_rep[:], din["final_g"][0:1, :].partition_broadcast(128), writes=["fg_rep"])

        if "p0" in phases:
            phase0(k)
            S.barrier()
        k.fuse1b = ("p1b" in phases and "pk" in phases)
        if "p1" in phases:
            phase1(k)
            S.barrier()
        if "pk" in phases:
            phaseK(k)
            S.barrier()
        if "p1b" in phases and not k.fuse1b:
            phase1b(k)
            S.barrier()
        if "pa" in phases:
            phaseA(k)
            S.barrier()
        if "ph" in phases:
            phaseH(k)
            S.barrier()
        if "pf" in phases:
            phaseF(k)
            S.barrier()
        S.flush()
    return nc


def phase0(k):
    nc, S, din, ps = k.nc, k.S, k.din, k.ps
    with ExitStack() as st:
        wada = st.enter_context(nc.sbuf_tensor("s_wada", [128, 8, 3 * DM], F32))
        cT = st.enter_context(nc.sbuf_tensor("s_cT", [128, 8, 2], F32))
        scT = st.enter_context(nc.sbuf_tensor("s_scT", [128, 8, 2], F32))
        screp = st.enter_context(nc.sbuf_tensor("s_screp", [128, 2, 8, 128], F32))
        brep = st.enter_context(nc.sbuf_tensor("s_brep", [128, 3 * DM], F32))
        ngrep = st.enter_context(nc.sbuf_tensor("s_ngrep", [128, DM], F32))
        k.modr = st.enter_context(nc.sbuf_tensor("s_modr", [128, 2, 3 * DM], F32))
        S.dma(cT[:], din["cT"][:, :, :], writes=["cT"])
        for kk in range(8):
            S.dma(wada[:, kk, :], din["w_ada"][kk * 128:(kk + 1) * 128, :], writes=[("wada", kk)])
        S.dma(brep[:], din["b_ada"][0:1, :].partition_broadcast(128), writes=["brep"])
        S.dma(ngrep[:], din["norm_g"][0:1, :].partition_broadcast(128), writes=["ngrep"])
        S.op("act", lambda e: e.activation(out=scT[:], in_=cT[:], func=AF.Silu), reads=["cT"], writes=["scT"])
        for s in range(2):
            S.op("dve", lambda e, s=s: e.tensor_copy(out=screp[:, s, :, :], in_=bc(scT[:, :, s:s + 1], [128, 8, 128])),
                 reads=["scT"], writes=[("screp", s)])
        for s in range(2):
            for cc in range(6):
                pb = ps[(s * 6 + cc) % 4]
                pr = ("ps", (s * 6 + cc) % 4)
                for kk in range(8):
                    S.op("pe", lambda e, s=s, cc=cc, kk=kk, pb=pb: e.matmul(
                        pb[:, :], lhsT=screp[:, s, kk, :], rhs=wada[:, kk, cc * 512:(cc + 1) * 512],
                        start=(kk == 0), stop=(kk == 7)),
                        reads=[("screp", s), ("wada", kk)], writes=[pr])
                S.op("dve", lambda e, s=s, cc=cc, pb=pb: e.tensor_tensor(
                    out=k.modr[:, s, cc * 512:(cc + 1) * 512], in0=pb[:, :], in1=brep[:, cc * 512:(cc + 1) * 512], op=ALU.add),
                    reads=[pr, "brep"], writes=[("modr", s)])
            S.op("dve", lambda e, s=s: e.scalar_tensor_tensor(
                out=k.modr[:, s, DM:2 * DM], in0=k.modr[:, s, DM:2 * DM], scalar=1.0, in1=ngrep[:], op0=ALU.add, op1=ALU.mult),
                reads=[("modr", s), "ngrep"], writes=[("modr", s)])
            S.dma(k.modrep[s:s + 1, :], k.modr[0:1, s, :], reads=[("modr", s)])
        S.barrier()


def phase1(k):
    nc, S, din, ps = k.nc, k.S, k.din, k.ps
    with ExitStack() as st:
        winb = st.enter_context(nc.sbuf_tensor("s_winb", [128, 8, 8192], BF16))
        stg_ctx = ExitStack()
        stg = [stg_ctx.enter_context(nc.sbuf_tensor("s_wstg%d" % i, [128, 2048], F32)) for i in range(2)]
        n = 0
        for kk in range(8):
            for cc in range(4):
                b = n % 2
                S.dma(stg[b][:], din["w_in"][kk * 128:(kk + 1) * 128, cc * 2048:(cc + 1) * 2048], writes=[("wstg", b)])
                eng = ("act", "dve", "pool")[n % 3]
                if eng == "act":
                    S.op("act", lambda e, b=b, kk=kk, cc=cc: e.copy(out=winb[:, kk, cc * 2048:(cc + 1) * 2048], in_=stg[b][:]),
                         reads=[("wstg", b)], writes=[("winb", kk)])
                else:
                    S.op(eng, lambda e, b=b, kk=kk, cc=cc: e.tensor_copy(out=winb[:, kk, cc * 2048:(cc + 1) * 2048], in_=stg[b][:]),
                         reads=[("wstg", b)], writes=[("winb", kk)])
                n += 1
        S.barrier()
        stg_ctx.close()
        modr1 = st.enter_context(nc.sbuf_tensor("s_modr1", [128, 2, 2 * DM], F32))
        for s_ in range(2):
            S.dma(modr1[:, s_, :], k.modrep[s_:s_ + 1, 0:2 * DM].partition_broadcast(128), writes=[("modr", s_)])
        xt = [st.enter_context(nc.sbuf_tensor("s_xt%d" % i, [128, DM], F32)) for i in range(2)]
        xm = [st.enter_context(nc.sbuf_tensor("s_xm%d" % i, [128, DM], F32)) for i in range(1)]
        hb = [st.enter_context(nc.sbuf_tensor("s_hb%d" % i, [128, DM], BF16)) for i in range(4)]
        sq = st.enter_context(nc.sbuf_tensor("s_sqj", [128, DM], BF16))
        ss = [st.enter_context(nc.sbuf_tensor("s_ss%d" % i, [128, 2], F32)) for i in range(3)]
        hT = [st.enter_context(nc.sbuf_tensor("s_hT%d" % i, [128, 8, 512], BF16)) for i in range(2)]
        ev = [st.enter_context(nc.sbuf_tensor("s_ev%d" % i, [128, 512], BF16)) for i in range(6)]
        vst = [st.enter_context(nc.sbuf_tensor("s_vst%d" % i, [128, 12, 128], BF16)) for i in range(2)]
        for i in range(2):
            S.op("pool", lambda e, i=i: e.memset(vst[i][:], 1.0), writes=[("vst", i)])

        blocks = []
        for j in range(6):
            blocks.append((OQ + j * 128, k.qkT, j * 128, "q"))
        for j in range(6):
            blocks.append((OK_ + j * 128, k.qkT, 768 + j * 128, "copy"))
        for j in range(18):
            blocks.append((OU + j * 128, k.uraw, j * 128, "copy"))
        for j in range(6):
            blocks.append((OGA + j * 128, k.gaT, j * 128, "silu"))
        for j in range(6):
            blocks.append((OGH + j * 128, k.ghT, j * 128, "silu"))
        for j in range(8):
            blocks.append((OMA + j * 128, k.mT, j * 128, "sig"))
        for j in range(8):
            blocks.append((OMH + j * 128, k.mT, 1024 + j * 128, "sig"))

        tcount = 0
        evn = 0

        def prep(ci):
            seg = 0 if ci < NCHUNK // 2 else 1
            hTc = hT[ci % 2]
            hres = ("hT", ci % 2)
            for tt in range(4):
                t = ci * 4 + tt
                xb, xr = xt[t % 2], ("xt", t % 2)
                sb, sr = ss[t % 3], ("ss", t % 3)
                mb, mr = xm[0], ("xm", 0)
                hbb, hbr = hb[t % 4], ("hb", t % 4)
                S.dma(xb[:], din["x"][t * 128:(t + 1) * 128, :], writes=[xr])
                S.op("act", lambda e, xb=xb, sb=sb: e.activation(out=sq[:], in_=xb[:], func=AF.Square, scale=1.0 / 32.0,
                                                                 accum_out=sb[:, 0:1]),
                     reads=[xr], writes=["sqj", sr])
                S.op("act", lambda e, sb=sb: e.activation(out=sb[:, 1:2], in_=sb[:, 0:1], func=AF.Sqrt, bias=k.epsc[:, 0:1]),
                     reads=[sr], writes=[sr])
                S.op("dve", lambda e, sb=sb: e.reciprocal(out=sb[:, 1:2], in_=sb[:, 1:2]), reads=[sr], writes=[sr])
                S.op("dve", lambda e, xb=xb, sb=sb, mb=mb, seg=seg: e.scalar_tensor_tensor(
                    out=mb[:], in0=xb[:], scalar=sb[:, 1:2], in1=modr1[:, seg, DM:2 * DM], op0=ALU.mult, op1=ALU.mult),
                    reads=[xr, sr, ("modr", seg)], writes=[mr])
                S.op("dve", lambda e, mb=mb, hbb=hbb, seg=seg: e.tensor_tensor(
                    out=hbb[:], in0=mb[:], in1=modr1[:, seg, 0:DM], op=ALU.add),
                    reads=[mr, ("modr", seg)], writes=[hbr])

        def prepB(ci):
            hTc = hT[ci % 2]
            hres = ("hT", ci % 2)
            for tt in range(4):
                t = ci * 4 + tt
                hbb, hbr = hb[t % 4], ("hb", t % 4)
                pbank = ps[6 + (t % 2)]
                pres = ("ps", 6 + (t % 2))
                pT = pbank[:].bitcast(BF16)
                for kk in range(8):
                    S.op("pe", lambda e, kk=kk, pT=pT, hbb=hbb: e.transpose(
                        out=pT[:, kk * 128:(kk + 1) * 128], in_=hbb[:, kk * 128:(kk + 1) * 128], identity=k.ident[:]),
                        reads=[hbr, "ident"], writes=[pres])
                S.op("act", lambda e, pT=pT, hTc=hTc, tt=tt: e.copy(
                    out=hTc[:, :, tt * 128:(tt + 1) * 128], in_=pT.rearrange("p (k t) -> p k t", k=8)),
                    reads=[pres], writes=[hres])
        def run_blocks(ci):
            nonlocal evn
            hTc = hT[ci % 2]
            hres = ("hT", ci % 2)
            for bi, (wc, dst, drow, kind) in enumerate(blocks):
                pb = ps[bi % 4]
                pr = ("ps", bi % 4)
                for kk in range(8):
                    S.op("pe", lambda e, kk=kk, pb=pb, wc=wc, hTc=hTc: e.matmul(
                        pb[:, :], lhsT=winb[:, kk, wc:wc + 128], rhs=hTc[:, kk, :], start=(kk == 0), stop=(kk == 7)),
                        reads=[hres, ("winb", kk)], writes=[pr])
                eb, er = ev[evn % 6], ("ev", evn % 6)
                evn += 1
                if kind == "q":
                    S.op("act", lambda e, pb=pb, eb=eb: e.activation(out=eb[:], in_=pb[:, :], func=AF.Copy, scale=0.125),
                         reads=[pr], writes=[er])
                elif kind == "copy":
                    S.op("dve", lambda e, pb=pb, eb=eb: e.tensor_copy(out=eb[:], in_=pb[:, :]), reads=[pr], writes=[er])
                elif kind == "silu":
                    S.op("act", lambda e, pb=pb, eb=eb: e.activation(out=eb[:], in_=pb[:, :], func=AF.Silu), reads=[pr], writes=[er])
                else:
                    S.op("act", lambda e, pb=pb, eb=eb: e.activation(out=eb[:], in_=pb[:, :], func=AF.Sigmoid), reads=[pr], writes=[er])
                S.dma(dst[drow:drow + 128, ci * 512:(ci + 1) * 512], eb[:], reads=[er], eng="pool")
            for tt in range(4):
                t = ci * 4 + tt
                pa, pb2 = ps[4], ps[5]
                for kk in range(8):
                    S.op("pe", lambda e, kk=kk, tt=tt, hTc=hTc: e.matmul(
                        ps[4][:, :], lhsT=hTc[:, kk, tt * 128:(tt + 1) * 128], rhs=winb[:, kk, OV:OV + 512],
                        start=(kk == 0), stop=(kk == 7)), reads=[hres, ("winb", kk)], writes=[("ps", 4)])
                for kk in range(8):
                    S.op("pe", lambda e, kk=kk, tt=tt, hTc=hTc: e.matmul(
                        ps[5][:, 0:256], lhsT=hTc[:, kk, tt * 128:(tt + 1) * 128], rhs=winb[:, kk, OV + 512:OV + 768],
                        start=(kk == 0), stop=(kk == 7)), reads=[hres, ("winb", kk)], writes=[("ps", 5)])
                vb, vr = vst[t % 2], ("vst", t % 2)
                def vdst(vb, p0, npair):
                    base = vb[:, 2 * p0:2 * p0 + 1, 0:1]
                    return AP(base.tensor, base.offset, [list(base.ap[0]), [256, npair], [192, 2], [1, 64]])
                S.op("dve", lambda e, vb=vb, vdst=vdst: e.tensor_copy(
                    out=vdst(vb, 0, 4), in_=ps[4][:, :].rearrange("p (a b c) -> p a b c", a=4, b=2)),
                    reads=[("ps", 4)], writes=[vr])
                S.op("dve", lambda e, vb=vb, vdst=vdst: e.tensor_copy(
                    out=vdst(vb, 4, 2), in_=ps[5][:, 0:256].rearrange("p (a b c) -> p a b c", a=2, b=2)),
                    reads=[("ps", 5)], writes=[vr])
                S.dma(k.vaug[t * 128:(t + 1) * 128, :], vb[:].rearrange("p a b -> p (a b)"), reads=[vr], eng="pool")

        prep(0)
        prepB(0)
        for ci in range(NCHUNK):
            if ci + 1 < NCHUNK:
                prep(ci + 1)
            run_blocks(ci)
            if ci + 1 < NCHUNK:
                prepB(ci + 1)


def core_assignment():
    return [("p", 0), ("p", 1), ("s", 0, 1), ("s", 2, 3), ("s", 4, 5), ("s", 6, 7), ("s", 6, 7), ("s", 6, 7)]


def prep_core_inputs(inp, role):
    f32 = lambda a: np.ascontiguousarray(np.asarray(a, dtype=np.float32))
    m = {}
    if role[0] == "p":
        b = role[1]
        m["x"] = f32(inp["x_prompt"][b])
        c2 = np.stack([inp["c_prompt"][b], inp["c_prompt"][b]], 0)
    else:
        m["x"] = f32(np.concatenate([inp["x_sample"][role[1]], inp["x_sample"][role[2]]], 0))
        c2 = np.stack([inp["c_sample"][role[1]], inp["c_sample"][role[2]]], 0)
    c2 = np.asarray(c2, np.float32)
    m["cT"] = f32(c2.reshape(2, 8, 128).transpose(2, 1, 0))
    m["w_ada"] = f32(inp["w_ada"][0])
    m["b_ada"] = f32(inp["b_ada"][0][None])
    m["norm_g"] = f32(inp["norm_g"][0][None])
    m["w_in"] = f32(inp["w_in"][0])
    m["short_wT"] = f32(np.asarray(inp["short_w"][0]).reshape(3, 18, 128).transpose(2, 1, 0))
    m["short_bT"] = f32(np.asarray(inp["short_b"][0]).reshape(18, 128).T)
    m["filt_w1"] = f32(inp["filt_w1"][0])
    m["filt_b1"] = f32(np.asarray(inp["filt_b1"][0])[:, None])
    m["filt_fr1"] = f32(np.asarray(inp["filt_freq1"][0])[:, None])
    m["filt_w2"] = f32(inp["filt_w2"][0])
    m["filt_b2"] = f32(np.asarray(inp["filt_b2"][0])[:, None])
    m["filt_fr2"] = f32(np.asarray(inp["filt_freq2"][0])[:, None])
    m["filt_w3"] = f32(inp["filt_w3"][0])
    m["hyena_d"] = f32(inp["hyena_d"][0])
    m["w_proj_attn"] = f32(inp["w_proj_attn"][0])
    m["w_proj_hyena"] = f32(inp["w_proj_hyena"][0])
    m["w_out"] = f32(inp["w_out"][0])
    m["rel_bias"] = f32(inp["rel_bias"])
    m["final_g"] = f32(np.asarray(inp["final_g"])[None])
    m.update(build_consts(role[0] == "p"))
    return m


_NC_CACHE = {}


def kernel(**inputs):
    inp = {k_: np.asarray(v) for k_, v in inputs.items()}
    if "full" not in _NC_CACHE:
        _NC_CACHE["full"] = build_program()
    nc = _NC_CACHE["full"]
    roles = core_assignment()
    in_maps = [prep_core_inputs(inp, r) for r in roles]
    res = run_bass_kernel_spmd(nc, in_maps, core_ids=list(range(8)))
    outs = [np.asarray(r["y"], dtype=np.float32) for r in res.results]
    y_prompt = np.stack([outs[0], outs[1]], 0)
    ys = []
    for c in range(2, 6):
        ys.append(outs[c][:8192])
        ys.append(outs[c][8192:])
    y_sample = np.stack(ys, 0)
    return (y_prompt, y_sample)


def phase1b_gen(k, st, W=512):
    nc, S, din = k.nc, k.S, k.din
    swT = st.enter_context(nc.sbuf_tensor("s_swT", [128, 18, 3], F32))
    sbT = st.enter_context(nc.sbuf_tensor("s_sbT", [128, 18], F32))
    bfl = st.enter_context(nc.sbuf_tensor("s_bfl", [128, 1], F32))
    S.dma(swT[:], din["short_wT"][:, :, :], writes=["swT"])
    S.dma(sbT[:], din["short_bT"][:, :], writes=["sbT"])
    S.dma(bfl[:], din["bflag"][:, :], writes=["bfl"])
    ib = [st.enter_context(nc.sbuf_tensor("s_cin%d" % i, [128, W + 2], BF16)) for i in range(3)]
    t1 = [st.enter_context(nc.sbuf_tensor("s_ct%d" % i, [128, W], F32)) for i in range(2)]
    ob = [st.enter_context(nc.sbuf_tensor("s_cout%d" % i, [128, W], BF16)) for i in range(3)]
    n = 0
    for ub in range(18):
        for tc in range(T // W):
            a, ar = ib[n % 3], ("cin", n % 3)
            tb, tr = t1[n % 2], ("ct", n % 2)
            o, orr = ob[n % 3], ("cout", n % 3)
            lo = tc * W - 1
            hi = tc * W + W + 1
            c0 = 0
            if lo < 0:
                S.op("pool", lambda e, a=a: e.memset(a[:, 0:1], 0.0), writes=[ar])
                lo, c0 = 0, 1
            c1 = W + 2
            if hi > T:
                S.op("pool", lambda e, a=a: e.memset(a[:, W + 1:W + 2], 0.0), writes=[ar])
                hi, c1 = T, W + 1
            S.dma(a[:, c0:c1], k.uraw[ub * 128:(ub + 1) * 128, lo:hi], writes=[ar])
            if tc * W == T // 2:
                S.op("pool", lambda e, a=a: e.tensor_scalar(out=a[:, 0:1], in0=a[:, 0:1], scalar1=bfl[:, 0:1], scalar2=None,
                                                            op0=ALU.mult), reads=[ar, "bfl"], writes=[ar])
            if tc * W + W == T // 2:
                S.op("pool", lambda e, a=a: e.tensor_scalar(out=a[:, W + 1:W + 2], in0=a[:, W + 1:W + 2], scalar1=bfl[:, 0:1],
                                                            scalar2=None, op0=ALU.mult), reads=[ar, "bfl"], writes=[ar])
            S.op("act", lambda e, a=a, tb=tb, ub=ub: e.activation(
                out=tb[:], in_=a[:, 1:W + 1], func=AF.Identity, scale=swT[:, ub, 1:2], bias=sbT[:, ub:ub + 1]),
                reads=[ar, "swT", "sbT"], writes=[tr])
            S.op("dve", lambda e, a=a, tb=tb, ub=ub: e.scalar_tensor_tensor(
                out=tb[:], in0=a[:, 0:W], scalar=swT[:, ub, 0:1], in1=tb[:], op0=ALU.mult, op1=ALU.add),
                reads=[ar, tr, "swT"], writes=[tr])
            S.op("dve", lambda e, a=a, tb=tb, o=o, ub=ub: e.scalar_tensor_tensor(
                out=o[:], in0=a[:, 2:W + 2], scalar=swT[:, ub, 2:3], in1=tb[:], op0=ALU.mult, op1=ALU.add),
                reads=[ar, tr, "swT"], writes=[orr])
            S.dma(k.uT[ub * 128:(ub + 1) * 128, tc * W:(tc + 1) * W], o[:], reads=[orr], eng="pool")
            n += 1
            yield


def phase1b(k):
    with ExitStack() as st:
        for _ in phase1b_gen(k, st, W=2048):
            pass


def sin_wrapped(S, src_ps, pres, dst, dres, scale_ap, bias_ap, tmp, tres, tmp2, t2res, nparts, ncols):
    PI = math.pi
    S.op("dve", lambda e: e.tensor_scalar(out=tmp[0:nparts, 0:ncols], in0=src_ps, scalar1=scale_ap, scalar2=bias_ap,
                                          op0=ALU.mult, op1=ALU.add), reads=[pres], writes=[tres])
    S.op("dve", lambda e: e.tensor_scalar(out=tmp2[0:nparts, 0:ncols], in0=tmp[0:nparts, 0:ncols], scalar1=PI, scalar2=-2 * PI,
                                          op0=ALU.is_gt, op1=ALU.mult), reads=[tres], writes=[t2res])
    S.op("dve", lambda e: e.tensor_tensor(out=tmp2[0:nparts, 0:ncols], in0=tmp2[0:nparts, 0:ncols], in1=tmp[0:nparts, 0:ncols],
                                          op=ALU.add), reads=[tres, t2res], writes=[t2res])
    S.op("dve", lambda e: e.tensor_scalar(out=tmp[0:nparts, 0:ncols], in0=tmp[0:nparts, 0:ncols], scalar1=-PI, scalar2=2 * PI,
                                          op0=ALU.is_lt, op1=ALU.mult), reads=[tres], writes=[tres])
    S.op("dve", lambda e: e.tensor_tensor(out=tmp[0:nparts, 0:ncols], in0=tmp2[0:nparts, 0:ncols], in1=tmp[0:nparts, 0:ncols],
                                          op=ALU.add), reads=[tres, t2res], writes=[tres])
    S.op("act", lambda e: e.activation(out=dst, in_=tmp[0:nparts, 0:ncols], func=AF.Sin), reads=[tres], writes=[dres])


def phaseK(k):
    nc, S, din, ps = k.nc, k.S, k.din, k.ps
    with ExitStack() as st:
        hdn = st.enter_context(nc.sbuf_tensor("s_hdn2T", [128, T], BF16))
        w3sd = st.enter_context(nc.sbuf_tensor("s_w3sd", [128, 2, NCB, 2, CB], BF16))
        S.op("pool", lambda e: e.memset(hdn[64:128, :], 0.0), writes=["hdn"])
        S.op("pool", lambda e: e.memset(w3sd[64:128], 0.0), writes=["w3sd"])
        with ExitStack() as s2:
            w1 = s2.enter_context(nc.sbuf_tensor("s_fw1", [33, 64], F32))
            w2 = s2.enter_context(nc.sbuf_tensor("s_fw2", [64, 64], F32))
            w3 = s2.enter_context(nc.sbuf_tensor("s_fw3", [64, 3072], F32))
            fv = s2.enter_context(nc.sbuf_tensor("s_fv", [64, 6], F32))
            zc = [s2.enter_context(nc.sbuf_tensor("s_zc%d" % i, [33, 512], F32)) for i in range(2)]
            ta = s2.enter_context(nc.sbuf_tensor("s_fta", [64, 512], F32))
            tb = s2.enter_context(nc.sbuf_tensor("s_ftb", [64, 512], F32))
            h1 = s2.enter_context(nc.sbuf_tensor("s_fh1", [64, 512], F32))
            S.dma(w1[:], din["filt_w1"][:, :], writes=["fw1"])
            S.dma(w2[:], din["filt_w2"][:, :], writes=["fw2"])
            S.dma(w3[:], din["filt_w3"][:, :], writes=["fw3"])
            for i, nm in enumerate(("filt_b1", "filt_fr1", "filt_b2", "filt_fr2")):
                S.dma(fv[:, i:i + 1], din[nm][:, :], writes=["fv"])
            S.op("dve", lambda e: e.tensor_tensor(out=fv[:, 4:5], in0=fv[:, 0:1], in1=fv[:, 1:2], op=ALU.mult), reads=["fv"], writes=["fv"])
            S.op("dve", lambda e: e.tensor_tensor(out=fv[:, 5:6], in0=fv[:, 2:3], in1=fv[:, 3:4], op=ALU.mult), reads=["fv"], writes=["fv"])
            w3v = w3[:].rearrange("p (o d b c) -> p o d b c", o=2, d=2, b=NCB)
            for o in range(2):
                S.op("dve", lambda e, o=o: e.tensor_tensor(out=w3sd[0:64, o, :, 0, :], in0=w3v[:, o, 0], in1=w3v[:, o, 1], op=ALU.add),
                     reads=["fw3"], writes=["w3sd"])
                S.op("dve", lambda e, o=o: e.tensor_tensor(out=w3sd[0:64, o, :, 1, :], in0=w3v[:, o, 0], in1=w3v[:, o, 1], op=ALU.subtract),
                     reads=["fw3"], writes=["w3sd"])
            for ci in range(T // 512):
                z, zr = zc[ci % 2], ("zc", ci % 2)
                S.dma(z[:], din["zfT"][:, ci * 512:(ci + 1) * 512], writes=[zr])
                S.op("pe", lambda e, z=z: e.matmul(ps[0][0:64, :], lhsT=w1[:], rhs=z[:], start=True, stop=True),
                     reads=[zr, "fw1"], writes=[("ps", 0)])
                sin_wrapped(S, ps[0][0:64, :], ("ps", 0), h1[:], "fh1", fv[:, 1:2], fv[:, 4:5], ta, "fta", tb, "ftb", 64, 512)
                S.op("pe", lambda e: e.matmul(ps[1][0:64, :], lhsT=w2[:], rhs=h1[:], start=True, stop=True),
                     reads=["fh1", "fw2"], writes=[("ps", 1)])
                sin_wrapped(S, ps[1][0:64, :], ("ps", 1), hdn[0:64, ci * 512:(ci + 1) * 512], "hdn", fv[:, 3:4], fv[:, 5:6],
                            ta, "fta", tb, "ftb", 64, 512)
            S.barrier()
        H = st.enter_context(nc.sbuf_tensor("s_H", [128, 128, 2, CB], BF16))
        Yk = st.enter_context(nc.sbuf_tensor("s_Yk", [128, 4, 128, CB], BF16))
        decs = [st.enter_context(nc.sbuf_tensor("s_dec%d" % i, [128, 128, CB], BF16)) for i in range(2)]
        negtn = st.enter_context(nc.sbuf_tensor("s_negtn", [128, 128], F32))
        drep = st.enter_context(nc.sbuf_tensor("s_drep", [128, 768], F32))
        EkS = st.enter_context(nc.sbuf_tensor("s_EkS", [128, 256], BF16))
        EkD = st.enter_context(nc.sbuf_tensor("s_EkD", [128, 256], BF16))
        Mfb = [st.enter_context(nc.sbuf_tensor("s_Mfb%d" % i, [128, 8, 256], BF16)) for i in range(3)]
        KFs = [st.enter_context(nc.sbuf_tensor("s_KFs%d" % i, [128, 4, 3, CB], BF16)) for i in range(3)]
        dK = st.enter_context(nc.sbuf_tensor("s_dK", [128, 2, CB], F32))
        S.dma(negtn[:], din["negtn"][:, :], writes=["negtn"])
        S.dma(drep[:], din["deltas_rep"][:, :], writes=["drep"])
        S.dma(EkS[:], din["E_kS"][:, :], writes=["EkS"])
        S.dma(EkD[:], din["E_kD"][:, :], writes=["EkD"])
        mfn = 0
        kfn = 0
        pn = 0
        cgen = phase1b_gen(k, st, W=512) if k.fuse1b else None
        cstep = [0]

        def cstepf():
            cstep[0] += 1
            if cgen is not None and cstep[0] % 4 == 0:
                next(cgen, None)

        def ykv(k2, off):
            b0 = off // 128
            b = Yk[:, b0:b0 + 1, k2:k2 + 1, 0:1]
            return AP(b.tensor, b.offset, [list(b.ap[0]), [2 * 128 * CB, 2], [1, CB]])

        def emit_dec(cbx, n1s):
            dd = decs[cbx % 2]
            for n1 in n1s:
                S.op("act", lambda e, n1=n1, dd=dd, cbx=cbx: e.activation(out=dd[:, n1, :], in_=drep[:, cbx * CB:(cbx + 1) * CB], func=AF.Exp,
                                                                        scale=negtn[:, n1:n1 + 1]),
                     reads=["drep", "negtn"], writes=[("dec", cbx % 2)])

        emit_dec(0, range(128))
        for cb in range(NCB):
            c0 = cb * CB
            dec = decs[cb % 2]
            for o in range(2):
                S.dma(dK[:, o, :], din["hyena_d"][o:o + 1, c0:c0 + CB].partition_broadcast(128), writes=["dK"])
            for o in range(2):
                for g in range(32):
                    pb, pr = ps[pn % 4], ("ps", pn % 4)
                    pn += 1
                    for j in range(4):
                        n1 = 4 * g + j
                        S.op("pe", lambda e, pb=pb, j=j, n1=n1, o=o, cb=cb: e.matmul(
                            pb[:, j * 128:(j + 1) * 128], lhsT=hdn[:, n1:T:128],
                            rhs=w3sd[:, o, cb].rearrange("p a b -> p (a b)"), start=True, stop=True),
                            reads=["hdn", "w3sd"], writes=[pr])
                    hout = H[:, 4 * g:4 * g + 4, :, :]
                    dv = dec[:, 4 * g:4 * g + 1, 0:1]
                    din1 = AP(dv.tensor, dv.offset, [list(dv.ap[0]), [CB, 4], [0, 2], [1, CB]])
                    S.op("dve", lambda e, pb=pb, hout=hout, din1=din1: e.tensor_tensor(
                        out=hout, in0=pb[:, :].rearrange("p (j s c) -> p j s c", j=4, s=2), in1=din1, op=ALU.mult),
                        reads=[pr, ("dec", cb % 2)], writes=["H"])
                    cstepf()
                for cp in range(CB // 2):
                    pi = pn % 4
                    pn += 1
                    prl = [("ps", 2 * pi), ("ps", 2 * pi + 1)]
                    for h in range(2):
                        c = 2 * cp + h
                        pb = ps[2 * pi + h]
                        S.op("pe", lambda e, pb=pb, c=c: e.matmul(pb[:, 0:256], lhsT=H[:, :, 0, c], rhs=EkS[:], start=True, stop=True),
                             reads=["H", "EkS"], writes=prl)
                        S.op("pe", lambda e, pb=pb, c=c: e.matmul(pb[:, 256:512], lhsT=H[:, :, 1, c], rhs=EkD[:], start=True, stop=True),
                             reads=["H", "EkD"], writes=prl)
                    yv = Yk[:, 0:1, 0:1, 2 * cp:2 * cp + 1]
                    yout = AP(yv.tensor, yv.offset, [list(yv.ap[0]), [CB, 512], [1, 2]])
                    pin = k.pp[pi][:, :].rearrange("p (h x) -> p x h", h=2)
                    if cp % 2 == 0:
                        S.op("act", lambda e, yout=yout, pin=pin: e.copy(out=yout, in_=pin), reads=prl, writes=["Yk"])
                    else:
                        S.op("dve", lambda e, yout=yout, pin=pin: e.tensor_copy(out=yout, in_=pin), reads=prl, writes=["Yk"])
                    if o == 1 and cb + 1 < NCB:
                        emit_dec(cb + 1, range(4 * cp, 4 * cp + 4))
                    cstepf()
                for g in range(32):
                    pb, pr = ps[4 + pn % 4], ("ps", 4 + pn % 4)
                    pn += 1
                    for j in range(4):
                        k2 = 4 * g + j
                        if k2 % 8 == 0:
                            mb_, mr_ = Mfb[mfn % 3], ("Mfb", mfn % 3)
                            mfn += 1
                            S.dma(mb_[:], din["Mf"][:, k2:k2 + 8, :], writes=[mr_])
                        S.op("pe", lambda e, pb=pb, j=j, k2=k2, mb_=mb_: e.matmul(
                            pb[:, j * 128:(j + 1) * 128], lhsT=mb_[:, k2 % 8, 0:128],
                            rhs=ykv(k2, 0), start=True, stop=False),
                            reads=["Yk", mr_], writes=[pr])
                        S.op("pe", lambda e, pb=pb, j=j, k2=k2, mb_=mb_: e.matmul(
                            pb[:, j * 128:(j + 1) * 128], lhsT=mb_[:, k2 % 8, 128:256],
                            rhs=ykv(k2, 128), start=False, stop=True),
                            reads=["Yk", mr_], writes=[pr])
                    kb, kr = KFs[kfn % 3], ("KFs", kfn % 3)
                    kfn += 1
                    pv = pb[:, :].rearrange("p (j s c) -> p j s c", j=4, s=2)
                    S.op("dve", lambda e, kb=kb, pv=pv, o=o: e.tensor_tensor(out=kb[:, :, 0, :], in0=pv[:, :, 0, :],
                                                                             in1=bc(dK[:, o:o + 1, :], [128, 4, CB]), op=ALU.add),
                         reads=[pr, "dK"], writes=[kr])
                    S.op("act", lambda e, kb=kb, pv=pv: e.copy(out=kb[:, :, 1, :], in_=pv[:, :, 1, :]), reads=[pr], writes=[kr])
                    S.op("dve", lambda e, kb=kb, pv=pv: e.tensor_scalar(out=kb[:, :, 2, :], in0=pv[:, :, 1, :], scalar1=-1.0, scalar2=None,
                                                                        op0=ALU.mult), reads=[pr], writes=[kr])
                    S.dma(k.KFd[o, cb, :, g * 4 * 3 * CB:(g + 1) * 4 * 3 * CB], kb[:].rearrange("p a b c -> p (a b c)"), reads=[kr], eng="pool")
                    cstepf()
        if cgen is not None:
            for _ in cgen:
                pass


def phaseH(k):
    nc, S, din, ps = k.nc, k.S, k.din, k.ps
    with ExitStack() as st:
        bufA = st.enter_context(nc.sbuf_tensor("s_hA", [128, CB, 128], BF16))
        bufB = st.enter_context(nc.sbuf_tensor("s_hB", [128, CB, 128], BF16))
        bufC = st.enter_context(nc.sbuf_tensor("s_hC", [128, 128, CB], BF16))
        Yd = st.enter_context(nc.sbuf_tensor("s_Yd", [128, 3, 128, CB], BF16))
        Pb = st.enter_context(nc.sbuf_tensor("s_P", [128, 128, 2, CB], BF16))
        Zs = st.enter_context(nc.sbuf_tensor("s_Zs", [128, 2, 128, CB], BF16))
        Ed = st.enter_context(nc.sbuf_tensor("s_Ed", [128, 384], BF16))
        G1 = st.enter_context(nc.sbuf_tensor("s_G1", [128, 256], BF16))
        G2 = st.enter_context(nc.sbuf_tensor("s_G2", [128, 256], BF16))
        drep = st.enter_context(nc.sbuf_tensor("s_hdrep", [128, 2, CB], F32))
        Mb = [st.enter_context(nc.sbuf_tensor("s_Mb%d" % i, [128, 8, 256], BF16)) for i in range(4)]
        KFb = [st.enter_context(nc.sbuf_tensor("s_KFb%d" % i, [128, 4, 3, CB], BF16)) for i in range(6)]
        t1 = [st.enter_context(nc.sbuf_tensor("s_ht1_%d" % i, [128, 4, 2, CB], BF16)) for i in range(2)]
        t2 = [st.enter_context(nc.sbuf_tensor("s_ht2_%d" % i, [128, 4, 2, CB], BF16)) for i in range(2)]
        te = [st.enter_context(nc.sbuf_tensor("s_hte%d" % i, [128, 8, CB], F32)) for i in range(2)]
        S.dma(Ed[:], din["E_d"][:, :], writes=["Ed"])
        S.dma(G1[:], din["G1"][:, :], writes=["G1"])
        S.dma(G2[:], din["G2"][:, :], writes=["G2"])
        ohs = AP(Pb[:].tensor, Pb[:].offset, [[Pb[:].ap[0][0], 64], [1, T]])
        mn = 0
        kn = 0
        tn_ = 0
        pn = 0

        def ydv(k2, off):
            b0 = off // 128
            return Yd[:, b0:b0 + 2, k2, :]

        def load_blk(buf, res, row0):
            src = AP(k.uT.tensor, k.uT[row0:row0 + 1, 0:1].offset, [[128, 128], [T, CB], [1, 128]])
            S.dma(buf[:], src, writes=[res])

        for cb in range(NCB):
            c0 = cb * CB
            load_blk(bufA, "hA", c0)
            load_blk(bufB, "hB", 768 + c0)
            for o in range(2):
                S.dma(drep[:, o, :], din["hyena_d"][o:o + 1, c0:c0 + CB].partition_broadcast(128), writes=["hdrep"])
            for o in range(2):
                Din, dres = (bufA, "hA") if o == 0 else (bufC, "hC")
                for cp in range(CB // 2):
                    pi = pn % 4
                    pn += 1
                    prl = [("ps", 2 * pi), ("ps", 2 * pi + 1)]
                    for h in range(2):
                        c = 2 * cp + h
                        pb = ps[2 * pi + h]
                        S.op("pe", lambda e, pb=pb, c=c, Din=Din, o=o: e.matmul(pb[:, 0:384], lhsT=(Din[:, c, :] if o == 0 else Din[:, :, c]),
                                                                             rhs=Ed[:], start=True, stop=True),
                             reads=[dres, "Ed"], writes=prl)
                    yv = Yd[:, 0:1, 0:1, 2 * cp:2 * cp + 1]
                    yout = AP(yv.tensor, yv.offset, [list(yv.ap[0]), [CB, 384], [1, 2]])
                    pin = k.pp[pi][:, :].rearrange("p (h x) -> p x h", h=2)[:, 0:384, :]
                    if cp % 2 == 0:
                        S.op("act", lambda e, yout=yout, pin=pin: e.copy(out=yout, in_=pin), reads=prl, writes=["Yd"])
                    else:
                        S.op("dve", lambda e, yout=yout, pin=pin: e.tensor_copy(out=yout, in_=pin), reads=prl, writes=["Yd"])
                for g in range(32):
                    pb, pr = ps[4 + pn % 4], ("ps", 4 + pn % 4)
                    pn += 1
                    kb, kr = KFb[kn % 6], ("KFb", kn % 6)
                    kn += 1
                    S.dma(kb[:].rearrange("p a b c -> p (a b c)"), k.KFd[o, cb, :, g * 12 * CB:(g + 1) * 12 * CB], writes=[kr])
                    for j in range(4):
                        k2 = 4 * g + j
                        if k2 % 8 == 0:
                            mb_, mr_ = Mb[mn % 4], ("Mb", mn % 4)
                            mn += 1
                            S.dma(mb_[:], din["Mf"][:, k2:k2 + 8, :], writes=[mr_])
                        S.op("pe", lambda e, pb=pb, j=j, k2=k2, mb_=mb_: e.matmul(
                            pb[:, j * 128:(j + 1) * 128], lhsT=mb_[:, k2 % 8, 0:128],
                            rhs=ydv(k2, 128), start=True, stop=False),
                            reads=["Yd", mr_], writes=[pr])
                        S.op("pe", lambda e, pb=pb, j=j, k2=k2, mb_=mb_: e.matmul(
                            pb[:, j * 128:(j + 1) * 128], lhsT=mb_[:, k2 % 8, 128:256],
                            rhs=ydv(k2, 0), start=False, stop=True),
                            reads=["Yd", mr_], writes=[pr])
                    a1, a1r = t1[tn_ % 2], ("ht1", tn_ % 2)
                    a2, a2r = t2[tn_ % 2], ("ht2", tn_ % 2)
                    tn_ += 1
                    pv = pb[:, :].rearrange("p (j s c) -> p j s c", j=4, s=2)
                    S.op("dve", lambda e, a1=a1, pv=pv, kb=kb: e.tensor_tensor(
                        out=a1[:], in0=pv, in1=bc(kb[:, :, 0:1, :], [128, 4, 2, CB]), op=ALU.mult), reads=[pr, kr], writes=[a1r])
                    S.op("dve", lambda e, a2=a2, pv=pv, kb=kb: e.tensor_tensor(
                        out=a2[:], in0=pv, in1=kb[:, :, 1:3, :], op=ALU.mult), reads=[pr, kr], writes=[a2r])
                    S.op("pool", lambda e, a1=a1, a2=a2, g=g: e.tensor_tensor(
                        out=Pb[:, 4 * g:4 * g + 4, 0, :], in0=a1[:, :, 0, :], in1=a2[:, :, 1, :], op=ALU.add),
                        reads=[a1r, a2r], writes=["P"])
                    S.op("pool", lambda e, a1=a1, a2=a2, g=g: e.tensor_tensor(
                        out=Pb[:, 4 * g:4 * g + 4, 1, :], in0=a1[:, :, 1, :], in1=a2[:, :, 0, :], op=ALU.add),
                        reads=[a1r, a2r], writes=["P"])
                for c2 in range(CB // 2):
                    pb, pr = ps[pn % 8], ("ps", pn % 8)
                    pn += 1
                    for h in range(2):
                        c = 2 * c2 + h
                        S.op("pe", lambda e, pb=pb, c=c, h=h: e.matmul(pb[:, h * 256:(h + 1) * 256], lhsT=Pb[:, :, 0, c], rhs=G1[:],
                                                                       start=True, stop=False), reads=["P", "G1"], writes=[pr])
                        S.op("pe", lambda e, pb=pb, c=c, h=h: e.matmul(pb[:, h * 256:(h + 1) * 256], lhsT=Pb[:, :, 1, c], rhs=G2[:],
                                                                       start=False, stop=True), reads=["P", "G2"], writes=[pr])
                    zv = Zs[:, 0:1, 0:1, 2 * c2:2 * c2 + 1]
                    zout = AP(zv.tensor, zv.offset, [list(zv.ap[0]), [CB, 256], [1, 2]])
                    pin = pb[:, :].rearrange("p (h x) -> p x h", h=2)
                    if c2 % 2 == 0:
                        S.op("act", lambda e, zout=zout, pin=pin: e.copy(out=zout, in_=pin), reads=[pr], writes=["Zs"])
                    else:
                        S.op("dve", lambda e, zout=zout, pin=pin: e.tensor_copy(out=zout, in_=pin), reads=[pr], writes=["Zs"])
                if o == 1:
                    load_blk(bufA, "hA", 1536 + c0)
                Xg, xres = (bufB, "hB") if o == 0 else (bufA, "hA")
                for g in range(16):
                    pb, pr = ps[4 + pn % 4], ("ps", 4 + pn % 4)
                    pn += 1
                    for j in range(8):
                        n1 = 8 * g + j
                        if n1 % 8 == 0:
                            mb_, mr_ = Mb[mn % 4], ("Mb", mn % 4)
                            mn += 1
                            S.dma(mb_[:], din["Minv"][:, n1:n1 + 8, :], writes=[mr_])
                        S.op("pe", lambda e, pb=pb, j=j, n1=n1, mb_=mb_: e.matmul(
                            pb[:, j * CB:(j + 1) * CB], lhsT=mb_[:, n1 % 8, 0:128], rhs=Zs[:, 0, n1, :], start=True, stop=False),
                            reads=["Zs", mr_], writes=[pr])
                        S.op("pe", lambda e, pb=pb, j=j, n1=n1, mb_=mb_: e.matmul(
                            pb[:, j * CB:(j + 1) * CB], lhsT=mb_[:, n1 % 8, 128:256], rhs=Zs[:, 1, n1, :], start=False, stop=True),
                            reads=["Zs", mr_], writes=[pr])
                    xin = Xg[:, :, 8 * g:8 * g + 8].rearrange("p c j -> p j c")
                    pv8 = pb[:, :].rearrange("p (j c) -> p j c", j=8)
                    if o == 0:
                        zo = bufC[:, 8 * g:8 * g + 8, :]
                        S.op("dve", lambda e, pv8=pv8, xin=xin, zo=zo: e.tensor_tensor(out=zo, in0=pv8, in1=xin, op=ALU.mult),
                             reads=[pr, xres], writes=["hC"])
                    else:
                        z3v = bufB[:].rearrange("p c j -> p (c j)")[:, 8 * g * CB:(8 * g + 8) * CB].rearrange("p (j c) -> p j c", j=8)
                        S.op("dve", lambda e, pv8=pv8, xin=xin, z3v=z3v: e.tensor_tensor(out=z3v, in0=pv8, in1=xin, op=ALU.mult),
                             reads=[pr, xres], writes=["hB"])
            z3 = bufB[:].rearrange("p c j -> p (c j)")
            for g in range(16):
                pb, pr = ps[pn % 4], ("ps", pn % 4)
                pn += 1
                pT = pb[:].bitcast(BF16)
                for j in range(8):
                    n1 = 8 * g + j
                    S.op("pe", lambda e, pT=pT, j=j, n1=n1: e.transpose(out=pT[0:CB, j * 128:(j + 1) * 128],
                                                                        in_=z3[:, n1 * CB:(n1 + 1) * CB], identity=k.ident[:]),
                         reads=["hB", "ident"], writes=[pr])
                ov = AP(ohs.tensor, ohs.offset + 8 * g, [list(ohs.ap[0]), [128, 128], [1, 8]])
                pin = pT[0:CB, :].rearrange("p (j n) -> p n j", j=8)
                if g % 2 == 0:
                    S.op("act", lambda e, ov=ov, pin=pin: e.copy(out=ov, in_=pin), reads=[pr], writes=["P"])
                else:
                    S.op("dve", lambda e, ov=ov, pin=pin: e.tensor_copy(out=ov, in_=pin), reads=[pr], writes=["P"])
            S.dma(k.ohT[c0:c0 + CB, :], ohs, reads=["P"])


def phaseA(k):
    nc, S, din, ps = k.nc, k.S, k.din, k.ps
    SPAN = 4096
    with ExitStack() as st:
        Hk = st.enter_context(nc.sbuf_tensor("s_Hk", [128, 3, 12, 256], BF16))
        J = st.enter_context(nc.sbuf_tensor("s_J", [128, 128], BF16))
        bm = st.enter_context(nc.sbuf_tensor("s_bm", [128, 256], BF16))
        swb = st.enter_context(nc.sbuf_tensor("s_swb", [128, 128], BF16))
        swf = st.enter_context(nc.sbuf_tensor("s_swf", [128, 128], F32))
        selA = st.enter_context(nc.sbuf_tensor("s_selA", [128, 128], F32))
        selB = st.enter_context(nc.sbuf_tensor("s_selB", [128, 128], F32))
        with ExitStack() as s2:
            rb = s2.enter_context(nc.sbuf_tensor("s_rb", [32, 12], F32))
            oh = s2.enter_context(nc.sbuf_tensor("s_oh", [32, 1152], F32))
            mr = s2.enter_context(nc.sbuf_tensor("s_mrow", [12, 1152], F32))
            av = s2.enter_context(nc.sbuf_tensor("s_av", [12, 1152], BF16))
            S.dma(rb[:], din["rel_bias"][:, :], writes=["rb"])
            S.dma(oh[:], din["OH"][:, :], writes=["oh"])
            S.dma(mr[:], din["mrow"][:, :], writes=["mrow"])
            for i in range(3):
                S.op("pe", lambda e, i=i: e.matmul(ps[i][0:12, 0:384], lhsT=rb[:], rhs=oh[:, i * 384:(i + 1) * 384], start=True, stop=True),
                     reads=["rb", "oh"], writes=[("ps", i)])
                S.op("dve", lambda e, i=i: e.tensor_tensor(out=av[:, i * 384:(i + 1) * 384], in0=ps[i][0:12, 0:384],
                                                           in1=mr[:, i * 384:(i + 1) * 384], op=ALU.add),
                     reads=[("ps", i), "mrow"], writes=["av"])
            S.dma(k.Avec[:, :], av[:], reads=["av"], writes=["Avec"])
            for h in range(12):
                for ri in range(3):
                    src = AP(k.Avec.tensor, k.Avec[h:h + 1, ri * 384:ri * 384 + 1].offset, [[1, 128], [1, 256]])
                    S.dma(Hk[:, ri, h, :], src, reads=["Avec"], writes=["Hk"])
            S.dma(J[:], din["antiid"][:, :], writes=["J"])
            S.dma(bm[:], din["bmask"][:, :], writes=["bm"])
            S.dma(swb[:], din["swap"][:, :], writes=["swb"])
            S.op("dve", lambda e: e.tensor_copy(out=swf[:], in_=swb[:]), reads=["swb"], writes=["swf"])
            S.op("pool", lambda e: e.memset(selA[:], 0.0), writes=["sel"])
            S.op("pool", lambda e: e.memset(selB[:], 0.0), writes=["sel"])
            S.op("dve", lambda e: e.tensor_copy(out=selA[:, 0:64], in_=swb[:, 0:64]), reads=["swb", "sel"], writes=["sel"])
            S.op("dve", lambda e: e.tensor_copy(out=selB[:, 64:128], in_=swb[:, 64:128]), reads=["swb", "sel"], writes=["sel"])
            S.barrier()
        TP = T + 2 * PAD
        qs1 = [st.enter_context(nc.sbuf_tensor("s_qs1_%d" % i, [128, 2, SPAN], BF16)) for i in range(2)]
        q4 = st.enter_context(nc.sbuf_tensor("s_q4", [128, 2, 4, SPAN // 4], BF16))
        q16 = st.enter_context(nc.sbuf_tensor("s_q16", [128, 2, 16, SPAN // 16], BF16))
        kT = st.enter_context(nc.sbuf_tensor("s_kT", [128, TP], BF16))
        for i in range(2):
            S.op("pool", lambda e, i=i: e.memset(qs1[i][:], 0.0), writes=[("qs1", i)])
        S.op("pool", lambda e: e.memset(kT[:, 0:PAD], 0.0), writes=["kT"])
        S.op("pool", lambda e: e.memset(kT[:, PAD + T:TP], 0.0), writes=["kT"])
        acc = st.enter_context(nc.sbuf_tensor("s_acc", [128, 2, SPAN], F32))
        OW = 2048
        oT = [st.enter_context(nc.sbuf_tensor("s_oT%d" % i, [128, OW], BF16)) for i in range(2)]
        rden = [st.enter_context(nc.sbuf_tensor("s_rden%d" % i, [128, 512], F32)) for i in range(2)]
        Vt = [st.enter_context(nc.sbuf_tensor("s_Vt%d" % i, [128, 256], BF16)) for i in range(8)]
        PT = [st.enter_context(nc.sbuf_tensor("s_PT%d" % i, [128, 2, 256], BF16)) for i in range(6)]
        vn = 0
        ptn = 0
        sn = 0
        on = 0
        rn = 0
        spn = 0
        DSK = 3
        cgen = None
        cstep = 0
        for hp in range(6):
            S.dma(kT[:, PAD:PAD + T], k.qkT[768 + hp * 128:768 + (hp + 1) * 128, :], writes=["kT"])

            def load_q(hp_, s_, gi):
                qb, qr_ = qs1[gi % 2], ("qs1", gi % 2)
                S.dma(qb[0:64, 0, :], k.qkT[hp_ * 128:hp_ * 128 + 64, s_ * SPAN:(s_ + 1) * SPAN], writes=[qr_])
                S.dma(qb[64:128, 1, :], k.qkT[hp_ * 128 + 64:hp_ * 128 + 128, s_ * SPAN:(s_ + 1) * SPAN], writes=[qr_])

            NSP = T // SPAN
            if hp == 0:
                load_q(0, 0, 0)
            for s in range(NSP):
                gi = hp * NSP + s
                qb, qbr = qs1[gi % 2], ("qs1", gi % 2)
                if s + 1 < NSP:
                    load_q(hp, s + 1, gi + 1)
                elif hp + 1 < 6:
                    load_q(hp + 1, 0, gi + 1)
                S.op("act", lambda e, qb=qb: e.copy(out=q4[:], in_=qb[:].rearrange("p h (m r) -> p h r m", r=4)), reads=[qbr], writes=["q4"])
                S.op("act", lambda e, qb=qb: e.copy(out=q16[:], in_=qb[:].rearrange("p h (m r) -> p h r m", r=16)), reads=[qbr], writes=["q16"])
                pendB = []
                S.op("pool", lambda e: e.memset(acc[:], 0.0), writes=["acc"])
                for ri, r in enumerate((1, 4, 16)):
                    Lr = T // r
                    jb = (T // 2) // (r * 128)
                    nqb = SPAN // (128 * r)
                    for rho in range(r):
                        ja, jbnd = s * nqb, s * nqb + nqb
                        pobank = {}
                        for j in range(ja, jbnd + 1):
                            c_lo = 128 if j == ja else 0
                            c_hi = 128 if j == jbnd else 256
                            ncol = c_hi - c_lo
                            vt, vr = Vt[vn % 8], ("Vt", vn % 8)
                            vn += 1
                            m0 = 128 * j - 64
                            lo, hi = 0, 128
                            if m0 < 0:
                                lo = 64
                            if m0 + 128 > Lr:
                                hi = 64
                            if lo > 0 or hi < 128:
                                S.op("pool", lambda e, vt=vt: e.memset(vt[:], 0.0), writes=[vr])
                            tok0 = rho + r * (m0 + lo)
                            src = AP(k.vaug.tensor, k.vaug[tok0:tok0 + 1, hp * 256:hp * 256 + 1].offset, [[1536 * r, hi - lo], [1, 256]])
                            S.dma(vt[lo:hi, :], src, writes=[vr])
                            mq0 = 128 * j - 128 + c_lo
                            ml = mq0 - s * (SPAN // r)
                            if r == 1:
                                qmov, qres = qb[:, :, ml:ml + ncol], qbr
                            elif r == 4:
                                qmov, qres = q4[:, :, rho, ml:ml + ncol], "q4"
                            else:
                                qmov, qres = q16[:, :, rho, ml:ml + ncol], "q16"
                            kc0 = PAD + rho + r * m0
                            ksl = slice(kc0, kc0 + 127 * r + 1, r)
                            straddle = (j == jb)
                            pS, psr = ps[sn % 4], ("ps", sn % 4)
                            sn += 1
                            pt, ptr = PT[ptn % 6], ("PT", ptn % 6)
                            ptn += 1
                            pSv = pS[:, :].rearrange("p (h c) -> p h c", h=2)[:, :, 0:ncol]

                            def stageA(pSv=pSv, psr=psr, ksl=ksl, qmov=qmov, qres=qres, ncol=ncol, ri=ri, c_lo=c_lo, c_hi=c_hi,
                                       straddle=straddle, pt=pt, ptr=ptr, hp=hp):
                                S.op("pe", lambda e: e.matmul(pSv, lhsT=kT[:, ksl], rhs=qmov, start=True, stop=False),
                                     reads=["kT", qres], writes=[psr])
                                S.op("pe", lambda e: e.matmul(pSv, lhsT=J[:], rhs=Hk[:, ri, 2 * hp:2 * hp + 2, c_lo:c_hi],
                                                              start=False, stop=not straddle), reads=["J", "Hk"], writes=[psr])
                                if straddle:
                                    S.op("pe", lambda e: e.matmul(pSv, lhsT=k.ident[:], rhs=bc(bm[:, c_lo:c_hi].rearrange("p (o c) -> p o c", o=1), [128, 2, ncol]),
                                                                  start=False, stop=True), reads=["ident", "bm"], writes=[psr])
                                S.op("act", lambda e: e.activation(out=pt[:, :, 0:ncol], in_=pSv, func=AF.Exp), reads=[psr], writes=[ptr])

                            pieces = []
                            if c_lo == 0:
                                pieces.append((0, j - 1))
                            if c_hi == 256:
                                pieces.append((1, j))
                            for (half, jq) in pieces:
                                if half == 1:
                                    pobank[jq] = (ps[4 + on % 4], ("ps", 4 + on % 4))
                                    on += 1
                            pbs = {jq: pobank[jq] for (_, jq) in pieces}

                            def stageB(pieces=pieces, pbs=pbs, vt=vt, vr=vr, pt=pt, ptr=ptr, c_lo=c_lo, r=r, rho=rho, s=s):
                                for (half, jq) in pieces:
                                    po, por = pbs[jq]
                                    off = half * 128 - c_lo
                                    for hh in range(2):
                                        S.op("pe", lambda e, po=po, hh=hh, off=off, half=half: e.matmul(
                                            po[:, hh * 128:(hh + 1) * 128], lhsT=vt[:, hh * 128:(hh + 1) * 128], rhs=pt[:, hh, off:off + 128],
                                            start=(half == 1 and hh == 0), stop=(half == 0), skip_group_check=True),
                                            reads=[vr, ptr], writes=[por])
                                    if half == 0:
                                        a0 = rho + r * 128 * jq - s * SPAN
                                        asl = slice(a0, a0 + 127 * r + 1, r)
                                        S.op("dve", lambda e, po=po, asl=asl: e.tensor_tensor(
                                            out=acc[:, :, asl], in0=po[:, 0:256].rearrange("p (h c) -> p h c", h=2), in1=acc[:, :, asl],
                                            op=ALU.add), reads=[por, "acc"], writes=["acc"])

                            stageA()
                            pendB.append(stageB)
                            if len(pendB) > DSK:
                                pendB.pop(0)()
                            cstep += 1
                            if cgen is not None and cstep % 4 == 0:
                                next(cgen, None)
                while pendB:
                    pendB.pop(0)()
                for ow in range(SPAN // OW):
                    ot, otr = oT[spn % 2], ("oT", spn % 2)
                    spn += 1
                    for cc in range(OW // 512):
                        c0 = ow * OW + cc * 512
                        pw, pwr = ps[sn % 4], ("ps", sn % 4)
                        sn += 1
                        S.op("pe", lambda e, pw=pw, c0=c0: e.matmul(pw[:, :], lhsT=selA[:], rhs=acc[:, 0, c0:c0 + 512],
                                                                    start=True, stop=False), reads=["acc", "sel"], writes=[pwr])
                        S.op("pe", lambda e, pw=pw, c0=c0: e.matmul(pw[:, :], lhsT=selB[:], rhs=acc[:, 1, c0:c0 + 512],
                                                                    start=False, stop=True), reads=["acc", "sel"], writes=[pwr])
                        rd, rdr = rden[rn % 2], ("rden", rn % 2)
                        rn += 1
                        S.op("dve", lambda e, rd=rd, pw=pw: e.reciprocal(out=rd[:], in_=pw[:, :]), reads=[pwr], writes=[rdr])
                        for hh in range(2):
                            nlo = 64 * hh
                            S.op("pool", lambda e, rd=rd, ot=ot, hh=hh, cc=cc, c0=c0, nlo=nlo: e.tensor_tensor(
                                out=ot[nlo:nlo + 64, cc * 512:(cc + 1) * 512], in0=acc[nlo:nlo + 64, hh, c0:c0 + 512],
                                in1=rd[nlo:nlo + 64, :], op=ALU.mult), reads=["acc", rdr], writes=[otr])
                    t0_ = s * SPAN + ow * OW
                    S.dma(k.oaT[hp * 128:(hp + 1) * 128, t0_:t0_ + OW], ot[:], reads=[otr])
        if cgen is not None:
            for _ in cgen:
                pass


def phaseF(k):
    nc, S, din, ps = k.nc, k.S, k.din, k.ps
    with ExitStack() as st:
        wpa = st.enter_context(nc.sbuf_tensor("s_wpa", [128, 6, DM], BF16))
        wph = st.enter_context(nc.sbuf_tensor("s_wph", [128, 6, DM], BF16))
        wo = st.enter_context(nc.sbuf_tensor("s_wo", [128, 2, 8, DM], BF16))
        gater = st.enter_context(nc.sbuf_tensor("s_gater", [128, 2, DM], F32))
        for s_ in range(2):
            S.dma(gater[:, s_, :], k.modrep[s_:s_ + 1, 2 * DM:3 * DM].partition_broadcast(128), writes=[("modr", s_)])
        with ExitStack() as s2:
            stg = [s2.enter_context(nc.sbuf_tensor("s_fstg%d" % i, [128, DM], F32)) for i in range(2)]
            n = 0
            for (wt, nm, src, nk) in ((wpa, "wpa", "w_proj_attn", 6), (wph, "wph", "w_proj_hyena", 6), (wo, "wo", "w_out", 8)):
                for kk in range(nk):
                    b = n % 2
                    S.dma(stg[b][:], din[src][kk * 128:(kk + 1) * 128, :], writes=[("fstg", b)])
                    eng = ("dve", "pool")[n % 2]
                    if nm == "wo":
                        for s_ in range(2):
                            S.op(("dve", "pool")[s_], lambda e, b=b, wt=wt, kk=kk, s_=s_: e.tensor_tensor(
                                out=wt[:, s_, kk, :], in0=stg[b][:], in1=gater[:, s_, :], op=ALU.mult),
                                reads=[("fstg", b), ("modr", s_)], writes=[nm])
                    else:
                        S.op(eng, lambda e, b=b, wt=wt, kk=kk: e.tensor_copy(out=wt[:, kk, :], in_=stg[b][:]), reads=[("fstg", b)], writes=[nm])
                    n += 1
            S.barrier()
        oa = [st.enter_context(nc.sbuf_tensor("s_foa%d" % i, [128, 6, 512], BF16)) for i in range(2)]
        ga = [st.enter_context(nc.sbuf_tensor("s_fga%d" % i, [128, 6, 512], BF16)) for i in range(2)]
        oh_ = [st.enter_context(nc.sbuf_tensor("s_foh%d" % i, [128, 6, 512], BF16)) for i in range(2)]
        gh = [st.enter_context(nc.sbuf_tensor("s_fgh%d" % i, [128, 6, 512], BF16)) for i in range(2)]
        mt = [st.enter_context(nc.sbuf_tensor("s_fmt%d" % i, [128, 16, 512], BF16)) for i in range(2)]
        mix = [st.enter_context(nc.sbuf_tensor("s_fmix%d" % i, [128, 8, 512], BF16)) for i in range(2)]
        ta = [st.enter_context(nc.sbuf_tensor("s_fta%d" % i, [128, 512], F32)) for i in range(2)]
        tb = [st.enter_context(nc.sbuf_tensor("s_ftb%d" % i, [128, 512], F32)) for i in range(2)]
        xt = [st.enter_context(nc.sbuf_tensor("s_fx%d" % i, [128, DM], F32)) for i in range(2)]
        r1 = [st.enter_context(nc.sbuf_tensor("s_fr%d" % i, [128, DM], F32)) for i in range(2)]
        yo = [st.enter_context(nc.sbuf_tensor("s_fy%d" % i, [128, DM], F32)) for i in range(2)]
        sqj = st.enter_context(nc.sbuf_tensor("s_fsq", [128, DM], BF16))
        ss = [st.enter_context(nc.sbuf_tensor("s_fss%d" % i, [128, 2], F32)) for i in range(2)]
        pn = 0
        tn_ = 0

        def fprep(ci):
            b = ci % 2
            cs = slice(ci * 512, (ci + 1) * 512)
            S.dma(oa[b][:], k.oaT[:, cs].rearrange("(a p) t -> p a t", p=128), writes=[("foa", b)])
            S.dma(ga[b][:], k.gaT[:, cs].rearrange("(a p) t -> p a t", p=128), writes=[("fga", b)])
            S.dma(oh_[b][:], k.ohT[:, cs].rearrange("(a p) t -> p a t", p=128), writes=[("foh", b)])
            S.dma(gh[b][:], k.ghT[:, cs].rearrange("(a p) t -> p a t", p=128), writes=[("fgh", b)])
            S.dma(mt[b][:], k.mT[:, cs].rearrange("(a p) t -> p a t", p=128), writes=[("fmt", b)])
            S.op("pool", lambda e, b=b: e.tensor_tensor(out=oa[b][:], in0=oa[b][:], in1=ga[b][:], op=ALU.mult),
                 reads=[("foa", b), ("fga", b)], writes=[("foa", b)])
            S.op("dve", lambda e, b=b: e.tensor_tensor(out=oh_[b][:], in0=oh_[b][:], in1=gh[b][:], op=ALU.mult),
                 reads=[("foh", b), ("fgh", b)], writes=[("foh", b)])

        def frun(ci):
            nonlocal pn, tn_
            seg = 0 if ci < NCHUNK // 2 else 1
            b = ci % 2
            for fb in range(8):
                pA, pAr = ps[pn % 4], ("ps", pn % 4)
                pn += 1
                pH, pHr = ps[pn % 4], ("ps", pn % 4)
                pn += 1
                for kk in range(6):
                    S.op("pe", lambda e, pA=pA, kk=kk, fb=fb, b=b: e.matmul(pA[:, :], lhsT=wpa[:, kk, fb * 128:(fb + 1) * 128], rhs=oa[b][:, kk, :],
                                                                             start=(kk == 0), stop=(kk == 5)), reads=["wpa", ("foa", b)], writes=[pAr])
                for kk in range(6):
                    S.op("pe", lambda e, pH=pH, kk=kk, fb=fb, b=b: e.matmul(pH[:, :], lhsT=wph[:, kk, fb * 128:(fb + 1) * 128], rhs=oh_[b][:, kk, :],
                                                                             start=(kk == 0), stop=(kk == 5)), reads=["wph", ("foh", b)], writes=[pHr])
                a_, ar_ = ta[tn_ % 2], ("fta", tn_ % 2)
                b_, br_ = tb[tn_ % 2], ("ftb", tn_ % 2)
                tn_ += 1
                S.op("dve", lambda e, a_=a_, pA=pA, fb=fb, b=b: e.tensor_tensor(out=a_[:], in0=pA[:, :], in1=mt[b][:, fb, :], op=ALU.mult),
                     reads=[pAr, ("fmt", b)], writes=[ar_])
                S.op("dve", lambda e, b_=b_, pH=pH, fb=fb, b=b: e.tensor_tensor(out=b_[:], in0=pH[:, :], in1=mt[b][:, 8 + fb, :], op=ALU.mult),
                     reads=[pHr, ("fmt", b)], writes=[br_])
                S.op("pool", lambda e, a_=a_, b_=b_, fb=fb, b=b: e.tensor_tensor(out=mix[b][:, fb, :], in0=a_[:], in1=b_[:], op=ALU.add),
                     reads=[ar_, br_], writes=[("fmix", b)])
            pend = []
            for tt in range(4):
                t = ci * 4 + tt
                tb2 = t % 2
                S.dma(xt[tb2][:], din["x"][t * 128:(t + 1) * 128, :], writes=[("fx", tb2)], eng="act")
                p0, p1 = ps[4 + 2 * tb2], ps[5 + 2 * tb2]
                p0r, p1r = ("ps", 4 + 2 * tb2), ("ps", 5 + 2 * tb2)
                for half, (pp_, ppr) in enumerate(((p0, p0r), (p1, p1r))):
                    for fb in range(8):
                        S.op("pe", lambda e, pp_=pp_, fb=fb, tt=tt, half=half, b=b, seg=seg: e.matmul(
                            pp_[:, :], lhsT=mix[b][:, fb, tt * 128:(tt + 1) * 128], rhs=wo[:, seg, fb, half * 512:(half + 1) * 512],
                            start=(fb == 0), stop=(fb == 7)), reads=[("fmix", b), "wo"], writes=[ppr])
                rr, rrr = r1[tb2], ("fr", tb2)
                S.op("dve", lambda e, rr=rr, tb2=tb2: e.tensor_tensor(
                    out=rr[:], in0=k.pp[2 + tb2][:, :], in1=xt[tb2][:], op=ALU.add),
                    reads=[p0r, p1r, ("fx", tb2)], writes=[rrr])
                sb, sr = ss[tb2], ("fss", tb2)
                S.op("act", lambda e, rr=rr, sb=sb: e.activation(out=sqj[:], in_=rr[:], func=AF.Square, scale=1.0 / 32.0, accum_out=sb[:, 0:1]),
                     reads=[rrr], writes=["fsq", sr])
                S.op("act", lambda e, sb=sb: e.activation(out=sb[:, 1:2], in_=sb[:, 0:1], func=AF.Sqrt, bias=k.epsc[:, 0:1]),
                     reads=[sr, "epsc"], writes=[sr])

                def stage2(rr=rr, rrr=rrr, sb=sb, sr=sr, tb2=tb2, t=t):
                    S.op("dve", lambda e: e.reciprocal(out=sb[:, 1:2], in_=sb[:, 1:2]), reads=[sr], writes=[sr])
                    yb, yr = yo[tb2], ("fy", tb2)
                    S.op("dve", lambda e: e.scalar_tensor_tensor(
                        out=yb[:], in0=rr[:], scalar=sb[:, 1:2], in1=k.fg_rep[:], op0=ALU.mult, op1=ALU.mult),
                        reads=[rrr, sr, "fg_rep"], writes=[yr])
                    S.dma(k.y[t * 128:(t + 1) * 128, :], yb[:], reads=[yr], eng="pool")

                pend.append(stage2)
                if len(pend) > 1:
                    pend.pop(0)()
            while pend:
                pend.pop(0)()

        fprep(0)
        for ci in range(NCHUNK):
            if ci + 1 < NCHUNK:
                fprep(ci + 1)
            frun(ci)
```
